# Optimizing a Trainium2 kernel written in Bass

```python
import jax, jax.numpy as jnp
from jax import lax
import numpy as np

D_MODEL = 1024
BATCH = 4
SEQ = 4096
DEPTH = 2

CHUNK = 64
LEFT_CHUNKS = 8
BAND_CHUNKS = LEFT_CHUNKS + 1
N_HEADS = 16
HEAD_DIM = D_MODEL // N_HEADS
REL_CLIP = 128
LRU_WIDTH = ((4 * D_MODEL // 3 + 127) // 128) * 128
N_LRU_BLOCKS = 16
LRU_BLOCK = LRU_WIDTH // N_LRU_BLOCKS
LRU_C = 8.0
CONV_WIDTH = 4
D_FF = 4 * D_MODEL
N_A_LAYERS = DEPTH // 2
N_B_LAYERS = DEPTH - N_A_LAYERS
EPS = 1e-6

kernel_name = "yoco_rglru_chunk_relbias_trunk"


def rms_norm(x, g):
    xf = x.astype(jnp.float32)
    y = xf * lax.rsqrt(jnp.mean(xf * xf, axis=-1, keepdims=True) + EPS)
    return (y * g.astype(jnp.float32)).astype(x.dtype)


def modulate(x, shift, scale):
    return x * (1 + scale[:, None, :]) + shift[:, None, :]


def causal_depthwise_conv(x, w, b):
    y = lax.conv_general_dilated(
        x, w[:, None, :].astype(x.dtype), window_strides=(1,),
        padding=[(CONV_WIDTH - 1, 0)],
        dimension_numbers=("NWC", "WIO", "NWC"),
        feature_group_count=x.shape[-1])
    return y + b


def rg_lru(x, w_a, b_a, w_x, b_x, lam):
    bsz, s, w = x.shape
    xb = x.reshape(bsz, s, N_LRU_BLOCKS, LRU_BLOCK)
    r = jax.nn.sigmoid(jnp.einsum("bsnd,nde->bsne", xb, w_a).reshape(bsz, s, w) + b_a)
    i = jax.nn.sigmoid(jnp.einsum("bsnd,nde->bsne", xb, w_x).reshape(bsz, s, w) + b_x)
    log_a = (-LRU_C * r.astype(jnp.float32)) * jax.nn.softplus(-lam.astype(jnp.float32))
    a = jnp.exp(log_a)
    mult = jnp.sqrt(-jnp.expm1(2.0 * log_a))
    u = mult * (i * x).astype(jnp.float32)

    def combine(left, right):
        a_l, b_l = left
        a_r, b_r = right
        return a_r * a_l, a_r * b_l + b_r

    _, h = lax.associative_scan(combine, (a, u), axis=1)
    return h.astype(x.dtype)


def recurrent_mixer(hn, w_in, conv_w, conv_b, gate_a_w, gate_a_b, gate_x_w, gate_x_b, lam, w_out):
    z = hn @ w_in
    gate, xr = jnp.split(z, 2, axis=-1)
    xr = causal_depthwise_conv(xr, conv_w, conv_b)
    y = rg_lru(xr, gate_a_w, gate_a_b, gate_x_w, gate_x_b, lam)
    return (y * jax.nn.gelu(gate)) @ w_out


def shared_kv(h, c_act, kv_norm_g, kv_ada_w, kv_ada_b, kv_w, k_norm_g):
    bsz, s, _ = h.shape
    shift, scale = jnp.split(c_act @ kv_ada_w + kv_ada_b, 2, axis=-1)
    hn = modulate(rms_norm(h, kv_norm_g), shift, scale)
    k, v = jnp.split(hn @ kv_w, 2, axis=-1)
    k = rms_norm(k.reshape(bsz, s, N_HEADS, HEAD_DIM), k_norm_g)
    v = v.reshape(bsz, s, N_HEADS, HEAD_DIM)
    pad = ((0, 0), (LEFT_CHUNKS * CHUNK, 0), (0, 0), (0, 0))
    return jnp.pad(k, pad), jnp.pad(v, pad)


def chunk_attention(hn, w_q, q_norm_g, rel_bias, w_o, k_pad, v_pad):
    bsz, s, d = hn.shape
    n_chunks = s // CHUNK
    band = BAND_CHUNKS * CHUNK
    left = LEFT_CHUNKS * CHUNK
    q = rms_norm((hn @ w_q).reshape(bsz, s, N_HEADS, HEAD_DIM), q_norm_g)
    q_chunks = q.reshape(bsz, n_chunks, CHUNK, N_HEADS, HEAD_DIM).transpose(1, 0, 2, 3, 4)
    rel = jnp.arange(CHUNK)[:, None] - (jnp.arange(band)[None, :] - left)
    rel_idx = jnp.clip(rel, -REL_CLIP, REL_CLIP) + REL_CLIP
    bias = rel_bias[:, rel_idx].astype(jnp.float32)
    scale = HEAD_DIM ** -0.5

    def one_chunk(args):
        qc, ci = args
        start = ci * CHUNK
        kc = lax.dynamic_slice_in_dim(k_pad, start, band, axis=1)
        vc = lax.dynamic_slice_in_dim(v_pad, start, band, axis=1)
        sc = jnp.einsum("bqhd,bkhd->bhqk", qc, kc,
                        preferred_element_type=jnp.float32) * scale + bias
        valid = (start - left + jnp.arange(band)) >= 0
        sc = jnp.where(valid[None, None, None, :], sc, -1e30)
        p = jax.nn.softmax(sc, axis=-1).astype(vc.dtype)
        return jnp.einsum("bhqk,bkhd->bqhd", p, vc)

    out = lax.map(one_chunk, (q_chunks, jnp.arange(n_chunks)))
    out = out.transpose(1, 0, 2, 3, 4).reshape(bsz, s, d)
    return out @ w_o


def sqrelu_mlp(hn, w1, w2):
    return jnp.square(jax.nn.relu(hn @ w1)) @ w2


def setup_inputs(seed: int = 0) -> dict:
    key = jax.random.key(seed)
    ks = jax.random.split(key, 32)
    f32 = jnp.float32
    D, W = D_MODEL, LRU_WIDTH

    def nrm(k, shape, fan_in):
        return jax.random.normal(k, shape, f32) * (fan_in ** -0.5)

    u = jax.random.uniform(ks[10], (N_A_LAYERS, W), f32, 0.9, 0.999)
    a0 = u ** (1.0 / LRU_C)
    lru_lambda = jnp.log(a0) - jnp.log1p(-a0)
    return {
        "x": jax.random.normal(ks[0], (BATCH, SEQ, D), f32),
        "c": jax.random.normal(ks[1], (BATCH, D), f32),
        "ada_w": nrm(ks[2], (DEPTH, D, 6 * D), D),
        "ada_b": 0.02 * jax.random.normal(ks[3], (DEPTH, 6 * D), f32),
        "norm1_g": 1.0 + 0.05 * jax.random.normal(ks[4], (DEPTH, D), f32),
        "norm2_g": 1.0 + 0.05 * jax.random.normal(ks[5], (DEPTH, D), f32),
        "mlp_w1": nrm(ks[6], (DEPTH, D, D_FF), D),
        "mlp_w2": nrm(ks[7], (DEPTH, D_FF, D), D_FF),
        "lru_w_in": nrm(ks[8], (N_A_LAYERS, D, 2 * W), D),
        "lru_conv_w": nrm(ks[9], (N_A_LAYERS, CONV_WIDTH, W), CONV_WIDTH),
        "lru_conv_b": 0.02 * jax.random.normal(ks[11], (N_A_LAYERS, W), f32),
        "lru_gate_a_w": nrm(ks[12], (N_A_LAYERS, N_LRU_BLOCKS, LRU_BLOCK, LRU_BLOCK), LRU_BLOCK),
        "lru_gate_a_b": 0.02 * jax.random.normal(ks[13], (N_A_LAYERS, W), f32),
        "lru_gate_x_w": nrm(ks[14], (N_A_LAYERS, N_LRU_BLOCKS, LRU_BLOCK, LRU_BLOCK), LRU_BLOCK),
        "lru_gate_x_b": 0.02 * jax.random.normal(ks[15], (N_A_LAYERS, W), f32),
        "lru_lambda": lru_lambda,
        "lru_w_out": nrm(ks[16], (N_A_LAYERS, W, D), W),
        "kv_norm_g": 1.0 + 0.05 * jax.random.normal(ks[17], (D,), f32),
        "kv_ada_w": nrm(ks[18], (D, 2 * D), D),
        "kv_ada_b": 0.02 * jax.random.normal(ks[19], (2 * D,), f32),
        "kv_w": nrm(ks[20], (D, 2 * D), D),
        "k_norm_g": 1.0 + 0.05 * jax.random.normal(ks[21], (HEAD_DIM,), f32),
        "attn_w_q": nrm(ks[22], (N_B_LAYERS, D, D), D),
        "q_norm_g": 1.0 + 0.05 * jax.random.normal(ks[23], (N_B_LAYERS, HEAD_DIM), f32),
        "rel_bias": 0.5 * jax.random.normal(ks[24], (N_B_LAYERS, N_HEADS, 2 * REL_CLIP + 1), f32),
        "attn_w_o": nrm(ks[25], (N_B_LAYERS, D, D), D),
    }


def reference(x, c, ada_w, ada_b, norm1_g, norm2_g, mlp_w1, mlp_w2,
              lru_w_in, lru_conv_w, lru_conv_b, lru_gate_a_w, lru_gate_a_b,
              lru_gate_x_w, lru_gate_x_b, lru_lambda, lru_w_out,
              kv_norm_g, kv_ada_w, kv_ada_b, kv_w, k_norm_g,
              attn_w_q, q_norm_g, rel_bias, attn_w_o):
    c_act = jax.nn.silu(c)
    h = x
    k_pad = None
    v_pad = None
    for layer in range(DEPTH):
        mod = c_act @ ada_w[layer] + ada_b[layer]
        sh1, sc1, g1, sh2, sc2, g2 = jnp.split(mod, 6, axis=-1)
        hn = modulate(rms_norm(h, norm1_g[layer]), sh1, sc1)
        if layer < N_A_LAYERS:
            y = recurrent_mixer(hn, lru_w_in[layer], lru_conv_w[layer], lru_conv_b[layer],
                                lru_gate_a_w[layer], lru_gate_a_b[layer],
                                lru_gate_x_w[layer], lru_gate_x_b[layer],
                                lru_lambda[layer], lru_w_out[layer])
        else:
            j = layer - N_A_LAYERS
            y = chunk_attention(hn, attn_w_q[j], q_norm_g[j], rel_bias[j], attn_w_o[j],
                                k_pad, v_pad)
        h = h + g1[:, None, :] * y
        hn = modulate(rms_norm(h, norm2_g[layer]), sh2, sc2)
        h = h + g2[:, None, :] * sqrelu_mlp(hn, mlp_w1[layer], mlp_w2[layer])
        if layer == N_A_LAYERS - 1:
            k_pad, v_pad = shared_kv(h, c_act, kv_norm_g, kv_ada_w, kv_ada_b, kv_w, k_norm_g)
    return h
```

```python
import numpy as np
from contextlib import ExitStack
import concourse.bass as bass
import concourse.mybir as mybir
from concourse.bass_utils import run_bass_kernel_spmd

F32 = mybir.dt.float32
BF16 = mybir.dt.bfloat16
AF = mybir.ActivationFunctionType
ALU = mybir.AluOpType
AX = mybir.AxisListType

D = 1024
W = 1408
NJ = 11
DFF = 4096
TT = 512
NTILE = 8
NKV = 5
EPS = 1e-6


class Buf:
    __slots__ = ("name", "w", "r", "dsem", "dcnt")

    def __init__(self, name):
        self.name = name
        self.w = None
        self.r = []
        self.dsem = None
        self.dcnt = 0


class Prog:
    def __init__(self, nc, stack):
        self.nc = nc
        self.stack = stack
        self.eng = {"pe": nc.tensor, "act": nc.scalar, "dve": nc.vector, "pool": nc.gpsimd, "sp": nc.sync}
        self.sem = {k: stack.enter_context(nc.semaphore("s_" + k)) for k in self.eng}
        self.cnt = {k: 0 for k in self.eng}
        self.seen = {k: {} for k in self.eng}
        self.pending = {k: False for k in self.eng}
        self.dbufs = []
        self.nsem = 0

    def _wait(self, e, tok):
        if tok is None:
            return
        sem, val, key = tok
        if key == "pe" and e == "pe":
            return
        if self.seen[e].get(key, 0) >= val:
            return
        self.eng[e].wait_ge(sem, val)
        self.seen[e][key] = val

    def _hazards(self, e, reads, writes, group_sem=None):
        for b in reads:
            self._wait(e, b.w)
        for b in writes:
            if not (group_sem is not None and b.w is not None and b.w[0] is group_sem):
                self._wait(e, b.w)
            for t in b.r:
                self._wait(e, t)

    def op(self, e, ins_fn, reads=(), writes=(), signal=True):
        self._hazards(e, reads, writes)
        ins = ins_fn(self.eng[e])
        if signal:
            self.cnt[e] += 1
            ins.then_inc(self.sem[e], 1)
            tok = (self.sem[e], self.cnt[e], e)
            self.pending[e] = False
        else:
            assert e == "pe"
            tok = (self.sem[e], self.cnt[e] + 1, e)
            self.pending[e] = True
        for b in reads:
            b.r.append(tok)
        for b in writes:
            b.w = tok
            b.r = []
        return tok

    def dma(self, e, out, in_, sembuf, reads=(), writes=(), group=False, **kw):
        if sembuf.dsem is None:
            sembuf.dsem = self.stack.enter_context(self.nc.semaphore("d%d_%s" % (self.nsem, sembuf.name)))
            self.nsem += 1
            self.dbufs.append(sembuf)
        self._hazards(e, reads, writes, group_sem=sembuf.dsem if group else None)
        ins = self.eng[e].dma_start(out=out, in_=in_, **kw)
        sembuf.dcnt += 16
        ins.then_inc(sembuf.dsem, 16)
        tok = (sembuf.dsem, sembuf.dcnt, ("d", id(sembuf)))
        for b in reads:
            b.r.append(tok)
        for b in writes:
            b.w = tok
            b.r = []
        return tok

    def barrier(self, engines=None):
        toks = [(self.sem[k], self.cnt[k], k) for k in self.eng if self.cnt[k] > 0]
        toks += [(b.dsem, b.dcnt, ("d", id(b))) for b in self.dbufs]
        for e in (engines or self.eng):
            for t in toks:
                if t[2] == e:
                    continue
                self._wait(e, t)
        assert not self.pending["pe"]


def gate_pieces():
    pcs = []
    for n in range(16):
        lo, hi = 88 * n, 88 * n + 88
        for ci in range(NJ):
            r0, r1 = max(lo, 128 * ci), min(hi, 128 * ci + 128)
            if r0 >= r1:
                continue
            for co in range(NJ):
                c0, c1 = max(lo, 128 * co), min(hi, 128 * co + 128)
                if c0 >= c1:
                    continue
                d = ci - co + 1
                assert 0 <= d <= 2
                pcs.append((n, r0 - lo, r1 - lo, c0 - lo, c1 - lo, ci, co, d, r0 - 128 * ci, r1 - 128 * ci, c0 - 128 * co, c1 - 128 * co))
    return pcs


def gate_nbrs():
    nb = {co: set() for co in range(NJ)}
    for p in gate_pieces():
        nb[p[6]].add(p[5])
    return {co: sorted(v) for co, v in nb.items()}


VOFF = {}
_o = 0
for _name, _n in [("n1g0", 8), ("n2g0", 8), ("kvg", 8), ("n1g1", 8), ("n2g1", 8), ("convw", 44), ("convb", 11),
                  ("gab", 11), ("gxb", 11), ("lam", 11), ("gk", 1), ("gq", 1)]:
    VOFF[_name] = (_o, _n)
    _o += _n
NV = _o


def build(stop=None):
    nc = bass.Bass("TRN2", target_bir_lowering=False)
    dt_in = lambda name, shape: nc.dram_tensor(name, shape, F32, kind="ExternalInput").ap()
    xin = dt_in("xin", [NTILE * TT, D])
    ct_d = dt_in("ct", [128, 8])
    tflag_d = dt_in("tflag", [128, NTILE + 1])
    vecs_d = dt_in("vecs", [128, NV])
    ada_w = dt_in("ada_w", [2, D, 6 * D])
    ada_b = dt_in("ada_b", [2, 6 * D])
    kv_ada_w = dt_in("kv_ada_w", [D, 2 * D])
    kv_ada_b = dt_in("kv_ada_b", [1, 2 * D])
    w_in_d = dt_in("w_in", [D, 2 * W])
    ga_w_d = dt_in("ga_w", [16, 88, 88])
    gx_w_d = dt_in("gx_w", [16, 88, 88])
    w_out_d = dt_in("w_out", [W, D])
    w1_d = dt_in("mlp_w1", [2, D, DFF])
    w2_d = dt_in("mlp_w2", [2, DFF, D])
    kv_w_d = dt_in("kv_w", [D, 2 * D])
    wq_d = dt_in("wq", [D, D])
    wo_d = dt_in("wo", [D, D])
    biasT_d = dt_in("biasT", [128, 16 * 5 * 128])
    amask_d = dt_in("amask", [128, 5 * 128])
    ident_d = dt_in("ident", [128, 128])
    blk_d = dt_in("blk64", [128, 128])
    out_d = nc.dram_tensor("out", [4 * TT, D], F32, kind="ExternalOutput").ap()
    H = nc.dram_tensor("Hs", [NKV * TT, D], F32, kind="ExternalOutput" if stop else "Internal").ap()
    Gs = nc.dram_tensor("Gs", [4, 128, D], F32, kind="Internal").ap()

    with ExitStack() as top:
        P = Prog(nc, top)

        def SB(stack, name, shape, dt):
            return stack.enter_context(nc.sbuf_tensor("sb_" + name, shape, dt))

        def PS(stack, name, shape, dt):
            return stack.enter_context(nc.psum_tensor("ps_" + name, shape, dt))

        vecs = SB(top, "vecs", [128, NV], F32); b_vecs = Buf("vecs")
        tflag = SB(top, "tflag", [128, NTILE + 1], F32); b_tflag = Buf("tflag")
        identb = SB(top, "identb", [128, 128], BF16); b_identb = Buf("identb")
        Asc = SB(top, "Asc", [128, 5, 8], F32); Ash = SB(top, "Ash", [128, 5, 8], F32); b_mod = Buf("modvec")
        cch = SB(top, "cch", [128, 2, NJ], F32); b_cch = Buf("cch")
        nbias = SB(top, "nbias", [128, 2, NJ], F32); b_nbias = Buf("nbias")
        gqk = SB(top, "gqk", [128, 2], F32); b_gqk = Buf("gqk")

        P.dma("sp", vecs[:], vecs_d[:, :], b_vecs, writes=[b_vecs])
        P.dma("sp", tflag[:], tflag_d[:, :], b_tflag, writes=[b_tflag])
        P.dma("pool", identb[:], ident_d[:, :], b_identb, writes=[b_identb])

        def V(name):
            o, n = VOFF[name]
            return vecs[:, o:o + n]

        with ExitStack() as ph:
            ct = SB(ph, "ct", [128, 8], F32); b_ct = Buf("ct")
            t8a = SB(ph, "t8a", [128, 8], F32); t8b = SB(ph, "t8b", [128, 8], F32); b_t8 = Buf("t8")
            cact = SB(ph, "cact", [128, 8], F32); b_cact = Buf("cact")
            lc = SB(ph, "lc", [128, 8, 128], BF16); b_lc = Buf("lc")
            identf = SB(ph, "identf", [128, 128], F32); b_identf = Buf("identf")
            modrow = [SB(ph, "modrow%d" % i, [128, 6 * D], F32) for i in range(2)]
            kvrow = SB(ph, "kvrow", [128, 2 * D], F32)
            b_rows = [Buf("modrow0"), Buf("modrow1"), Buf("kvrow")]
            wch = [SB(ph, "wch%d" % i, [128, 8, 512], BF16) for i in range(3)]; b_wch = [Buf("wch%d" % i) for i in range(3)]
            dtmp = SB(ph, "dtmp", [128, 8, 128], F32); b_dtmp = Buf("dtmp")
            t11 = SB(ph, "t11", [128, NJ], F32); b_t11 = Buf("t11")
            pm = [PS(ph, "pm%d" % i, [128, 512], F32) for i in range(2)]; b_pm = [Buf("pm%d" % i) for i in range(2)]

            P.dma("sp", ct[:], ct_d[:, :], b_ct, writes=[b_ct])
            P.dma("sp", identf[:], ident_d[:, :], b_identf, writes=[b_identf])
            rows = [modrow[0], modrow[1], kvrow]
            P.dma("sp", modrow[0][:], ada_b[0:1, :].partition_broadcast(128), b_rows[0], writes=[b_rows[0]])
            P.dma("sp", modrow[1][:], ada_b[1:2, :].partition_broadcast(128), b_rows[1], writes=[b_rows[1]])
            P.dma("sp", kvrow[:], kv_ada_b[0:1, :].partition_broadcast(128), b_rows[2], writes=[b_rows[2]])
            P.op("act", lambda e: e.activation(out=t8a[:], in_=ct[:], func=AF.Exp, scale=-1.0), reads=[b_ct], writes=[b_t8])
            P.op("act", lambda e: e.activation(out=t8b[:], in_=t8a[:], func=AF.Ln, bias=1.0), reads=[b_t8], writes=[b_t8])
            P.op("act", lambda e: e.activation(out=t8a[:], in_=t8b[:], func=AF.Exp, scale=-1.0), reads=[b_t8], writes=[b_t8])
            P.op("dve", lambda e: e.tensor_tensor(out=cact[:], in0=t8a[:], in1=ct[:], op=ALU.mult), reads=[b_t8, b_ct], writes=[b_cact])
            P.op("dve", lambda e: e.tensor_copy(out=lc[:], in_=cact[:].unsqueeze(2).to_broadcast([128, 8, 128])), reads=[b_cact], writes=[b_lc])
            P.op("act", lambda e: e.activation(out=t11[:], in_=V("lam"), func=AF.Exp, scale=-1.0), reads=[b_vecs], writes=[b_t11])
            P.op("act", lambda e: e.activation(out=t11[:], in_=t11[:], func=AF.Ln, bias=1.0), reads=[b_t11], writes=[b_t11])
            P.op("dve", lambda e: e.tensor_scalar(out=cch[:, 0, :], in0=t11[:], scalar1=-8.0, scalar2=None, op0=ALU.mult), reads=[b_t11], writes=[b_cch])
            P.op("dve", lambda e: e.tensor_scalar(out=cch[:, 1, :], in0=t11[:], scalar1=-16.0, scalar2=None, op0=ALU.mult), reads=[b_t11], writes=[b_cch])
            P.op("dve", lambda e: e.tensor_scalar(out=nbias[:, 0, :], in0=V("gab"), scalar1=-1.0, scalar2=None, op0=ALU.mult), reads=[b_vecs], writes=[b_nbias])
            P.op("dve", lambda e: e.tensor_scalar(out=nbias[:, 1, :], in0=V("gxb"), scalar1=-1.0, scalar2=None, op0=ALU.mult), reads=[b_vecs], writes=[b_nbias])
            P.op("dve", lambda e: e.tensor_copy(out=gqk[:, 0:1], in_=V("gk")), reads=[b_vecs], writes=[b_gqk])
            P.op("dve", lambda e: e.tensor_scalar(out=gqk[:, 1:2], in0=V("gq"), scalar1=0.125, scalar2=None, op0=ALU.mult), reads=[b_vecs], writes=[b_gqk])

            srcs = [(ada_w[0], 6 * D, 0), (ada_w[1], 6 * D, 1), (kv_ada_w, 2 * D, 2)]
            it = 0
            for (wsrc, ncols, ri) in srcs:
                wv = wsrc.rearrange("(kc p) n -> p kc n", p=128)
                for j in range(ncols // 512):
                    wb_, bb_ = wch[it % 3], b_wch[it % 3]
                    pp, bp = pm[it % 2], b_pm[it % 2]
                    P.dma("pool", wb_[:], wv[:, :, j * 512:(j + 1) * 512], bb_, writes=[bb_])
                    for kc in range(8):
                        P.op("pe", lambda e, kc=kc, wb_=wb_, pp=pp: e.matmul(pp[:], lhsT=lc[:, kc, :], rhs=wb_[:, kc, :], start=(kc == 0), stop=(kc == 7)),
                             reads=[b_lc, bb_], writes=[bp], signal=(kc == 7))
                    rr = rows[ri]
                    P.op("dve", lambda e, rr=rr, pp=pp, j=j: e.tensor_tensor(out=rr[:, j * 512:(j + 1) * 512], in0=pp[:], in1=rr[:, j * 512:(j + 1) * 512], op=ALU.add),
                         reads=[bp, b_rows[ri]], writes=[b_rows[ri]])
                    it += 1

            specs = [(0, modrow[0], 0, 1, "n1g0"), (1, modrow[0], 3, 4, "n2g0"), (2, kvrow, 0, 1, "kvg"),
                     (3, modrow[1], 0, 1, "n1g1"), (4, modrow[1], 3, 4, "n2g1")]
            rbuf = {0: b_rows[0], 1: b_rows[0], 2: b_rows[2], 3: b_rows[1], 4: b_rows[1]}
            for (m, rr, sseg, cseg, gname) in specs:
                for (seg, dst) in [(sseg, Ash), (cseg, t8a)]:
                    P.op("dve", lambda e, rr=rr, seg=seg: e.tensor_tensor(out=dtmp[:], in0=rr[:, seg * D:(seg + 1) * D].rearrange("p (g k) -> p g k", k=128),
                                                                     in1=identf[:].unsqueeze(1).to_broadcast([128, 8, 128]), op=ALU.mult),
                         reads=[rbuf[m], b_identf], writes=[b_dtmp])
                    if dst is Ash:
                        P.op("dve", lambda e, m=m: e.tensor_reduce(out=Ash[:, m, :], in_=dtmp[:], axis=AX.X, op=ALU.add), reads=[b_dtmp], writes=[b_mod])
                    else:
                        P.op("dve", lambda e: e.tensor_reduce(out=t8a[:], in_=dtmp[:], axis=AX.X, op=ALU.add), reads=[b_dtmp], writes=[b_t8])
                P.op("dve", lambda e: e.tensor_scalar(out=t8b[:], in0=t8a[:], scalar1=1.0, scalar2=32.0, op0=ALU.add, op1=ALU.mult), reads=[b_t8], writes=[b_t8])
                P.op("dve", lambda e, m=m, gname=gname: e.tensor_tensor(out=Asc[:, m, :], in0=t8b[:], in1=V(gname), op=ALU.mult), reads=[b_t8, b_vecs], writes=[b_mod])
            for gi, (rr, seg, rb) in enumerate([(modrow[0], 2, b_rows[0]), (modrow[0], 5, b_rows[0]), (modrow[1], 2, b_rows[1]), (modrow[1], 5, b_rows[1])]):
                P.dma("sp", Gs[gi], rr[:, seg * D:(seg + 1) * D], rb, reads=[rb])
            P.barrier()

        def load_w(stack, name, src_view, shape, nsplit):
            t = SB(stack, name, shape, BF16)
            b = Buf(name)
            a = shape[1]
            step = (a + nsplit - 1) // nsplit
            for s0 in range(0, a, step):
                s1 = min(a, s0 + step)
                P.dma("pool", t[:, s0:s1, :], src_view[:, s0:s1, :], b, writes=[b], group=True)
            return t, b

        def norm_T(xt, b_xt, m, S):
            P.op("act", lambda e: e.activation(out=S["junk"][:], in_=xt[:, 0, :], func=AF.Square, accum_out=S["ss"][:, 0:1]), reads=[b_xt], writes=[S["b_junk"], S["b_ss"]])
            for s in range(1, 4):
                P.op("act", lambda e, s=s: e.activation(out=S["junk"][:], in_=xt[:, s, :], func=AF.Square, accum_out=S["ss"][:, s:s + 1]), reads=[b_xt], writes=[S["b_junk"], S["b_ss"]])
            P.op("act", lambda e: e.activation(out=S["ss"][:, 4:8], in_=S["ss"][:, 0:4], func=AF.Ln, bias=S["epsb"][:, 0:1]), reads=[S["b_ss"], S["b_epsb"]], writes=[S["b_ss"]])
            P.op("act", lambda e: e.activation(out=S["ss"][:, 8:12], in_=S["ss"][:, 4:8], func=AF.Exp, scale=-0.5), reads=[S["b_ss"]], writes=[S["b_ss"]])
            for s in range(4):
                P.op("dve", lambda e, s=s: e.tensor_scalar(out=S["xn"][:, s, :], in0=xt[:, s, :], scalar1=S["ss"][:, 8 + s:9 + s], scalar2=None, op0=ALU.mult),
                     reads=[b_xt, S["b_ss"]], writes=[S["b_xn"]])
            for fc in range(8):
                pT, bT = S["pT"][fc % 2], S["b_pT"][fc % 2]
                for s in range(4):
                    P.op("pe", lambda e, s=s, fc=fc, pT=pT: e.transpose(out=pT[:, s * 128:(s + 1) * 128], in_=S["xn"][:, s, fc * 128:(fc + 1) * 128], identity=identb[:]),
                         reads=[S["b_xn"], b_identb], writes=[bT], signal=(s == 3))
                P.op("act", lambda e, fc=fc, pT=pT: e.activation(out=S["hnT"][:, fc, :], in_=pT[:, 0:TT], func=AF.Identity, scale=Asc[:, m, fc:fc + 1], bias=Ash[:, m, fc:fc + 1]),
                     reads=[bT, b_mod], writes=[S["b_hnT"]])

        def alloc_norm(stack, pfx):
            S = {}
            S["junk"] = SB(stack, pfx + "junk", [128, D], BF16); S["b_junk"] = Buf("junk")
            S["ss"] = SB(stack, pfx + "ss", [128, 12], F32); S["b_ss"] = Buf("ss")
            S["epsb"] = SB(stack, pfx + "epsb", [128, 2], F32); S["b_epsb"] = Buf("epsb")
            S["hnT"] = SB(stack, pfx + "hnT", [128, 8, TT], BF16); S["b_hnT"] = Buf("hnT")
            S["pT"] = [PS(stack, pfx + "pT%d" % i, [128, 2 * TT], BF16) for i in range(2)]; S["b_pT"] = [Buf("pT0"), Buf("pT1")]
            P.op("dve", lambda e: e.memset(S["epsb"][:, 0:1], 1024.0 * EPS), writes=[S["b_epsb"]])
            P.op("dve", lambda e: e.memset(S["epsb"][:, 1:2], EPS), writes=[S["b_epsb"]])
            return S

        def load_g(stack, name, gi):
            g = SB(stack, name, [128, D], F32); b = Buf(name)
            P.dma("sp", g[:], Gs[gi], b, writes=[b])
            return g, b

        def resid_out(xt, b_xt, s, nh, po, bpo, gi, S):
            P.op("dve", lambda e: e.tensor_tensor(out=S["rtmp"][:], in0=po[:], in1=S["g"][:, nh * 512:(nh + 1) * 512], op=ALU.mult),
                 reads=[bpo, S["b_g"]], writes=[S["b_rtmp"]])
            P.op("dve", lambda e: e.tensor_tensor(out=xt[:, s, nh * 512:(nh + 1) * 512], in0=S["rtmp"][:], in1=xt[:, s, nh * 512:(nh + 1) * 512], op=ALU.add),
                 reads=[S["b_rtmp"], b_xt], writes=[b_xt])

        nbrs = gate_nbrs()
        with ExitStack() as ph:
            w_in_sb, b_w_in = load_w(ph, "w_in_sb", w_in_d.rearrange("(kc p) n -> p kc n", p=128), [128, 8, 2 * W], 4)
            wg = SB(ph, "wg", [128, 2, NJ * 3, 128], BF16); b_wg = Buf("wg")
            P.op("dve", lambda e: e.memset(wg[:], 0.0), writes=[b_wg])
            for gidx, gsrc in enumerate([ga_w_d, gx_w_d]):
                for (n, r0, r1, c0, c1, ci, co, d, p0, p1, q0, q1) in gate_pieces():
                    P.dma("pool", wg[p0:p1, gidx, co * 3 + d, q0:q1], gsrc[n, r0:r1, c0:c1], b_wg, writes=[b_wg], group=True)
            w_out_sb, b_w_out = load_w(ph, "w_out_sb", w_out_d.rearrange("(jc p) n -> p jc n", p=128), [128, NJ, D], 2)

            S = alloc_norm(ph, "p1")
            S["g"], S["b_g"] = load_g(ph, "p1g", 0)
            xts = [SB(ph, "p1xt%d" % i, [128, 4, D], F32) for i in range(2)]; b_xts = [Buf("xt0"), Buf("xt1")]
            xnyb = SB(ph, "p1xnyb", [128, NJ, TT], BF16)
            S["xn"] = xnyb[:, 0:8, :].rearrange("p a c -> p (a c)").rearrange("p (s d) -> p s d", s=4)
            b_xnyb = Buf("xnyb"); S["b_xn"] = b_xnyb
            xrb = [SB(ph, "p1xrb%d" % i, [128, TT + 3], F32) for i in range(3)]; b_xrb = [Buf("xrb%d" % i) for i in range(3)]
            halo = SB(ph, "p1halo", [128, NJ, 3], F32); b_halo = Buf("halo")
            xc = SB(ph, "p1xc", [128, NJ, TT], F32); b_xc = [Buf("xc%d" % j) for j in range(NJ)]
            xcb = SB(ph, "p1xcb", [128, NJ, TT], BF16); b_xcb = [Buf("xcb%d" % j) for j in range(NJ)]
            state = SB(ph, "p1state", [128, NJ], F32); b_state = Buf("state")
            NTMP = 2
            tA = [SB(ph, "p1tA%d" % i, [128, TT], F32) for i in range(NTMP)]; b_tA = [Buf("tA%d" % i) for i in range(NTMP)]
            tB = [SB(ph, "p1tB%d" % i, [128, TT], F32) for i in range(NTMP)]; b_tB = [Buf("tB%d" % i) for i in range(NTMP)]
            tC = [SB(ph, "p1tC%d" % i, [128, TT], F32) for i in range(NTMP)]; b_tC = [Buf("tC%d" % i) for i in range(NTMP)]
            tG = [SB(ph, "p1tG%d" % i, [128, TT], F32) for i in range(NTMP)]; b_tG = [Buf("tG%d" % i) for i in range(NTMP)]
            S["rtmp"] = tA[0]; S["b_rtmp"] = b_tA[0]
            pz = [PS(ph, "p1pz%d" % i, [128, TT], F32) for i in range(6)]; b_pz = [Buf("pz%d" % i) for i in range(6)]
            pzi = [0]

            def nextpz():
                i = pzi[0] % 6
                pzi[0] += 1
                return pz[i], b_pz[i]

            P.op("dve", lambda e: e.memset(halo[:], 0.0), writes=[b_halo])
            P.op("dve", lambda e: e.memset(state[:], 0.0), writes=[b_state])
            cw = V("convw")
            cb = V("convb")

            def load_x(t):
                P.dma("sp", xts[t % 2][:], xin[t * TT:(t + 1) * TT, :].rearrange("(s p) d -> p s d", p=128), b_xts[t % 2], writes=[b_xts[t % 2]])

            load_x(0)
            ntile_p1 = NTILE if stop != "p1a" else 5
            for t in range(ntile_p1):
                full = t >= 3
                xt, b_xt = xts[t % 2], b_xts[t % 2]
                if t + 1 < ntile_p1:
                    load_x(t + 1)
                norm_T(xt, b_xt, 0, S)
                hnT, b_hnT = S["hnT"], S["b_hnT"]
                for j in range(NJ):
                    oc = NJ + j
                    pp, bp = nextpz()
                    for kc in range(8):
                        P.op("pe", lambda e, kc=kc, oc=oc, pp=pp: e.matmul(pp[:], lhsT=w_in_sb[:, kc, oc * 128:(oc + 1) * 128], rhs=hnT[:, kc, :], start=(kc == 0), stop=(kc == 7)),
                             reads=[b_w_in, b_hnT], writes=[bp], signal=(kc == 7))
                    xr, bxr = xrb[j % 3], b_xrb[j % 3]
                    P.op("dve", lambda e, xr=xr, j=j: e.tensor_copy(out=xr[:, 0:3], in_=halo[:, j, :]), reads=[b_halo], writes=[bxr])
                    P.op("act", lambda e, xr=xr, pp=pp: e.activation(out=xr[:, 3:TT + 3], in_=pp[:], func=AF.Copy), reads=[bp], writes=[bxr])
                    P.op("act", lambda e, pp=pp, j=j: e.activation(out=xc[:, j, :], in_=pp[:], func=AF.Identity, scale=cw[:, j * 4 + 3:j * 4 + 4], bias=cb[:, j:j + 1]),
                         reads=[bp, b_vecs], writes=[b_xc[j]])
                    for k in range(3):
                        P.op("dve", lambda e, xr=xr, j=j, k=k: e.scalar_tensor_tensor(out=xc[:, j, :], in0=xr[:, k:k + TT], scalar=cw[:, j * 4 + k:j * 4 + k + 1], in1=xc[:, j, :],
                                                                                  op0=ALU.mult, op1=ALU.add),
                             reads=[bxr, b_xc[j], b_vecs], writes=[b_xc[j]])
                    P.op("dve", lambda e, j=j: e.tensor_copy(out=xcb[:, j, :], in_=xc[:, j, :]), reads=[b_xc[j]], writes=[b_xcb[j]])
                    P.op("dve", lambda e, xr=xr, j=j, t=t: e.tensor_scalar(out=halo[:, j, :], in0=xr[:, TT:TT + 3], scalar1=tflag[:, t + 1:t + 2], scalar2=None, op0=ALU.mult),
                         reads=[bxr, b_tflag], writes=[b_halo])
                for co in range(NJ):
                    pa, bpa = nextpz()
                    px, bpx = nextpz()
                    for gidx, (pg, bpg) in enumerate([(pa, bpa), (px, bpx)]):
                        cis = nbrs[co]
                        for n_, ci in enumerate(cis):
                            d = ci - co + 1
                            P.op("pe", lambda e, pg=pg, gidx=gidx, co=co, d=d, ci=ci, n_=n_, cis=cis: e.matmul(pg[:], lhsT=wg[:, gidx, co * 3 + d, :], rhs=xcb[:, ci, :],
                                                                                                     start=(n_ == 0), stop=(n_ == len(cis) - 1)),
                                 reads=[b_wg, b_xcb[ci]], writes=[bpg], signal=(n_ == len(cis) - 1))
                    k_ = co % NTMP
                    A_, bA = tA[k_], b_tA[k_]
                    B_, bB = tB[k_], b_tB[k_]
                    C_, bC = tC[k_], b_tC[k_]
                    P.op("act", lambda e, A_=A_, pa=pa, co=co: e.activation(out=A_[:], in_=pa[:], func=AF.Exp, scale=-1.0, bias=nbias[:, 0, co:co + 1]), reads=[bpa, b_nbias], writes=[bA])
                    P.op("act", lambda e, A_=A_: e.activation(out=A_[:], in_=A_[:], func=AF.Ln, bias=1.0), reads=[bA], writes=[bA])
                    P.op("act", lambda e, A_=A_: e.activation(out=A_[:], in_=A_[:], func=AF.Exp, scale=-1.0), reads=[bA], writes=[bA])
                    P.op("act", lambda e, B_=B_, px=px, co=co: e.activation(out=B_[:], in_=px[:], func=AF.Exp, scale=-1.0, bias=nbias[:, 1, co:co + 1]), reads=[bpx, b_nbias], writes=[bB])
                    P.op("act", lambda e, B_=B_: e.activation(out=B_[:], in_=B_[:], func=AF.Ln, bias=1.0), reads=[bB], writes=[bB])
                    P.op("act", lambda e, A_=A_, C_=C_, co=co: e.activation(out=C_[:], in_=A_[:], func=AF.Exp, scale=cch[:, 0, co:co + 1]), reads=[bA, b_cch], writes=[bC])
                    P.op("act", lambda e, A_=A_, co=co: e.activation(out=A_[:], in_=A_[:], func=AF.Exp, scale=cch[:, 1, co:co + 1]), reads=[bA, b_cch], writes=[bA])
                    P.op("dve", lambda e, A_=A_: e.tensor_scalar(out=A_[:], in0=A_[:], scalar1=0.9999999, scalar2=None, op0=ALU.min), reads=[bA], writes=[bA])
                    P.op("act", lambda e, A_=A_: e.activation(out=A_[:], in_=A_[:], func=AF.Ln, scale=-1.0, bias=1.0), reads=[bA], writes=[bA])
                    P.op("dve", lambda e, A_=A_, B_=B_: e.scalar_tensor_tensor(out=B_[:], in0=A_[:], scalar=0.5, in1=B_[:], op0=ALU.mult, op1=ALU.subtract), reads=[bA, bB], writes=[bB])
                    P.op("act", lambda e, B_=B_: e.activation(out=B_[:], in_=B_[:], func=AF.Exp), reads=[bB], writes=[bB])
                    P.op("dve", lambda e, B_=B_, co=co: e.tensor_tensor(out=B_[:], in0=B_[:], in1=xc[:, co, :], op=ALU.mult), reads=[bB, b_xc[co]], writes=[bB])
                    P.op("dve", lambda e, B_=B_, C_=C_, co=co: e.tensor_tensor_scan(out=xc[:, co, :], data0=C_[:], data1=B_[:], initial=state[:, co:co + 1], op0=ALU.mult, op1=ALU.add),
                         reads=[bB, bC, b_state], writes=[b_xc[co]])
                P.op("dve", lambda e, t=t: e.tensor_scalar(out=state[:], in0=xc[:, :, TT - 1], scalar1=tflag[:, t + 1:t + 2], scalar2=None, op0=ALU.mult),
                     reads=b_xc + [b_tflag], writes=[b_state])
                if not full:
                    continue
                for j in range(NJ):
                    pp, bp = nextpz()
                    for kc in range(8):
                        P.op("pe", lambda e, kc=kc, j=j, pp=pp: e.matmul(pp[:], lhsT=w_in_sb[:, kc, j * 128:(j + 1) * 128], rhs=hnT[:, kc, :], start=(kc == 0), stop=(kc == 7)),
                             reads=[b_w_in, b_hnT], writes=[bp], signal=(kc == 7))
                    G_, bG = tG[j % NTMP], b_tG[j % NTMP]
                    P.op("act", lambda e, G_=G_, pp=pp: e.activation(out=G_[:], in_=pp[:], func=AF.Gelu_apprx_tanh), reads=[bp], writes=[bG])
                    P.op("dve", lambda e, G_=G_, j=j: e.tensor_tensor(out=xnyb[:, j, :], in0=G_[:], in1=xc[:, j, :], op=ALU.mult), reads=[bG, b_xc[j]], writes=[b_xnyb])
                for s in range(4):
                    for nh in range(2):
                        pp, bp = nextpz()
                        for jc in range(NJ):
                            P.op("pe", lambda e, jc=jc, s=s, nh=nh, pp=pp: e.matmul(pp[:], lhsT=xnyb[:, jc, s * 128:(s + 1) * 128], rhs=w_out_sb[:, jc, nh * 512:(nh + 1) * 512],
                                                                                start=(jc == 0), stop=(jc == NJ - 1)),
                                 reads=[b_xnyb, b_w_out], writes=[bp], signal=(jc == NJ - 1))
                        resid_out(xt, b_xt, s, nh, pp, bp, 0, S)
                P.dma("sp", H[(t - 3) * TT:(t - 2) * TT, :].rearrange("(s p) d -> p s d", p=128), xt[:], b_xt, reads=[b_xt])
            P.barrier()
        if stop in ("p1", "p1a"):
            return nc

        def mlp_phase(l, tiles, m, gi, final):
            with ExitStack() as ph:
                w1_sb, b_w1 = load_w(ph, "w1_sb%d" % l, w1_d[l].rearrange("(kc p) n -> p kc n", p=128), [128, 8, DFF], 8)
                w2_sb, b_w2 = load_w(ph, "w2_sb%d" % l, w2_d[l].rearrange("(fc p) n -> p fc n", p=128), [128, 32, D], 8)
                S = alloc_norm(ph, "m%d" % l)
                S["g"], S["b_g"] = load_g(ph, "m%dg" % l, gi)
                xt = SB(ph, "m%dxt" % l, [128, 4, D], F32); b_xt = Buf("xt")
                hid = SB(ph, "m%dhid" % l, [128, 32, TT], BF16); b_hid = Buf("hid")
                S["xn"] = hid[:, 0:8, :].rearrange("p a c -> p (a c)").rearrange("p (s d) -> p s d", s=4)
                S["b_xn"] = b_hid
                rl = [SB(ph, "m%drl%d" % (l, i), [128, TT], BF16) for i in range(3)]; b_rl = [Buf("rl%d" % i) for i in range(3)]
                S["rtmp"] = SB(ph, "m%drtmp" % l, [128, 512], F32); S["b_rtmp"] = Buf("rtmp")
                pz = [PS(ph, "m%dpz%d" % (l, i), [128, TT], F32) for i in range(6)]; b_pz = [Buf("pz%d" % i) for i in range(6)]
                pzi = [0]

                def nextpz():
                    i = pzi[0] % 6
                    pzi[0] += 1
                    return pz[i], b_pz[i]

                for t in tiles:
                    hrow = H[(t - 3) * TT:(t - 2) * TT, :].rearrange("(s p) d -> p s d", p=128)
                    P.dma("sp", xt[:], hrow, b_xt, writes=[b_xt])
                    norm_T(xt, b_xt, m, S)
                    hnT, b_hnT = S["hnT"], S["b_hnT"]
                    for fc in range(32):
                        pp, bp = nextpz()
                        for kc in range(8):
                            P.op("pe", lambda e, kc=kc, fc=fc, pp=pp: e.matmul(pp[:], lhsT=w1_sb[:, kc, fc * 128:(fc + 1) * 128], rhs=hnT[:, kc, :], start=(kc == 0), stop=(kc == 7)),
                                 reads=[b_w1, b_hnT], writes=[bp], signal=(kc == 7))
                        r_, br = rl[fc % 3], b_rl[fc % 3]
                        P.op("act", lambda e, r_=r_, pp=pp: e.activation(out=r_[:], in_=pp[:], func=AF.Relu), reads=[bp], writes=[br])
                        P.op("dve", lambda e, r_=r_, fc=fc: e.tensor_tensor(out=hid[:, fc, :], in0=r_[:], in1=r_[:], op=ALU.mult), reads=[br], writes=[b_hid])
                    for s in range(4):
                        for nh in range(2):
                            pp, bp = nextpz()
                            for fc in range(32):
                                P.op("pe", lambda e, fc=fc, s=s, nh=nh, pp=pp: e.matmul(pp[:], lhsT=hid[:, fc, s * 128:(s + 1) * 128], rhs=w2_sb[:, fc, nh * 512:(nh + 1) * 512],
                                                                                    start=(fc == 0), stop=(fc == 31)),
                                     reads=[b_hid, b_w2], writes=[bp], signal=(fc == 31))
                            resid_out(xt, b_xt, s, nh, pp, bp, gi, S)
                    if final:
                        dst = out_d[(t - 4) * TT:(t - 3) * TT, :].rearrange("(s p) d -> p s d", p=128)
                    else:
                        dst = hrow
                    P.dma("sp", dst, xt[:], b_xt, reads=[b_xt])
                P.barrier()

        mlp_phase(0, range(3, 8), 1, 1, False)
        if stop == "p2":
            return nc

        with ExitStack() as ph:
            blkb = SB(ph, "blkb", [128, 128], BF16); b_blk = Buf("blkb")
            P.dma("pool", blkb[:], blk_d[:, :], b_blk, writes=[b_blk])
            expB = SB(ph, "expB", [128, 16, 640], BF16); b_expB = Buf("expB")
            KT = SB(ph, "KT", [128, 8, NKV * TT], BF16); b_KT = Buf("KT")
            Va = SB(ph, "Va", [128, NKV * 4, 16, 65], BF16); b_Va = Buf("Va")
            S = alloc_norm(ph, "p3")
            xt = SB(ph, "p3xt", [128, 4, D], F32); b_xt = Buf("xt")
            sq = [SB(ph, "p3sq%d" % i, [128, TT], BF16) for i in range(2)]; b_sq = [Buf("sq0"), Buf("sq1")]
            rs = [SB(ph, "p3rs%d" % i, [128, TT], F32) for i in range(2)]; b_rs = [Buf("rs0"), Buf("rs1")]
            pz = [PS(ph, "p3pz%d" % i, [128, TT], F32) for i in range(2)]; b_pz = [Buf("pz%d" % i) for i in range(2)]
            pS = [PS(ph, "p3pS%d" % i, [128, 1024], F32) for i in range(1)]; b_pS = [Buf("pS%d" % i) for i in range(1)]
            pO = [PS(ph, "p3pO%d" % i, [128, 4, 128], F32) for i in range(2)]; b_pO = [Buf("pO%d" % i) for i in range(2)]
            pzi = [0]

            def nextpz():
                i = pzi[0] % 2
                pzi[0] += 1
                return pz[i], b_pz[i]

            def headnorm(pp, bp, dstT, b_dst, fc, col0, gcol):
                k_ = fc % 2
                P.op("act", lambda e: e.activation(out=sq[k_][:], in_=pp[:], func=AF.Square), reads=[bp], writes=[b_sq[k_]])
                pm_, bpm = nextpz()
                P.op("pe", lambda e: e.matmul(pm_[:], lhsT=blkb[:], rhs=sq[k_][:], start=True, stop=True), reads=[b_blk, b_sq[k_]], writes=[bpm])
                P.op("act", lambda e: e.activation(out=rs[k_][:], in_=pm_[:], func=AF.Ln, bias=S["epsb"][:, 1:2]), reads=[bpm, S["b_epsb"]], writes=[b_rs[k_]])
                P.op("act", lambda e: e.activation(out=rs[k_][:], in_=rs[k_][:], func=AF.Exp, scale=-0.5), reads=[b_rs[k_]], writes=[b_rs[k_]])
                P.op("dve", lambda e: e.scalar_tensor_tensor(out=dstT[:, fc, col0:col0 + TT], in0=pp[:], scalar=gqk[:, gcol:gcol + 1], in1=rs[k_][:], op0=ALU.mult, op1=ALU.mult),
                     reads=[bp, b_gqk, b_rs[k_]], writes=[b_dst])

            with ExitStack() as p1s:
                kvw_sb, b_kvw = load_w(p1s, "kvw_sb", kv_w_d.rearrange("(kc p) n -> p kc n", p=128), [128, 8, 2 * D], 4)
                amask = SB(p1s, "amask", [128, 640], F32); b_amask = Buf("amask")
                P.dma("sp", amask[:], amask_d[:, :], b_amask, writes=[b_amask])
                btmp = [SB(p1s, "p3btmp%d" % i, [128, 640], F32) for i in range(2)]; b_btmp = [Buf("btmp0"), Buf("btmp1")]
                xn = SB(p1s, "p3xn", [128, 4, D], BF16); S["xn"] = xn; S["b_xn"] = Buf("xn")
                biasv = biasT_d.rearrange("p (h x) -> p h x", h=16)
                for h in range(16):
                    bt, bbt = btmp[h % 2], b_btmp[h % 2]
                    P.dma("sp", bt[:], biasv[:, h, :], bbt, writes=[bbt])
                    P.op("act", lambda e, bt=bt: e.activation(out=bt[:], in_=bt[:], func=AF.Exp), reads=[bbt], writes=[bbt])
                    P.op("dve", lambda e, h=h, bt=bt: e.tensor_tensor(out=expB[:, h, :], in0=bt[:], in1=amask[:], op=ALU.mult), reads=[bbt, b_amask], writes=[b_expB])
                P.op("dve", lambda e: e.memset(Va[:, :, :, 64:65], 1.0), writes=[b_Va])
                P.op("dve", lambda e: e.tensor_scalar(out=Va[:, 0:4, :, 64:65], in0=Va[:, 0:4, :, 64:65], scalar1=tflag[:, 4:5], scalar2=None, op0=ALU.mult),
                     reads=[b_tflag, b_Va], writes=[b_Va])
                for t in range(3, 8):
                    kt0 = t - 3
                    hrow = H[kt0 * TT:(kt0 + 1) * TT, :].rearrange("(s p) d -> p s d", p=128)
                    P.dma("sp", xt[:], hrow, b_xt, writes=[b_xt])
                    norm_T(xt, b_xt, 2, S)
                    hnT, b_hnT = S["hnT"], S["b_hnT"]
                    for fc in range(8):
                        pp, bp = nextpz()
                        for kc in range(8):
                            P.op("pe", lambda e, kc=kc, fc=fc, pp=pp: e.matmul(pp[:], lhsT=kvw_sb[:, kc, fc * 128:(fc + 1) * 128], rhs=hnT[:, kc, :], start=(kc == 0), stop=(kc == 7)),
                                 reads=[b_kvw, b_hnT], writes=[bp], signal=(kc == 7))
                        headnorm(pp, bp, KT, b_KT, fc, kt0 * TT, 0)
                    for s in range(4):
                        for nh in range(2):
                            pp, bp = nextpz()
                            for kc in range(8):
                                P.op("pe", lambda e, kc=kc, s=s, nh=nh, pp=pp: e.matmul(pp[:], lhsT=hnT[:, kc, s * 128:(s + 1) * 128], rhs=kvw_sb[:, kc, D + nh * 512:D + (nh + 1) * 512],
                                                                                    start=(kc == 0), stop=(kc == 7)),
                                     reads=[b_kvw, b_hnT], writes=[bp], signal=(kc == 7))
                            if t == 3:
                                P.op("act", lambda e, s=s, nh=nh, pp=pp, kt0=kt0: e.activation(out=Va[:, kt0 * 4 + s, nh * 8:(nh + 1) * 8, 0:64], in_=pp[:].rearrange("p (h d) -> p h d", d=64),
                                                                                           func=AF.Identity, scale=tflag[:, 4:5]),
                                     reads=[bp, b_tflag], writes=[b_Va])
                            else:
                                P.op("act", lambda e, s=s, nh=nh, pp=pp, kt0=kt0: e.activation(out=Va[:, kt0 * 4 + s, nh * 8:(nh + 1) * 8, 0:64], in_=pp[:].rearrange("p (h d) -> p h d", d=64),
                                                                                           func=AF.Copy),
                                     reads=[bp], writes=[b_Va])
                P.barrier()
            if stop == "p3a":
                return nc
            with ExitStack() as p2s:
                wq_sb, b_wq = load_w(p2s, "wq_sb", wq_d.rearrange("(kc p) n -> p kc n", p=128), [128, 8, D], 2)
                wo_sb, b_wo = load_w(p2s, "wo_sb", wo_d.rearrange("(kc p) n -> p kc n", p=128), [128, 8, D], 2)
                S["g"], S["b_g"] = load_g(p2s, "p3g", 2)
                QT = SB(p2s, "QT", [128, 8, TT], BF16); b_QT = Buf("QT")
                S["rtmp"] = SB(p2s, "p3rtmp", [128, 512], F32); S["b_rtmp"] = Buf("rtmp")
                Eb = [SB(p2s, "p3E%d" % i, [128, 640], BF16) for i in range(3)]; b_E = [Buf("E%d" % i) for i in range(3)]
                Pb = [SB(p2s, "p3P%d" % i, [128, 640], BF16) for i in range(3)]; b_P = [Buf("P%d" % i) for i in range(3)]
                rec = SB(p2s, "p3rec", [128, 2, 4], F32); b_rec = [Buf("rec0"), Buf("rec1")]
                attn = SB(p2s, "p3attn", [128, D], BF16); b_attn = Buf("attn")
                attnT = SB(p2s, "p3attnT", [128, 8, TT], BF16); b_attnT = Buf("attnT")
                S["xn"] = attnT[:].rearrange("p a c -> p (a c)").rearrange("p (s d) -> p s d", s=4); S["b_xn"] = b_attnT
                hcount = 0
                for t in range(4, 8):
                    kt0 = t - 3
                    hrow = H[kt0 * TT:(kt0 + 1) * TT, :].rearrange("(s p) d -> p s d", p=128)
                    P.dma("sp", xt[:], hrow, b_xt, writes=[b_xt])
                    norm_T(xt, b_xt, 3, S)
                    hnT, b_hnT = S["hnT"], S["b_hnT"]
                    for fc in range(8):
                        pp, bp = nextpz()
                        for kc in range(8):
                            P.op("pe", lambda e, kc=kc, fc=fc, pp=pp: e.matmul(pp[:], lhsT=wq_sb[:, kc, fc * 128:(fc + 1) * 128], rhs=hnT[:, kc, :], start=(kc == 0), stop=(kc == 7)),
                                 reads=[b_wq, b_hnT], writes=[bp], signal=(kc == 7))
                        headnorm(pp, bp, QT, b_QT, fc, 0, 1)
                    for qg in range(4):
                        G = kt0 * 4 + qg
                        for h4 in range(4):
                            pO_, bpO = pO[h4 % 2], b_pO[h4 % 2]
                            for hh in range(4):
                                h = h4 * 4 + hh
                                fc, hf = h // 2, h % 2
                                pS_, bpS = pS[0], b_pS[0]
                                for kt in range(5):
                                    P.op("pe", lambda e, kt=kt, fc=fc, hf=hf, qg=qg, G=G, pS_=pS_: e.matmul(pS_[:, kt * 128:(kt + 1) * 128],
                                                                                                     lhsT=KT[hf * 64:(hf + 1) * 64, fc, (G - 4 + kt) * 128:(G - 3 + kt) * 128],
                                                                                                     rhs=QT[hf * 64:(hf + 1) * 64, fc, qg * 128:(qg + 1) * 128], start=True, stop=True),
                                         reads=[b_KT, b_QT], writes=[bpS], signal=(kt == 4))
                                E_, bE = Eb[hcount % 3], b_E[hcount % 3]
                                P_, bP = Pb[hcount % 3], b_P[hcount % 3]
                                hcount += 1
                                P.op("act", lambda e, E_=E_, pS_=pS_: e.activation(out=E_[:], in_=pS_[:, 0:640], func=AF.Exp), reads=[bpS], writes=[bE])
                                P.op("dve", lambda e, E_=E_, P_=P_, h=h: e.tensor_tensor(out=P_[:], in0=E_[:], in1=expB[:, h, :], op=ALU.mult), reads=[bE, b_expB], writes=[bP])
                                for kt in range(5):
                                    P.op("pe", lambda e, kt=kt, hh=hh, h=h, G=G, P_=P_, pO_=pO_: e.matmul(pO_[:, hh, 0:65], lhsT=P_[:, kt * 128:(kt + 1) * 128], rhs=Va[:, G - 4 + kt, h, :],
                                                                                                   start=(kt == 0), stop=(kt == 4)),
                                         reads=[bP, b_Va], writes=[bpO], signal=(kt == 4))
                            r_ = h4 % 2
                            P.op("dve", lambda e, r_=r_, pO_=pO_: e.reciprocal(out=rec[:, r_, :], in_=pO_[:, :, 64]), reads=[bpO], writes=[b_rec[r_]])
                            P.op("dve", lambda e, r_=r_, pO_=pO_, h4=h4: e.tensor_tensor(out=attn[:, h4 * 256:(h4 + 1) * 256].rearrange("p (h d) -> p h d", d=64), in0=pO_[:, :, 0:64],
                                                                                    in1=rec[:, r_, :].unsqueeze(2).to_broadcast([128, 4, 64]), op=ALU.mult),
                                 reads=[bpO, b_rec[r_]], writes=[b_attn])
                        for fc in range(8):
                            pT, bT = S["pT"][fc % 2], S["b_pT"][fc % 2]
                            P.op("pe", lambda e, fc=fc, pT=pT: e.transpose(out=pT[:, 0:128], in_=attn[:, fc * 128:(fc + 1) * 128], identity=identb[:]), reads=[b_attn, b_identb], writes=[bT])
                            P.op("act", lambda e, fc=fc, pT=pT, qg=qg: e.activation(out=attnT[:, fc, qg * 128:(qg + 1) * 128], in_=pT[:, 0:128], func=AF.Copy), reads=[bT], writes=[b_attnT])
                    for s in range(4):
                        for nh in range(2):
                            pp, bp = nextpz()
                            for kc in range(8):
                                P.op("pe", lambda e, kc=kc, s=s, nh=nh, pp=pp: e.matmul(pp[:], lhsT=attnT[:, kc, s * 128:(s + 1) * 128], rhs=wo_sb[:, kc, nh * 512:(nh + 1) * 512],
                                                                                    start=(kc == 0), stop=(kc == 7)),
                                     reads=[b_attnT, b_wo], writes=[bp], signal=(kc == 7))
                            resid_out(xt, b_xt, s, nh, pp, bp, 2, S)
                    P.dma("sp", hrow, xt[:], b_xt, reads=[b_xt])
                P.barrier()
        if stop == "p3":
            return nc

        mlp_phase(1, range(4, 8), 4, 3, True)
    return nc


def _host_consts():
    ident = np.eye(128, dtype=np.float32)
    blk = np.zeros((128, 128), np.float32)
    blk[:64, :64] = 1.0 / 64
    blk[64:, 64:] = 1.0 / 64
    kk = np.arange(128)[:, None, None]
    kt = np.arange(5)[None, :, None]
    q = np.arange(128)[None, None, :]
    kidx = kt * 128 + kk
    ck = kidx // 64
    cq = q // 64
    amask = ((ck >= cq) & (ck <= cq + 8)).astype(np.float32)
    bidx = np.minimum(640 + q - kidx, 256)
    bidx = np.maximum(bidx, 0)
    return ident, blk, amask.reshape(128, 640), bidx


def _pvec(v, n):
    return np.ascontiguousarray(np.asarray(v, np.float32).reshape(n, 128).T)


def make_in_maps(inputs):
    f = lambda k: np.ascontiguousarray(np.asarray(inputs[k], dtype=np.float32))
    x = f("x"); c = f("c")
    ident, blk, amask, bidx = _host_consts()
    rel_bias = f("rel_bias")[0]
    biasT = rel_bias[:, bidx]
    biasT = np.ascontiguousarray(biasT.transpose(1, 0, 2, 3).reshape(128, 16 * 640))
    conv_w = f("lru_conv_w")[0]
    convw = np.ascontiguousarray(conv_w.reshape(4, NJ, 128).transpose(2, 1, 0).reshape(128, NJ * 4))
    vec_parts = {
        "n1g0": _pvec(f("norm1_g")[0], 8), "n2g0": _pvec(f("norm2_g")[0], 8), "kvg": _pvec(f("kv_norm_g"), 8),
        "n1g1": _pvec(f("norm1_g")[1], 8), "n2g1": _pvec(f("norm2_g")[1], 8), "convw": convw,
        "convb": _pvec(f("lru_conv_b")[0], NJ), "gab": _pvec(f("lru_gate_a_b")[0], NJ), "gxb": _pvec(f("lru_gate_x_b")[0], NJ),
        "lam": _pvec(f("lru_lambda")[0], NJ),
        "gk": np.tile(f("k_norm_g"), 2).reshape(128, 1), "gq": np.tile(f("q_norm_g")[0], 2).reshape(128, 1),
    }
    vecs = np.zeros((128, NV), np.float32)
    for k, (o, n) in VOFF.items():
        vecs[:, o:o + n] = vec_parts[k]
    shared = {
        "vecs": vecs, "ada_w": f("ada_w"), "ada_b": f("ada_b"), "kv_ada_w": f("kv_ada_w"), "kv_ada_b": f("kv_ada_b").reshape(1, -1),
        "w_in": f("lru_w_in")[0], "ga_w": f("lru_gate_a_w")[0], "gx_w": f("lru_gate_x_w")[0], "w_out": f("lru_w_out")[0],
        "mlp_w1": f("mlp_w1"), "mlp_w2": f("mlp_w2"), "kv_w": f("kv_w"), "wq": f("attn_w_q")[0], "wo": f("attn_w_o")[0],
        "biasT": biasT, "amask": amask, "ident": ident, "blk64": blk,
    }
    maps = []
    for core in range(8):
        b, half = core // 2, core % 2
        if half == 1:
            xin = x[b]
        else:
            xin = np.concatenate([np.zeros((2048, D), np.float32), x[b, :2048]], axis=0)
        tflag = np.ones((128, NTILE + 1), np.float32)
        tflag[:, 4] = float(half)
        m = dict(shared)
        m["xin"] = np.ascontiguousarray(xin)
        m["ct"] = _pvec(c[b], 8)
        m["tflag"] = tflag
        maps.append(m)
    return maps


def kernel(**inputs):
    nc = build()
    maps = make_in_maps(inputs)
    res = run_bass_kernel_spmd(nc, maps, core_ids=list(range(8)))
    out = np.zeros((4, 4096, D), np.float32)
    for core in range(8):
        b, half = core // 2, core % 2
        out[b, half * 2048:(half + 1) * 2048] = res.results[core]["out"]
    return out
```

```python
import numpy as np
from contextlib import ExitStack
import concourse.bass as bass
import concourse.mybir as mybir
from concourse.bass_utils import run_bass_kernel_spmd

F32 = mybir.dt.float32
BF16 = mybir.dt.bfloat16
AF = mybir.ActivationFunctionType
ALU = mybir.AluOpType
AX = mybir.AxisListType

D = 1024
W = 1408
NJ = 11
DFF = 4096
TT = 512
NTILE = 8
NKV = 5
EPS = 1e-6


class Buf:
    __slots__ = ("name", "w", "r", "dsem", "dcnt")

    def __init__(self, name):
        self.name = name
        self.w = None
        self.r = []
        self.dsem = None
        self.dcnt = 0


class Prog:
    def __init__(self, nc, stack):
        self.nc = nc
        self.stack = stack
        self.eng = {"pe": nc.tensor, "act": nc.scalar, "dve": nc.vector, "pool": nc.gpsimd, "sp": nc.sync}
        self.sem = {k: stack.enter_context(nc.semaphore("s_" + k)) for k in self.eng}
        self.cnt = {k: 0 for k in self.eng}
        self.seen = {k: {} for k in self.eng}
        self.pending = {k: False for k in self.eng}
        self.dbufs = []
        self.nsem = 0

    def _wait(self, e, tok):
        if tok is None:
            return
        sem, val, key = tok
        if key == "pe" and e == "pe":
            return
        if self.seen[e].get(key, 0) >= val:
            return
        self.eng[e].wait_ge(sem, val)
        self.seen[e][key] = val

    def _hazards(self, e, reads, writes, group_sem=None):
        for b in reads:
            self._wait(e, b.w)
        for b in writes:
            if not (group_sem is not None and b.w is not None and b.w[0] is group_sem):
                self._wait(e, b.w)
            for t in b.r:
                self._wait(e, t)

    def op(self, e, ins_fn, reads=(), writes=(), signal=True):
        self._hazards(e, reads, writes)
        ins = ins_fn(self.eng[e])
        if signal:
            self.cnt[e] += 1
            ins.then_inc(self.sem[e], 1)
            tok = (self.sem[e], self.cnt[e], e)
            self.pending[e] = False
        else:
            assert e == "pe"
            tok = (self.sem[e], self.cnt[e] + 1, e)
            self.pending[e] = True
        for b in reads:
            b.r.append(tok)
        for b in writes:
            b.w = tok
            b.r = []
        return tok

    def dma(self, e, out, in_, sembuf, reads=(), writes=(), group=False, **kw):
        if sembuf.dsem is None:
            sembuf.dsem = self.stack.enter_context(self.nc.semaphore("d%d_%s" % (self.nsem, sembuf.name)))
            self.nsem += 1
            self.dbufs.append(sembuf)
        self._hazards(e, reads, writes, group_sem=sembuf.dsem if group else None)
        ins = self.eng[e].dma_start(out=out, in_=in_, **kw)
        sembuf.dcnt += 16
        ins.then_inc(sembuf.dsem, 16)
        tok = (sembuf.dsem, sembuf.dcnt, ("d", id(sembuf)))
        for b in reads:
            b.r.append(tok)
        for b in writes:
            b.w = tok
            b.r = []
        return tok

    def barrier(self, engines=None):
        toks = [(self.sem[k], self.cnt[k], k) for k in self.eng if self.cnt[k] > 0]
        toks += [(b.dsem, b.dcnt, ("d", id(b))) for b in self.dbufs]
        for e in (engines or self.eng):
            for t in toks:
                if t[2] == e:
                    continue
                self._wait(e, t)
        assert not self.pending["pe"]


def gate_pieces():
    pcs = []
    for n in range(16):
        lo, hi = 88 * n, 88 * n + 88
        for ci in range(NJ):
            r0, r1 = max(lo, 128 * ci), min(hi, 128 * ci + 128)
            if r0 >= r1:
                continue
            for co in range(NJ):
                c0, c1 = max(lo, 128 * co), min(hi, 128 * co + 128)
                if c0 >= c1:
                    continue
                d = ci - co + 1
                assert 0 <= d <= 2
                pcs.append((n, r0 - lo, r1 - lo, c0 - lo, c1 - lo, ci, co, d, r0 - 128 * ci, r1 - 128 * ci, c0 - 128 * co, c1 - 128 * co))
    return pcs


def gate_nbrs():
    nb = {co: set() for co in range(NJ)}
    for p in gate_pieces():
        nb[p[6]].add(p[5])
    return {co: sorted(v) for co, v in nb.items()}


VOFF = {}
_o = 0
for _name, _n in [("n1g0", 8), ("n2g0", 8), ("kvg", 8), ("n1g1", 8), ("n2g1", 8), ("convw", 44), ("convb", 11),
                  ("gab", 11), ("gxb", 11), ("lam", 11), ("gk", 1), ("gq", 1)]:
    VOFF[_name] = (_o, _n)
    _o += _n
NV = _o


def build(stop=None):
    nc = bass.Bass("TRN2", target_bir_lowering=False)
    dt_in = lambda name, shape: nc.dram_tensor(name, shape, F32, kind="ExternalInput").ap()
    xin = dt_in("xin", [NTILE * TT, D])
    ct_d = dt_in("ct", [128, 8])
    tflag_d = dt_in("tflag", [128, NTILE + 1])
    vecs_d = dt_in("vecs", [128, NV])
    ada_w = dt_in("ada_w", [2, D, 6 * D])
    ada_b = dt_in("ada_b", [2, 6 * D])
    kv_ada_w = dt_in("kv_ada_w", [D, 2 * D])
    kv_ada_b = dt_in("kv_ada_b", [1, 2 * D])
    w_in_d = dt_in("w_in", [D, 2 * W])
    ga_w_d = dt_in("ga_w", [16, 88, 88])
    gx_w_d = dt_in("gx_w", [16, 88, 88])
    w_out_d = dt_in("w_out", [W, D])
    w1_d = dt_in("mlp_w1", [2, D, DFF])
    w2_d = dt_in("mlp_w2", [2, DFF, D])
    kv_w_d = dt_in("kv_w", [D, 2 * D])
    wq_d = dt_in("wq", [D, D])
    wo_d = dt_in("wo", [D, D])
    biasT_d = dt_in("biasT", [128, 16 * 5 * 128])
    amask_d = dt_in("amask", [128, 5 * 128])
    ident_d = dt_in("ident", [128, 128])
    blk_d = dt_in("blk64", [128, 128])
    out_d = nc.dram_tensor("out", [4 * TT, D], F32, kind="ExternalOutput").ap()
    H = nc.dram_tensor("Hs", [NKV * TT, D], F32, kind="ExternalOutput" if stop else "Internal").ap()
    Gs = nc.dram_tensor("Gs", [4, 128, D], F32, kind="Internal").ap()

    with ExitStack() as top:
        P = Prog(nc, top)

        def SB(stack, name, shape, dt):
            return stack.enter_context(nc.sbuf_tensor("sb_" + name, shape, dt))

        def PS(stack, name, shape, dt):
            return stack.enter_context(nc.psum_tensor("ps_" + name, shape, dt))

        vecs = SB(top, "vecs", [128, NV], F32); b_vecs = Buf("vecs")
        tflag = SB(top, "tflag", [128, NTILE + 1], F32); b_tflag = Buf("tflag")
        identb = SB(top, "identb", [128, 128], BF16); b_identb = Buf("identb")
        Asc = SB(top, "Asc", [128, 5, 8], F32); Ash = SB(top, "Ash", [128, 5, 8], F32); b_mod = Buf("modvec")
        cch = SB(top, "cch", [128, 2, NJ], F32); b_cch = Buf("cch")
        nbias = SB(top, "nbias", [128, 2, NJ], F32); b_nbias = Buf("nbias")
        gqk = SB(top, "gqk", [128, 2], F32); b_gqk = Buf("gqk")

        P.dma("sp", vecs[:], vecs_d[:, :], b_vecs, writes=[b_vecs])
        P.dma("sp", tflag[:], tflag_d[:, :], b_tflag, writes=[b_tflag])
        P.dma("pool", identb[:], ident_d[:, :], b_identb, writes=[b_identb])

        def V(name):
            o, n = VOFF[name]
            return vecs[:, o:o + n]

        with ExitStack() as ph:
            ct = SB(ph, "ct", [128, 8], F32); b_ct = Buf("ct")
            t8a = SB(ph, "t8a", [128, 8], F32); t8b = SB(ph, "t8b", [128, 8], F32); b_t8 = Buf("t8")
            cact = SB(ph, "cact", [128, 8], F32); b_cact = Buf("cact")
            lc = SB(ph, "lc", [128, 8, 128], BF16); b_lc = Buf("lc")
            identf = SB(ph, "identf", [128, 128], F32); b_identf = Buf("identf")
            modrow = [SB(ph, "modrow%d" % i, [128, 6 * D], F32) for i in range(2)]
            kvrow = SB(ph, "kvrow", [128, 2 * D], F32)
            b_rows = [Buf("modrow0"), Buf("modrow1"), Buf("kvrow")]
            wch = [SB(ph, "wch%d" % i, [128, 8, 512], BF16) for i in range(3)]; b_wch = [Buf("wch%d" % i) for i in range(3)]
            dtmp = SB(ph, "dtmp", [128, 8, 128], F32); b_dtmp = Buf("dtmp")
            t11 = SB(ph, "t11", [128, NJ], F32); b_t11 = Buf("t11")
            pm = [PS(ph, "pm%d" % i, [128, 512], F32) for i in range(2)]; b_pm = [Buf("pm%d" % i) for i in range(2)]

            P.dma("sp", ct[:], ct_d[:, :], b_ct, writes=[b_ct])
            P.dma("sp", identf[:], ident_d[:, :], b_identf, writes=[b_identf])
            rows = [modrow[0], modrow[1], kvrow]
            P.dma("sp", modrow[0][:], ada_b[0:1, :].partition_broadcast(128), b_rows[0], writes=[b_rows[0]])
            P.dma("sp", modrow[1][:], ada_b[1:2, :].partition_broadcast(128), b_rows[1], writes=[b_rows[1]])
            P.dma("sp", kvrow[:], kv_ada_b[0:1, :].partition_broadcast(128), b_rows[2], writes=[b_rows[2]])
            P.op("act", lambda e: e.activation(out=t8a[:], in_=ct[:], func=AF.Exp, scale=-1.0), reads=[b_ct], writes=[b_t8])
            P.op("act", lambda e: e.activation(out=t8b[:], in_=t8a[:], func=AF.Ln, bias=1.0), reads=[b_t8], writes=[b_t8])
            P.op("act", lambda e: e.activation(out=t8a[:], in_=t8b[:], func=AF.Exp, scale=-1.0), reads=[b_t8], writes=[b_t8])
            P.op("dve", lambda e: e.tensor_tensor(out=cact[:], in0=t8a[:], in1=ct[:], op=ALU.mult), reads=[b_t8, b_ct], writes=[b_cact])
            P.op("dve", lambda e: e.tensor_copy(out=lc[:], in_=cact[:].unsqueeze(2).to_broadcast([128, 8, 128])), reads=[b_cact], writes=[b_lc])
            P.op("act", lambda e: e.activation(out=t11[:], in_=V("lam"), func=AF.Exp, scale=-1.0), reads=[b_vecs], writes=[b_t11])
            P.op("act", lambda e: e.activation(out=t11[:], in_=t11[:], func=AF.Ln, bias=1.0), reads=[b_t11], writes=[b_t11])
            P.op("dve", lambda e: e.tensor_scalar(out=cch[:, 0, :], in0=t11[:], scalar1=-8.0, scalar2=None, op0=ALU.mult), reads=[b_t11], writes=[b_cch])
            P.op("dve", lambda e: e.tensor_scalar(out=cch[:, 1, :], in0=t11[:], scalar1=-16.0, scalar2=None, op0=ALU.mult), reads=[b_t11], writes=[b_cch])
            P.op("dve", lambda e: e.tensor_scalar(out=nbias[:, 0, :], in0=V("gab"), scalar1=-1.0, scalar2=None, op0=ALU.mult), reads=[b_vecs], writes=[b_nbias])
            P.op("dve", lambda e: e.tensor_scalar(out=nbias[:, 1, :], in0=V("gxb"), scalar1=-1.0, scalar2=None, op0=ALU.mult), reads=[b_vecs], writes=[b_nbias])
            P.op("dve", lambda e: e.tensor_copy(out=gqk[:, 0:1], in_=V("gk")), reads=[b_vecs], writes=[b_gqk])
            P.op("dve", lambda e: e.tensor_scalar(out=gqk[:, 1:2], in0=V("gq"), scalar1=0.125, scalar2=None, op0=ALU.mult), reads=[b_vecs], writes=[b_gqk])

            srcs = [(ada_w[0], 6 * D, 0), (ada_w[1], 6 * D, 1), (kv_ada_w, 2 * D, 2)]
            it = 0
            for (wsrc, ncols, ri) in srcs:
                wv = wsrc.rearrange("(kc p) n -> p kc n", p=128)
                for j in range(ncols // 512):
                    wb_, bb_ = wch[it % 3], b_wch[it % 3]
                    pp, bp = pm[it % 2], b_pm[it % 2]
                    P.dma("pool", wb_[:], wv[:, :, j * 512:(j + 1) * 512], bb_, writes=[bb_])
                    for kc in range(8):
                        P.op("pe", lambda e, kc=kc, wb_=wb_, pp=pp: e.matmul(pp[:], lhsT=lc[:, kc, :], rhs=wb_[:, kc, :], start=(kc == 0), stop=(kc == 7)),
                             reads=[b_lc, bb_], writes=[bp], signal=(kc == 7))
                    rr = rows[ri]
                    P.op("dve", lambda e, rr=rr, pp=pp, j=j: e.tensor_tensor(out=rr[:, j * 512:(j + 1) * 512], in0=pp[:], in1=rr[:, j * 512:(j + 1) * 512], op=ALU.add),
                         reads=[bp, b_rows[ri]], writes=[b_rows[ri]])
                    it += 1

            specs = [(0, modrow[0], 0, 1, "n1g0"), (1, modrow[0], 3, 4, "n2g0"), (2, kvrow, 0, 1, "kvg"),
                     (3, modrow[1], 0, 1, "n1g1"), (4, modrow[1], 3, 4, "n2g1")]
            rbuf = {0: b_rows[0], 1: b_rows[0], 2: b_rows[2], 3: b_rows[1], 4: b_rows[1]}
            for (m, rr, sseg, cseg, gname) in specs:
                for (seg, dst) in [(sseg, Ash), (cseg, t8a)]:
                    P.op("dve", lambda e, rr=rr, seg=seg: e.tensor_tensor(out=dtmp[:], in0=rr[:, seg * D:(seg + 1) * D].rearrange("p (g k) -> p g k", k=128),
                                                                     in1=identf[:].unsqueeze(1).to_broadcast([128, 8, 128]), op=ALU.mult),
                         reads=[rbuf[m], b_identf], writes=[b_dtmp])
                    if dst is Ash:
                        P.op("dve", lambda e, m=m: e.tensor_reduce(out=Ash[:, m, :], in_=dtmp[:], axis=AX.X, op=ALU.add), reads=[b_dtmp], writes=[b_mod])
                    else:
                        P.op("dve", lambda e: e.tensor_reduce(out=t8a[:], in_=dtmp[:], axis=AX.X, op=ALU.add), reads=[b_dtmp], writes=[b_t8])
                P.op("dve", lambda e: e.tensor_scalar(out=t8b[:], in0=t8a[:], scalar1=1.0, scalar2=32.0, op0=ALU.add, op1=ALU.mult), reads=[b_t8], writes=[b_t8])
                P.op("dve", lambda e, m=m, gname=gname: e.tensor_tensor(out=Asc[:, m, :], in0=t8b[:], in1=V(gname), op=ALU.mult), reads=[b_t8, b_vecs], writes=[b_mod])
            for gi, (rr, seg, rb) in enumerate([(modrow[0], 2, b_rows[0]), (modrow[0], 5, b_rows[0]), (modrow[1], 2, b_rows[1]), (modrow[1], 5, b_rows[1])]):
                P.dma("sp", Gs[gi], rr[:, seg * D:(seg + 1) * D], rb, reads=[rb])
            P.barrier()

        def load_w(stack, name, src_view, shape, nsplit):
            t = SB(stack, name, shape, BF16)
            b = Buf(name)
            a = shape[1]
            step = (a + nsplit - 1) // nsplit
            for s0 in range(0, a, step):
                s1 = min(a, s0 + step)
                P.dma("pool", t[:, s0:s1, :], src_view[:, s0:s1, :], b, writes=[b], group=True)
            return t, b

        def load_w_cols(stack, name, src_view, shape, splits):
            t = SB(stack, name, shape, BF16)
            blocks = []
            for (c0, c1) in splits:
                b = Buf("%s_%d" % (name, c0))
                P.dma("pool", t[:, :, c0:c1], src_view[:, :, c0:c1], b, writes=[b])
                blocks.append((c0, c1, b))

            def bufs(c0, c1):
                return [b for (a0, a1, b) in blocks if a0 < c1 and c0 < a1]
            return t, bufs

        def norm_T(xt, b_xt, m, S):
            P.op("act", lambda e: e.activation(out=S["junk"][:], in_=xt[:, 0, :], func=AF.Square, accum_out=S["ss"][:, 0:1]), reads=[b_xt], writes=[S["b_junk"], S["b_ss"]])
            for s in range(1, 4):
                P.op("act", lambda e, s=s: e.activation(out=S["junk"][:], in_=xt[:, s, :], func=AF.Square, accum_out=S["ss"][:, s:s + 1]), reads=[b_xt], writes=[S["b_junk"], S["b_ss"]])
            P.op("act", lambda e: e.activation(out=S["ss"][:, 4:8], in_=S["ss"][:, 0:4], func=AF.Ln, bias=S["epsb"][:, 0:1]), reads=[S["b_ss"], S["b_epsb"]], writes=[S["b_ss"]])
            P.op("act", lambda e: e.activation(out=S["ss"][:, 8:12], in_=S["ss"][:, 4:8], func=AF.Exp, scale=-0.5), reads=[S["b_ss"]], writes=[S["b_ss"]])
            for s in range(4):
                P.op("dve", lambda e, s=s: e.tensor_scalar(out=S["xn"][:, s, :], in0=xt[:, s, :], scalar1=S["ss"][:, 8 + s:9 + s], scalar2=None, op0=ALU.mult),
                     reads=[b_xt, S["b_ss"]], writes=[S["b_xn"]])
            for fc in range(8):
                pT, bT = S["pT"][fc % 2], S["b_pT"][fc % 2]
                for s in range(4):
                    P.op("pe", lambda e, s=s, fc=fc, pT=pT: e.transpose(out=pT[:, s * 128:(s + 1) * 128], in_=S["xn"][:, s, fc * 128:(fc + 1) * 128], identity=identb[:]),
                         reads=[S["b_xn"], b_identb], writes=[bT], signal=(s == 3))
                P.op("act", lambda e, fc=fc, pT=pT: e.activation(out=S["hnT"][:, fc, :], in_=pT[:, 0:TT], func=AF.Identity, scale=Asc[:, m, fc:fc + 1], bias=Ash[:, m, fc:fc + 1]),
                     reads=[bT, b_mod], writes=[S["b_hnT"]])

        def alloc_norm(stack, pfx):
            S = {}
            S["junk"] = SB(stack, pfx + "junk", [128, D], BF16); S["b_junk"] = Buf("junk")
            S["ss"] = SB(stack, pfx + "ss", [128, 12], F32); S["b_ss"] = Buf("ss")
            S["epsb"] = SB(stack, pfx + "epsb", [128, 2], F32); S["b_epsb"] = Buf("epsb")
            S["hnT"] = SB(stack, pfx + "hnT", [128, 8, TT], BF16); S["b_hnT"] = Buf("hnT")
            S["pT"] = [PS(stack, pfx + "pT%d" % i, [128, 2 * TT], BF16) for i in range(2)]; S["b_pT"] = [Buf("pT0"), Buf("pT1")]
            P.op("dve", lambda e: e.memset(S["epsb"][:, 0:1], 1024.0 * EPS), writes=[S["b_epsb"]])
            P.op("dve", lambda e: e.memset(S["epsb"][:, 1:2], EPS), writes=[S["b_epsb"]])
            return S

        def load_g(stack, name, gi):
            g = SB(stack, name, [128, D], F32); b = Buf(name)
            P.dma("sp", g[:], Gs[gi], b, writes=[b])
            return g, b

        def resid_out(xt, b_xt, s, nh, po, bpo, gi, S):
            P.op("dve", lambda e: e.tensor_tensor(out=S["rtmp"][:], in0=po[:], in1=S["g"][:, nh * 512:(nh + 1) * 512], op=ALU.mult),
                 reads=[bpo, S["b_g"]], writes=[S["b_rtmp"]])
            P.op("dve", lambda e: e.tensor_tensor(out=xt[:, s, nh * 512:(nh + 1) * 512], in0=S["rtmp"][:], in1=xt[:, s, nh * 512:(nh + 1) * 512], op=ALU.add),
                 reads=[S["b_rtmp"], b_xt], writes=[b_xt])

        nbrs = gate_nbrs()
        with ExitStack() as ph:
            w_in_sb, w_in_bufs = load_w_cols(ph, "w_in_sb", w_in_d.rearrange("(kc p) n -> p kc n", p=128), [128, 8, 2 * W],
                                             [(W, W + 512), (W + 512, 2 * W), (0, 704), (704, W)])
            wg = SB(ph, "wg", [128, 2, NJ * 3, 128], BF16); b_wg = Buf("wg")
            P.op("dve", lambda e: e.memset(wg[:], 0.0), writes=[b_wg])
            for gidx, gsrc in enumerate([ga_w_d, gx_w_d]):
                for (n, r0, r1, c0, c1, ci, co, d, p0, p1, q0, q1) in gate_pieces():
                    P.dma("pool", wg[p0:p1, gidx, co * 3 + d, q0:q1], gsrc[n, r0:r1, c0:c1], b_wg, writes=[b_wg], group=True)
            w_out_sb, b_w_out = load_w(ph, "w_out_sb", w_out_d.rearrange("(jc p) n -> p jc n", p=128), [128, NJ, D], 2)

            S = alloc_norm(ph, "p1")
            S["g"], S["b_g"] = load_g(ph, "p1g", 0)
            xts = [SB(ph, "p1xt%d" % i, [128, 4, D], F32) for i in range(2)]; b_xts = [Buf("xt0"), Buf("xt1")]
            xnyb = SB(ph, "p1xnyb", [128, NJ, TT], BF16)
            S["xn"] = xnyb[:, 0:8, :].rearrange("p a c -> p (a c)").rearrange("p (s d) -> p s d", s=4)
            b_xnyb = Buf("xnyb"); S["b_xn"] = b_xnyb
            xrb = [SB(ph, "p1xrb%d" % i, [128, TT + 3], F32) for i in range(3)]; b_xrb = [Buf("xrb%d" % i) for i in range(3)]
            halo = SB(ph, "p1halo", [128, NJ, 3], F32); b_halo = Buf("halo")
            xc = SB(ph, "p1xc", [128, NJ, TT], F32); b_xc = [Buf("xc%d" % j) for j in range(NJ)]
            xcb = SB(ph, "p1xcb", [128, NJ, TT], BF16); b_xcb = [Buf("xcb%d" % j) for j in range(NJ)]
            state = SB(ph, "p1state", [128, NJ], F32); b_state = Buf("state")
            NTMP = 2
            tA = [SB(ph, "p1tA%d" % i, [128, TT], F32) for i in range(NTMP)]; b_tA = [Buf("tA%d" % i) for i in range(NTMP)]
            tB = [SB(ph, "p1tB%d" % i, [128, TT], F32) for i in range(NTMP)]; b_tB = [Buf("tB%d" % i) for i in range(NTMP)]
            tC = [SB(ph, "p1tC%d" % i, [128, TT], F32) for i in range(NTMP)]; b_tC = [Buf("tC%d" % i) for i in range(NTMP)]
            tG = [SB(ph, "p1tG%d" % i, [128, TT], F32) for i in range(NTMP)]; b_tG = [Buf("tG%d" % i) for i in range(NTMP)]
            S["rtmp"] = tA[0]; S["b_rtmp"] = b_tA[0]
            pz = [PS(ph, "p1pz%d" % i, [128, TT], F32) for i in range(6)]; b_pz = [Buf("pz%d" % i) for i in range(6)]
            pzi = [0]

            def nextpz():
                i = pzi[0] % 6
                pzi[0] += 1
                return pz[i], b_pz[i]

            P.op("dve", lambda e: e.memset(halo[:], 0.0), writes=[b_halo])
            P.op("dve", lambda e: e.memset(state[:], 0.0), writes=[b_state])
            cw = V("convw")
            cb = V("convb")

            def load_x(t):
                P.dma("sp", xts[t % 2][:], xin[t * TT:(t + 1) * TT, :].rearrange("(s p) d -> p s d", p=128), b_xts[t % 2], writes=[b_xts[t % 2]])

            load_x(0)
            ntile_p1 = NTILE if stop != "p1a" else 5
            for t in range(ntile_p1):
                full = t >= 3
                xt, b_xt = xts[t % 2], b_xts[t % 2]
                if t + 1 < ntile_p1:
                    load_x(t + 1)
                norm_T(xt, b_xt, 0, S)
                hnT, b_hnT = S["hnT"], S["b_hnT"]
                for j in range(NJ):
                    oc = NJ + j
                    pp, bp = nextpz()
                    for kc in range(8):
                        P.op("pe", lambda e, kc=kc, oc=oc, pp=pp: e.matmul(pp[:], lhsT=w_in_sb[:, kc, oc * 128:(oc + 1) * 128], rhs=hnT[:, kc, :], start=(kc == 0), stop=(kc == 7)),
                             reads=w_in_bufs(oc * 128, oc * 128 + 128) + [b_hnT], writes=[bp], signal=(kc == 7))
                    xr, bxr = xrb[j % 3], b_xrb[j % 3]
                    P.op("dve", lambda e, xr=xr, j=j: e.tensor_copy(out=xr[:, 0:3], in_=halo[:, j, :]), reads=[b_halo], writes=[bxr])
                    P.op("act", lambda e, xr=xr, pp=pp: e.activation(out=xr[:, 3:TT + 3], in_=pp[:], func=AF.Copy), reads=[bp], writes=[bxr])
                    P.op("act", lambda e, pp=pp, j=j: e.activation(out=xc[:, j, :], in_=pp[:], func=AF.Identity, scale=cw[:, j * 4 + 3:j * 4 + 4], bias=cb[:, j:j + 1]),
                         reads=[bp, b_vecs], writes=[b_xc[j]])
                    for k in range(3):
                        P.op("dve", lambda e, xr=xr, j=j, k=k: e.scalar_tensor_tensor(out=xc[:, j, :], in0=xr[:, k:k + TT], scalar=cw[:, j * 4 + k:j * 4 + k + 1], in1=xc[:, j, :],
                                                                                  op0=ALU.mult, op1=ALU.add),
                             reads=[bxr, b_xc[j], b_vecs], writes=[b_xc[j]])
                    P.op("dve", lambda e, j=j: e.tensor_copy(out=xcb[:, j, :], in_=xc[:, j, :]), reads=[b_xc[j]], writes=[b_xcb[j]])
                    P.op("dve", lambda e, xr=xr, j=j, t=t: e.tensor_scalar(out=halo[:, j, :], in0=xr[:, TT:TT + 3], scalar1=tflag[:, t + 1:t + 2], scalar2=None, op0=ALU.mult),
                         reads=[bxr, b_tflag], writes=[b_halo])
                def gate_stages(co):
                    k_ = co % NTMP
                    A_, bA = tA[k_], b_tA[k_]
                    B_, bB = tB[k_], b_tB[k_]
                    C_, bC = tC[k_], b_tC[k_]
                    pa, bpa = nextpz()
                    px, bpx = nextpz()
                    cis = nbrs[co]
                    st = []

                    def mm():
                        for gidx, (pg, bpg) in enumerate([(pa, bpa), (px, bpx)]):
                            for n_, ci in enumerate(cis):
                                d = ci - co + 1
                                P.op("pe", lambda e, pg=pg, gidx=gidx, d=d, ci=ci, n_=n_: e.matmul(pg[:], lhsT=wg[:, gidx, co * 3 + d, :], rhs=xcb[:, ci, :],
                                                                                             start=(n_ == 0), stop=(n_ == len(cis) - 1)),
                                     reads=[b_wg, b_xcb[ci]], writes=[bpg], signal=(n_ == len(cis) - 1))
                    st.append(mm)
                    st.append(lambda: P.op("act", lambda e: e.activation(out=A_[:], in_=pa[:], func=AF.Exp, scale=-1.0, bias=nbias[:, 0, co:co + 1]), reads=[bpa, b_nbias], writes=[bA]))
                    st.append(lambda: P.op("act", lambda e: e.activation(out=B_[:], in_=px[:], func=AF.Exp, scale=-1.0, bias=nbias[:, 1, co:co + 1]), reads=[bpx, b_nbias], writes=[bB]))
                    st.append(lambda: P.op("act", lambda e: e.activation(out=A_[:], in_=A_[:], func=AF.Ln, bias=1.0), reads=[bA], writes=[bA]))
                    st.append(lambda: P.op("act", lambda e: e.activation(out=B_[:], in_=B_[:], func=AF.Ln, bias=1.0), reads=[bB], writes=[bB]))
                    st.append(lambda: P.op("act", lambda e: e.activation(out=A_[:], in_=A_[:], func=AF.Exp, scale=-1.0), reads=[bA], writes=[bA]))
                    st.append(lambda: P.op("act", lambda e: e.activation(out=C_[:], in_=A_[:], func=AF.Exp, scale=cch[:, 0, co:co + 1]), reads=[bA, b_cch], writes=[bC]))
                    st.append(lambda: P.op("pool", lambda e: e.tensor_tensor(out=A_[:], in0=C_[:], in1=C_[:], op=ALU.mult), reads=[bC, bA], writes=[bA]))
                    st.append(lambda: P.op("act", lambda e: e.activation(out=A_[:], in_=A_[:], func=AF.Ln, scale=-1.0, bias=1.000001), reads=[bA], writes=[bA]))
                    st.append(lambda: P.op("dve", lambda e: e.scalar_tensor_tensor(out=B_[:], in0=A_[:], scalar=0.5, in1=B_[:], op0=ALU.mult, op1=ALU.subtract), reads=[bA, bB], writes=[bB]))
                    st.append(lambda: P.op("act", lambda e: e.activation(out=B_[:], in_=B_[:], func=AF.Exp), reads=[bB], writes=[bB]))
                    st.append(lambda: P.op("pool", lambda e: e.tensor_tensor(out=B_[:], in0=B_[:], in1=xc[:, co, :], op=ALU.mult), reads=[bB, b_xc[co]], writes=[bB]))
                    st.append(lambda: P.op("dve", lambda e: e.tensor_tensor_scan(out=xc[:, co, :], data0=C_[:], data1=B_[:], initial=state[:, co:co + 1], op0=ALU.mult, op1=ALU.add),
                                           reads=[bB, bC, b_state], writes=[b_xc[co]]))
                    return st

                for c0 in range(0, NJ, 2):
                    chains = [gate_stages(co) for co in range(c0, min(NJ, c0 + 2))]
                    for si in range(len(chains[0])):
                        for ch in chains:
                            ch[si]()

                P.op("dve", lambda e, t=t: e.tensor_scalar(out=state[:], in0=xc[:, :, TT - 1], scalar1=tflag[:, t + 1:t + 2], scalar2=None, op0=ALU.mult),
                     reads=b_xc + [b_tflag], writes=[b_state])
                if not full:
                    continue
                for j in range(NJ):
                    pp, bp = nextpz()
                    for kc in range(8):
                        P.op("pe", lambda e, kc=kc, j=j, pp=pp: e.matmul(pp[:], lhsT=w_in_sb[:, kc, j * 128:(j + 1) * 128], rhs=hnT[:, kc, :], start=(kc == 0), stop=(kc == 7)),
                             reads=w_in_bufs(j * 128, j * 128 + 128) + [b_hnT], writes=[bp], signal=(kc == 7))
                    G_, bG = tG[j % NTMP], b_tG[j % NTMP]
                    P.op("act", lambda e, G_=G_, pp=pp: e.activation(out=G_[:], in_=pp[:], func=AF.Gelu_apprx_tanh), reads=[bp], writes=[bG])
                    P.op("dve", lambda e, G_=G_, j=j: e.tensor_tensor(out=xnyb[:, j, :], in0=G_[:], in1=xc[:, j, :], op=ALU.mult), reads=[bG, b_xc[j]], writes=[b_xnyb])
                for s in range(4):
                    for nh in range(2):
                        pp, bp = nextpz()
                        for jc in range(NJ):
                            P.op("pe", lambda e, jc=jc, s=s, nh=nh, pp=pp: e.matmul(pp[:], lhsT=xnyb[:, jc, s * 128:(s + 1) * 128], rhs=w_out_sb[:, jc, nh * 512:(nh + 1) * 512],
                                                                                start=(jc == 0), stop=(jc == NJ - 1)),
                                 reads=[b_xnyb, b_w_out], writes=[bp], signal=(jc == NJ - 1))
                        resid_out(xt, b_xt, s, nh, pp, bp, 0, S)
                P.dma("sp", H[(t - 3) * TT:(t - 2) * TT, :].rearrange("(s p) d -> p s d", p=128), xt[:], b_xt, reads=[b_xt])
            P.barrier()
        if stop in ("p1", "p1a"):
            return nc

        def mlp_phase(l, tiles, m, gi, final):
            with ExitStack() as ph:
                w1_sb, w1_bufs = load_w_cols(ph, "w1_sb%d" % l, w1_d[l].rearrange("(kc p) n -> p kc n", p=128), [128, 8, DFF],
                                             [(i * 512, (i + 1) * 512) for i in range(8)])
                w2_sb, b_w2 = load_w(ph, "w2_sb%d" % l, w2_d[l].rearrange("(fc p) n -> p fc n", p=128), [128, 32, D], 8)
                S = alloc_norm(ph, "m%d" % l)
                S["g"], S["b_g"] = load_g(ph, "m%dg" % l, gi)
                xt = SB(ph, "m%dxt" % l, [128, 4, D], F32); b_xt = Buf("xt")
                hid = SB(ph, "m%dhid" % l, [128, 32, TT], BF16); b_hid = Buf("hid")
                S["xn"] = hid[:, 0:8, :].rearrange("p a c -> p (a c)").rearrange("p (s d) -> p s d", s=4)
                S["b_xn"] = b_hid
                rl = [SB(ph, "m%drl%d" % (l, i), [128, TT], BF16) for i in range(3)]; b_rl = [Buf("rl%d" % i) for i in range(3)]
                S["rtmp"] = SB(ph, "m%drtmp" % l, [128, 512], F32); S["b_rtmp"] = Buf("rtmp")
                pz = [PS(ph, "m%dpz%d" % (l, i), [128, TT], F32) for i in range(6)]; b_pz = [Buf("pz%d" % i) for i in range(6)]
                pzi = [0]

                def nextpz():
                    i = pzi[0] % 6
                    pzi[0] += 1
                    return pz[i], b_pz[i]

                for t in tiles:
                    hrow = H[(t - 3) * TT:(t - 2) * TT, :].rearrange("(s p) d -> p s d", p=128)
                    P.dma("sp", xt[:], hrow, b_xt, writes=[b_xt])
                    norm_T(xt, b_xt, m, S)
                    hnT, b_hnT = S["hnT"], S["b_hnT"]
                    for fc in range(32):
                        pp, bp = nextpz()
                        for kc in range(8):
                            P.op("pe", lambda e, kc=kc, fc=fc, pp=pp: e.matmul(pp[:], lhsT=w1_sb[:, kc, fc * 128:(fc + 1) * 128], rhs=hnT[:, kc, :], start=(kc == 0), stop=(kc == 7)),
                                 reads=w1_bufs(fc * 128, fc * 128 + 128) + [b_hnT], writes=[bp], signal=(kc == 7))
                        r_, br = rl[fc % 3], b_rl[fc % 3]
                        P.op("act", lambda e, r_=r_, pp=pp: e.activation(out=r_[:], in_=pp[:], func=AF.Relu), reads=[bp], writes=[br])
                        P.op("dve", lambda e, r_=r_, fc=fc: e.tensor_tensor(out=hid[:, fc, :], in0=r_[:], in1=r_[:], op=ALU.mult), reads=[br], writes=[b_hid])
                    for s in range(4):
                        for nh in range(2):
                            pp, bp = nextpz()
                            for fc in range(32):
                                P.op("pe", lambda e, fc=fc, s=s, nh=nh, pp=pp: e.matmul(pp[:], lhsT=hid[:, fc, s * 128:(s + 1) * 128], rhs=w2_sb[:, fc, nh * 512:(nh + 1) * 512],
                                                                                    start=(fc == 0), stop=(fc == 31)),
                                     reads=[b_hid, b_w2], writes=[bp], signal=(fc == 31))
                            resid_out(xt, b_xt, s, nh, pp, bp, gi, S)
                    if final:
                        dst = out_d[(t - 4) * TT:(t - 3) * TT, :].rearrange("(s p) d -> p s d", p=128)
                    else:
                        dst = hrow
                    P.dma("sp", dst, xt[:], b_xt, reads=[b_xt])
                P.barrier()

        mlp_phase(0, range(3, 8), 1, 1, False)
        if stop == "p2":
            return nc

        with ExitStack() as ph:
            blkb = SB(ph, "blkb", [128, 128], BF16); b_blk = Buf("blkb")
            P.dma("pool", blkb[:], blk_d[:, :], b_blk, writes=[b_blk])
            expB = SB(ph, "expB", [128, 16, 640], BF16); b_expB = Buf("expB")
            KT = SB(ph, "KT", [128, 8, NKV * TT], BF16); b_KT = Buf("KT")
            Va = SB(ph, "Va", [128, NKV * 4, 16, 65], BF16); b_Va = Buf("Va")
            S = alloc_norm(ph, "p3")
            xt = SB(ph, "p3xt", [128, 4, D], F32); b_xt = Buf("xt")
            sq = [SB(ph, "p3sq%d" % i, [128, TT], BF16) for i in range(2)]; b_sq = [Buf("sq0"), Buf("sq1")]
            rs = [SB(ph, "p3rs%d" % i, [128, TT], F32) for i in range(2)]; b_rs = [Buf("rs0"), Buf("rs1")]
            pSt = [PS(ph, "p3pS%d" % i, [128, 1024], F32) for i in range(2)]
            pOt = [PS(ph, "p3pO%d" % i, [128, 4, 128], F32) for i in range(2)]
            b_bank = [Buf("bank%d" % i) for i in range(6)]
            pS = pSt; b_pS = [[b_bank[0], b_bank[1]], [b_bank[2], b_bank[3]]]
            pO = pOt; b_pO = [b_bank[4], b_bank[5]]
            pz = [pSt[0][:, 0:512], pSt[0][:, 512:1024], pSt[1][:, 0:512], pSt[1][:, 512:1024],
                  pOt[0][:].rearrange("p a b -> p (a b)"), pOt[1][:].rearrange("p a b -> p (a b)")]
            pzi = [0]

            def nextpz():
                i = pzi[0] % 6
                pzi[0] += 1
                return pz[i], b_bank[i]

            def headnorm(pp, bp, dstT, b_dst, fc, col0, gcol):
                k_ = fc % 2
                P.op("act", lambda e: e.activation(out=sq[k_][:], in_=pp[:], func=AF.Square), reads=[bp], writes=[b_sq[k_]])
                pm_, bpm = nextpz()
                P.op("pe", lambda e: e.matmul(pm_[:], lhsT=blkb[:], rhs=sq[k_][:], start=True, stop=True), reads=[b_blk, b_sq[k_]], writes=[bpm])
                P.op("act", lambda e: e.activation(out=rs[k_][:], in_=pm_[:], func=AF.Ln, bias=S["epsb"][:, 1:2]), reads=[bpm, S["b_epsb"]], writes=[b_rs[k_]])
                P.op("act", lambda e: e.activation(out=rs[k_][:], in_=rs[k_][:], func=AF.Exp, scale=-0.5), reads=[b_rs[k_]], writes=[b_rs[k_]])
                P.op("dve", lambda e: e.scalar_tensor_tensor(out=dstT[:, fc, col0:col0 + TT], in0=pp[:], scalar=gqk[:, gcol:gcol + 1], in1=rs[k_][:], op0=ALU.mult, op1=ALU.mult),
                     reads=[bp, b_gqk, b_rs[k_]], writes=[b_dst])

            with ExitStack() as p1s:
                kvw_sb, kvw_bufs = load_w_cols(p1s, "kvw_sb", kv_w_d.rearrange("(kc p) n -> p kc n", p=128), [128, 8, 2 * D],
                                               [(i * 512, (i + 1) * 512) for i in range(4)])
                amask = SB(p1s, "amask", [128, 640], F32); b_amask = Buf("amask")
                P.dma("sp", amask[:], amask_d[:, :], b_amask, writes=[b_amask])
                btmp = [SB(p1s, "p3btmp%d" % i, [128, 640], F32) for i in range(2)]; b_btmp = [Buf("btmp0"), Buf("btmp1")]
                xn = SB(p1s, "p3xn", [128, 4, D], BF16); S["xn"] = xn; S["b_xn"] = Buf("xn")
                biasv = biasT_d.rearrange("p (h x) -> p h x", h=16)
                for h in range(16):
                    bt, bbt = btmp[h % 2], b_btmp[h % 2]
                    P.dma("sp", bt[:], biasv[:, h, :], bbt, writes=[bbt])
                    P.op("act", lambda e, bt=bt: e.activation(out=bt[:], in_=bt[:], func=AF.Exp), reads=[bbt], writes=[bbt])
                    P.op("dve", lambda e, h=h, bt=bt: e.tensor_tensor(out=expB[:, h, :], in0=bt[:], in1=amask[:], op=ALU.mult), reads=[bbt, b_amask], writes=[b_expB])
                P.op("dve", lambda e: e.memset(Va[:, :, :, 64:65], 1.0), writes=[b_Va])
                P.op("dve", lambda e: e.tensor_scalar(out=Va[:, 0:4, :, 64:65], in0=Va[:, 0:4, :, 64:65], scalar1=tflag[:, 4:5], scalar2=None, op0=ALU.mult),
                     reads=[b_tflag, b_Va], writes=[b_Va])
                for t in range(3, 8):
                    kt0 = t - 3
                    hrow = H[kt0 * TT:(kt0 + 1) * TT, :].rearrange("(s p) d -> p s d", p=128)
                    P.dma("sp", xt[:], hrow, b_xt, writes=[b_xt])
                    norm_T(xt, b_xt, 2, S)
                    hnT, b_hnT = S["hnT"], S["b_hnT"]
                    for fc in range(8):
                        pp, bp = nextpz()
                        for kc in range(8):
                            P.op("pe", lambda e, kc=kc, fc=fc, pp=pp: e.matmul(pp[:], lhsT=kvw_sb[:, kc, fc * 128:(fc + 1) * 128], rhs=hnT[:, kc, :], start=(kc == 0), stop=(kc == 7)),
                                 reads=kvw_bufs(fc * 128, fc * 128 + 128) + [b_hnT], writes=[bp], signal=(kc == 7))
                        headnorm(pp, bp, KT, b_KT, fc, kt0 * TT, 0)
                    for s in range(4):
                        for nh in range(2):
                            pp, bp = nextpz()
                            for kc in range(8):
                                P.op("pe", lambda e, kc=kc, s=s, nh=nh, pp=pp: e.matmul(pp[:], lhsT=hnT[:, kc, s * 128:(s + 1) * 128], rhs=kvw_sb[:, kc, D + nh * 512:D + (nh + 1) * 512],
                                                                                    start=(kc == 0), stop=(kc == 7)),
                                     reads=kvw_bufs(D + nh * 512, D + (nh + 1) * 512) + [b_hnT], writes=[bp], signal=(kc == 7))
                            if t == 3:
                                P.op("act", lambda e, s=s, nh=nh, pp=pp, kt0=kt0: e.activation(out=Va[:, kt0 * 4 + s, nh * 8:(nh + 1) * 8, 0:64], in_=pp[:].rearrange("p (h d) -> p h d", d=64),
                                                                                           func=AF.Identity, scale=tflag[:, 4:5]),
                                     reads=[bp, b_tflag], writes=[b_Va])
                            else:
                                P.op("act", lambda e, s=s, nh=nh, pp=pp, kt0=kt0: e.activation(out=Va[:, kt0 * 4 + s, nh * 8:(nh + 1) * 8, 0:64], in_=pp[:].rearrange("p (h d) -> p h d", d=64),
                                                                                           func=AF.Copy),
                                     reads=[bp], writes=[b_Va])
                P.barrier()
            if stop == "p3a":
                return nc
            with ExitStack() as p2s:
                wq_sb, wq_bufs = load_w_cols(p2s, "wq_sb", wq_d.rearrange("(kc p) n -> p kc n", p=128), [128, 8, D], [(0, 512), (512, 1024)])
                wo_sb, b_wo = load_w(p2s, "wo_sb", wo_d.rearrange("(kc p) n -> p kc n", p=128), [128, 8, D], 2)
                S["g"], S["b_g"] = load_g(p2s, "p3g", 2)
                QT = SB(p2s, "QT", [128, 8, TT], BF16); b_QT = Buf("QT")
                S["rtmp"] = SB(p2s, "p3rtmp", [128, 512], F32); S["b_rtmp"] = Buf("rtmp")
                Eb = [SB(p2s, "p3E%d" % i, [128, 640], BF16) for i in range(3)]; b_E = [Buf("E%d" % i) for i in range(3)]
                Pb = [SB(p2s, "p3P%d" % i, [128, 640], BF16) for i in range(3)]; b_P = [Buf("P%d" % i) for i in range(3)]
                rec = SB(p2s, "p3rec", [128, 2, 4], F32); b_rec = [Buf("rec0"), Buf("rec1")]
                attn = SB(p2s, "p3attn", [128, D], BF16); b_attn = Buf("attn")
                attnT = SB(p2s, "p3attnT", [128, 8, TT], BF16); b_attnT = Buf("attnT")
                S["xn"] = attnT[:].rearrange("p a c -> p (a c)").rearrange("p (s d) -> p s d", s=4); S["b_xn"] = b_attnT
                hcount = 0
                for t in range(4, 8):
                    kt0 = t - 3
                    hrow = H[kt0 * TT:(kt0 + 1) * TT, :].rearrange("(s p) d -> p s d", p=128)
                    P.dma("sp", xt[:], hrow, b_xt, writes=[b_xt])
                    norm_T(xt, b_xt, 3, S)
                    hnT, b_hnT = S["hnT"], S["b_hnT"]
                    for fc in range(8):
                        pp, bp = nextpz()
                        for kc in range(8):
                            P.op("pe", lambda e, kc=kc, fc=fc, pp=pp: e.matmul(pp[:], lhsT=wq_sb[:, kc, fc * 128:(fc + 1) * 128], rhs=hnT[:, kc, :], start=(kc == 0), stop=(kc == 7)),
                                 reads=wq_bufs(fc * 128, fc * 128 + 128) + [b_hnT], writes=[bp], signal=(kc == 7))
                        headnorm(pp, bp, QT, b_QT, fc, 0, 1)
                    for qg in range(4):
                        G = kt0 * 4 + qg
                        for h4 in range(4):
                            pO_, bpO = pO[h4 % 2], b_pO[h4 % 2]
                            for hh in range(4):
                                h = h4 * 4 + hh
                                fc, hf = h // 2, h % 2
                                pS_, bpS = pS[hcount % 2], b_pS[hcount % 2]
                                for kt in range(5):
                                    P.op("pe", lambda e, kt=kt, fc=fc, hf=hf, qg=qg, G=G, pS_=pS_: e.matmul(pS_[:, kt * 128:(kt + 1) * 128],
                                                                                                     lhsT=KT[hf * 64:(hf + 1) * 64, fc, (G - 4 + kt) * 128:(G - 3 + kt) * 128],
                                                                                                     rhs=QT[hf * 64:(hf + 1) * 64, fc, qg * 128:(qg + 1) * 128], start=True, stop=True),
                                         reads=[b_KT, b_QT], writes=bpS, signal=(kt == 4))
                                E_, bE = Eb[hcount % 3], b_E[hcount % 3]
                                P_, bP = Pb[hcount % 3], b_P[hcount % 3]
                                hcount += 1
                                P.op("act", lambda e, E_=E_, pS_=pS_: e.activation(out=E_[:], in_=pS_[:, 0:640], func=AF.Exp), reads=bpS, writes=[bE])
                                P.op("dve", lambda e, E_=E_, P_=P_, h=h: e.tensor_tensor(out=P_[:], in0=E_[:], in1=expB[:, h, :], op=ALU.mult), reads=[bE, b_expB], writes=[bP])
                                for kt in range(5):
                                    P.op("pe", lambda e, kt=kt, hh=hh, h=h, G=G, P_=P_, pO_=pO_: e.matmul(pO_[:, hh, 0:65], lhsT=P_[:, kt * 128:(kt + 1) * 128], rhs=Va[:, G - 4 + kt, h, :],
                                                                                                   start=(kt == 0), stop=(kt == 4)),
                                         reads=[bP, b_Va], writes=[bpO], signal=(kt == 4))
                            r_ = h4 % 2
                            P.op("dve", lambda e, r_=r_, pO_=pO_: e.reciprocal(out=rec[:, r_, :], in_=pO_[:, :, 64]), reads=[bpO], writes=[b_rec[r_]])
                            P.op("dve", lambda e, r_=r_, pO_=pO_, h4=h4: e.tensor_tensor(out=attn[:, h4 * 256:(h4 + 1) * 256].rearrange("p (h d) -> p h d", d=64), in0=pO_[:, :, 0:64],
                                                                                    in1=rec[:, r_, :].unsqueeze(2).to_broadcast([128, 4, 64]), op=ALU.mult),
                                 reads=[bpO, b_rec[r_]], writes=[b_attn])
                        for fc in range(8):
                            pT, bT = S["pT"][fc % 2], S["b_pT"][fc % 2]
                            P.op("pe", lambda e, fc=fc, pT=pT: e.transpose(out=pT[:, 0:128], in_=attn[:, fc * 128:(fc + 1) * 128], identity=identb[:]), reads=[b_attn, b_identb], writes=[bT])
                            P.op("act", lambda e, fc=fc, pT=pT, qg=qg: e.activation(out=attnT[:, fc, qg * 128:(qg + 1) * 128], in_=pT[:, 0:128], func=AF.Copy), reads=[bT], writes=[b_attnT])
                    for s in range(4):
                        for nh in range(2):
                            pp, bp = nextpz()
                            for kc in range(8):
                                P.op("pe", lambda e, kc=kc, s=s, nh=nh, pp=pp: e.matmul(pp[:], lhsT=attnT[:, kc, s * 128:(s + 1) * 128], rhs=wo_sb[:, kc, nh * 512:(nh + 1) * 512],
                                                                                    start=(kc == 0), stop=(kc == 7)),
                                     reads=[b_attnT, b_wo], writes=[bp], signal=(kc == 7))
                            resid_out(xt, b_xt, s, nh, pp, bp, 2, S)
                    P.dma("sp", hrow, xt[:], b_xt, reads=[b_xt])
                P.barrier()
        if stop == "p3":
            return nc

        mlp_phase(1, range(4, 8), 4, 3, True)
    return nc


def _host_consts():
    ident = np.eye(128, dtype=np.float32)
    blk = np.zeros((128, 128), np.float32)
    blk[:64, :64] = 1.0 / 64
    blk[64:, 64:] = 1.0 / 64
    kk = np.arange(128)[:, None, None]
    kt = np.arange(5)[None, :, None]
    q = np.arange(128)[None, None, :]
    kidx = kt * 128 + kk
    ck = kidx // 64
    cq = q // 64
    amask = ((ck >= cq) & (ck <= cq + 8)).astype(np.float32)
    bidx = np.minimum(640 + q - kidx, 256)
    bidx = np.maximum(bidx, 0)
    return ident, blk, amask.reshape(128, 640), bidx


def _pvec(v, n):
    return np.ascontiguousarray(np.asarray(v, np.float32).reshape(n, 128).T)


def make_in_maps(inputs):
    f = lambda k: np.ascontiguousarray(np.asarray(inputs[k], dtype=np.float32))
    x = f("x"); c = f("c")
    ident, blk, amask, bidx = _host_consts()
    rel_bias = f("rel_bias")[0]
    biasT = rel_bias[:, bidx]
    biasT = np.ascontiguousarray(biasT.transpose(1, 0, 2, 3).reshape(128, 16 * 640))
    conv_w = f("lru_conv_w")[0]
    convw = np.ascontiguousarray(conv_w.reshape(4, NJ, 128).transpose(2, 1, 0).reshape(128, NJ * 4))
    vec_parts = {
        "n1g0": _pvec(f("norm1_g")[0], 8), "n2g0": _pvec(f("norm2_g")[0], 8), "kvg": _pvec(f("kv_norm_g"), 8),
        "n1g1": _pvec(f("norm1_g")[1], 8), "n2g1": _pvec(f("norm2_g")[1], 8), "convw": convw,
        "convb": _pvec(f("lru_conv_b")[0], NJ), "gab": _pvec(f("lru_gate_a_b")[0], NJ), "gxb": _pvec(f("lru_gate_x_b")[0], NJ),
        "lam": _pvec(f("lru_lambda")[0], NJ),
        "gk": np.tile(f("k_norm_g"), 2).reshape(128, 1), "gq": np.tile(f("q_norm_g")[0], 2).reshape(128, 1),
    }
    vecs = np.zeros((128, NV), np.float32)
    for k, (o, n) in VOFF.items():
        vecs[:, o:o + n] = vec_parts[k]
    shared = {
        "vecs": vecs, "ada_w": f("ada_w"), "ada_b": f("ada_b"), "kv_ada_w": f("kv_ada_w"), "kv_ada_b": f("kv_ada_b").reshape(1, -1),
        "w_in": f("lru_w_in")[0], "ga_w": f("lru_gate_a_w")[0], "gx_w": f("lru_gate_x_w")[0], "w_out": f("lru_w_out")[0],
        "mlp_w1": f("mlp_w1"), "mlp_w2": f("mlp_w2"), "kv_w": f("kv_w"), "wq": f("attn_w_q")[0], "wo": f("attn_w_o")[0],
        "biasT": biasT, "amask": amask, "ident": ident, "blk64": blk,
    }
    maps = []
    for core in range(8):
        b, half = core // 2, core % 2
        if half == 1:
            xin = x[b]
        else:
            xin = np.concatenate([np.zeros((2048, D), np.float32), x[b, :2048]], axis=0)
        tflag = np.ones((128, NTILE + 1), np.float32)
        tflag[:, 4] = float(half)
        m = dict(shared)
        m["xin"] = np.ascontiguousarray(xin)
        m["ct"] = _pvec(c[b], 8)
        m["tflag"] = tflag
        maps.append(m)
    return maps


def kernel(**inputs):
    nc = build()
    maps = make_in_maps(inputs)
    res = run_bass_kernel_spmd(nc, maps, core_ids=list(range(8)))
    out = np.zeros((4, 4096, D), np.float32)
    for core in range(8):
        b, half = core // 2, core % 2
        out[b, half * 2048:(half + 1) * 2048] = res.results[core]["out"]
    return out
```

```python
import numpy as np
from contextlib import ExitStack
import concourse.bass as bass
import concourse.mybir as mybir
from concourse.bass_utils import run_bass_kernel_spmd

F32 = mybir.dt.float32
BF16 = mybir.dt.bfloat16
AF = mybir.ActivationFunctionType
ALU = mybir.AluOpType
AX = mybir.AxisListType

D = 1024
W = 1408
NJ = 11
DFF = 4096
TT = 512
NTILE = 8
NKV = 5
EPS = 1e-6


class Buf:
    __slots__ = ("name", "w", "r", "dsem", "dcnt")

    def __init__(self, name):
        self.name = name
        self.w = None
        self.r = []
        self.dsem = None
        self.dcnt = 0


class Prog:
    def __init__(self, nc, stack):
        self.nc = nc
        self.stack = stack
        self.eng = {"pe": nc.tensor, "act": nc.scalar, "dve": nc.vector, "pool": nc.gpsimd, "sp": nc.sync}
        self.sem = {k: stack.enter_context(nc.semaphore("s_" + k)) for k in self.eng}
        self.cnt = {k: 0 for k in self.eng}
        self.seen = {k: {} for k in self.eng}
        self.pending = {k: False for k in self.eng}
        self.dbufs = []
        self.nsem = 0

    def _wait(self, e, tok):
        if tok is None:
            return
        sem, val, key = tok
        if key == "pe" and e == "pe":
            return
        if self.seen[e].get(key, 0) >= val:
            return
        self.eng[e].wait_ge(sem, val)
        self.seen[e][key] = val

    def _hazards(self, e, reads, writes, group_sem=None):
        for b in reads:
            self._wait(e, b.w)
        for b in writes:
            if not (group_sem is not None and b.w is not None and b.w[0] is group_sem):
                self._wait(e, b.w)
            for t in b.r:
                self._wait(e, t)

    def op(self, e, ins_fn, reads=(), writes=(), signal=True):
        self._hazards(e, reads, writes)
        ins = ins_fn(self.eng[e])
        if signal:
            self.cnt[e] += 1
            ins.then_inc(self.sem[e], 1)
            tok = (self.sem[e], self.cnt[e], e)
            self.pending[e] = False
        else:
            assert e == "pe"
            tok = (self.sem[e], self.cnt[e] + 1, e)
            self.pending[e] = True
        for b in reads:
            b.r.append(tok)
        for b in writes:
            b.w = tok
            b.r = []
        return tok

    def dma(self, e, out, in_, sembuf, reads=(), writes=(), group=False, **kw):
        if sembuf.dsem is None:
            sembuf.dsem = self.stack.enter_context(self.nc.semaphore("d%d_%s" % (self.nsem, sembuf.name)))
            self.nsem += 1
            self.dbufs.append(sembuf)
        self._hazards(e, reads, writes, group_sem=sembuf.dsem if group else None)
        ins = self.eng[e].dma_start(out=out, in_=in_, **kw)
        sembuf.dcnt += 16
        ins.then_inc(sembuf.dsem, 16)
        tok = (sembuf.dsem, sembuf.dcnt, ("d", id(sembuf)))
        for b in reads:
            b.r.append(tok)
        for b in writes:
            b.w = tok
            b.r = []
        return tok

    def barrier(self, engines=None):
        toks = [(self.sem[k], self.cnt[k], k) for k in self.eng if self.cnt[k] > 0]
        toks += [(b.dsem, b.dcnt, ("d", id(b))) for b in self.dbufs]
        for e in (engines or self.eng):
            for t in toks:
                if t[2] == e:
                    continue
                self._wait(e, t)
        assert not self.pending["pe"]


def gate_pieces():
    pcs = []
    for n in range(16):
        lo, hi = 88 * n, 88 * n + 88
        for ci in range(NJ):
            r0, r1 = max(lo, 128 * ci), min(hi, 128 * ci + 128)
            if r0 >= r1:
                continue
            for co in range(NJ):
                c0, c1 = max(lo, 128 * co), min(hi, 128 * co + 128)
                if c0 >= c1:
                    continue
                d = ci - co + 1
                assert 0 <= d <= 2
                pcs.append((n, r0 - lo, r1 - lo, c0 - lo, c1 - lo, ci, co, d, r0 - 128 * ci, r1 - 128 * ci, c0 - 128 * co, c1 - 128 * co))
    return pcs


def gate_nbrs():
    nb = {co: set() for co in range(NJ)}
    for p in gate_pieces():
        nb[p[6]].add(p[5])
    return {co: sorted(v) for co, v in nb.items()}


VOFF = {}
_o = 0
for _name, _n in [("n1g0", 8), ("n2g0", 8), ("kvg", 8), ("n1g1", 8), ("n2g1", 8), ("convw", 44), ("convb", 11),
                  ("gab", 11), ("gxb", 11), ("lam", 11), ("gk", 1), ("gq", 1)]:
    VOFF[_name] = (_o, _n)
    _o += _n
NV = _o


def build(stop=None):
    nc = bass.Bass("TRN2", target_bir_lowering=False)
    dt_in = lambda name, shape: nc.dram_tensor(name, shape, F32, kind="ExternalInput").ap()
    xin = dt_in("xin", [NTILE * TT, D])
    ct_d = dt_in("ct", [128, 8])
    tflag_d = dt_in("tflag", [128, NTILE + 1])
    vecs_d = dt_in("vecs", [128, NV])
    ada_w = dt_in("ada_w", [2, D, 6 * D])
    ada_b = dt_in("ada_b", [2, 6 * D])
    kv_ada_w = dt_in("kv_ada_w", [D, 2 * D])
    kv_ada_b = dt_in("kv_ada_b", [1, 2 * D])
    w_in_d = dt_in("w_in", [D, 2 * W])
    ga_w_d = dt_in("ga_w", [16, 88, 88])
    gx_w_d = dt_in("gx_w", [16, 88, 88])
    w_out_d = dt_in("w_out", [W, D])
    w1_d = dt_in("mlp_w1", [2, D, DFF])
    w2_d = dt_in("mlp_w2", [2, DFF, D])
    kv_w_d = dt_in("kv_w", [D, 2 * D])
    wq_d = dt_in("wq", [D, D])
    wo_d = dt_in("wo", [D, D])
    biasT_d = dt_in("biasT", [128, 16 * 5 * 128])
    amask_d = dt_in("amask", [128, 5 * 128])
    ident_d = dt_in("ident", [128, 128])
    blk_d = dt_in("blk64", [128, 128])
    out_d = nc.dram_tensor("out", [4 * TT, D], F32, kind="ExternalOutput").ap()
    H = nc.dram_tensor("Hs", [NKV * TT, D], F32, kind="ExternalOutput" if stop else "Internal").ap()
    Gs = nc.dram_tensor("Gs", [4, 128, D], F32, kind="Internal").ap()

    with ExitStack() as top:
        P = Prog(nc, top)

        def SB(stack, name, shape, dt):
            return stack.enter_context(nc.sbuf_tensor("sb_" + name, shape, dt))

        def PS(stack, name, shape, dt):
            return stack.enter_context(nc.psum_tensor("ps_" + name, shape, dt))

        vecs = SB(top, "vecs", [128, NV], F32); b_vecs = Buf("vecs")
        tflag = SB(top, "tflag", [128, NTILE + 1], F32); b_tflag = Buf("tflag")
        identb = SB(top, "identb", [128, 128], BF16); b_identb = Buf("identb")
        Asc = SB(top, "Asc", [128, 5, 8], F32); Ash = SB(top, "Ash", [128, 5, 8], F32); b_mod = Buf("modvec")
        cch = SB(top, "cch", [128, 2, NJ], F32); b_cch = Buf("cch")
        nbias = SB(top, "nbias", [128, 2, NJ], F32); b_nbias = Buf("nbias")
        gqk = SB(top, "gqk", [128, 2], F32); b_gqk = Buf("gqk")

        P.dma("sp", vecs[:], vecs_d[:, :], b_vecs, writes=[b_vecs])
        P.dma("sp", tflag[:], tflag_d[:, :], b_tflag, writes=[b_tflag])
        P.dma("pool", identb[:], ident_d[:, :], b_identb, writes=[b_identb])

        def V(name):
            o, n = VOFF[name]
            return vecs[:, o:o + n]

        with ExitStack() as ph:
            ct = SB(ph, "ct", [128, 8], F32); b_ct = Buf("ct")
            t8a = SB(ph, "t8a", [128, 8], F32); t8b = SB(ph, "t8b", [128, 8], F32); b_t8 = Buf("t8")
            cact = SB(ph, "cact", [128, 8], F32); b_cact = Buf("cact")
            lc = SB(ph, "lc", [128, 8, 128], BF16); b_lc = Buf("lc")
            identf = SB(ph, "identf", [128, 128], F32); b_identf = Buf("identf")
            modrow = [SB(ph, "modrow%d" % i, [128, 6 * D], F32) for i in range(2)]
            kvrow = SB(ph, "kvrow", [128, 2 * D], F32)
            b_rows = [Buf("modrow0"), Buf("modrow1"), Buf("kvrow")]
            wch = [SB(ph, "wch%d" % i, [128, 8, 512], BF16) for i in range(3)]; b_wch = [Buf("wch%d" % i) for i in range(3)]
            dtmp = SB(ph, "dtmp", [128, 8, 128], F32); b_dtmp = Buf("dtmp")
            t11 = SB(ph, "t11", [128, NJ], F32); b_t11 = Buf("t11")
            pm = [PS(ph, "pm%d" % i, [128, 512], F32) for i in range(2)]; b_pm = [Buf("pm%d" % i) for i in range(2)]

            P.dma("sp", ct[:], ct_d[:, :], b_ct, writes=[b_ct])
            P.dma("sp", identf[:], ident_d[:, :], b_identf, writes=[b_identf])
            rows = [modrow[0], modrow[1], kvrow]
            P.dma("sp", modrow[0][:], ada_b[0:1, :].partition_broadcast(128), b_rows[0], writes=[b_rows[0]])
            P.dma("sp", modrow[1][:], ada_b[1:2, :].partition_broadcast(128), b_rows[1], writes=[b_rows[1]])
            P.dma("sp", kvrow[:], kv_ada_b[0:1, :].partition_broadcast(128), b_rows[2], writes=[b_rows[2]])
            P.op("act", lambda e: e.activation(out=t8a[:], in_=ct[:], func=AF.Exp, scale=-1.0), reads=[b_ct], writes=[b_t8])
            P.op("act", lambda e: e.activation(out=t8b[:], in_=t8a[:], func=AF.Ln, bias=1.0), reads=[b_t8], writes=[b_t8])
            P.op("act", lambda e: e.activation(out=t8a[:], in_=t8b[:], func=AF.Exp, scale=-1.0), reads=[b_t8], writes=[b_t8])
            P.op("dve", lambda e: e.tensor_tensor(out=cact[:], in0=t8a[:], in1=ct[:], op=ALU.mult), reads=[b_t8, b_ct], writes=[b_cact])
            P.op("dve", lambda e: e.tensor_copy(out=lc[:], in_=cact[:].unsqueeze(2).to_broadcast([128, 8, 128])), reads=[b_cact], writes=[b_lc])
            P.op("act", lambda e: e.activation(out=t11[:], in_=V("lam"), func=AF.Exp, scale=-1.0), reads=[b_vecs], writes=[b_t11])
            P.op("act", lambda e: e.activation(out=t11[:], in_=t11[:], func=AF.Ln, bias=1.0), reads=[b_t11], writes=[b_t11])
            P.op("dve", lambda e: e.tensor_scalar(out=cch[:, 0, :], in0=t11[:], scalar1=-8.0, scalar2=None, op0=ALU.mult), reads=[b_t11], writes=[b_cch])
            P.op("dve", lambda e: e.tensor_scalar(out=cch[:, 1, :], in0=t11[:], scalar1=-16.0, scalar2=None, op0=ALU.mult), reads=[b_t11], writes=[b_cch])
            P.op("dve", lambda e: e.tensor_scalar(out=nbias[:, 0, :], in0=V("gab"), scalar1=-1.0, scalar2=None, op0=ALU.mult), reads=[b_vecs], writes=[b_nbias])
            P.op("dve", lambda e: e.tensor_scalar(out=nbias[:, 1, :], in0=V("gxb"), scalar1=-1.0, scalar2=None, op0=ALU.mult), reads=[b_vecs], writes=[b_nbias])
            P.op("dve", lambda e: e.tensor_copy(out=gqk[:, 0:1], in_=V("gk")), reads=[b_vecs], writes=[b_gqk])
            P.op("dve", lambda e: e.tensor_scalar(out=gqk[:, 1:2], in0=V("gq"), scalar1=0.125, scalar2=None, op0=ALU.mult), reads=[b_vecs], writes=[b_gqk])

            srcs = [(ada_w[0], 6 * D, 0), (ada_w[1], 6 * D, 1), (kv_ada_w, 2 * D, 2)]
            it = 0
            for (wsrc, ncols, ri) in srcs:
                wv = wsrc.rearrange("(kc p) n -> p kc n", p=128)
                for j in range(ncols // 512):
                    wb_, bb_ = wch[it % 3], b_wch[it % 3]
                    pp, bp = pm[it % 2], b_pm[it % 2]
                    P.dma("pool", wb_[:], wv[:, :, j * 512:(j + 1) * 512], bb_, writes=[bb_])
                    for kc in range(8):
                        P.op("pe", lambda e, kc=kc, wb_=wb_, pp=pp: e.matmul(pp[:], lhsT=lc[:, kc, :], rhs=wb_[:, kc, :], start=(kc == 0), stop=(kc == 7)),
                             reads=[b_lc, bb_], writes=[bp], signal=(kc == 7))
                    rr = rows[ri]
                    P.op("dve", lambda e, rr=rr, pp=pp, j=j: e.tensor_tensor(out=rr[:, j * 512:(j + 1) * 512], in0=pp[:], in1=rr[:, j * 512:(j + 1) * 512], op=ALU.add),
                         reads=[bp, b_rows[ri]], writes=[b_rows[ri]])
                    it += 1

            specs = [(0, modrow[0], 0, 1, "n1g0"), (1, modrow[0], 3, 4, "n2g0"), (2, kvrow, 0, 1, "kvg"),
                     (3, modrow[1], 0, 1, "n1g1"), (4, modrow[1], 3, 4, "n2g1")]
            rbuf = {0: b_rows[0], 1: b_rows[0], 2: b_rows[2], 3: b_rows[1], 4: b_rows[1]}
            for (m, rr, sseg, cseg, gname) in specs:
                for (seg, dst) in [(sseg, Ash), (cseg, t8a)]:
                    P.op("dve", lambda e, rr=rr, seg=seg: e.tensor_tensor(out=dtmp[:], in0=rr[:, seg * D:(seg + 1) * D].rearrange("p (g k) -> p g k", k=128),
                                                                     in1=identf[:].unsqueeze(1).to_broadcast([128, 8, 128]), op=ALU.mult),
                         reads=[rbuf[m], b_identf], writes=[b_dtmp])
                    if dst is Ash:
                        P.op("dve", lambda e, m=m: e.tensor_reduce(out=Ash[:, m, :], in_=dtmp[:], axis=AX.X, op=ALU.add), reads=[b_dtmp], writes=[b_mod])
                    else:
                        P.op("dve", lambda e: e.tensor_reduce(out=t8a[:], in_=dtmp[:], axis=AX.X, op=ALU.add), reads=[b_dtmp], writes=[b_t8])
                P.op("dve", lambda e: e.tensor_scalar(out=t8b[:], in0=t8a[:], scalar1=1.0, scalar2=32.0, op0=ALU.add, op1=ALU.mult), reads=[b_t8], writes=[b_t8])
                P.op("dve", lambda e, m=m, gname=gname: e.tensor_tensor(out=Asc[:, m, :], in0=t8b[:], in1=V(gname), op=ALU.mult), reads=[b_t8, b_vecs], writes=[b_mod])
            for gi, (rr, seg, rb) in enumerate([(modrow[0], 2, b_rows[0]), (modrow[0], 5, b_rows[0]), (modrow[1], 2, b_rows[1]), (modrow[1], 5, b_rows[1])]):
                P.dma("sp", Gs[gi], rr[:, seg * D:(seg + 1) * D], rb, reads=[rb])
            P.barrier()

        def load_w(stack, name, src_view, shape, nsplit):
            t = SB(stack, name, shape, BF16)
            b = Buf(name)
            a = shape[1]
            step = (a + nsplit - 1) // nsplit
            for s0 in range(0, a, step):
                s1 = min(a, s0 + step)
                P.dma("pool", t[:, s0:s1, :], src_view[:, s0:s1, :], b, writes=[b], group=True)
            return t, b

        def load_w_cols(stack, name, src_view, shape, splits):
            t = SB(stack, name, shape, BF16)
            blocks = []
            for (c0, c1) in splits:
                b = Buf("%s_%d" % (name, c0))
                P.dma("pool", t[:, :, c0:c1], src_view[:, :, c0:c1], b, writes=[b])
                blocks.append((c0, c1, b))

            def bufs(c0, c1):
                return [b for (a0, a1, b) in blocks if a0 < c1 and c0 < a1]
            return t, bufs

        def norm_T(xt, b_xt, m, S):
            P.op("act", lambda e: e.activation(out=S["junk"][:], in_=xt[:, 0, :], func=AF.Square, accum_out=S["ss"][:, 0:1]), reads=[b_xt], writes=[S["b_junk"], S["b_ss"]])
            for s in range(1, 4):
                P.op("act", lambda e, s=s: e.activation(out=S["junk"][:], in_=xt[:, s, :], func=AF.Square, accum_out=S["ss"][:, s:s + 1]), reads=[b_xt], writes=[S["b_junk"], S["b_ss"]])
            P.op("act", lambda e: e.activation(out=S["ss"][:, 4:8], in_=S["ss"][:, 0:4], func=AF.Ln, bias=S["epsb"][:, 0:1]), reads=[S["b_ss"], S["b_epsb"]], writes=[S["b_ss"]])
            P.op("act", lambda e: e.activation(out=S["ss"][:, 8:12], in_=S["ss"][:, 4:8], func=AF.Exp, scale=-0.5), reads=[S["b_ss"]], writes=[S["b_ss"]])
            for s in range(4):
                P.op("dve", lambda e, s=s: e.tensor_scalar(out=S["xn"][:, s, :], in0=xt[:, s, :], scalar1=S["ss"][:, 8 + s:9 + s], scalar2=None, op0=ALU.mult),
                     reads=[b_xt, S["b_ss"]], writes=[S["b_xn"]])
            for fc in range(8):
                pT, bT = S["pT"][fc % 2], S["b_pT"][fc % 2]
                for s in range(4):
                    P.op("pe", lambda e, s=s, fc=fc, pT=pT: e.transpose(out=pT[:, s * 128:(s + 1) * 128], in_=S["xn"][:, s, fc * 128:(fc + 1) * 128], identity=identb[:]),
                         reads=[S["b_xn"], b_identb], writes=[bT], signal=(s == 3))
                P.op("act", lambda e, fc=fc, pT=pT: e.activation(out=S["hnT"][:, fc, :], in_=pT[:, 0:TT], func=AF.Identity, scale=Asc[:, m, fc:fc + 1], bias=Ash[:, m, fc:fc + 1]),
                     reads=[bT, b_mod], writes=[S["b_hnT"]])

        def alloc_norm(stack, pfx):
            S = {}
            S["junk"] = SB(stack, pfx + "junk", [128, D], BF16); S["b_junk"] = Buf("junk")
            S["ss"] = SB(stack, pfx + "ss", [128, 12], F32); S["b_ss"] = Buf("ss")
            S["epsb"] = SB(stack, pfx + "epsb", [128, 2], F32); S["b_epsb"] = Buf("epsb")
            S["hnT"] = SB(stack, pfx + "hnT", [128, 8, TT], BF16); S["b_hnT"] = Buf("hnT")
            S["pT"] = [PS(stack, pfx + "pT%d" % i, [128, 2 * TT], BF16) for i in range(2)]; S["b_pT"] = [Buf("pT0"), Buf("pT1")]
            P.op("dve", lambda e: e.memset(S["epsb"][:, 0:1], 1024.0 * EPS), writes=[S["b_epsb"]])
            P.op("dve", lambda e: e.memset(S["epsb"][:, 1:2], EPS), writes=[S["b_epsb"]])
            return S

        def load_g(stack, name, gi):
            g = SB(stack, name, [128, D], F32); b = Buf(name)
            P.dma("sp", g[:], Gs[gi], b, writes=[b])
            return g, b

        def resid_out(xt, b_xt, s, nh, po, bpo, gi, S):
            P.op("dve", lambda e: e.tensor_tensor(out=S["rtmp"][:], in0=po[:], in1=S["g"][:, nh * 512:(nh + 1) * 512], op=ALU.mult),
                 reads=[bpo, S["b_g"]], writes=[S["b_rtmp"]])
            P.op("dve", lambda e: e.tensor_tensor(out=xt[:, s, nh * 512:(nh + 1) * 512], in0=S["rtmp"][:], in1=xt[:, s, nh * 512:(nh + 1) * 512], op=ALU.add),
                 reads=[S["b_rtmp"], b_xt], writes=[b_xt])

        nbrs = gate_nbrs()
        with ExitStack() as ph:
            w_in_sb, w_in_bufs = load_w_cols(ph, "w_in_sb", w_in_d.rearrange("(kc p) n -> p kc n", p=128), [128, 8, 2 * W],
                                             [(W, W + 512), (W + 512, 2 * W), (0, 704), (704, W)])
            wg = SB(ph, "wg", [128, 2, NJ * 3, 128], BF16); b_wg = Buf("wg")
            P.op("dve", lambda e: e.memset(wg[:], 0.0), writes=[b_wg])
            for gidx, gsrc in enumerate([ga_w_d, gx_w_d]):
                for (n, r0, r1, c0, c1, ci, co, d, p0, p1, q0, q1) in gate_pieces():
                    P.dma("pool", wg[p0:p1, gidx, co * 3 + d, q0:q1], gsrc[n, r0:r1, c0:c1], b_wg, writes=[b_wg], group=True)
            w_out_sb, b_w_out = load_w(ph, "w_out_sb", w_out_d.rearrange("(jc p) n -> p jc n", p=128), [128, NJ, D], 2)

            S = alloc_norm(ph, "p1")
            S["g"], S["b_g"] = load_g(ph, "p1g", 0)
            xts = [SB(ph, "p1xt%d" % i, [128, 4, D], F32) for i in range(2)]; b_xts = [Buf("xt0"), Buf("xt1")]
            xnyb = SB(ph, "p1xnyb", [128, NJ, TT], BF16)
            S["xn"] = xnyb[:, 0:8, :].rearrange("p a c -> p (a c)").rearrange("p (s d) -> p s d", s=4)
            b_xnyb = Buf("xnyb"); S["b_xn"] = b_xnyb
            NXR = 2
            xrb = [SB(ph, "p1xrb%d" % i, [128, TT + 3], F32) for i in range(NXR)]; b_xrb = [Buf("xrb%d" % i) for i in range(NXR)]
            halo = SB(ph, "p1halo", [128, NJ, 3], F32); b_halo = [Buf("halo%d" % j) for j in range(NJ)]
            xc = SB(ph, "p1xc", [128, NJ, TT], F32); b_xc = [Buf("xc%d" % j) for j in range(NJ)]
            xcb = SB(ph, "p1xcb", [128, NJ, TT], BF16); b_xcb = [Buf("xcb%d" % j) for j in range(NJ)]
            state = SB(ph, "p1state", [128, NJ], F32); b_state = Buf("state")
            NTMP = 3
            tA = [SB(ph, "p1tA%d" % i, [128, TT], F32) for i in range(NTMP)]; b_tA = [Buf("tA%d" % i) for i in range(NTMP)]
            tQ = [SB(ph, "p1tQ%d" % i, [128, TT], F32) for i in range(NTMP)]; b_tQ = [Buf("tQ%d" % i) for i in range(NTMP)]
            tG, b_tG = tA, b_tA
            NGRP = 6
            Rst = SB(ph, "p1R", [128, NGRP, TT], F32); b_R = [Buf("R%d" % i) for i in range(NGRP)]
            S["rtmp"] = tA[0]; S["b_rtmp"] = b_tA[0]
            pz = [PS(ph, "p1pz%d" % i, [128, TT], F32) for i in range(6)]; b_pz = [Buf("pz%d" % i) for i in range(6)]
            pzi = [0]

            def nextpz():
                i = pzi[0] % 6
                pzi[0] += 1
                return pz[i], b_pz[i]

            P.op("dve", lambda e: e.memset(halo[:], 0.0), writes=b_halo)
            P.op("dve", lambda e: e.memset(state[:], 0.0), writes=[b_state])
            cw = V("convw")
            cb = V("convb")

            def load_x(t):
                P.dma("sp", xts[t % 2][:], xin[t * TT:(t + 1) * TT, :].rearrange("(s p) d -> p s d", p=128), b_xts[t % 2], writes=[b_xts[t % 2]])

            load_x(0)
            ntile_p1 = NTILE if stop != "p1a" else 5
            for t in range(ntile_p1):
                full = t >= 3
                xt, b_xt = xts[t % 2], b_xts[t % 2]
                if t + 1 < ntile_p1:
                    load_x(t + 1)
                norm_T(xt, b_xt, 0, S)
                hnT, b_hnT = S["hnT"], S["b_hnT"]
                for j in range(NJ):
                    oc = NJ + j
                    pp, bp = nextpz()
                    for kc in range(8):
                        P.op("pe", lambda e, kc=kc, oc=oc, pp=pp: e.matmul(pp[:], lhsT=w_in_sb[:, kc, oc * 128:(oc + 1) * 128], rhs=hnT[:, kc, :], start=(kc == 0), stop=(kc == 7)),
                             reads=w_in_bufs(oc * 128, oc * 128 + 128) + [b_hnT], writes=[bp], signal=(kc == 7))
                    xr, bxr = xrb[j % NXR], b_xrb[j % NXR]
                    P.op("pool", lambda e, xr=xr, j=j: e.tensor_copy(out=xr[:, 0:3], in_=halo[:, j, :]), reads=[b_halo[j]], writes=[bxr])
                    P.op("act", lambda e, xr=xr, pp=pp: e.activation(out=xr[:, 3:TT + 3], in_=pp[:], func=AF.Copy), reads=[bp], writes=[bxr])
                    P.op("act", lambda e, pp=pp, j=j: e.activation(out=xc[:, j, :], in_=pp[:], func=AF.Identity, scale=cw[:, j * 4 + 3:j * 4 + 4], bias=cb[:, j:j + 1]),
                         reads=[bp, b_vecs], writes=[b_xc[j]])
                    for k in range(3):
                        P.op("dve", lambda e, xr=xr, j=j, k=k: e.scalar_tensor_tensor(out=xc[:, j, :], in0=xr[:, k:k + TT], scalar=cw[:, j * 4 + k:j * 4 + k + 1], in1=xc[:, j, :],
                                                                                  op0=ALU.mult, op1=ALU.add),
                             reads=[bxr, b_xc[j], b_vecs], writes=[b_xc[j]])
                    P.op("pool", lambda e, j=j: e.tensor_copy(out=xcb[:, j, :], in_=xc[:, j, :]), reads=[b_xc[j]], writes=[b_xcb[j]])
                    P.op("pool", lambda e, xr=xr, j=j, t=t: e.tensor_scalar(out=halo[:, j, :], in0=xr[:, TT:TT + 3], scalar1=tflag[:, t + 1:t + 2], scalar2=None, op0=ALU.mult),
                         reads=[bxr, b_tflag], writes=[b_halo[j]])
                gab, gxb = V("gab"), V("gxb")
                for grp in [list(range(0, NGRP)), list(range(NGRP, NJ))]:
                    for gi_, co in enumerate(grp):
                        pa, bpa = nextpz()
                        px, bpx = nextpz()
                        cis = nbrs[co]
                        for gidx, (pg, bpg) in enumerate([(pa, bpa), (px, bpx)]):
                            for n_, ci in enumerate(cis):
                                d = ci - co + 1
                                P.op("pe", lambda e, pg=pg, gidx=gidx, d=d, ci=ci, n_=n_, co=co, cis=cis: e.matmul(pg[:], lhsT=wg[:, gidx, co * 3 + d, :], rhs=xcb[:, ci, :],
                                                                                                         start=(n_ == 0), stop=(n_ == len(cis) - 1)),
                                     reads=[b_wg, b_xcb[ci]], writes=[bpg], signal=(n_ == len(cis) - 1))
                        I_, bI = tQ[co % NTMP], b_tQ[co % NTMP]
                        P.op("act", lambda e, pa=pa, gi_=gi_, co=co: e.activation(out=Rst[:, gi_, :], in_=pa[:], func=AF.Sigmoid, bias=gab[:, co:co + 1]), reads=[bpa, b_vecs], writes=[b_R[gi_]])
                        P.op("act", lambda e, px=px, I_=I_, co=co: e.activation(out=I_[:], in_=px[:], func=AF.Sigmoid, bias=gxb[:, co:co + 1]), reads=[bpx, b_vecs], writes=[bI])
                        P.op("pool", lambda e, I_=I_, co=co: e.tensor_tensor(out=xc[:, co, :], in0=I_[:], in1=xc[:, co, :], op=ALU.mult), reads=[bI, b_xc[co]], writes=[b_xc[co]])

                    def stage_b(co, gi_, k_):
                        A_, bA = tA[k_], b_tA[k_]
                        Q_, bQ = tQ[k_], b_tQ[k_]
                        return [
                            lambda: P.op("act", lambda e: e.activation(out=A_[:], in_=Rst[:, gi_, :], func=AF.Exp, scale=cch[:, 0, co:co + 1]), reads=[b_R[gi_], b_cch], writes=[bA]),
                            lambda: P.op("pool", lambda e: e.tensor_tensor(out=Q_[:], in0=A_[:], in1=A_[:], op=ALU.mult), reads=[bA], writes=[bQ]),
                            lambda: P.op("act", lambda e: e.activation(out=Q_[:], in_=Q_[:], func=AF.Ln, scale=-1.0, bias=1.000001), reads=[bQ], writes=[bQ]),
                            lambda: P.op("act", lambda e: e.activation(out=Q_[:], in_=Q_[:], func=AF.Exp, scale=0.5), reads=[bQ], writes=[bQ]),
                            lambda: P.op("pool", lambda e: e.tensor_tensor(out=Q_[:], in0=Q_[:], in1=xc[:, co, :], op=ALU.mult), reads=[bQ, b_xc[co]], writes=[bQ]),
                            lambda: P.op("dve", lambda e: e.tensor_tensor_scan(out=xc[:, co, :], data0=A_[:], data1=Q_[:], initial=state[:, co:co + 1], op0=ALU.mult, op1=ALU.add),
                                         reads=[bA, bQ, b_state], writes=[b_xc[co]]),
                        ]

                    for c3 in range(0, len(grp), NTMP):
                        chains = [stage_b(co, c3 + k_, k_) for k_, co in enumerate(grp[c3:c3 + NTMP])]
                        for si in range(len(chains[0])):
                            for ch in chains:
                                ch[si]()
                P.op("dve", lambda e, t=t: e.tensor_scalar(out=state[:], in0=xc[:, :, TT - 1], scalar1=tflag[:, t + 1:t + 2], scalar2=None, op0=ALU.mult),
                     reads=b_xc + [b_tflag], writes=[b_state])
                if not full:
                    continue
                for j in range(NJ):
                    pp, bp = nextpz()
                    for kc in range(8):
                        P.op("pe", lambda e, kc=kc, j=j, pp=pp: e.matmul(pp[:], lhsT=w_in_sb[:, kc, j * 128:(j + 1) * 128], rhs=hnT[:, kc, :], start=(kc == 0), stop=(kc == 7)),
                             reads=w_in_bufs(j * 128, j * 128 + 128) + [b_hnT], writes=[bp], signal=(kc == 7))
                    G_, bG = tG[j % NTMP], b_tG[j % NTMP]
                    P.op("act", lambda e, G_=G_, pp=pp: e.activation(out=G_[:], in_=pp[:], func=AF.Gelu_apprx_tanh), reads=[bp], writes=[bG])
                    P.op("dve", lambda e, G_=G_, j=j: e.tensor_tensor(out=xnyb[:, j, :], in0=G_[:], in1=xc[:, j, :], op=ALU.mult), reads=[bG, b_xc[j]], writes=[b_xnyb])
                for s in range(4):
                    for nh in range(2):
                        pp, bp = nextpz()
                        for jc in range(NJ):
                            P.op("pe", lambda e, jc=jc, s=s, nh=nh, pp=pp: e.matmul(pp[:], lhsT=xnyb[:, jc, s * 128:(s + 1) * 128], rhs=w_out_sb[:, jc, nh * 512:(nh + 1) * 512],
                                                                                start=(jc == 0), stop=(jc == NJ - 1)),
                                 reads=[b_xnyb, b_w_out], writes=[bp], signal=(jc == NJ - 1))
                        resid_out(xt, b_xt, s, nh, pp, bp, 0, S)
                P.dma("sp", H[(t - 3) * TT:(t - 2) * TT, :].rearrange("(s p) d -> p s d", p=128), xt[:], b_xt, reads=[b_xt])
            P.barrier()
        if stop in ("p1", "p1a"):
            return nc

        def mlp_phase(l, tiles, m, gi, final):
            with ExitStack() as ph:
                w1_sb, w1_bufs = load_w_cols(ph, "w1_sb%d" % l, w1_d[l].rearrange("(kc p) n -> p kc n", p=128), [128, 8, DFF],
                                             [(i * 512, (i + 1) * 512) for i in range(8)])
                w2_sb, b_w2 = load_w(ph, "w2_sb%d" % l, w2_d[l].rearrange("(fc p) n -> p fc n", p=128), [128, 32, D], 8)
                S = alloc_norm(ph, "m%d" % l)
                S["g"], S["b_g"] = load_g(ph, "m%dg" % l, gi)
                xt = SB(ph, "m%dxt" % l, [128, 4, D], F32); b_xt = Buf("xt")
                hid = SB(ph, "m%dhid" % l, [128, 32, TT], BF16); b_hid = Buf("hid")
                S["xn"] = hid[:, 0:8, :].rearrange("p a c -> p (a c)").rearrange("p (s d) -> p s d", s=4)
                S["b_xn"] = b_hid
                rl = [SB(ph, "m%drl%d" % (l, i), [128, TT], BF16) for i in range(3)]; b_rl = [Buf("rl%d" % i) for i in range(3)]
                S["rtmp"] = SB(ph, "m%drtmp" % l, [128, 512], F32); S["b_rtmp"] = Buf("rtmp")
                pz = [PS(ph, "m%dpz%d" % (l, i), [128, TT], F32) for i in range(6)]; b_pz = [Buf("pz%d" % i) for i in range(6)]
                pzi = [0]

                def nextpz():
                    i = pzi[0] % 6
                    pzi[0] += 1
                    return pz[i], b_pz[i]

                for t in tiles:
                    hrow = H[(t - 3) * TT:(t - 2) * TT, :].rearrange("(s p) d -> p s d", p=128)
                    P.dma("sp", xt[:], hrow, b_xt, writes=[b_xt])
                    norm_T(xt, b_xt, m, S)
                    hnT, b_hnT = S["hnT"], S["b_hnT"]
                    for fc in range(32):
                        pp, bp = nextpz()
                        for kc in range(8):
                            P.op("pe", lambda e, kc=kc, fc=fc, pp=pp: e.matmul(pp[:], lhsT=w1_sb[:, kc, fc * 128:(fc + 1) * 128], rhs=hnT[:, kc, :], start=(kc == 0), stop=(kc == 7)),
                                 reads=w1_bufs(fc * 128, fc * 128 + 128) + [b_hnT], writes=[bp], signal=(kc == 7))
                        r_, br = rl[fc % 3], b_rl[fc % 3]
                        P.op("act", lambda e, r_=r_, pp=pp: e.activation(out=r_[:], in_=pp[:], func=AF.Relu), reads=[bp], writes=[br])
                        P.op("dve", lambda e, r_=r_, fc=fc: e.tensor_tensor(out=hid[:, fc, :], in0=r_[:], in1=r_[:], op=ALU.mult), reads=[br], writes=[b_hid])
                    for s in range(4):
                        for nh in range(2):
                            pp, bp = nextpz()
                            for fc in range(32):
                                P.op("pe", lambda e, fc=fc, s=s, nh=nh, pp=pp: e.matmul(pp[:], lhsT=hid[:, fc, s * 128:(s + 1) * 128], rhs=w2_sb[:, fc, nh * 512:(nh + 1) * 512],
                                                                                    start=(fc == 0), stop=(fc == 31)),
                                     reads=[b_hid, b_w2], writes=[bp], signal=(fc == 31))
                            resid_out(xt, b_xt, s, nh, pp, bp, gi, S)
                    if final:
                        dst = out_d[(t - 4) * TT:(t - 3) * TT, :].rearrange("(s p) d -> p s d", p=128)
                    else:
                        dst = hrow
                    P.dma("sp", dst, xt[:], b_xt, reads=[b_xt])
                P.barrier()

        mlp_phase(0, range(3, 8), 1, 1, False)
        if stop == "p2":
            return nc

        with ExitStack() as ph:
            blkb = SB(ph, "blkb", [128, 128], BF16); b_blk = Buf("blkb")
            P.dma("pool", blkb[:], blk_d[:, :], b_blk, writes=[b_blk])
            expB = SB(ph, "expB", [128, 16, 640], BF16); b_expB = Buf("expB")
            KT = SB(ph, "KT", [128, 8, NKV * TT], BF16); b_KT = Buf("KT")
            Va = SB(ph, "Va", [128, NKV * 4, 16, 65], BF16); b_Va = Buf("Va")
            S = alloc_norm(ph, "p3")
            xt = SB(ph, "p3xt", [128, 4, D], F32); b_xt = Buf("xt")
            sq = [SB(ph, "p3sq%d" % i, [128, TT], BF16) for i in range(3)]; b_sq = [Buf("sq%d" % i) for i in range(3)]
            rs = [SB(ph, "p3rs%d" % i, [128, TT], F32) for i in range(3)]; b_rs = [Buf("rs%d" % i) for i in range(3)]
            pSt = [PS(ph, "p3pS%d" % i, [128, 1024], F32) for i in range(2)]
            pOt = [PS(ph, "p3pO%d" % i, [128, 4, 128], F32) for i in range(2)]
            b_bank = [Buf("bank%d" % i) for i in range(6)]
            pS = pSt; b_pS = [[b_bank[0], b_bank[1]], [b_bank[2], b_bank[3]]]
            pO = pOt; b_pO = [b_bank[4], b_bank[5]]
            pz = [pSt[0][:, 0:512], pSt[0][:, 512:1024], pSt[1][:, 0:512], pSt[1][:, 512:1024],
                  pOt[0][:].rearrange("p a b -> p (a b)"), pOt[1][:].rearrange("p a b -> p (a b)")]
            pzi = [0]

            def nextpz():
                i = pzi[0] % 6
                pzi[0] += 1
                return pz[i], b_bank[i]

            def proj_headnorm(w_sb, w_bufs, hnT, b_hnT, dstT, b_dst, col0, gcol):
                for g0 in range(0, 8, 3):
                    fcs = list(range(g0, min(8, g0 + 3)))
                    pps, pms = [], []
                    for fc in fcs:
                        pp, bp = nextpz()
                        for kc in range(8):
                            P.op("pe", lambda e, kc=kc, fc=fc, pp=pp: e.matmul(pp[:], lhsT=w_sb[:, kc, fc * 128:(fc + 1) * 128], rhs=hnT[:, kc, :], start=(kc == 0), stop=(kc == 7)),
                                 reads=w_bufs(fc * 128, fc * 128 + 128) + [b_hnT], writes=[bp], signal=(kc == 7))
                        pps.append((pp, bp))
                    for i, fc in enumerate(fcs):
                        pp, bp = pps[i]
                        P.op("act", lambda e, i=i, pp=pp: e.activation(out=sq[i][:], in_=pp[:], func=AF.Square), reads=[bp], writes=[b_sq[i]])
                    for i, fc in enumerate(fcs):
                        pm_, bpm = nextpz()
                        P.op("pe", lambda e, i=i, pm_=pm_: e.matmul(pm_[:], lhsT=blkb[:], rhs=sq[i][:], start=True, stop=True), reads=[b_blk, b_sq[i]], writes=[bpm])
                        pms.append((pm_, bpm))
                    for i, fc in enumerate(fcs):
                        pm_, bpm = pms[i]
                        P.op("act", lambda e, i=i, pm_=pm_: e.activation(out=rs[i][:], in_=pm_[:], func=AF.Ln, bias=S["epsb"][:, 1:2]), reads=[bpm, S["b_epsb"]], writes=[b_rs[i]])
                    for i, fc in enumerate(fcs):
                        P.op("act", lambda e, i=i: e.activation(out=rs[i][:], in_=rs[i][:], func=AF.Exp, scale=-0.5), reads=[b_rs[i]], writes=[b_rs[i]])
                    for i, fc in enumerate(fcs):
                        pp, bp = pps[i]
                        P.op("dve", lambda e, i=i, fc=fc, pp=pp: e.scalar_tensor_tensor(out=dstT[:, fc, col0:col0 + TT], in0=pp[:], scalar=gqk[:, gcol:gcol + 1], in1=rs[i][:],
                                                                                 op0=ALU.mult, op1=ALU.mult),
                             reads=[bp, b_gqk, b_rs[i]], writes=[b_dst])

            with ExitStack() as p1s:
                kvw_sb, kvw_bufs = load_w_cols(p1s, "kvw_sb", kv_w_d.rearrange("(kc p) n -> p kc n", p=128), [128, 8, 2 * D],
                                               [(i * 512, (i + 1) * 512) for i in range(4)])
                amask = SB(p1s, "amask", [128, 640], F32); b_amask = Buf("amask")
                P.dma("sp", amask[:], amask_d[:, :], b_amask, writes=[b_amask])
                btmp = [SB(p1s, "p3btmp%d" % i, [128, 640], F32) for i in range(2)]; b_btmp = [Buf("btmp0"), Buf("btmp1")]
                xn = SB(p1s, "p3xn", [128, 4, D], BF16); S["xn"] = xn; S["b_xn"] = Buf("xn")
                biasv = biasT_d.rearrange("p (h x) -> p h x", h=16)
                for h in range(16):
                    bt, bbt = btmp[h % 2], b_btmp[h % 2]
                    P.dma("sp", bt[:], biasv[:, h, :], bbt, writes=[bbt])
                    P.op("act", lambda e, bt=bt: e.activation(out=bt[:], in_=bt[:], func=AF.Exp), reads=[bbt], writes=[bbt])
                    P.op("dve", lambda e, h=h, bt=bt: e.tensor_tensor(out=expB[:, h, :], in0=bt[:], in1=amask[:], op=ALU.mult), reads=[bbt, b_amask], writes=[b_expB])
                P.op("dve", lambda e: e.memset(Va[:, :, :, 64:65], 1.0), writes=[b_Va])
                P.op("dve", lambda e: e.tensor_scalar(out=Va[:, 0:4, :, 64:65], in0=Va[:, 0:4, :, 64:65], scalar1=tflag[:, 4:5], scalar2=None, op0=ALU.mult),
                     reads=[b_tflag, b_Va], writes=[b_Va])
                for t in range(3, 8):
                    kt0 = t - 3
                    hrow = H[kt0 * TT:(kt0 + 1) * TT, :].rearrange("(s p) d -> p s d", p=128)
                    P.dma("sp", xt[:], hrow, b_xt, writes=[b_xt])
                    norm_T(xt, b_xt, 2, S)
                    hnT, b_hnT = S["hnT"], S["b_hnT"]
                    proj_headnorm(kvw_sb, kvw_bufs, hnT, b_hnT, KT, b_KT, kt0 * TT, 0)
                    for s in range(4):
                        for nh in range(2):
                            pp, bp = nextpz()
                            for kc in range(8):
                                P.op("pe", lambda e, kc=kc, s=s, nh=nh, pp=pp: e.matmul(pp[:], lhsT=hnT[:, kc, s * 128:(s + 1) * 128], rhs=kvw_sb[:, kc, D + nh * 512:D + (nh + 1) * 512],
                                                                                    start=(kc == 0), stop=(kc == 7)),
                                     reads=kvw_bufs(D + nh * 512, D + (nh + 1) * 512) + [b_hnT], writes=[bp], signal=(kc == 7))
                            if t == 3:
                                P.op("act", lambda e, s=s, nh=nh, pp=pp, kt0=kt0: e.activation(out=Va[:, kt0 * 4 + s, nh * 8:(nh + 1) * 8, 0:64], in_=pp[:].rearrange("p (h d) -> p h d", d=64),
                                                                                           func=AF.Identity, scale=tflag[:, 4:5]),
                                     reads=[bp, b_tflag], writes=[b_Va])
                            else:
                                P.op("act", lambda e, s=s, nh=nh, pp=pp, kt0=kt0: e.activation(out=Va[:, kt0 * 4 + s, nh * 8:(nh + 1) * 8, 0:64], in_=pp[:].rearrange("p (h d) -> p h d", d=64),
                                                                                           func=AF.Copy),
                                     reads=[bp], writes=[b_Va])
                P.barrier()
            if stop == "p3a":
                return nc
            with ExitStack() as p2s:
                wq_sb, wq_bufs = load_w_cols(p2s, "wq_sb", wq_d.rearrange("(kc p) n -> p kc n", p=128), [128, 8, D], [(0, 512), (512, 1024)])
                wo_sb, b_wo = load_w(p2s, "wo_sb", wo_d.rearrange("(kc p) n -> p kc n", p=128), [128, 8, D], 2)
                S["g"], S["b_g"] = load_g(p2s, "p3g", 2)
                QT = SB(p2s, "QT", [128, 8, TT], BF16); b_QT = Buf("QT")
                S["rtmp"] = SB(p2s, "p3rtmp", [128, 512], F32); S["b_rtmp"] = Buf("rtmp")
                Eb = [SB(p2s, "p3E%d" % i, [128, 640], BF16) for i in range(3)]; b_E = [Buf("E%d" % i) for i in range(3)]
                Pb = [SB(p2s, "p3P%d" % i, [128, 640], BF16) for i in range(3)]; b_P = [Buf("P%d" % i) for i in range(3)]
                rec = SB(p2s, "p3rec", [128, 2, 4], F32); b_rec = [Buf("rec0"), Buf("rec1")]
                attn = SB(p2s, "p3attn", [128, D], BF16); b_attn = Buf("attn")
                attnT = SB(p2s, "p3attnT", [128, 8, TT], BF16); b_attnT = Buf("attnT")
                S["xn"] = attnT[:].rearrange("p a c -> p (a c)").rearrange("p (s d) -> p s d", s=4); S["b_xn"] = b_attnT
                for t in range(4, 8):
                    kt0 = t - 3
                    hrow = H[kt0 * TT:(kt0 + 1) * TT, :].rearrange("(s p) d -> p s d", p=128)
                    P.dma("sp", xt[:], hrow, b_xt, writes=[b_xt])
                    norm_T(xt, b_xt, 3, S)
                    hnT, b_hnT = S["hnT"], S["b_hnT"]
                    proj_headnorm(wq_sb, wq_bufs, hnT, b_hnT, QT, b_QT, 0, 1)
                    items = [(qg, h) for qg in range(4) for h in range(16)]

                    def emit_S(idx):
                        qg, h = items[idx]
                        G = kt0 * 4 + qg
                        fc, hf = h // 2, h % 2
                        pS_, bpS = pS[idx % 2], b_pS[idx % 2]
                        for kt in range(5):
                            P.op("pe", lambda e, kt=kt: e.matmul(pS_[:, kt * 128:(kt + 1) * 128],
                                                                 lhsT=KT[hf * 64:(hf + 1) * 64, fc, (G - 4 + kt) * 128:(G - 3 + kt) * 128],
                                                                 rhs=QT[hf * 64:(hf + 1) * 64, fc, qg * 128:(qg + 1) * 128], start=True, stop=True),
                                 reads=[b_KT, b_QT], writes=bpS, signal=(kt == 4))

                    emit_S(0)
                    for idx, (qg, h) in enumerate(items):
                        G = kt0 * 4 + qg
                        h4, hh = h // 4, h % 4
                        if idx + 1 < len(items):
                            emit_S(idx + 1)
                        pS_, bpS = pS[idx % 2], b_pS[idx % 2]
                        pO_, bpO = pO[h4 % 2], b_pO[h4 % 2]
                        E_, bE = Eb[idx % 3], b_E[idx % 3]
                        P_, bP = Pb[idx % 3], b_P[idx % 3]
                        P.op("act", lambda e, E_=E_, pS_=pS_: e.activation(out=E_[:], in_=pS_[:, 0:640], func=AF.Exp), reads=bpS, writes=[bE])
                        P.op("dve", lambda e, E_=E_, P_=P_, h=h: e.tensor_tensor(out=P_[:], in0=E_[:], in1=expB[:, h, :], op=ALU.mult), reads=[bE, b_expB], writes=[bP])
                        for kt in range(5):
                            P.op("pe", lambda e, kt=kt, hh=hh, h=h, G=G, P_=P_, pO_=pO_: e.matmul(pO_[:, hh, 0:65], lhsT=P_[:, kt * 128:(kt + 1) * 128], rhs=Va[:, G - 4 + kt, h, :],
                                                                                           start=(kt == 0), stop=(kt == 4)),
                                 reads=[bP, b_Va], writes=[bpO], signal=(kt == 4))
                        if hh == 3:
                            r_ = h4 % 2
                            P.op("dve", lambda e, r_=r_, pO_=pO_: e.reciprocal(out=rec[:, r_, :], in_=pO_[:, :, 64]), reads=[bpO], writes=[b_rec[r_]])
                            P.op("dve", lambda e, r_=r_, pO_=pO_, h4=h4: e.tensor_tensor(out=attn[:, h4 * 256:(h4 + 1) * 256].rearrange("p (h d) -> p h d", d=64), in0=pO_[:, :, 0:64],
                                                                                    in1=rec[:, r_, :].unsqueeze(2).to_broadcast([128, 4, 64]), op=ALU.mult),
                                 reads=[bpO, b_rec[r_]], writes=[b_attn])
                        if h == 15:
                            for fc in range(8):
                                pT, bT = S["pT"][fc % 2], S["b_pT"][fc % 2]
                                P.op("pe", lambda e, fc=fc, pT=pT: e.transpose(out=pT[:, 0:128], in_=attn[:, fc * 128:(fc + 1) * 128], identity=identb[:]), reads=[b_attn, b_identb], writes=[bT])
                                P.op("act", lambda e, fc=fc, pT=pT, qg=qg: e.activation(out=attnT[:, fc, qg * 128:(qg + 1) * 128], in_=pT[:, 0:128], func=AF.Copy), reads=[bT], writes=[b_attnT])

                    for s in range(4):
                        for nh in range(2):
                            pp, bp = nextpz()
                            for kc in range(8):
                                P.op("pe", lambda e, kc=kc, s=s, nh=nh, pp=pp: e.matmul(pp[:], lhsT=attnT[:, kc, s * 128:(s + 1) * 128], rhs=wo_sb[:, kc, nh * 512:(nh + 1) * 512],
                                                                                    start=(kc == 0), stop=(kc == 7)),
                                     reads=[b_attnT, b_wo], writes=[bp], signal=(kc == 7))
                            resid_out(xt, b_xt, s, nh, pp, bp, 2, S)
                    P.dma("sp", hrow, xt[:], b_xt, reads=[b_xt])
                P.barrier()
        if stop == "p3":
            return nc

        mlp_phase(1, range(4, 8), 4, 3, True)
    return nc


def _host_consts():
    ident = np.eye(128, dtype=np.float32)
    blk = np.zeros((128, 128), np.float32)
    blk[:64, :64] = 1.0 / 64
    blk[64:, 64:] = 1.0 / 64
    kk = np.arange(128)[:, None, None]
    kt = np.arange(5)[None, :, None]
    q = np.arange(128)[None, None, :]
    kidx = kt * 128 + kk
    ck = kidx // 64
    cq = q // 64
    amask = ((ck >= cq) & (ck <= cq + 8)).astype(np.float32)
    bidx = np.minimum(640 + q - kidx, 256)
    bidx = np.maximum(bidx, 0)
    return ident, blk, amask.reshape(128, 640), bidx


def _pvec(v, n):
    return np.ascontiguousarray(np.asarray(v, np.float32).reshape(n, 128).T)


def make_in_maps(inputs):
    f = lambda k: np.ascontiguousarray(np.asarray(inputs[k], dtype=np.float32))
    x = f("x"); c = f("c")
    ident, blk, amask, bidx = _host_consts()
    rel_bias = f("rel_bias")[0]
    biasT = rel_bias[:, bidx]
    biasT = np.ascontiguousarray(biasT.transpose(1, 0, 2, 3).reshape(128, 16 * 640))
    conv_w = f("lru_conv_w")[0]
    convw = np.ascontiguousarray(conv_w.reshape(4, NJ, 128).transpose(2, 1, 0).reshape(128, NJ * 4))
    vec_parts = {
        "n1g0": _pvec(f("norm1_g")[0], 8), "n2g0": _pvec(f("norm2_g")[0], 8), "kvg": _pvec(f("kv_norm_g"), 8),
        "n1g1": _pvec(f("norm1_g")[1], 8), "n2g1": _pvec(f("norm2_g")[1], 8), "convw": convw,
        "convb": _pvec(f("lru_conv_b")[0], NJ), "gab": _pvec(f("lru_gate_a_b")[0], NJ), "gxb": _pvec(f("lru_gate_x_b")[0], NJ),
        "lam": _pvec(f("lru_lambda")[0], NJ),
        "gk": np.tile(f("k_norm_g"), 2).reshape(128, 1), "gq": np.tile(f("q_norm_g")[0], 2).reshape(128, 1),
    }
    vecs = np.zeros((128, NV), np.float32)
    for k, (o, n) in VOFF.items():
        vecs[:, o:o + n] = vec_parts[k]
    shared = {
        "vecs": vecs, "ada_w": f("ada_w"), "ada_b": f("ada_b"), "kv_ada_w": f("kv_ada_w"), "kv_ada_b": f("kv_ada_b").reshape(1, -1),
        "w_in": f("lru_w_in")[0], "ga_w": f("lru_gate_a_w")[0], "gx_w": f("lru_gate_x_w")[0], "w_out": f("lru_w_out")[0],
        "mlp_w1": f("mlp_w1"), "mlp_w2": f("mlp_w2"), "kv_w": f("kv_w"), "wq": f("attn_w_q")[0], "wo": f("attn_w_o")[0],
        "biasT": biasT, "amask": amask, "ident": ident, "blk64": blk,
    }
    maps = []
    for core in range(8):
        b, half = core // 2, core % 2
        if half == 1:
            xin = x[b]
        else:
            xin = np.concatenate([np.zeros((2048, D), np.float32), x[b, :2048]], axis=0)
        tflag = np.ones((128, NTILE + 1), np.float32)
        tflag[:, 4] = float(half)
        m = dict(shared)
        m["xin"] = np.ascontiguousarray(xin)
        m["ct"] = _pvec(c[b], 8)
        m["tflag"] = tflag
        maps.append(m)
    return maps


def kernel(**inputs):
    nc = build()
    maps = make_in_maps(inputs)
    res = run_bass_kernel_spmd(nc, maps, core_ids=list(range(8)))
    out = np.zeros((4, 4096, D), np.float32)
    for core in range(8):
        b, half = core // 2, core % 2
        out[b, half * 2048:(half + 1) * 2048] = res.results[core]["out"]
    return out
```

```python
import numpy as np
from contextlib import ExitStack
import concourse.bass as bass
import concourse.mybir as mybir
from concourse.bass_utils import run_bass_kernel_spmd

F32 = mybir.dt.float32
BF16 = mybir.dt.bfloat16
AF = mybir.ActivationFunctionType
ALU = mybir.AluOpType
AX = mybir.AxisListType

D = 1024
W = 1408
NJ = 11
DFF = 4096
TT = 512
NTILE = 8
NKV = 5
EPS = 1e-6


class Buf:
    __slots__ = ("name", "w", "r", "dsem", "dcnt")

    def __init__(self, name):
        self.name = name
        self.w = None
        self.r = []
        self.dsem = None
        self.dcnt = 0


class Prog:
    def __init__(self, nc, stack):
        self.nc = nc
        self.stack = stack
        self.eng = {"pe": nc.tensor, "act": nc.scalar, "dve": nc.vector, "pool": nc.gpsimd, "sp": nc.sync}
        self.sem = {k: stack.enter_context(nc.semaphore("s_" + k)) for k in self.eng}
        self.cnt = {k: 0 for k in self.eng}
        self.seen = {k: {} for k in self.eng}
        self.pending = {k: False for k in self.eng}
        self.dbufs = []
        self.nsem = 0

    def _wait(self, e, tok):
        if tok is None:
            return
        sem, val, key = tok
        if key == "pe" and e == "pe":
            return
        if self.seen[e].get(key, 0) >= val:
            return
        self.eng[e].wait_ge(sem, val)
        self.seen[e][key] = val

    def _hazards(self, e, reads, writes, group_sem=None):
        for b in reads:
            self._wait(e, b.w)
        for b in writes:
            if not (group_sem is not None and b.w is not None and b.w[0] is group_sem):
                self._wait(e, b.w)
            for t in b.r:
                self._wait(e, t)

    def op(self, e, ins_fn, reads=(), writes=(), signal=True):
        self._hazards(e, reads, writes)
        ins = ins_fn(self.eng[e])
        if signal:
            self.cnt[e] += 1
            ins.then_inc(self.sem[e], 1)
            tok = (self.sem[e], self.cnt[e], e)
            self.pending[e] = False
        else:
            assert e == "pe"
            tok = (self.sem[e], self.cnt[e] + 1, e)
            self.pending[e] = True
        for b in reads:
            b.r.append(tok)
        for b in writes:
            b.w = tok
            b.r = []
        return tok

    def dma(self, e, out, in_, sembuf, reads=(), writes=(), group=False, **kw):
        if sembuf.dsem is None:
            sembuf.dsem = self.stack.enter_context(self.nc.semaphore("d%d_%s" % (self.nsem, sembuf.name)))
            self.nsem += 1
            self.dbufs.append(sembuf)
        self._hazards(e, reads, writes, group_sem=sembuf.dsem if group else None)
        ins = self.eng[e].dma_start(out=out, in_=in_, **kw)
        sembuf.dcnt += 16
        ins.then_inc(sembuf.dsem, 16)
        tok = (sembuf.dsem, sembuf.dcnt, ("d", id(sembuf)))
        for b in reads:
            b.r.append(tok)
        for b in writes:
            b.w = tok
            b.r = []
        return tok

    def barrier(self, engines=None):
        toks = [(self.sem[k], self.cnt[k], k) for k in self.eng if self.cnt[k] > 0]
        toks += [(b.dsem, b.dcnt, ("d", id(b))) for b in self.dbufs]
        for e in (engines or self.eng):
            for t in toks:
                if t[2] == e:
                    continue
                self._wait(e, t)
        assert not self.pending["pe"]


def gate_pieces():
    pcs = []
    for n in range(16):
        lo, hi = 88 * n, 88 * n + 88
        for ci in range(NJ):
            r0, r1 = max(lo, 128 * ci), min(hi, 128 * ci + 128)
            if r0 >= r1:
                continue
            for co in range(NJ):
                c0, c1 = max(lo, 128 * co), min(hi, 128 * co + 128)
                if c0 >= c1:
                    continue
                d = ci - co + 1
                assert 0 <= d <= 2
                pcs.append((n, r0 - lo, r1 - lo, c0 - lo, c1 - lo, ci, co, d, r0 - 128 * ci, r1 - 128 * ci, c0 - 128 * co, c1 - 128 * co))
    return pcs


def gate_nbrs():
    nb = {co: set() for co in range(NJ)}
    for p in gate_pieces():
        nb[p[6]].add(p[5])
    return {co: sorted(v) for co, v in nb.items()}


VOFF = {}
_o = 0
for _name, _n in [("n1g0", 8), ("n2g0", 8), ("kvg", 8), ("n1g1", 8), ("n2g1", 8), ("convw", 44), ("convb", 11),
                  ("gab", 11), ("gxb", 11), ("lam", 11), ("gk", 1), ("gq", 1)]:
    VOFF[_name] = (_o, _n)
    _o += _n
NV = _o


def build(stop=None):
    nc = bass.Bass("TRN2", target_bir_lowering=False)
    dt_in = lambda name, shape: nc.dram_tensor(name, shape, F32, kind="ExternalInput").ap()
    xin = dt_in("xin", [NTILE * TT, D])
    ct_d = dt_in("ct", [128, 8])
    tflag_d = dt_in("tflag", [128, NTILE + 1])
    vecs_d = dt_in("vecs", [128, NV])
    ada_w = dt_in("ada_w", [2, D, 6 * D])
    ada_b = dt_in("ada_b", [2, 6 * D])
    kv_ada_w = dt_in("kv_ada_w", [D, 2 * D])
    kv_ada_b = dt_in("kv_ada_b", [1, 2 * D])
    w_in_d = dt_in("w_in", [D, 2 * W])
    ga_w_d = dt_in("ga_w", [16, 88, 88])
    gx_w_d = dt_in("gx_w", [16, 88, 88])
    w_out_d = dt_in("w_out", [W, D])
    w1_d = dt_in("mlp_w1", [2, D, DFF])
    w2_d = dt_in("mlp_w2", [2, DFF, D])
    kv_w_d = dt_in("kv_w", [D, 2 * D])
    wq_d = dt_in("wq", [D, D])
    wo_d = dt_in("wo", [D, D])
    biasT_d = dt_in("biasT", [128, 16 * 5 * 128])
    amask_d = dt_in("amask", [128, 5 * 128])
    ident_d = dt_in("ident", [128, 128])
    blk_d = dt_in("blk64", [128, 128])
    out_d = nc.dram_tensor("out", [4 * TT, D], F32, kind="ExternalOutput").ap()
    H = nc.dram_tensor("Hs", [NKV * TT, D], F32, kind="ExternalOutput" if stop else "Internal").ap()
    Gs = nc.dram_tensor("Gs", [4, 128, D], F32, kind="Internal").ap()

    with ExitStack() as top:
        P = Prog(nc, top)

        def SB(stack, name, shape, dt):
            return stack.enter_context(nc.sbuf_tensor("sb_" + name, shape, dt))

        def PS(stack, name, shape, dt):
            return stack.enter_context(nc.psum_tensor("ps_" + name, shape, dt))

        vecs = SB(top, "vecs", [128, NV], F32); b_vecs = Buf("vecs")
        tflag = SB(top, "tflag", [128, NTILE + 1], F32); b_tflag = Buf("tflag")
        identb = SB(top, "identb", [128, 128], BF16); b_identb = Buf("identb")
        Asc = SB(top, "Asc", [128, 5, 8], F32); Ash = SB(top, "Ash", [128, 5, 8], F32); b_mod = Buf("modvec")
        cch = SB(top, "cch", [128, 2, NJ], F32); b_cch = Buf("cch")
        nbias = SB(top, "nbias", [128, 2, NJ], F32); b_nbias = Buf("nbias")
        gqk = SB(top, "gqk", [128, 2], F32); b_gqk = Buf("gqk")

        P.dma("sp", vecs[:], vecs_d[:, :], b_vecs, writes=[b_vecs])
        P.dma("sp", tflag[:], tflag_d[:, :], b_tflag, writes=[b_tflag])
        P.dma("pool", identb[:], ident_d[:, :], b_identb, writes=[b_identb])

        def V(name):
            o, n = VOFF[name]
            return vecs[:, o:o + n]

        with ExitStack() as ph:
            ct = SB(ph, "ct", [128, 8], F32); b_ct = Buf("ct")
            t8a = SB(ph, "t8a", [128, 8], F32); t8b = SB(ph, "t8b", [128, 8], F32); b_t8 = Buf("t8")
            cact = SB(ph, "cact", [128, 8], F32); b_cact = Buf("cact")
            lc = SB(ph, "lc", [128, 8, 128], BF16); b_lc = Buf("lc")
            identf = SB(ph, "identf", [128, 128], F32); b_identf = Buf("identf")
            modrow = [SB(ph, "modrow%d" % i, [128, 6 * D], F32) for i in range(2)]
            kvrow = SB(ph, "kvrow", [128, 2 * D], F32)
            b_rows = [Buf("modrow0"), Buf("modrow1"), Buf("kvrow")]
            wch = [SB(ph, "wch%d" % i, [128, 8, 512], BF16) for i in range(3)]; b_wch = [Buf("wch%d" % i) for i in range(3)]
            dtmp = SB(ph, "dtmp", [128, 8, 128], F32); b_dtmp = Buf("dtmp")
            t11 = SB(ph, "t11", [128, NJ], F32); b_t11 = Buf("t11")
            pm = [PS(ph, "pm%d" % i, [128, 512], F32) for i in range(2)]; b_pm = [Buf("pm%d" % i) for i in range(2)]

            P.dma("sp", ct[:], ct_d[:, :], b_ct, writes=[b_ct])
            P.dma("sp", identf[:], ident_d[:, :], b_identf, writes=[b_identf])
            rows = [modrow[0], modrow[1], kvrow]
            P.dma("sp", modrow[0][:], ada_b[0:1, :].partition_broadcast(128), b_rows[0], writes=[b_rows[0]])
            P.dma("sp", modrow[1][:], ada_b[1:2, :].partition_broadcast(128), b_rows[1], writes=[b_rows[1]])
            P.dma("sp", kvrow[:], kv_ada_b[0:1, :].partition_broadcast(128), b_rows[2], writes=[b_rows[2]])
            P.op("act", lambda e: e.activation(out=t8a[:], in_=ct[:], func=AF.Exp, scale=-1.0), reads=[b_ct], writes=[b_t8])
            P.op("act", lambda e: e.activation(out=t8b[:], in_=t8a[:], func=AF.Ln, bias=1.0), reads=[b_t8], writes=[b_t8])
            P.op("act", lambda e: e.activation(out=t8a[:], in_=t8b[:], func=AF.Exp, scale=-1.0), reads=[b_t8], writes=[b_t8])
            P.op("dve", lambda e: e.tensor_tensor(out=cact[:], in0=t8a[:], in1=ct[:], op=ALU.mult), reads=[b_t8, b_ct], writes=[b_cact])
            P.op("dve", lambda e: e.tensor_copy(out=lc[:], in_=cact[:].unsqueeze(2).to_broadcast([128, 8, 128])), reads=[b_cact], writes=[b_lc])
            P.op("act", lambda e: e.activation(out=t11[:], in_=V("lam"), func=AF.Exp, scale=-1.0), reads=[b_vecs], writes=[b_t11])
            P.op("act", lambda e: e.activation(out=t11[:], in_=t11[:], func=AF.Ln, bias=1.0), reads=[b_t11], writes=[b_t11])
            P.op("dve", lambda e: e.tensor_scalar(out=cch[:, 0, :], in0=t11[:], scalar1=-8.0, scalar2=None, op0=ALU.mult), reads=[b_t11], writes=[b_cch])
            P.op("dve", lambda e: e.tensor_scalar(out=cch[:, 1, :], in0=t11[:], scalar1=-16.0, scalar2=None, op0=ALU.mult), reads=[b_t11], writes=[b_cch])
            P.op("dve", lambda e: e.tensor_scalar(out=nbias[:, 0, :], in0=V("gab"), scalar1=-1.0, scalar2=None, op0=ALU.mult), reads=[b_vecs], writes=[b_nbias])
            P.op("dve", lambda e: e.tensor_scalar(out=nbias[:, 1, :], in0=V("gxb"), scalar1=-1.0, scalar2=None, op0=ALU.mult), reads=[b_vecs], writes=[b_nbias])
            P.op("dve", lambda e: e.tensor_copy(out=gqk[:, 0:1], in_=V("gk")), reads=[b_vecs], writes=[b_gqk])
            P.op("dve", lambda e: e.tensor_scalar(out=gqk[:, 1:2], in0=V("gq"), scalar1=0.125, scalar2=None, op0=ALU.mult), reads=[b_vecs], writes=[b_gqk])

            srcs = [(ada_w[0], 6 * D, 0), (ada_w[1], 6 * D, 1), (kv_ada_w, 2 * D, 2)]
            it = 0
            for (wsrc, ncols, ri) in srcs:
                wv = wsrc.rearrange("(kc p) n -> p kc n", p=128)
                for j in range(ncols // 512):
                    wb_, bb_ = wch[it % 3], b_wch[it % 3]
                    pp, bp = pm[it % 2], b_pm[it % 2]
                    P.dma("pool", wb_[:], wv[:, :, j * 512:(j + 1) * 512], bb_, writes=[bb_])
                    for kc in range(8):
                        P.op("pe", lambda e, kc=kc, wb_=wb_, pp=pp: e.matmul(pp[:], lhsT=lc[:, kc, :], rhs=wb_[:, kc, :], start=(kc == 0), stop=(kc == 7)),
                             reads=[b_lc, bb_], writes=[bp], signal=(kc == 7))
                    rr = rows[ri]
                    P.op("dve", lambda e, rr=rr, pp=pp, j=j: e.tensor_tensor(out=rr[:, j * 512:(j + 1) * 512], in0=pp[:], in1=rr[:, j * 512:(j + 1) * 512], op=ALU.add),
                         reads=[bp, b_rows[ri]], writes=[b_rows[ri]])
                    it += 1

            specs = [(0, modrow[0], 0, 1, "n1g0"), (1, modrow[0], 3, 4, "n2g0"), (2, kvrow, 0, 1, "kvg"),
                     (3, modrow[1], 0, 1, "n1g1"), (4, modrow[1], 3, 4, "n2g1")]
            rbuf = {0: b_rows[0], 1: b_rows[0], 2: b_rows[2], 3: b_rows[1], 4: b_rows[1]}
            for (m, rr, sseg, cseg, gname) in specs:
                for (seg, dst) in [(sseg, Ash), (cseg, t8a)]:
                    P.op("dve", lambda e, rr=rr, seg=seg: e.tensor_tensor(out=dtmp[:], in0=rr[:, seg * D:(seg + 1) * D].rearrange("p (g k) -> p g k", k=128),
                                                                     in1=identf[:].unsqueeze(1).to_broadcast([128, 8, 128]), op=ALU.mult),
                         reads=[rbuf[m], b_identf], writes=[b_dtmp])
                    if dst is Ash:
                        P.op("dve", lambda e, m=m: e.tensor_reduce(out=Ash[:, m, :], in_=dtmp[:], axis=AX.X, op=ALU.add), reads=[b_dtmp], writes=[b_mod])
                    else:
                        P.op("dve", lambda e: e.tensor_reduce(out=t8a[:], in_=dtmp[:], axis=AX.X, op=ALU.add), reads=[b_dtmp], writes=[b_t8])
                P.op("dve", lambda e: e.tensor_scalar(out=t8b[:], in0=t8a[:], scalar1=1.0, scalar2=32.0, op0=ALU.add, op1=ALU.mult), reads=[b_t8], writes=[b_t8])
                P.op("dve", lambda e, m=m, gname=gname: e.tensor_tensor(out=Asc[:, m, :], in0=t8b[:], in1=V(gname), op=ALU.mult), reads=[b_t8, b_vecs], writes=[b_mod])
            for gi, (rr, seg, rb) in enumerate([(modrow[0], 2, b_rows[0]), (modrow[0], 5, b_rows[0]), (modrow[1], 2, b_rows[1]), (modrow[1], 5, b_rows[1])]):
                P.dma("sp", Gs[gi], rr[:, seg * D:(seg + 1) * D], rb, reads=[rb])
            P.barrier()

        def load_w(stack, name, src_view, shape, nsplit):
            t = SB(stack, name, shape, BF16)
            b = Buf(name)
            a = shape[1]
            step = (a + nsplit - 1) // nsplit
            for s0 in range(0, a, step):
                s1 = min(a, s0 + step)
                P.dma("pool", t[:, s0:s1, :], src_view[:, s0:s1, :], b, writes=[b], group=True)
            return t, b

        def load_w_cols(stack, name, src_view, shape, splits):
            t = SB(stack, name, shape, BF16)
            blocks = []
            for (c0, c1) in splits:
                b = Buf("%s_%d" % (name, c0))
                P.dma("pool", t[:, :, c0:c1], src_view[:, :, c0:c1], b, writes=[b])
                blocks.append((c0, c1, b))

            def bufs(c0, c1):
                return [b for (a0, a1, b) in blocks if a0 < c1 and c0 < a1]
            return t, bufs

        def norm_T(xt, b_xt, m, S):
            P.op("act", lambda e: e.activation(out=S["junk"][:], in_=xt[:, 0, :], func=AF.Square, accum_out=S["ss"][:, 0:1]), reads=[b_xt], writes=[S["b_junk"], S["b_ss"]])
            for s in range(1, 4):
                P.op("act", lambda e, s=s: e.activation(out=S["junk"][:], in_=xt[:, s, :], func=AF.Square, accum_out=S["ss"][:, s:s + 1]), reads=[b_xt], writes=[S["b_junk"], S["b_ss"]])
            P.op("act", lambda e: e.activation(out=S["ss"][:, 4:8], in_=S["ss"][:, 0:4], func=AF.Ln, bias=S["epsb"][:, 0:1]), reads=[S["b_ss"], S["b_epsb"]], writes=[S["b_ss"]])
            P.op("act", lambda e: e.activation(out=S["ss"][:, 8:12], in_=S["ss"][:, 4:8], func=AF.Exp, scale=-0.5), reads=[S["b_ss"]], writes=[S["b_ss"]])
            for s in range(4):
                P.op("dve", lambda e, s=s: e.tensor_scalar(out=S["xn"][:, s, :], in0=xt[:, s, :], scalar1=S["ss"][:, 8 + s:9 + s], scalar2=None, op0=ALU.mult),
                     reads=[b_xt, S["b_ss"]], writes=(S["b_xn"] if isinstance(S["b_xn"], list) else [S["b_xn"]]))
            for fc in range(8):
                pT, bT = S["pT"][fc % 2], S["b_pT"][fc % 2]
                for s in range(4):
                    P.op("pe", lambda e, s=s, fc=fc, pT=pT: e.transpose(out=pT[:, s * 128:(s + 1) * 128], in_=S["xn"][:, s, fc * 128:(fc + 1) * 128], identity=identb[:]),
                         reads=(S["b_xn"] if isinstance(S["b_xn"], list) else [S["b_xn"]]) + [b_identb], writes=[bT], signal=(s == 3))
                P.op("act", lambda e, fc=fc, pT=pT: e.activation(out=S["hnT"][:, fc, :], in_=pT[:, 0:TT], func=AF.Identity, scale=Asc[:, m, fc:fc + 1], bias=Ash[:, m, fc:fc + 1]),
                     reads=[bT, b_mod], writes=[S["b_hnT"]])

        def alloc_norm(stack, pfx):
            S = {}
            S["junk"] = SB(stack, pfx + "junk", [128, D], BF16); S["b_junk"] = Buf("junk")
            S["ss"] = SB(stack, pfx + "ss", [128, 12], F32); S["b_ss"] = Buf("ss")
            S["epsb"] = SB(stack, pfx + "epsb", [128, 2], F32); S["b_epsb"] = Buf("epsb")
            S["hnT"] = SB(stack, pfx + "hnT", [128, 8, TT], BF16); S["b_hnT"] = Buf("hnT")
            S["pT"] = [PS(stack, pfx + "pT%d" % i, [128, 2 * TT], BF16) for i in range(2)]; S["b_pT"] = [Buf("pT0"), Buf("pT1")]
            P.op("dve", lambda e: e.memset(S["epsb"][:, 0:1], 1024.0 * EPS), writes=[S["b_epsb"]])
            P.op("dve", lambda e: e.memset(S["epsb"][:, 1:2], EPS), writes=[S["b_epsb"]])
            return S

        def load_g(stack, name, gi):
            g = SB(stack, name, [128, D], F32); b = Buf(name)
            P.dma("sp", g[:], Gs[gi], b, writes=[b])
            return g, b

        def resid_out(xt, b_xt, s, nh, po, bpo, gi, S):
            P.op("dve", lambda e: e.tensor_tensor(out=S["rtmp"][:], in0=po[:], in1=S["g"][:, nh * 512:(nh + 1) * 512], op=ALU.mult),
                 reads=[bpo, S["b_g"]], writes=[S["b_rtmp"]])
            P.op("dve", lambda e: e.tensor_tensor(out=xt[:, s, nh * 512:(nh + 1) * 512], in0=S["rtmp"][:], in1=xt[:, s, nh * 512:(nh + 1) * 512], op=ALU.add),
                 reads=[S["b_rtmp"], b_xt], writes=[b_xt])

        nbrs = gate_nbrs()
        with ExitStack() as ph:
            w_in_sb, w_in_bufs = load_w_cols(ph, "w_in_sb", w_in_d.rearrange("(kc p) n -> p kc n", p=128), [128, 8, 2 * W],
                                             [(W, W + 512), (W + 512, 2 * W), (0, 704), (704, W)])
            wg = SB(ph, "wg", [128, 2, NJ * 3, 128], BF16); b_wg = Buf("wg")
            P.op("dve", lambda e: e.memset(wg[:], 0.0), writes=[b_wg])
            for gidx, gsrc in enumerate([ga_w_d, gx_w_d]):
                for (n, r0, r1, c0, c1, ci, co, d, p0, p1, q0, q1) in gate_pieces():
                    P.dma("pool", wg[p0:p1, gidx, co * 3 + d, q0:q1], gsrc[n, r0:r1, c0:c1], b_wg, writes=[b_wg], group=True)
            w_out_sb, b_w_out = load_w(ph, "w_out_sb", w_out_d.rearrange("(jc p) n -> p jc n", p=128), [128, NJ, D], 2)

            S = alloc_norm(ph, "p1")
            S["g"], S["b_g"] = load_g(ph, "p1g", 0)
            xts = [SB(ph, "p1xt%d" % i, [128, 4, D], F32) for i in range(2)]; b_xts = [Buf("xt0"), Buf("xt1")]
            yb = SB(ph, "p1yb", [128, NJ, TT], BF16); b_yb = Buf("yb")
            NXR = 2
            xrb = [SB(ph, "p1xrb%d" % i, [128, TT + 3], F32) for i in range(NXR)]; b_xrb = [Buf("xrb%d" % i) for i in range(NXR)]
            halo = SB(ph, "p1halo", [128, NJ, 3], F32); b_halo = [Buf("halo%d" % j) for j in range(NJ)]
            xc = SB(ph, "p1xc", [128, NJ, TT], F32); b_xc = [Buf("xc%d" % j) for j in range(NJ)]
            xcb = SB(ph, "p1xcb", [128, NJ, TT], BF16); b_xcb = [Buf("xcb%d" % j) for j in range(NJ)]
            S["xn"] = xcb[:, 0:8, :].rearrange("p a c -> p (a c)").rearrange("p (s d) -> p s d", s=4)
            S["b_xn"] = b_xcb[0:8]
            state = SB(ph, "p1state", [128, NJ], F32); b_state = Buf("state")
            NTMP = 3
            tA = [SB(ph, "p1tA%d" % i, [128, TT], F32) for i in range(NTMP)]; b_tA = [Buf("tA%d" % i) for i in range(NTMP)]
            tQ = [SB(ph, "p1tQ%d" % i, [128, TT], F32) for i in range(NTMP)]; b_tQ = [Buf("tQ%d" % i) for i in range(NTMP)]
            tG, b_tG = tA, b_tA
            NGRP = 6
            Rst = SB(ph, "p1R", [128, NGRP, TT], F32); b_R = [Buf("R%d" % i) for i in range(NGRP)]
            S["rtmp"] = tA[0]; S["b_rtmp"] = b_tA[0]
            pz = [PS(ph, "p1pz%d" % i, [128, TT], F32) for i in range(6)]; b_pz = [Buf("pz%d" % i) for i in range(6)]
            pzi = [0]

            def nextpz():
                i = pzi[0] % 6
                pzi[0] += 1
                return pz[i], b_pz[i]

            P.op("dve", lambda e: e.memset(halo[:], 0.0), writes=b_halo)
            P.op("dve", lambda e: e.memset(state[:], 0.0), writes=[b_state])
            cw = V("convw")
            cb = V("convb")

            def load_x(t):
                P.dma("sp", xts[t % 2][:], xin[t * TT:(t + 1) * TT, :].rearrange("(s p) d -> p s d", p=128), b_xts[t % 2], writes=[b_xts[t % 2]])

            ntile_p1 = NTILE if stop != "p1a" else 5
            hnT, b_hnT = S["hnT"], S["b_hnT"]
            gab, gxb = V("gab"), V("gxb")

            def norm_part(t):
                norm_T(xts[t % 2], b_xts[t % 2], 0, S)

            def main_part(t):
                full = t >= 3
                def halo_in(j):
                    xr, bxr = xrb[j % NXR], b_xrb[j % NXR]
                    P.op("pool", lambda e: e.tensor_copy(out=xr[:, 0:3], in_=halo[:, j, :]), reads=[b_halo[j]], writes=[bxr])

                halo_in(0)
                for j in range(NJ):
                    oc = NJ + j
                    pp, bp = nextpz()
                    for kc in range(8):
                        P.op("pe", lambda e, kc=kc, oc=oc, pp=pp: e.matmul(pp[:], lhsT=w_in_sb[:, kc, oc * 128:(oc + 1) * 128], rhs=hnT[:, kc, :], start=(kc == 0), stop=(kc == 7)),
                             reads=w_in_bufs(oc * 128, oc * 128 + 128) + [b_hnT], writes=[bp], signal=(kc == 7))
                    xr, bxr = xrb[j % NXR], b_xrb[j % NXR]
                    P.op("act", lambda e, xr=xr, pp=pp: e.activation(out=xr[:, 3:TT + 3], in_=pp[:], func=AF.Copy), reads=[bp], writes=[bxr])
                    P.op("act", lambda e, pp=pp, j=j: e.activation(out=xc[:, j, :], in_=pp[:], func=AF.Identity, scale=cw[:, j * 4 + 3:j * 4 + 4], bias=cb[:, j:j + 1]),
                         reads=[bp, b_vecs], writes=[b_xc[j]])
                    if j + 1 < NJ:
                        halo_in(j + 1)
                    for k in range(3):
                        P.op("dve", lambda e, xr=xr, j=j, k=k: e.scalar_tensor_tensor(out=xc[:, j, :], in0=xr[:, k:k + TT], scalar=cw[:, j * 4 + k:j * 4 + k + 1], in1=xc[:, j, :],
                                                                                  op0=ALU.mult, op1=ALU.add),
                             reads=[bxr, b_xc[j], b_vecs], writes=[b_xc[j]])
                    P.op("pool", lambda e, xr=xr, j=j: e.tensor_scalar(out=halo[:, j, :], in0=xr[:, TT:TT + 3], scalar1=tflag[:, t + 1:t + 2], scalar2=None, op0=ALU.mult),
                         reads=[bxr, b_tflag], writes=[b_halo[j]])
                    P.op("pool", lambda e, j=j: e.tensor_copy(out=xcb[:, j, :], in_=xc[:, j, :]), reads=[b_xc[j]], writes=[b_xcb[j]])
                if full:
                    for j in range(NJ):
                        pp, bp = nextpz()
                        for kc in range(8):
                            P.op("pe", lambda e, kc=kc, j=j, pp=pp: e.matmul(pp[:], lhsT=w_in_sb[:, kc, j * 128:(j + 1) * 128], rhs=hnT[:, kc, :], start=(kc == 0), stop=(kc == 7)),
                                 reads=w_in_bufs(j * 128, j * 128 + 128) + [b_hnT], writes=[bp], signal=(kc == 7))
                        P.op("act", lambda e, j=j, pp=pp: e.activation(out=yb[:, j, :], in_=pp[:], func=AF.Gelu_apprx_tanh), reads=[bp], writes=[b_yb])
                for grp in [list(range(0, NGRP)), list(range(NGRP, NJ))]:
                    for gi_, co in enumerate(grp):
                        pa, bpa = nextpz()
                        px, bpx = nextpz()
                        cis = nbrs[co]
                        for gidx, (pg, bpg) in enumerate([(pa, bpa), (px, bpx)]):
                            for n_, ci in enumerate(cis):
                                d = ci - co + 1
                                P.op("pe", lambda e, pg=pg, gidx=gidx, d=d, ci=ci, n_=n_, co=co, cis=cis: e.matmul(pg[:], lhsT=wg[:, gidx, co * 3 + d, :], rhs=xcb[:, ci, :],
                                                                                                         start=(n_ == 0), stop=(n_ == len(cis) - 1)),
                                     reads=[b_wg, b_xcb[ci]], writes=[bpg], signal=(n_ == len(cis) - 1))
                        I_, bI = tQ[co % NTMP], b_tQ[co % NTMP]
                        P.op("act", lambda e, pa=pa, gi_=gi_, co=co: e.activation(out=Rst[:, gi_, :], in_=pa[:], func=AF.Sigmoid, bias=gab[:, co:co + 1]), reads=[bpa, b_vecs], writes=[b_R[gi_]])
                        P.op("act", lambda e, px=px, I_=I_, co=co: e.activation(out=I_[:], in_=px[:], func=AF.Sigmoid, bias=gxb[:, co:co + 1]), reads=[bpx, b_vecs], writes=[bI])
                        P.op("pool", lambda e, I_=I_, co=co: e.tensor_tensor(out=xc[:, co, :], in0=I_[:], in1=xc[:, co, :], op=ALU.mult), reads=[bI, b_xc[co]], writes=[b_xc[co]])

                    def stage_b(co, gi_, k_):
                        A_, bA = tA[k_], b_tA[k_]
                        Q_, bQ = tQ[k_], b_tQ[k_]
                        return [
                            lambda: P.op("act", lambda e: e.activation(out=A_[:], in_=Rst[:, gi_, :], func=AF.Exp, scale=cch[:, 0, co:co + 1]), reads=[b_R[gi_], b_cch], writes=[bA]),
                            lambda: P.op("pool", lambda e: e.tensor_tensor(out=Q_[:], in0=A_[:], in1=A_[:], op=ALU.mult), reads=[bA], writes=[bQ]),
                            lambda: P.op("act", lambda e: e.activation(out=Q_[:], in_=Q_[:], func=AF.Ln, scale=-1.0, bias=1.000001), reads=[bQ], writes=[bQ]),
                            lambda: P.op("act", lambda e: e.activation(out=Q_[:], in_=Q_[:], func=AF.Exp, scale=0.5), reads=[bQ], writes=[bQ]),
                            lambda: P.op("pool", lambda e: e.tensor_tensor(out=Q_[:], in0=Q_[:], in1=xc[:, co, :], op=ALU.mult), reads=[bQ, b_xc[co]], writes=[bQ]),
                            lambda: P.op("dve", lambda e: e.tensor_tensor_scan(out=xc[:, co, :], data0=A_[:], data1=Q_[:], initial=state[:, co:co + 1], op0=ALU.mult, op1=ALU.add),
                                         reads=[bA, bQ, b_state], writes=[b_xc[co]]),
                        ]

                    for c3 in range(0, len(grp), NTMP):
                        chains = [stage_b(co, c3 + k_, k_) for k_, co in enumerate(grp[c3:c3 + NTMP])]
                        for si in range(len(chains[0])):
                            for ch in chains:
                                ch[si]()
                P.op("dve", lambda e: e.tensor_scalar(out=state[:], in0=xc[:, :, TT - 1], scalar1=tflag[:, t + 1:t + 2], scalar2=None, op0=ALU.mult),
                     reads=b_xc + [b_tflag], writes=[b_state])

            def tail_part(t):
                xt, b_xt = xts[t % 2], b_xts[t % 2]
                for j in range(NJ):
                    P.op("dve", lambda e, j=j: e.tensor_tensor(out=yb[:, j, :], in0=yb[:, j, :], in1=xc[:, j, :], op=ALU.mult), reads=[b_yb, b_xc[j]], writes=[b_yb])
                for s_ in range(4):
                    for nh in range(2):
                        pp, bp = nextpz()
                        for jc in range(NJ):
                            P.op("pe", lambda e, jc=jc, s_=s_, nh=nh, pp=pp: e.matmul(pp[:], lhsT=yb[:, jc, s_ * 128:(s_ + 1) * 128], rhs=w_out_sb[:, jc, nh * 512:(nh + 1) * 512],
                                                                                  start=(jc == 0), stop=(jc == NJ - 1)),
                                 reads=[b_yb, b_w_out], writes=[bp], signal=(jc == NJ - 1))
                        resid_out(xt, b_xt, s_, nh, pp, bp, 0, S)
                P.dma("sp", H[(t - 3) * TT:(t - 2) * TT, :].rearrange("(s p) d -> p s d", p=128), xt[:], b_xt, reads=[b_xt])

            load_x(0)
            load_x(1)
            norm_part(0)
            for t in range(ntile_p1):
                main_part(t)
                if t + 1 < ntile_p1:
                    norm_part(t + 1)
                if t >= 3:
                    tail_part(t)
                if t + 2 < ntile_p1:
                    load_x(t + 2)
            P.barrier()
        if stop in ("p1", "p1a"):
            return nc

        def mlp_phase(l, tiles, m, gi, final):
            with ExitStack() as ph:
                w1_sb, w1_bufs = load_w_cols(ph, "w1_sb%d" % l, w1_d[l].rearrange("(kc p) n -> p kc n", p=128), [128, 8, DFF],
                                             [(i * 512, (i + 1) * 512) for i in range(8)])
                w2_sb, b_w2 = load_w(ph, "w2_sb%d" % l, w2_d[l].rearrange("(fc p) n -> p fc n", p=128), [128, 32, D], 8)
                S = alloc_norm(ph, "m%d" % l)
                S["g"], S["b_g"] = load_g(ph, "m%dg" % l, gi)
                xt = SB(ph, "m%dxt" % l, [128, 4, D], F32); b_xt = Buf("xt")
                hid = SB(ph, "m%dhid" % l, [128, 32, TT], BF16); b_hid = Buf("hid")
                S["xn"] = hid[:, 0:8, :].rearrange("p a c -> p (a c)").rearrange("p (s d) -> p s d", s=4)
                S["b_xn"] = b_hid
                rl = [SB(ph, "m%drl%d" % (l, i), [128, TT], BF16) for i in range(3)]; b_rl = [Buf("rl%d" % i) for i in range(3)]
                S["rtmp"] = SB(ph, "m%drtmp" % l, [128, 512], F32); S["b_rtmp"] = Buf("rtmp")
                pz = [PS(ph, "m%dpz%d" % (l, i), [128, TT], F32) for i in range(6)]; b_pz = [Buf("pz%d" % i) for i in range(6)]
                pzi = [0]

                def nextpz():
                    i = pzi[0] % 6
                    pzi[0] += 1
                    return pz[i], b_pz[i]

                for t in tiles:
                    hrow = H[(t - 3) * TT:(t - 2) * TT, :].rearrange("(s p) d -> p s d", p=128)
                    P.dma("sp", xt[:], hrow, b_xt, writes=[b_xt])
                    norm_T(xt, b_xt, m, S)
                    hnT, b_hnT = S["hnT"], S["b_hnT"]
                    for fc in range(32):
                        pp, bp = nextpz()
                        for kc in range(8):
                            P.op("pe", lambda e, kc=kc, fc=fc, pp=pp: e.matmul(pp[:], lhsT=w1_sb[:, kc, fc * 128:(fc + 1) * 128], rhs=hnT[:, kc, :], start=(kc == 0), stop=(kc == 7)),
                                 reads=w1_bufs(fc * 128, fc * 128 + 128) + [b_hnT], writes=[bp], signal=(kc == 7))
                        r_, br = rl[fc % 3], b_rl[fc % 3]
                        P.op("act", lambda e, r_=r_, pp=pp: e.activation(out=r_[:], in_=pp[:], func=AF.Relu), reads=[bp], writes=[br])
                        P.op("dve", lambda e, r_=r_, fc=fc: e.tensor_tensor(out=hid[:, fc, :], in0=r_[:], in1=r_[:], op=ALU.mult), reads=[br], writes=[b_hid])
                    for s in range(4):
                        for nh in range(2):
                            pp, bp = nextpz()
                            for fc in range(32):
                                P.op("pe", lambda e, fc=fc, s=s, nh=nh, pp=pp: e.matmul(pp[:], lhsT=hid[:, fc, s * 128:(s + 1) * 128], rhs=w2_sb[:, fc, nh * 512:(nh + 1) * 512],
                                                                                    start=(fc == 0), stop=(fc == 31)),
                                     reads=[b_hid, b_w2], writes=[bp], signal=(fc == 31))
                            resid_out(xt, b_xt, s, nh, pp, bp, gi, S)
                    if final:
                        dst = out_d[(t - 4) * TT:(t - 3) * TT, :].rearrange("(s p) d -> p s d", p=128)
                    else:
                        dst = hrow
                    P.dma("sp", dst, xt[:], b_xt, reads=[b_xt])
                P.barrier()

        mlp_phase(0, range(3, 8), 1, 1, False)
        if stop == "p2":
            return nc

        with ExitStack() as ph:
            blkb = SB(ph, "blkb", [128, 128], BF16); b_blk = Buf("blkb")
            P.dma("pool", blkb[:], blk_d[:, :], b_blk, writes=[b_blk])
            expB = SB(ph, "expB", [128, 16, 640], BF16); b_expB = Buf("expB")
            KT = SB(ph, "KT", [128, 8, NKV * TT], BF16); b_KT = Buf("KT")
            Va = SB(ph, "Va", [128, NKV * 4, 16, 65], BF16); b_Va = Buf("Va")
            S = alloc_norm(ph, "p3")
            xt = SB(ph, "p3xt", [128, 4, D], F32); b_xt = Buf("xt")
            sq = [SB(ph, "p3sq%d" % i, [128, TT], BF16) for i in range(3)]; b_sq = [Buf("sq%d" % i) for i in range(3)]
            rs = [SB(ph, "p3rs%d" % i, [128, TT], F32) for i in range(3)]; b_rs = [Buf("rs%d" % i) for i in range(3)]
            pSt = [PS(ph, "p3pS%d" % i, [128, 1024], F32) for i in range(2)]
            pOt = [PS(ph, "p3pO%d" % i, [128, 4, 128], F32) for i in range(2)]
            b_bank = [Buf("bank%d" % i) for i in range(6)]
            pS = pSt; b_pS = [[b_bank[0], b_bank[1]], [b_bank[2], b_bank[3]]]
            pO = pOt; b_pO = [b_bank[4], b_bank[5]]
            pz = [pSt[0][:, 0:512], pSt[0][:, 512:1024], pSt[1][:, 0:512], pSt[1][:, 512:1024],
                  pOt[0][:].rearrange("p a b -> p (a b)"), pOt[1][:].rearrange("p a b -> p (a b)")]
            pzi = [0]

            def nextpz():
                i = pzi[0] % 6
                pzi[0] += 1
                return pz[i], b_bank[i]

            def proj_headnorm(w_sb, w_bufs, hnT, b_hnT, dstT, b_dst, col0, gcol):
                for g0 in range(0, 8, 3):
                    fcs = list(range(g0, min(8, g0 + 3)))
                    pps, pms = [], []
                    for fc in fcs:
                        pp, bp = nextpz()
                        for kc in range(8):
                            P.op("pe", lambda e, kc=kc, fc=fc, pp=pp: e.matmul(pp[:], lhsT=w_sb[:, kc, fc * 128:(fc + 1) * 128], rhs=hnT[:, kc, :], start=(kc == 0), stop=(kc == 7)),
                                 reads=w_bufs(fc * 128, fc * 128 + 128) + [b_hnT], writes=[bp], signal=(kc == 7))
                        pps.append((pp, bp))
                    for i, fc in enumerate(fcs):
                        pp, bp = pps[i]
                        P.op("act", lambda e, i=i, pp=pp: e.activation(out=sq[i][:], in_=pp[:], func=AF.Square), reads=[bp], writes=[b_sq[i]])
                    for i, fc in enumerate(fcs):
                        pm_, bpm = nextpz()
                        P.op("pe", lambda e, i=i, pm_=pm_: e.matmul(pm_[:], lhsT=blkb[:], rhs=sq[i][:], start=True, stop=True), reads=[b_blk, b_sq[i]], writes=[bpm])
                        pms.append((pm_, bpm))
                    for i, fc in enumerate(fcs):
                        pm_, bpm = pms[i]
                        P.op("act", lambda e, i=i, pm_=pm_: e.activation(out=rs[i][:], in_=pm_[:], func=AF.Ln, bias=S["epsb"][:, 1:2]), reads=[bpm, S["b_epsb"]], writes=[b_rs[i]])
                    for i, fc in enumerate(fcs):
                        P.op("act", lambda e, i=i: e.activation(out=rs[i][:], in_=rs[i][:], func=AF.Exp, scale=-0.5), reads=[b_rs[i]], writes=[b_rs[i]])
                    for i, fc in enumerate(fcs):
                        pp, bp = pps[i]
                        P.op("dve", lambda e, i=i, fc=fc, pp=pp: e.scalar_tensor_tensor(out=dstT[:, fc, col0:col0 + TT], in0=pp[:], scalar=gqk[:, gcol:gcol + 1], in1=rs[i][:],
                                                                                 op0=ALU.mult, op1=ALU.mult),
                             reads=[bp, b_gqk, b_rs[i]], writes=[b_dst])

            with ExitStack() as p1s:
                kvw_sb, kvw_bufs = load_w_cols(p1s, "kvw_sb", kv_w_d.rearrange("(kc p) n -> p kc n", p=128), [128, 8, 2 * D],
                                               [(i * 512, (i + 1) * 512) for i in range(4)])
                amask = SB(p1s, "amask", [128, 640], F32); b_amask = Buf("amask")
                P.dma("sp", amask[:], amask_d[:, :], b_amask, writes=[b_amask])
                btmp = [SB(p1s, "p3btmp%d" % i, [128, 640], F32) for i in range(2)]; b_btmp = [Buf("btmp0"), Buf("btmp1")]
                xn = SB(p1s, "p3xn", [128, 4, D], BF16); S["xn"] = xn; S["b_xn"] = Buf("xn")
                biasv = biasT_d.rearrange("p (h x) -> p h x", h=16)
                for h in range(16):
                    bt, bbt = btmp[h % 2], b_btmp[h % 2]
                    P.dma("sp", bt[:], biasv[:, h, :], bbt, writes=[bbt])
                    P.op("act", lambda e, bt=bt: e.activation(out=bt[:], in_=bt[:], func=AF.Exp), reads=[bbt], writes=[bbt])
                    P.op("dve", lambda e, h=h, bt=bt: e.tensor_tensor(out=expB[:, h, :], in0=bt[:], in1=amask[:], op=ALU.mult), reads=[bbt, b_amask], writes=[b_expB])
                P.op("dve", lambda e: e.memset(Va[:, :, :, 64:65], 1.0), writes=[b_Va])
                P.op("dve", lambda e: e.tensor_scalar(out=Va[:, 0:4, :, 64:65], in0=Va[:, 0:4, :, 64:65], scalar1=tflag[:, 4:5], scalar2=None, op0=ALU.mult),
                     reads=[b_tflag, b_Va], writes=[b_Va])
                for t in range(3, 8):
                    kt0 = t - 3
                    hrow = H[kt0 * TT:(kt0 + 1) * TT, :].rearrange("(s p) d -> p s d", p=128)
                    P.dma("sp", xt[:], hrow, b_xt, writes=[b_xt])
                    norm_T(xt, b_xt, 2, S)
                    hnT, b_hnT = S["hnT"], S["b_hnT"]
                    proj_headnorm(kvw_sb, kvw_bufs, hnT, b_hnT, KT, b_KT, kt0 * TT, 0)
                    for s in range(4):
                        for nh in range(2):
                            pp, bp = nextpz()
                            for kc in range(8):
                                P.op("pe", lambda e, kc=kc, s=s, nh=nh, pp=pp: e.matmul(pp[:], lhsT=hnT[:, kc, s * 128:(s + 1) * 128], rhs=kvw_sb[:, kc, D + nh * 512:D + (nh + 1) * 512],
                                                                                    start=(kc == 0), stop=(kc == 7)),
                                     reads=kvw_bufs(D + nh * 512, D + (nh + 1) * 512) + [b_hnT], writes=[bp], signal=(kc == 7))
                            if t == 3:
                                P.op("act", lambda e, s=s, nh=nh, pp=pp, kt0=kt0: e.activation(out=Va[:, kt0 * 4 + s, nh * 8:(nh + 1) * 8, 0:64], in_=pp[:].rearrange("p (h d) -> p h d", d=64),
                                                                                           func=AF.Identity, scale=tflag[:, 4:5]),
                                     reads=[bp, b_tflag], writes=[b_Va])
                            else:
                                P.op("act", lambda e, s=s, nh=nh, pp=pp, kt0=kt0: e.activation(out=Va[:, kt0 * 4 + s, nh * 8:(nh + 1) * 8, 0:64], in_=pp[:].rearrange("p (h d) -> p h d", d=64),
                                                                                           func=AF.Copy),
                                     reads=[bp], writes=[b_Va])
                P.barrier()
            if stop == "p3a":
                return nc
            with ExitStack() as p2s:
                wq_sb, wq_bufs = load_w_cols(p2s, "wq_sb", wq_d.rearrange("(kc p) n -> p kc n", p=128), [128, 8, D], [(0, 512), (512, 1024)])
                wo_sb, b_wo = load_w(p2s, "wo_sb", wo_d.rearrange("(kc p) n -> p kc n", p=128), [128, 8, D], 2)
                S["g"], S["b_g"] = load_g(p2s, "p3g", 2)
                QT = SB(p2s, "QT", [128, 8, TT], BF16); b_QT = Buf("QT")
                S["rtmp"] = SB(p2s, "p3rtmp", [128, 512], F32); S["b_rtmp"] = Buf("rtmp")
                Eb = [SB(p2s, "p3E%d" % i, [128, 640], BF16) for i in range(3)]; b_E = [Buf("E%d" % i) for i in range(3)]
                Pb = [SB(p2s, "p3P%d" % i, [128, 640], BF16) for i in range(3)]; b_P = [Buf("P%d" % i) for i in range(3)]
                rec = SB(p2s, "p3rec", [128, 2, 4], F32); b_rec = [Buf("rec0"), Buf("rec1")]
                attn = SB(p2s, "p3attn", [128, D], BF16); b_attn = Buf("attn")
                attnT = SB(p2s, "p3attnT", [128, 8, TT], BF16); b_attnT = Buf("attnT")
                S["xn"] = attnT[:].rearrange("p a c -> p (a c)").rearrange("p (s d) -> p s d", s=4); S["b_xn"] = b_attnT
                for t in range(4, 8):
                    kt0 = t - 3
                    hrow = H[kt0 * TT:(kt0 + 1) * TT, :].rearrange("(s p) d -> p s d", p=128)
                    P.dma("sp", xt[:], hrow, b_xt, writes=[b_xt])
                    norm_T(xt, b_xt, 3, S)
                    hnT, b_hnT = S["hnT"], S["b_hnT"]
                    proj_headnorm(wq_sb, wq_bufs, hnT, b_hnT, QT, b_QT, 0, 1)
                    items = [(qg, h) for qg in range(4) for h in range(16)]

                    def emit_S(idx):
                        qg, h = items[idx]
                        G = kt0 * 4 + qg
                        fc, hf = h // 2, h % 2
                        pS_, bpS = pS[idx % 2], b_pS[idx % 2]
                        for kt in range(5):
                            P.op("pe", lambda e, kt=kt: e.matmul(pS_[:, kt * 128:(kt + 1) * 128],
                                                                 lhsT=KT[hf * 64:(hf + 1) * 64, fc, (G - 4 + kt) * 128:(G - 3 + kt) * 128],
                                                                 rhs=QT[hf * 64:(hf + 1) * 64, fc, qg * 128:(qg + 1) * 128], start=True, stop=True),
                                 reads=[b_KT, b_QT], writes=bpS, signal=(kt == 4))

                    emit_S(0)
                    for idx, (qg, h) in enumerate(items):
                        G = kt0 * 4 + qg
                        h4, hh = h // 4, h % 4
                        if idx + 1 < len(items):
                            emit_S(idx + 1)
                        pS_, bpS = pS[idx % 2], b_pS[idx % 2]
                        pO_, bpO = pO[h4 % 2], b_pO[h4 % 2]
                        E_, bE = Eb[idx % 3], b_E[idx % 3]
                        P_, bP = Pb[idx % 3], b_P[idx % 3]
                        P.op("act", lambda e, E_=E_, pS_=pS_: e.activation(out=E_[:], in_=pS_[:, 0:640], func=AF.Exp), reads=bpS, writes=[bE])
                        P.op("dve", lambda e, E_=E_, P_=P_, h=h: e.tensor_tensor(out=P_[:], in0=E_[:], in1=expB[:, h, :], op=ALU.mult), reads=[bE, b_expB], writes=[bP])
                        for kt in range(5):
                            P.op("pe", lambda e, kt=kt, hh=hh, h=h, G=G, P_=P_, pO_=pO_: e.matmul(pO_[:, hh, 0:65], lhsT=P_[:, kt * 128:(kt + 1) * 128], rhs=Va[:, G - 4 + kt, h, :],
                                                                                           start=(kt == 0), stop=(kt == 4)),
                                 reads=[bP, b_Va], writes=[bpO], signal=(kt == 4))
                        if hh == 3:
                            r_ = h4 % 2
                            P.op("dve", lambda e, r_=r_, pO_=pO_: e.reciprocal(out=rec[:, r_, :], in_=pO_[:, :, 64]), reads=[bpO], writes=[b_rec[r_]])
                            P.op("dve", lambda e, r_=r_, pO_=pO_, h4=h4: e.tensor_tensor(out=attn[:, h4 * 256:(h4 + 1) * 256].rearrange("p (h d) -> p h d", d=64), in0=pO_[:, :, 0:64],
                                                                                    in1=rec[:, r_, :].unsqueeze(2).to_broadcast([128, 4, 64]), op=ALU.mult),
                                 reads=[bpO, b_rec[r_]], writes=[b_attn])
                        if h == 15:
                            for fc in range(8):
                                pT, bT = S["pT"][fc % 2], S["b_pT"][fc % 2]
                                P.op("pe", lambda e, fc=fc, pT=pT: e.transpose(out=pT[:, 0:128], in_=attn[:, fc * 128:(fc + 1) * 128], identity=identb[:]), reads=[b_attn, b_identb], writes=[bT])
                                P.op("act", lambda e, fc=fc, pT=pT, qg=qg: e.activation(out=attnT[:, fc, qg * 128:(qg + 1) * 128], in_=pT[:, 0:128], func=AF.Copy), reads=[bT], writes=[b_attnT])

                    for s in range(4):
                        for nh in range(2):
                            pp, bp = nextpz()
                            for kc in range(8):
                                P.op("pe", lambda e, kc=kc, s=s, nh=nh, pp=pp: e.matmul(pp[:], lhsT=attnT[:, kc, s * 128:(s + 1) * 128], rhs=wo_sb[:, kc, nh * 512:(nh + 1) * 512],
                                                                                    start=(kc == 0), stop=(kc == 7)),
                                     reads=[b_attnT, b_wo], writes=[bp], signal=(kc == 7))
                            resid_out(xt, b_xt, s, nh, pp, bp, 2, S)
                    P.dma("sp", hrow, xt[:], b_xt, reads=[b_xt])
                P.barrier()
        if stop == "p3":
            return nc

        mlp_phase(1, range(4, 8), 4, 3, True)
    return nc


def _host_consts():
    ident = np.eye(128, dtype=np.float32)
    blk = np.zeros((128, 128), np.float32)
    blk[:64, :64] = 1.0 / 64
    blk[64:, 64:] = 1.0 / 64
    kk = np.arange(128)[:, None, None]
    kt = np.arange(5)[None, :, None]
    q = np.arange(128)[None, None, :]
    kidx = kt * 128 + kk
    ck = kidx // 64
    cq = q // 64
    amask = ((ck >= cq) & (ck <= cq + 8)).astype(np.float32)
    bidx = np.minimum(640 + q - kidx, 256)
    bidx = np.maximum(bidx, 0)
    return ident, blk, amask.reshape(128, 640), bidx


def _pvec(v, n):
    return np.ascontiguousarray(np.asarray(v, np.float32).reshape(n, 128).T)


def make_in_maps(inputs):
    f = lambda k: np.ascontiguousarray(np.asarray(inputs[k], dtype=np.float32))
    x = f("x"); c = f("c")
    ident, blk, amask, bidx = _host_consts()
    rel_bias = f("rel_bias")[0]
    biasT = rel_bias[:, bidx]
    biasT = np.ascontiguousarray(biasT.transpose(1, 0, 2, 3).reshape(128, 16 * 640))
    conv_w = f("lru_conv_w")[0]
    convw = np.ascontiguousarray(conv_w.reshape(4, NJ, 128).transpose(2, 1, 0).reshape(128, NJ * 4))
    vec_parts = {
        "n1g0": _pvec(f("norm1_g")[0], 8), "n2g0": _pvec(f("norm2_g")[0], 8), "kvg": _pvec(f("kv_norm_g"), 8),
        "n1g1": _pvec(f("norm1_g")[1], 8), "n2g1": _pvec(f("norm2_g")[1], 8), "convw": convw,
        "convb": _pvec(f("lru_conv_b")[0], NJ), "gab": _pvec(f("lru_gate_a_b")[0], NJ), "gxb": _pvec(f("lru_gate_x_b")[0], NJ),
        "lam": _pvec(f("lru_lambda")[0], NJ),
        "gk": np.tile(f("k_norm_g"), 2).reshape(128, 1), "gq": np.tile(f("q_norm_g")[0], 2).reshape(128, 1),
    }
    vecs = np.zeros((128, NV), np.float32)
    for k, (o, n) in VOFF.items():
        vecs[:, o:o + n] = vec_parts[k]
    shared = {
        "vecs": vecs, "ada_w": f("ada_w"), "ada_b": f("ada_b"), "kv_ada_w": f("kv_ada_w"), "kv_ada_b": f("kv_ada_b").reshape(1, -1),
        "w_in": f("lru_w_in")[0], "ga_w": f("lru_gate_a_w")[0], "gx_w": f("lru_gate_x_w")[0], "w_out": f("lru_w_out")[0],
        "mlp_w1": f("mlp_w1"), "mlp_w2": f("mlp_w2"), "kv_w": f("kv_w"), "wq": f("attn_w_q")[0], "wo": f("attn_w_o")[0],
        "biasT": biasT, "amask": amask, "ident": ident, "blk64": blk,
    }
    maps = []
    for core in range(8):
        b, half = core // 2, core % 2
        if half == 1:
            xin = x[b]
        else:
            xin = np.concatenate([np.zeros((2048, D), np.float32), x[b, :2048]], axis=0)
        tflag = np.ones((128, NTILE + 1), np.float32)
        tflag[:, 4] = float(half)
        m = dict(shared)
        m["xin"] = np.ascontiguousarray(xin)
        m["ct"] = _pvec(c[b], 8)
        m["tflag"] = tflag
        maps.append(m)
    return maps


def kernel(**inputs):
    nc = build()
    maps = make_in_maps(inputs)
    res = run_bass_kernel_spmd(nc, maps, core_ids=list(range(8)))
    out = np.zeros((4, 4096, D), np.float32)
    for core in range(8):
        b, half = core // 2, core % 2
        out[b, half * 2048:(half + 1) * 2048] = res.results[core]["out"]
    return out
```

```python
import numpy as np
from contextlib import ExitStack
import concourse.bass as bass
import concourse.mybir as mybir
from concourse.bass_utils import run_bass_kernel_spmd

F32 = mybir.dt.float32
BF16 = mybir.dt.bfloat16
AF = mybir.ActivationFunctionType
ALU = mybir.AluOpType
AX = mybir.AxisListType

D = 1024
W = 1408
NJ = 11
DFF = 4096
TT = 512
NTILE = 8
NKV = 5
EPS = 1e-6


class Buf:
    __slots__ = ("name", "w", "r", "dsem", "dcnt")

    def __init__(self, name):
        self.name = name
        self.w = None
        self.r = []
        self.dsem = None
        self.dcnt = 0


class Prog:
    def __init__(self, nc, stack):
        self.nc = nc
        self.stack = stack
        self.eng = {"pe": nc.tensor, "act": nc.scalar, "dve": nc.vector, "pool": nc.gpsimd, "sp": nc.sync}
        self.sem = {k: stack.enter_context(nc.semaphore("s_" + k)) for k in self.eng}
        self.cnt = {k: 0 for k in self.eng}
        self.seen = {k: {} for k in self.eng}
        self.pending = {k: False for k in self.eng}
        self.dbufs = []
        self.nsem = 0
        self.swq = []
        self.swq_sum = 0

    def _wait(self, e, tok):
        if tok is None:
            return
        sem, val, key = tok
        if key == "pe" and e == "pe":
            return
        if self.seen[e].get(key, 0) >= val:
            return
        self.eng[e].wait_ge(sem, val)
        self.seen[e][key] = val

    def _hazards(self, e, reads, writes, group_sem=None):
        for b in reads:
            self._wait(e, b.w)
        for b in writes:
            if not (group_sem is not None and b.w is not None and b.w[0] is group_sem):
                self._wait(e, b.w)
            for t in b.r:
                self._wait(e, t)

    def op(self, e, ins_fn, reads=(), writes=(), signal=True):
        self._hazards(e, reads, writes)
        ins = ins_fn(self.eng[e])
        if signal:
            self.cnt[e] += 1
            ins.then_inc(self.sem[e], 1)
            tok = (self.sem[e], self.cnt[e], e)
            self.pending[e] = False
        else:
            assert e == "pe"
            tok = (self.sem[e], self.cnt[e] + 1, e)
            self.pending[e] = True
        for b in reads:
            b.r.append(tok)
        for b in writes:
            b.w = tok
            b.r = []
        return tok

    def dma(self, e, out, in_, sembuf, reads=(), writes=(), group=False, **kw):
        if sembuf.dsem is None:
            sembuf.dsem = self.stack.enter_context(self.nc.semaphore("d%d_%s" % (self.nsem, sembuf.name)))
            self.nsem += 1
            self.dbufs.append(sembuf)
        self._hazards(e, reads, writes, group_sem=sembuf.dsem if group else None)
        if e == "pool":
            nd = 1
            for d_ in tuple(out.shape)[:-1]:
                nd *= int(d_)
            nd = max(4, (nd + 7) // 8)
            while self.swq and self.swq_sum + nd > 448:
                tok0, n0 = self.swq.pop(0)
                self.swq_sum -= n0
                self._wait(e, tok0)
        ins = self.eng[e].dma_start(out=out, in_=in_, **kw)
        sembuf.dcnt += 16
        ins.then_inc(sembuf.dsem, 16)
        tok = (sembuf.dsem, sembuf.dcnt, ("d", id(sembuf)))
        if e == "pool":
            self.swq.append((tok, nd))
            self.swq_sum += nd
        for b in reads:
            b.r.append(tok)
        for b in writes:
            b.w = tok
            b.r = []
        return tok

    def barrier(self, engines=None):
        toks = [(self.sem[k], self.cnt[k], k) for k in self.eng if self.cnt[k] > 0]
        toks += [(b.dsem, b.dcnt, ("d", id(b))) for b in self.dbufs]
        for e in (engines or self.eng):
            for t in toks:
                if t[2] == e:
                    continue
                self._wait(e, t)
        assert not self.pending["pe"]


def gate_pieces():
    pcs = []
    for n in range(16):
        lo, hi = 88 * n, 88 * n + 88
        for ci in range(NJ):
            r0, r1 = max(lo, 128 * ci), min(hi, 128 * ci + 128)
            if r0 >= r1:
                continue
            for co in range(NJ):
                c0, c1 = max(lo, 128 * co), min(hi, 128 * co + 128)
                if c0 >= c1:
                    continue
                d = ci - co + 1
                assert 0 <= d <= 2
                pcs.append((n, r0 - lo, r1 - lo, c0 - lo, c1 - lo, ci, co, d, r0 - 128 * ci, r1 - 128 * ci, c0 - 128 * co, c1 - 128 * co))
    return pcs


def gate_nbrs():
    nb = {co: set() for co in range(NJ)}
    for p in gate_pieces():
        nb[p[6]].add(p[5])
    return {co: sorted(v) for co, v in nb.items()}


VOFF = {}
_o = 0
for _name, _n in [("n1g0", 8), ("n2g0", 8), ("kvg", 8), ("n1g1", 8), ("n2g1", 8), ("convw", 44), ("convb", 11),
                  ("gab", 11), ("gxb", 11), ("lam", 11), ("gk", 1), ("gq", 1)]:
    VOFF[_name] = (_o, _n)
    _o += _n
NV = _o


def build(stop=None):
    nc = bass.Bass("TRN2", target_bir_lowering=False)
    dt_in = lambda name, shape: nc.dram_tensor(name, shape, F32, kind="ExternalInput").ap()
    xin = dt_in("xin", [NTILE * TT, D])
    ct_d = dt_in("ct", [128, 8])
    tflag_d = dt_in("tflag", [128, NTILE + 1])
    vecs_d = dt_in("vecs", [128, NV])
    ada_w = dt_in("ada_w", [2, D, 6 * D])
    ada_b = dt_in("ada_b", [2, 6 * D])
    kv_ada_w = dt_in("kv_ada_w", [D, 2 * D])
    kv_ada_b = dt_in("kv_ada_b", [1, 2 * D])
    w_in_d = dt_in("w_in", [D, 2 * W])
    ga_w_d = dt_in("ga_w", [16, 88, 88])
    gx_w_d = dt_in("gx_w", [16, 88, 88])
    w_out_d = dt_in("w_out", [W, D])
    w1_d = dt_in("mlp_w1", [2, D, DFF])
    w2_d = dt_in("mlp_w2", [2, DFF, D])
    kv_w_d = dt_in("kv_w", [D, 2 * D])
    wq_d = dt_in("wq", [D, D])
    wo_d = dt_in("wo", [D, D])
    biasT_d = dt_in("biasT", [128, 16 * 5 * 128])
    amask_d = dt_in("amask", [128, 5 * 128])
    ident_d = dt_in("ident", [128, 128])
    blk_d = dt_in("blk64", [128, 128])
    out_d = nc.dram_tensor("out", [4 * TT, D], F32, kind="ExternalOutput").ap()
    H = nc.dram_tensor("Hs", [NKV * TT, D], F32, kind="ExternalOutput" if stop else "Internal").ap()
    Gs = nc.dram_tensor("Gs", [4, 128, D], F32, kind="Internal").ap()

    with ExitStack() as top:
        P = Prog(nc, top)

        def SB(stack, name, shape, dt):
            return stack.enter_context(nc.sbuf_tensor("sb_" + name, shape, dt))

        def PS(stack, name, shape, dt):
            return stack.enter_context(nc.psum_tensor("ps_" + name, shape, dt))

        vecs = SB(top, "vecs", [128, NV], F32); b_vecs = Buf("vecs")
        tflag = SB(top, "tflag", [128, NTILE + 1], F32); b_tflag = Buf("tflag")
        identb = SB(top, "identb", [128, 128], BF16); b_identb = Buf("identb")
        Asc = SB(top, "Asc", [128, 5, 8], F32); Ash = SB(top, "Ash", [128, 5, 8], F32); b_mod = Buf("modvec")
        cch = SB(top, "cch", [128, 2, NJ], F32); b_cch = Buf("cch")
        nbias = SB(top, "nbias", [128, 2, NJ], F32); b_nbias = Buf("nbias")
        gqk = SB(top, "gqk", [128, 2], F32); b_gqk = Buf("gqk")

        P.dma("sp", vecs[:], vecs_d[:, :], b_vecs, writes=[b_vecs])
        P.dma("sp", tflag[:], tflag_d[:, :], b_tflag, writes=[b_tflag])
        P.dma("pool", identb[:], ident_d[:, :], b_identb, writes=[b_identb])

        def V(name):
            o, n = VOFF[name]
            return vecs[:, o:o + n]

        with ExitStack() as ph:
            ct = SB(ph, "ct", [128, 8], F32); b_ct = Buf("ct")
            t8a = SB(ph, "t8a", [128, 8], F32); t8b = SB(ph, "t8b", [128, 8], F32); b_t8 = Buf("t8")
            cact = SB(ph, "cact", [128, 8], F32); b_cact = Buf("cact")
            lc = SB(ph, "lc", [128, 8, 128], BF16); b_lc = Buf("lc")
            identf = SB(ph, "identf", [128, 128], F32); b_identf = Buf("identf")
            modrow = [SB(ph, "modrow%d" % i, [128, 6 * D], F32) for i in range(2)]
            kvrow = SB(ph, "kvrow", [128, 2 * D], F32)
            b_rows = [Buf("modrow0"), Buf("modrow1"), Buf("kvrow")]
            wch = [SB(ph, "wch%d" % i, [128, 8, 512], BF16) for i in range(3)]; b_wch = [Buf("wch%d" % i) for i in range(3)]
            dtmp = SB(ph, "dtmp", [128, 8, 128], F32); b_dtmp = Buf("dtmp")
            t11 = SB(ph, "t11", [128, NJ], F32); b_t11 = Buf("t11")
            pm = [PS(ph, "pm%d" % i, [128, 512], F32) for i in range(2)]; b_pm = [Buf("pm%d" % i) for i in range(2)]

            P.dma("sp", ct[:], ct_d[:, :], b_ct, writes=[b_ct])
            P.dma("sp", identf[:], ident_d[:, :], b_identf, writes=[b_identf])
            rows = [modrow[0], modrow[1], kvrow]
            P.dma("sp", modrow[0][:], ada_b[0:1, :].partition_broadcast(128), b_rows[0], writes=[b_rows[0]])
            P.dma("sp", modrow[1][:], ada_b[1:2, :].partition_broadcast(128), b_rows[1], writes=[b_rows[1]])
            P.dma("sp", kvrow[:], kv_ada_b[0:1, :].partition_broadcast(128), b_rows[2], writes=[b_rows[2]])
            P.op("act", lambda e: e.activation(out=t8a[:], in_=ct[:], func=AF.Exp, scale=-1.0), reads=[b_ct], writes=[b_t8])
            P.op("act", lambda e: e.activation(out=t8b[:], in_=t8a[:], func=AF.Ln, bias=1.0), reads=[b_t8], writes=[b_t8])
            P.op("act", lambda e: e.activation(out=t8a[:], in_=t8b[:], func=AF.Exp, scale=-1.0), reads=[b_t8], writes=[b_t8])
            P.op("dve", lambda e: e.tensor_tensor(out=cact[:], in0=t8a[:], in1=ct[:], op=ALU.mult), reads=[b_t8, b_ct], writes=[b_cact])
            P.op("dve", lambda e: e.tensor_copy(out=lc[:], in_=cact[:].unsqueeze(2).to_broadcast([128, 8, 128])), reads=[b_cact], writes=[b_lc])
            P.op("act", lambda e: e.activation(out=t11[:], in_=V("lam"), func=AF.Exp, scale=-1.0), reads=[b_vecs], writes=[b_t11])
            P.op("act", lambda e: e.activation(out=t11[:], in_=t11[:], func=AF.Ln, bias=1.0), reads=[b_t11], writes=[b_t11])
            P.op("dve", lambda e: e.tensor_scalar(out=cch[:, 0, :], in0=t11[:], scalar1=-8.0, scalar2=None, op0=ALU.mult), reads=[b_t11], writes=[b_cch])
            P.op("dve", lambda e: e.tensor_scalar(out=cch[:, 1, :], in0=t11[:], scalar1=-16.0, scalar2=None, op0=ALU.mult), reads=[b_t11], writes=[b_cch])
            P.op("dve", lambda e: e.tensor_scalar(out=nbias[:, 0, :], in0=V("gab"), scalar1=-1.0, scalar2=None, op0=ALU.mult), reads=[b_vecs], writes=[b_nbias])
            P.op("dve", lambda e: e.tensor_scalar(out=nbias[:, 1, :], in0=V("gxb"), scalar1=-1.0, scalar2=None, op0=ALU.mult), reads=[b_vecs], writes=[b_nbias])
            P.op("dve", lambda e: e.tensor_copy(out=gqk[:, 0:1], in_=V("gk")), reads=[b_vecs], writes=[b_gqk])
            P.op("dve", lambda e: e.tensor_scalar(out=gqk[:, 1:2], in0=V("gq"), scalar1=0.125, scalar2=None, op0=ALU.mult), reads=[b_vecs], writes=[b_gqk])

            srcs = [(ada_w[0], 6 * D, 0), (ada_w[1], 6 * D, 1), (kv_ada_w, 2 * D, 2)]
            it = 0
            for (wsrc, ncols, ri) in srcs:
                wv = wsrc.rearrange("(kc p) n -> p kc n", p=128)
                for j in range(ncols // 512):
                    wb_, bb_ = wch[it % 3], b_wch[it % 3]
                    pp, bp = pm[it % 2], b_pm[it % 2]
                    P.dma("pool", wb_[:], wv[:, :, j * 512:(j + 1) * 512], bb_, writes=[bb_])
                    for kc in range(8):
                        P.op("pe", lambda e, kc=kc, wb_=wb_, pp=pp: e.matmul(pp[:], lhsT=lc[:, kc, :], rhs=wb_[:, kc, :], start=(kc == 0), stop=(kc == 7)),
                             reads=[b_lc, bb_], writes=[bp], signal=(kc == 7))
                    rr = rows[ri]
                    P.op("dve", lambda e, rr=rr, pp=pp, j=j: e.tensor_tensor(out=rr[:, j * 512:(j + 1) * 512], in0=pp[:], in1=rr[:, j * 512:(j + 1) * 512], op=ALU.add),
                         reads=[bp, b_rows[ri]], writes=[b_rows[ri]])
                    it += 1

            specs = [(0, modrow[0], 0, 1, "n1g0"), (1, modrow[0], 3, 4, "n2g0"), (2, kvrow, 0, 1, "kvg"),
                     (3, modrow[1], 0, 1, "n1g1"), (4, modrow[1], 3, 4, "n2g1")]
            rbuf = {0: b_rows[0], 1: b_rows[0], 2: b_rows[2], 3: b_rows[1], 4: b_rows[1]}
            for (m, rr, sseg, cseg, gname) in specs:
                for (seg, dst) in [(sseg, Ash), (cseg, t8a)]:
                    P.op("dve", lambda e, rr=rr, seg=seg: e.tensor_tensor(out=dtmp[:], in0=rr[:, seg * D:(seg + 1) * D].rearrange("p (g k) -> p g k", k=128),
                                                                     in1=identf[:].unsqueeze(1).to_broadcast([128, 8, 128]), op=ALU.mult),
                         reads=[rbuf[m], b_identf], writes=[b_dtmp])
                    if dst is Ash:
                        P.op("dve", lambda e, m=m: e.tensor_reduce(out=Ash[:, m, :], in_=dtmp[:], axis=AX.X, op=ALU.add), reads=[b_dtmp], writes=[b_mod])
                    else:
                        P.op("dve", lambda e: e.tensor_reduce(out=t8a[:], in_=dtmp[:], axis=AX.X, op=ALU.add), reads=[b_dtmp], writes=[b_t8])
                P.op("dve", lambda e: e.tensor_scalar(out=t8b[:], in0=t8a[:], scalar1=1.0, scalar2=32.0, op0=ALU.add, op1=ALU.mult), reads=[b_t8], writes=[b_t8])
                P.op("dve", lambda e, m=m, gname=gname: e.tensor_tensor(out=Asc[:, m, :], in0=t8b[:], in1=V(gname), op=ALU.mult), reads=[b_t8, b_vecs], writes=[b_mod])
            for gi, (rr, seg, rb) in enumerate([(modrow[0], 2, b_rows[0]), (modrow[0], 5, b_rows[0]), (modrow[1], 2, b_rows[1]), (modrow[1], 5, b_rows[1])]):
                P.dma("sp", Gs[gi], rr[:, seg * D:(seg + 1) * D], rb, reads=[rb])
            P.barrier()

        def load_w(stack, name, src_view, shape, nsplit):
            t = SB(stack, name, shape, BF16)
            b = Buf(name)
            a = shape[1]
            step = (a + nsplit - 1) // nsplit
            for s0 in range(0, a, step):
                s1 = min(a, s0 + step)
                P.dma("pool", t[:, s0:s1, :], src_view[:, s0:s1, :], b, writes=[b], group=True)
            return t, b

        def load_w_cols(stack, name, src_view, shape, splits):
            t = SB(stack, name, shape, BF16)
            blocks = []
            for (c0, c1) in splits:
                b = Buf("%s_%d" % (name, c0))
                P.dma("pool", t[:, :, c0:c1], src_view[:, :, c0:c1], b, writes=[b])
                blocks.append((c0, c1, b))

            def bufs(c0, c1):
                return [b for (a0, a1, b) in blocks if a0 < c1 and c0 < a1]
            return t, bufs

        def norm_T(xt, b_xt, m, S):
            norm_a(xt, b_xt, S)
            norm_b(m, S)

        def norm_a(xt, b_xt, S):
            P.op("act", lambda e: e.activation(out=S["junk"][:], in_=xt[:, 0, :], func=AF.Square, accum_out=S["ss"][:, 0:1]), reads=[b_xt], writes=[S["b_junk"], S["b_ss"]])
            for s in range(1, 4):
                P.op("act", lambda e, s=s: e.activation(out=S["junk"][:], in_=xt[:, s, :], func=AF.Square, accum_out=S["ss"][:, s:s + 1]), reads=[b_xt], writes=[S["b_junk"], S["b_ss"]])
            P.op("act", lambda e: e.activation(out=S["ss"][:, 4:8], in_=S["ss"][:, 0:4], func=AF.Ln, bias=S["epsb"][:, 0:1]), reads=[S["b_ss"], S["b_epsb"]], writes=[S["b_ss"]])
            P.op("act", lambda e: e.activation(out=S["ss"][:, 8:12], in_=S["ss"][:, 4:8], func=AF.Exp, scale=-0.5), reads=[S["b_ss"]], writes=[S["b_ss"]])
            for s in range(4):
                P.op("dve", lambda e, s=s: e.tensor_scalar(out=S["xn"][:, s, :], in0=xt[:, s, :], scalar1=S["ss"][:, 8 + s:9 + s], scalar2=None, op0=ALU.mult),
                     reads=[b_xt, S["b_ss"]], writes=(S["b_xn"] if isinstance(S["b_xn"], list) else [S["b_xn"]]))

        def norm_b(m, S):
            for fc in range(8):
                pT, bT = S["pT"][fc % 2], S["b_pT"][fc % 2]
                for s in range(4):
                    P.op("pe", lambda e, s=s, fc=fc, pT=pT: e.transpose(out=pT[:, s * 128:(s + 1) * 128], in_=S["xn"][:, s, fc * 128:(fc + 1) * 128], identity=identb[:]),
                         reads=(S["b_xn"] if isinstance(S["b_xn"], list) else [S["b_xn"]]) + [b_identb], writes=[bT], signal=(s == 3))
                P.op("act", lambda e, fc=fc, pT=pT: e.activation(out=S["hnT"][:, fc, :], in_=pT[:, 0:TT], func=AF.Identity, scale=Asc[:, m, fc:fc + 1], bias=Ash[:, m, fc:fc + 1]),
                     reads=[bT, b_mod], writes=[S["b_hnT"]])

        def alloc_norm(stack, pfx):
            S = {}
            S["junk"] = SB(stack, pfx + "junk", [128, D], BF16); S["b_junk"] = Buf("junk")
            S["ss"] = SB(stack, pfx + "ss", [128, 12], F32); S["b_ss"] = Buf("ss")
            S["epsb"] = SB(stack, pfx + "epsb", [128, 2], F32); S["b_epsb"] = Buf("epsb")
            S["hnT"] = SB(stack, pfx + "hnT", [128, 8, TT], BF16); S["b_hnT"] = Buf("hnT")
            S["pT"] = [PS(stack, pfx + "pT%d" % i, [128, 2 * TT], BF16) for i in range(2)]; S["b_pT"] = [Buf("pT0"), Buf("pT1")]
            P.op("dve", lambda e: e.memset(S["epsb"][:, 0:1], 1024.0 * EPS), writes=[S["b_epsb"]])
            P.op("dve", lambda e: e.memset(S["epsb"][:, 1:2], EPS), writes=[S["b_epsb"]])
            return S

        def load_g(stack, name, gi):
            g = SB(stack, name, [128, D], F32); b = Buf(name)
            P.dma("sp", g[:], Gs[gi], b, writes=[b])
            return g, b

        def resid_out(xt, b_xt, s, nh, po, bpo, gi, S):
            P.op("dve", lambda e: e.tensor_tensor(out=S["rtmp"][:], in0=po[:], in1=S["g"][:, nh * 512:(nh + 1) * 512], op=ALU.mult),
                 reads=[bpo, S["b_g"]], writes=[S["b_rtmp"]])
            P.op("dve", lambda e: e.tensor_tensor(out=xt[:, s, nh * 512:(nh + 1) * 512], in0=S["rtmp"][:], in1=xt[:, s, nh * 512:(nh + 1) * 512], op=ALU.add),
                 reads=[S["b_rtmp"], b_xt], writes=[b_xt])

        nbrs = gate_nbrs()
        with ExitStack() as ph:
            w_in_sb, w_in_bufs = load_w_cols(ph, "w_in_sb", w_in_d.rearrange("(kc p) n -> p kc n", p=128), [128, 8, 2 * W],
                                             [(W, W + 512), (W + 512, 2 * W), (0, 704), (704, W)])
            wg = SB(ph, "wg", [128, 2, NJ * 3, 128], BF16); b_wg = Buf("wg")
            P.op("dve", lambda e: e.memset(wg[:], 0.0), writes=[b_wg])
            for gidx, gsrc in enumerate([ga_w_d, gx_w_d]):
                for (n, r0, r1, c0, c1, ci, co, d, p0, p1, q0, q1) in gate_pieces():
                    P.dma("pool", wg[p0:p1, gidx, co * 3 + d, q0:q1], gsrc[n, r0:r1, c0:c1], b_wg, writes=[b_wg], group=True)
            w_out_sb, b_w_out = load_w(ph, "w_out_sb", w_out_d.rearrange("(jc p) n -> p jc n", p=128), [128, NJ, D], 2)

            S = alloc_norm(ph, "p1")
            S["g"], S["b_g"] = load_g(ph, "p1g", 0)
            xts = [SB(ph, "p1xt%d" % i, [128, 4, D], F32) for i in range(2)]; b_xts = [Buf("xt0"), Buf("xt1")]
            yb = SB(ph, "p1yb", [128, NJ, TT], BF16); b_yb = Buf("yb")
            NXR = 2
            xrb = [SB(ph, "p1xrb%d" % i, [128, TT + 3], F32) for i in range(NXR)]; b_xrb = [Buf("xrb%d" % i) for i in range(NXR)]
            halo = SB(ph, "p1halo", [128, NJ, 3], F32); b_halo = [Buf("halo%d" % j) for j in range(NJ)]
            xc = SB(ph, "p1xc", [128, NJ, TT], F32); b_xc = [Buf("xc%d" % j) for j in range(NJ)]
            xcb = SB(ph, "p1xcb", [128, NJ, TT], BF16); b_xcb = [Buf("xcb%d" % j) for j in range(NJ)]
            S["xn"] = xcb[:, 0:8, :].rearrange("p a c -> p (a c)").rearrange("p (s d) -> p s d", s=4)
            S["b_xn"] = b_xcb[0:8]
            state = SB(ph, "p1state", [128, NJ], F32); b_state = Buf("state")
            NTMP = 3
            tA = [SB(ph, "p1tA%d" % i, [128, TT], F32) for i in range(NTMP)]; b_tA = [Buf("tA%d" % i) for i in range(NTMP)]
            tQ = [SB(ph, "p1tQ%d" % i, [128, TT], F32) for i in range(NTMP)]; b_tQ = [Buf("tQ%d" % i) for i in range(NTMP)]
            tG, b_tG = tA, b_tA
            NGRP = 6
            Rst = SB(ph, "p1R", [128, NGRP, TT], F32); b_R = [Buf("R%d" % i) for i in range(NGRP)]
            S["rtmp"] = tA[0]; S["b_rtmp"] = b_tA[0]
            pz = [PS(ph, "p1pz%d" % i, [128, TT], F32) for i in range(6)]; b_pz = [Buf("pz%d" % i) for i in range(6)]
            pzi = [0]

            def nextpz():
                i = pzi[0] % 6
                pzi[0] += 1
                return pz[i], b_pz[i]

            P.op("dve", lambda e: e.memset(halo[:], 0.0), writes=b_halo)
            P.op("dve", lambda e: e.memset(state[:], 0.0), writes=[b_state])
            cw = V("convw")
            cb = V("convb")

            def load_x(t):
                P.dma("sp", xts[t % 2][:], xin[t * TT:(t + 1) * TT, :].rearrange("(s p) d -> p s d", p=128), b_xts[t % 2], writes=[b_xts[t % 2]])

            ntile_p1 = NTILE if stop != "p1a" else 5
            hnT, b_hnT = S["hnT"], S["b_hnT"]
            gab, gxb = V("gab"), V("gxb")

            pending_cast = []

            def cast_chunk(j):
                P.op("act", lambda e: e.activation(out=xcb[:, j, :], in_=xc[:, j, :], func=AF.Copy), reads=[b_xc[j]], writes=[b_xcb[j]])

            def conv_part(t, j0, j1):
                def halo_in(j):
                    xr, bxr = xrb[j % NXR], b_xrb[j % NXR]
                    P.op("pool", lambda e: e.tensor_copy(out=xr[:, 0:3], in_=halo[:, j, :]), reads=[b_halo[j]], writes=[bxr])

                if j0 == 0:
                    halo_in(0)
                for j in range(j0, j1):
                    oc = NJ + j
                    pp, bp = nextpz()
                    for kc in range(8):
                        P.op("pe", lambda e, kc=kc, oc=oc, pp=pp: e.matmul(pp[:], lhsT=w_in_sb[:, kc, oc * 128:(oc + 1) * 128], rhs=hnT[:, kc, :], start=(kc == 0), stop=(kc == 7)),
                             reads=w_in_bufs(oc * 128, oc * 128 + 128) + [b_hnT], writes=[bp], signal=(kc == 7))
                    xr, bxr = xrb[j % NXR], b_xrb[j % NXR]
                    P.op("act", lambda e, xr=xr, pp=pp: e.activation(out=xr[:, 3:TT + 3], in_=pp[:], func=AF.Copy), reads=[bp], writes=[bxr])
                    P.op("act", lambda e, pp=pp, j=j: e.activation(out=xc[:, j, :], in_=pp[:], func=AF.Identity, scale=cw[:, j * 4 + 3:j * 4 + 4], bias=cb[:, j:j + 1]),
                         reads=[bp, b_vecs], writes=[b_xc[j]])
                    if j + 1 < NJ:
                        halo_in(j + 1)
                    while pending_cast:
                        cast_chunk(pending_cast.pop(0))
                    for k in range(3):
                        P.op("dve", lambda e, xr=xr, j=j, k=k: e.scalar_tensor_tensor(out=xc[:, j, :], in0=xr[:, k:k + TT], scalar=cw[:, j * 4 + k:j * 4 + k + 1], in1=xc[:, j, :],
                                                                                  op0=ALU.mult, op1=ALU.add),
                             reads=[bxr, b_xc[j], b_vecs], writes=[b_xc[j]])
                    P.op("pool", lambda e, xr=xr, j=j: e.tensor_scalar(out=halo[:, j, :], in0=xr[:, TT:TT + 3], scalar1=tflag[:, t + 1:t + 2], scalar2=None, op0=ALU.mult),
                         reads=[bxr, b_tflag], writes=[b_halo[j]])
                    pending_cast.append(j)
                if j1 == NJ:
                    while pending_cast:
                        cast_chunk(pending_cast.pop(0))

            def gates_part(t):
                full = t >= 3
                if full:
                    for j in range(NJ):
                        pp, bp = nextpz()
                        for kc in range(8):
                            P.op("pe", lambda e, kc=kc, j=j, pp=pp: e.matmul(pp[:], lhsT=w_in_sb[:, kc, j * 128:(j + 1) * 128], rhs=hnT[:, kc, :], start=(kc == 0), stop=(kc == 7)),
                                 reads=w_in_bufs(j * 128, j * 128 + 128) + [b_hnT], writes=[bp], signal=(kc == 7))
                        P.op("act", lambda e, j=j, pp=pp: e.activation(out=yb[:, j, :], in_=pp[:], func=AF.Gelu_apprx_tanh), reads=[bp], writes=[b_yb])
                for gno, grp in enumerate([list(range(0, NGRP)), list(range(NGRP, NJ))]):
                    for gi_, co in enumerate(grp):
                        pa, bpa = nextpz()
                        px, bpx = nextpz()
                        cis = nbrs[co]
                        for gidx, (pg, bpg) in enumerate([(pa, bpa), (px, bpx)]):
                            for n_, ci in enumerate(cis):
                                d = ci - co + 1
                                P.op("pe", lambda e, pg=pg, gidx=gidx, d=d, ci=ci, n_=n_, co=co, cis=cis: e.matmul(pg[:], lhsT=wg[:, gidx, co * 3 + d, :], rhs=xcb[:, ci, :],
                                                                                                         start=(n_ == 0), stop=(n_ == len(cis) - 1)),
                                     reads=[b_wg, b_xcb[ci]], writes=[bpg], signal=(n_ == len(cis) - 1))
                        I_, bI = tQ[co % NTMP], b_tQ[co % NTMP]
                        P.op("act", lambda e, pa=pa, gi_=gi_, co=co: e.activation(out=Rst[:, gi_, :], in_=pa[:], func=AF.Sigmoid, bias=gab[:, co:co + 1]), reads=[bpa, b_vecs], writes=[b_R[gi_]])
                        P.op("act", lambda e, px=px, I_=I_, co=co: e.activation(out=I_[:], in_=px[:], func=AF.Sigmoid, bias=gxb[:, co:co + 1]), reads=[bpx, b_vecs], writes=[bI])
                        P.op("pool", lambda e, I_=I_, co=co: e.tensor_tensor(out=xc[:, co, :], in0=I_[:], in1=xc[:, co, :], op=ALU.mult), reads=[bI, b_xc[co]], writes=[b_xc[co]])

                    def stage_b(co, gi_, k_):
                        A_, bA = tA[k_], b_tA[k_]
                        Q_, bQ = tQ[k_], b_tQ[k_]
                        return [
                            lambda: P.op("act", lambda e: e.activation(out=A_[:], in_=Rst[:, gi_, :], func=AF.Exp, scale=cch[:, 0, co:co + 1]), reads=[b_R[gi_], b_cch], writes=[bA]),
                            lambda: P.op("pool", lambda e: e.tensor_tensor(out=Q_[:], in0=A_[:], in1=A_[:], op=ALU.mult), reads=[bA], writes=[bQ]),
                            lambda: P.op("act", lambda e: e.activation(out=Q_[:], in_=Q_[:], func=AF.Ln, scale=-1.0, bias=1.000001), reads=[bQ], writes=[bQ]),
                            lambda: P.op("act", lambda e: e.activation(out=Q_[:], in_=Q_[:], func=AF.Exp, scale=0.5), reads=[bQ], writes=[bQ]),
                            lambda: P.op("pool", lambda e: e.tensor_tensor(out=Q_[:], in0=Q_[:], in1=xc[:, co, :], op=ALU.mult), reads=[bQ, b_xc[co]], writes=[bQ]),
                            lambda: P.op("dve", lambda e: e.tensor_tensor_scan(out=xc[:, co, :], data0=A_[:], data1=Q_[:], initial=state[:, co:co + 1], op0=ALU.mult, op1=ALU.add),
                                         reads=[bA, bQ, b_state], writes=[b_xc[co]]),
                        ]

                    if gno == 1 and t + 1 < ntile_p1:
                        norm_a(xts[(t + 1) % 2], b_xts[(t + 1) % 2], S)
                    for c3 in range(0, len(grp), NTMP):
                        chains = [stage_b(co, c3 + k_, k_) for k_, co in enumerate(grp[c3:c3 + NTMP])]
                        for si in range(len(chains[0])):
                            for ch in chains:
                                ch[si]()
                P.op("dve", lambda e: e.tensor_scalar(out=state[:], in0=xc[:, :, TT - 1], scalar1=tflag[:, t + 1:t + 2], scalar2=None, op0=ALU.mult),
                     reads=b_xc + [b_tflag], writes=[b_state])

            def tail_a(t):
                for j in range(NJ):
                    P.op("dve", lambda e, j=j: e.tensor_tensor(out=yb[:, j, :], in0=yb[:, j, :], in1=xc[:, j, :], op=ALU.mult), reads=[b_yb, b_xc[j]], writes=[b_yb])

            def tail_b(t):
                xt, b_xt = xts[t % 2], b_xts[t % 2]
                for s_ in range(4):
                    for nh in range(2):
                        pp, bp = nextpz()
                        for jc in range(NJ):
                            P.op("pe", lambda e, jc=jc, s_=s_, nh=nh, pp=pp: e.matmul(pp[:], lhsT=yb[:, jc, s_ * 128:(s_ + 1) * 128], rhs=w_out_sb[:, jc, nh * 512:(nh + 1) * 512],
                                                                                  start=(jc == 0), stop=(jc == NJ - 1)),
                                 reads=[b_yb, b_w_out], writes=[bp], signal=(jc == NJ - 1))
                        resid_out(xt, b_xt, s_, nh, pp, bp, 0, S)
                P.dma("sp", H[(t - 3) * TT:(t - 2) * TT, :].rearrange("(s p) d -> p s d", p=128), xt[:], b_xt, reads=[b_xt])

            load_x(0)
            load_x(1)
            norm_a(xts[0], b_xts[0], S)
            norm_b(0, S)
            for t in range(ntile_p1):
                prev_full = (t - 1) >= 3
                if prev_full:
                    tail_a(t - 1)
                conv_part(t, 0, 6)
                if prev_full:
                    tail_b(t - 1)
                    if t + 1 < ntile_p1:
                        load_x(t + 1)
                elif 1 <= t and t + 1 < ntile_p1:
                    load_x(t + 1)
                conv_part(t, 6, NJ)
                gates_part(t)
                if t + 1 < ntile_p1:
                    norm_b(0, S)
            tail_a(ntile_p1 - 1)
            tail_b(ntile_p1 - 1)
            P.barrier()
        if stop in ("p1", "p1a"):
            return nc

        def mlp_phase(l, tiles, m, gi, final):
            with ExitStack() as ph:
                w1_sb, w1_bufs = load_w_cols(ph, "w1_sb%d" % l, w1_d[l].rearrange("(kc p) n -> p kc n", p=128), [128, 8, DFF],
                                             [(i * 512, (i + 1) * 512) for i in range(8)])
                w2_sb, b_w2 = load_w(ph, "w2_sb%d" % l, w2_d[l].rearrange("(fc p) n -> p fc n", p=128), [128, 32, D], 8)
                S = alloc_norm(ph, "m%d" % l)
                S["g"], S["b_g"] = load_g(ph, "m%dg" % l, gi)
                xt = SB(ph, "m%dxt" % l, [128, 4, D], F32); b_xt = Buf("xt")
                hid = SB(ph, "m%dhid" % l, [128, 32, TT], BF16); b_hid = Buf("hid")
                S["xn"] = hid[:, 0:8, :].rearrange("p a c -> p (a c)").rearrange("p (s d) -> p s d", s=4)
                S["b_xn"] = b_hid
                rl = [SB(ph, "m%drl%d" % (l, i), [128, TT], BF16) for i in range(3)]; b_rl = [Buf("rl%d" % i) for i in range(3)]
                S["rtmp"] = SB(ph, "m%drtmp" % l, [128, 512], F32); S["b_rtmp"] = Buf("rtmp")
                pz = [PS(ph, "m%dpz%d" % (l, i), [128, TT], F32) for i in range(6)]; b_pz = [Buf("pz%d" % i) for i in range(6)]
                pzi = [0]

                def nextpz():
                    i = pzi[0] % 6
                    pzi[0] += 1
                    return pz[i], b_pz[i]

                for t in tiles:
                    hrow = H[(t - 3) * TT:(t - 2) * TT, :].rearrange("(s p) d -> p s d", p=128)
                    P.dma("sp", xt[:], hrow, b_xt, writes=[b_xt])
                    norm_T(xt, b_xt, m, S)
                    hnT, b_hnT = S["hnT"], S["b_hnT"]
                    for fc in range(32):
                        pp, bp = nextpz()
                        for kc in range(8):
                            P.op("pe", lambda e, kc=kc, fc=fc, pp=pp: e.matmul(pp[:], lhsT=w1_sb[:, kc, fc * 128:(fc + 1) * 128], rhs=hnT[:, kc, :], start=(kc == 0), stop=(kc == 7)),
                                 reads=w1_bufs(fc * 128, fc * 128 + 128) + [b_hnT], writes=[bp], signal=(kc == 7))
                        r_, br = rl[fc % 3], b_rl[fc % 3]
                        P.op("act", lambda e, r_=r_, pp=pp: e.activation(out=r_[:], in_=pp[:], func=AF.Relu), reads=[bp], writes=[br])
                        P.op("dve", lambda e, r_=r_, fc=fc: e.tensor_tensor(out=hid[:, fc, :], in0=r_[:], in1=r_[:], op=ALU.mult), reads=[br], writes=[b_hid])
                    for s in range(4):
                        for nh in range(2):
                            pp, bp = nextpz()
                            for fc in range(32):
                                P.op("pe", lambda e, fc=fc, s=s, nh=nh, pp=pp: e.matmul(pp[:], lhsT=hid[:, fc, s * 128:(s + 1) * 128], rhs=w2_sb[:, fc, nh * 512:(nh + 1) * 512],
                                                                                    start=(fc == 0), stop=(fc == 31)),
                                     reads=[b_hid, b_w2], writes=[bp], signal=(fc == 31))
                            resid_out(xt, b_xt, s, nh, pp, bp, gi, S)
                    if final:
                        dst = out_d[(t - 4) * TT:(t - 3) * TT, :].rearrange("(s p) d -> p s d", p=128)
                    else:
                        dst = hrow
                    P.dma("sp", dst, xt[:], b_xt, reads=[b_xt])
                P.barrier()

        mlp_phase(0, range(3, 8), 1, 1, False)
        if stop == "p2":
            return nc

        with ExitStack() as ph:
            blkb = SB(ph, "blkb", [128, 128], BF16); b_blk = Buf("blkb")
            P.dma("pool", blkb[:], blk_d[:, :], b_blk, writes=[b_blk])
            expB = SB(ph, "expB", [128, 16, 640], BF16); b_expB = Buf("expB")
            KT = SB(ph, "KT", [128, 8, NKV * TT], BF16); b_KT = Buf("KT")
            Va = SB(ph, "Va", [128, NKV * 4, 16, 65], BF16); b_Va = Buf("Va")
            S = alloc_norm(ph, "p3")
            xt = SB(ph, "p3xt", [128, 4, D], F32); b_xt = Buf("xt")
            sq = [SB(ph, "p3sq%d" % i, [128, TT], BF16) for i in range(3)]; b_sq = [Buf("sq%d" % i) for i in range(3)]
            rs = [SB(ph, "p3rs%d" % i, [128, TT], F32) for i in range(3)]; b_rs = [Buf("rs%d" % i) for i in range(3)]
            pSt = [PS(ph, "p3pS%d" % i, [128, 1024], F32) for i in range(2)]
            pOt = [PS(ph, "p3pO%d" % i, [128, 4, 128], F32) for i in range(2)]
            b_bank = [Buf("bank%d" % i) for i in range(6)]
            pS = pSt; b_pS = [[b_bank[0], b_bank[1]], [b_bank[2], b_bank[3]]]
            pO = pOt; b_pO = [b_bank[4], b_bank[5]]
            pz = [pSt[0][:, 0:512], pSt[0][:, 512:1024], pSt[1][:, 0:512], pSt[1][:, 512:1024],
                  pOt[0][:].rearrange("p a b -> p (a b)"), pOt[1][:].rearrange("p a b -> p (a b)")]
            pzi = [0]

            def nextpz():
                i = pzi[0] % 6
                pzi[0] += 1
                return pz[i], b_bank[i]

            def proj_headnorm(w_sb, w_bufs, hnT, b_hnT, dstT, b_dst, col0, gcol):
                for g0 in range(0, 8, 3):
                    fcs = list(range(g0, min(8, g0 + 3)))
                    pps, pms = [], []
                    for fc in fcs:
                        pp, bp = nextpz()
                        for kc in range(8):
                            P.op("pe", lambda e, kc=kc, fc=fc, pp=pp: e.matmul(pp[:], lhsT=w_sb[:, kc, fc * 128:(fc + 1) * 128], rhs=hnT[:, kc, :], start=(kc == 0), stop=(kc == 7)),
                                 reads=w_bufs(fc * 128, fc * 128 + 128) + [b_hnT], writes=[bp], signal=(kc == 7))
                        pps.append((pp, bp))
                    for i, fc in enumerate(fcs):
                        pp, bp = pps[i]
                        P.op("act", lambda e, i=i, pp=pp: e.activation(out=sq[i][:], in_=pp[:], func=AF.Square), reads=[bp], writes=[b_sq[i]])
                    for i, fc in enumerate(fcs):
                        pm_, bpm = nextpz()
                        P.op("pe", lambda e, i=i, pm_=pm_: e.matmul(pm_[:], lhsT=blkb[:], rhs=sq[i][:], start=True, stop=True), reads=[b_blk, b_sq[i]], writes=[bpm])
                        pms.append((pm_, bpm))
                    for i, fc in enumerate(fcs):
                        pm_, bpm = pms[i]
                        P.op("act", lambda e, i=i, pm_=pm_: e.activation(out=rs[i][:], in_=pm_[:], func=AF.Ln, bias=S["epsb"][:, 1:2]), reads=[bpm, S["b_epsb"]], writes=[b_rs[i]])
                    for i, fc in enumerate(fcs):
                        P.op("act", lambda e, i=i: e.activation(out=rs[i][:], in_=rs[i][:], func=AF.Exp, scale=-0.5), reads=[b_rs[i]], writes=[b_rs[i]])
                    for i, fc in enumerate(fcs):
                        pp, bp = pps[i]
                        P.op("dve", lambda e, i=i, fc=fc, pp=pp: e.scalar_tensor_tensor(out=dstT[:, fc, col0:col0 + TT], in0=pp[:], scalar=gqk[:, gcol:gcol + 1], in1=rs[i][:],
                                                                                 op0=ALU.mult, op1=ALU.mult),
                             reads=[bp, b_gqk, b_rs[i]], writes=[b_dst])

            with ExitStack() as p1s:
                kvw_sb, kvw_bufs = load_w_cols(p1s, "kvw_sb", kv_w_d.rearrange("(kc p) n -> p kc n", p=128), [128, 8, 2 * D],
                                               [(i * 512, (i + 1) * 512) for i in range(4)])
                amask = SB(p1s, "amask", [128, 640], F32); b_amask = Buf("amask")
                P.dma("sp", amask[:], amask_d[:, :], b_amask, writes=[b_amask])
                btmp = [SB(p1s, "p3btmp%d" % i, [128, 640], F32) for i in range(2)]; b_btmp = [Buf("btmp0"), Buf("btmp1")]
                xn = SB(p1s, "p3xn", [128, 4, D], BF16); S["xn"] = xn; S["b_xn"] = Buf("xn")
                biasv = biasT_d.rearrange("p (h x) -> p h x", h=16)
                for h in range(16):
                    bt, bbt = btmp[h % 2], b_btmp[h % 2]
                    P.dma("sp", bt[:], biasv[:, h, :], bbt, writes=[bbt])
                    P.op("act", lambda e, bt=bt: e.activation(out=bt[:], in_=bt[:], func=AF.Exp), reads=[bbt], writes=[bbt])
                    P.op("dve", lambda e, h=h, bt=bt: e.tensor_tensor(out=expB[:, h, :], in0=bt[:], in1=amask[:], op=ALU.mult), reads=[bbt, b_amask], writes=[b_expB])
                P.op("dve", lambda e: e.memset(Va[:, :, :, 64:65], 1.0), writes=[b_Va])
                P.op("dve", lambda e: e.tensor_scalar(out=Va[:, 0:4, :, 64:65], in0=Va[:, 0:4, :, 64:65], scalar1=tflag[:, 4:5], scalar2=None, op0=ALU.mult),
                     reads=[b_tflag, b_Va], writes=[b_Va])
                for t in range(3, 8):
                    kt0 = t - 3
                    hrow = H[kt0 * TT:(kt0 + 1) * TT, :].rearrange("(s p) d -> p s d", p=128)
                    P.dma("sp", xt[:], hrow, b_xt, writes=[b_xt])
                    norm_T(xt, b_xt, 2, S)
                    hnT, b_hnT = S["hnT"], S["b_hnT"]
                    proj_headnorm(kvw_sb, kvw_bufs, hnT, b_hnT, KT, b_KT, kt0 * TT, 0)
                    for s in range(4):
                        for nh in range(2):
                            pp, bp = nextpz()
                            for kc in range(8):
                                P.op("pe", lambda e, kc=kc, s=s, nh=nh, pp=pp: e.matmul(pp[:], lhsT=hnT[:, kc, s * 128:(s + 1) * 128], rhs=kvw_sb[:, kc, D + nh * 512:D + (nh + 1) * 512],
                                                                                    start=(kc == 0), stop=(kc == 7)),
                                     reads=kvw_bufs(D + nh * 512, D + (nh + 1) * 512) + [b_hnT], writes=[bp], signal=(kc == 7))
                            if t == 3:
                                P.op("act", lambda e, s=s, nh=nh, pp=pp, kt0=kt0: e.activation(out=Va[:, kt0 * 4 + s, nh * 8:(nh + 1) * 8, 0:64], in_=pp[:].rearrange("p (h d) -> p h d", d=64),
                                                                                           func=AF.Identity, scale=tflag[:, 4:5]),
                                     reads=[bp, b_tflag], writes=[b_Va])
                            else:
                                P.op("act", lambda e, s=s, nh=nh, pp=pp, kt0=kt0: e.activation(out=Va[:, kt0 * 4 + s, nh * 8:(nh + 1) * 8, 0:64], in_=pp[:].rearrange("p (h d) -> p h d", d=64),
                                                                                           func=AF.Copy),
                                     reads=[bp], writes=[b_Va])
                P.barrier()
            if stop == "p3a":
                return nc
            with ExitStack() as p2s:
                wq_sb, wq_bufs = load_w_cols(p2s, "wq_sb", wq_d.rearrange("(kc p) n -> p kc n", p=128), [128, 8, D], [(0, 512), (512, 1024)])
                wo_sb, b_wo = load_w(p2s, "wo_sb", wo_d.rearrange("(kc p) n -> p kc n", p=128), [128, 8, D], 2)
                S["g"], S["b_g"] = load_g(p2s, "p3g", 2)
                QT = SB(p2s, "QT", [128, 8, TT], BF16); b_QT = Buf("QT")
                S["rtmp"] = SB(p2s, "p3rtmp", [128, 512], F32); S["b_rtmp"] = Buf("rtmp")
                Eb = [SB(p2s, "p3E%d" % i, [128, 640], BF16) for i in range(3)]; b_E = [Buf("E%d" % i) for i in range(3)]
                Pb = [SB(p2s, "p3P%d" % i, [128, 640], BF16) for i in range(3)]; b_P = [Buf("P%d" % i) for i in range(3)]
                rec = SB(p2s, "p3rec", [128, 2, 4], F32); b_rec = [Buf("rec0"), Buf("rec1")]
                attn = SB(p2s, "p3attn", [128, D], BF16); b_attn = Buf("attn")
                attnT = SB(p2s, "p3attnT", [128, 8, TT], BF16); b_attnT = Buf("attnT")
                S["xn"] = attnT[:].rearrange("p a c -> p (a c)").rearrange("p (s d) -> p s d", s=4); S["b_xn"] = b_attnT
                for t in range(4, 8):
                    kt0 = t - 3
                    hrow = H[kt0 * TT:(kt0 + 1) * TT, :].rearrange("(s p) d -> p s d", p=128)
                    P.dma("sp", xt[:], hrow, b_xt, writes=[b_xt])
                    norm_T(xt, b_xt, 3, S)
                    hnT, b_hnT = S["hnT"], S["b_hnT"]
                    proj_headnorm(wq_sb, wq_bufs, hnT, b_hnT, QT, b_QT, 0, 1)
                    items = [(qg, h) for qg in range(4) for h in range(16)]

                    def emit_S(idx):
                        qg, h = items[idx]
                        G = kt0 * 4 + qg
                        fc, hf = h // 2, h % 2
                        pS_, bpS = pS[idx % 2], b_pS[idx % 2]
                        for kt in range(5):
                            P.op("pe", lambda e, kt=kt: e.matmul(pS_[:, kt * 128:(kt + 1) * 128],
                                                                 lhsT=KT[hf * 64:(hf + 1) * 64, fc, (G - 4 + kt) * 128:(G - 3 + kt) * 128],
                                                                 rhs=QT[hf * 64:(hf + 1) * 64, fc, qg * 128:(qg + 1) * 128], start=True, stop=True),
                                 reads=[b_KT, b_QT], writes=bpS, signal=(kt == 4))

                    emit_S(0)
                    for idx, (qg, h) in enumerate(items):
                        G = kt0 * 4 + qg
                        h4, hh = h // 4, h % 4
                        if idx + 1 < len(items):
                            emit_S(idx + 1)
                        pS_, bpS = pS[idx % 2], b_pS[idx % 2]
                        pO_, bpO = pO[h4 % 2], b_pO[h4 % 2]
                        E_, bE = Eb[idx % 3], b_E[idx % 3]
                        P_, bP = Pb[idx % 3], b_P[idx % 3]
                        P.op("act", lambda e, E_=E_, pS_=pS_: e.activation(out=E_[:], in_=pS_[:, 0:640], func=AF.Exp), reads=bpS, writes=[bE])
                        P.op("dve", lambda e, E_=E_, P_=P_, h=h: e.tensor_tensor(out=P_[:], in0=E_[:], in1=expB[:, h, :], op=ALU.mult), reads=[bE, b_expB], writes=[bP])
                        for kt in range(5):
                            P.op("pe", lambda e, kt=kt, hh=hh, h=h, G=G, P_=P_, pO_=pO_: e.matmul(pO_[:, hh, 0:65], lhsT=P_[:, kt * 128:(kt + 1) * 128], rhs=Va[:, G - 4 + kt, h, :],
                                                                                           start=(kt == 0), stop=(kt == 4)),
                                 reads=[bP, b_Va], writes=[bpO], signal=(kt == 4))
                        if hh == 3:
                            r_ = h4 % 2
                            P.op("dve", lambda e, r_=r_, pO_=pO_: e.reciprocal(out=rec[:, r_, :], in_=pO_[:, :, 64]), reads=[bpO], writes=[b_rec[r_]])
                            P.op("dve", lambda e, r_=r_, pO_=pO_, h4=h4: e.tensor_tensor(out=attn[:, h4 * 256:(h4 + 1) * 256].rearrange("p (h d) -> p h d", d=64), in0=pO_[:, :, 0:64],
                                                                                    in1=rec[:, r_, :].unsqueeze(2).to_broadcast([128, 4, 64]), op=ALU.mult),
                                 reads=[bpO, b_rec[r_]], writes=[b_attn])
                        if h == 15:
                            for fc in range(8):
                                pT, bT = S["pT"][fc % 2], S["b_pT"][fc % 2]
                                P.op("pe", lambda e, fc=fc, pT=pT: e.transpose(out=pT[:, 0:128], in_=attn[:, fc * 128:(fc + 1) * 128], identity=identb[:]), reads=[b_attn, b_identb], writes=[bT])
                                P.op("act", lambda e, fc=fc, pT=pT, qg=qg: e.activation(out=attnT[:, fc, qg * 128:(qg + 1) * 128], in_=pT[:, 0:128], func=AF.Copy), reads=[bT], writes=[b_attnT])

                    for s in range(4):
                        for nh in range(2):
                            pp, bp = nextpz()
                            for kc in range(8):
                                P.op("pe", lambda e, kc=kc, s=s, nh=nh, pp=pp: e.matmul(pp[:], lhsT=attnT[:, kc, s * 128:(s + 1) * 128], rhs=wo_sb[:, kc, nh * 512:(nh + 1) * 512],
                                                                                    start=(kc == 0), stop=(kc == 7)),
                                     reads=[b_attnT, b_wo], writes=[bp], signal=(kc == 7))
                            resid_out(xt, b_xt, s, nh, pp, bp, 2, S)
                    P.dma("sp", hrow, xt[:], b_xt, reads=[b_xt])
                P.barrier()
        if stop == "p3":
            return nc

        mlp_phase(1, range(4, 8), 4, 3, True)
    return nc


def _host_consts():
    ident = np.eye(128, dtype=np.float32)
    blk = np.zeros((128, 128), np.float32)
    blk[:64, :64] = 1.0 / 64
    blk[64:, 64:] = 1.0 / 64
    kk = np.arange(128)[:, None, None]
    kt = np.arange(5)[None, :, None]
    q = np.arange(128)[None, None, :]
    kidx = kt * 128 + kk
    ck = kidx // 64
    cq = q // 64
    amask = ((ck >= cq) & (ck <= cq + 8)).astype(np.float32)
    bidx = np.minimum(640 + q - kidx, 256)
    bidx = np.maximum(bidx, 0)
    return ident, blk, amask.reshape(128, 640), bidx


def _pvec(v, n):
    return np.ascontiguousarray(np.asarray(v, np.float32).reshape(n, 128).T)


def make_in_maps(inputs):
    f = lambda k: np.ascontiguousarray(np.asarray(inputs[k], dtype=np.float32))
    x = f("x"); c = f("c")
    ident, blk, amask, bidx = _host_consts()
    rel_bias = f("rel_bias")[0]
    biasT = rel_bias[:, bidx]
    biasT = np.ascontiguousarray(biasT.transpose(1, 0, 2, 3).reshape(128, 16 * 640))
    conv_w = f("lru_conv_w")[0]
    convw = np.ascontiguousarray(conv_w.reshape(4, NJ, 128).transpose(2, 1, 0).reshape(128, NJ * 4))
    vec_parts = {
        "n1g0": _pvec(f("norm1_g")[0], 8), "n2g0": _pvec(f("norm2_g")[0], 8), "kvg": _pvec(f("kv_norm_g"), 8),
        "n1g1": _pvec(f("norm1_g")[1], 8), "n2g1": _pvec(f("norm2_g")[1], 8), "convw": convw,
        "convb": _pvec(f("lru_conv_b")[0], NJ), "gab": _pvec(f("lru_gate_a_b")[0], NJ), "gxb": _pvec(f("lru_gate_x_b")[0], NJ),
        "lam": _pvec(f("lru_lambda")[0], NJ),
        "gk": np.tile(f("k_norm_g"), 2).reshape(128, 1), "gq": np.tile(f("q_norm_g")[0], 2).reshape(128, 1),
    }
    vecs = np.zeros((128, NV), np.float32)
    for k, (o, n) in VOFF.items():
        vecs[:, o:o + n] = vec_parts[k]
    shared = {
        "vecs": vecs, "ada_w": f("ada_w"), "ada_b": f("ada_b"), "kv_ada_w": f("kv_ada_w"), "kv_ada_b": f("kv_ada_b").reshape(1, -1),
        "w_in": f("lru_w_in")[0], "ga_w": f("lru_gate_a_w")[0], "gx_w": f("lru_gate_x_w")[0], "w_out": f("lru_w_out")[0],
        "mlp_w1": f("mlp_w1"), "mlp_w2": f("mlp_w2"), "kv_w": f("kv_w"), "wq": f("attn_w_q")[0], "wo": f("attn_w_o")[0],
        "biasT": biasT, "amask": amask, "ident": ident, "blk64": blk,
    }
    maps = []
    for core in range(8):
        b, half = core // 2, core % 2
        if half == 1:
            xin = x[b]
        else:
            xin = np.concatenate([np.zeros((2048, D), np.float32), x[b, :2048]], axis=0)
        tflag = np.ones((128, NTILE + 1), np.float32)
        tflag[:, 4] = float(half)
        m = dict(shared)
        m["xin"] = np.ascontiguousarray(xin)
        m["ct"] = _pvec(c[b], 8)
        m["tflag"] = tflag
        maps.append(m)
    return maps


def kernel(**inputs):
    nc = build()
    maps = make_in_maps(inputs)
    res = run_bass_kernel_spmd(nc, maps, core_ids=list(range(8)))
    out = np.zeros((4, 4096, D), np.float32)
    for core in range(8):
        b, half = core // 2, core % 2
        out[b, half * 2048:(half + 1) * 2048] = res.results[core]["out"]
    return out
```

```python
import numpy as np
from contextlib import ExitStack
import concourse.bass as bass
import concourse.mybir as mybir
from concourse.bass_utils import run_bass_kernel_spmd

F32 = mybir.dt.float32
BF16 = mybir.dt.bfloat16
AF = mybir.ActivationFunctionType
ALU = mybir.AluOpType
AX = mybir.AxisListType

D = 1024
W = 1408
NJ = 11
DFF = 4096
TT = 512
NTILE = 8
NKV = 5
EPS = 1e-6


class Buf:
    __slots__ = ("name", "w", "r", "dsem", "dcnt")

    def __init__(self, name):
        self.name = name
        self.w = None
        self.r = []
        self.dsem = None
        self.dcnt = 0


class Prog:
    def __init__(self, nc, stack):
        self.nc = nc
        self.stack = stack
        self.eng = {"pe": nc.tensor, "act": nc.scalar, "dve": nc.vector, "pool": nc.gpsimd, "sp": nc.sync}
        self.sem = {k: stack.enter_context(nc.semaphore("s_" + k)) for k in self.eng}
        self.cnt = {k: 0 for k in self.eng}
        self.seen = {k: {} for k in self.eng}
        self.pending = {k: False for k in self.eng}
        self.dbufs = []
        self.nsem = 0
        self.swq = []
        self.swq_sum = 0

    def _wait(self, e, tok):
        if tok is None:
            return
        sem, val, key = tok
        if key == "pe" and e == "pe":
            return
        if self.seen[e].get(key, 0) >= val:
            return
        self.eng[e].wait_ge(sem, val)
        self.seen[e][key] = val

    def _hazards(self, e, reads, writes, group_sem=None):
        for b in reads:
            self._wait(e, b.w)
        for b in writes:
            if not (group_sem is not None and b.w is not None and b.w[0] is group_sem):
                self._wait(e, b.w)
            for t in b.r:
                self._wait(e, t)

    def op(self, e, ins_fn, reads=(), writes=(), signal=True):
        self._hazards(e, reads, writes)
        ins = ins_fn(self.eng[e])
        if signal:
            self.cnt[e] += 1
            ins.then_inc(self.sem[e], 1)
            tok = (self.sem[e], self.cnt[e], e)
            self.pending[e] = False
        else:
            assert e == "pe"
            tok = (self.sem[e], self.cnt[e] + 1, e)
            self.pending[e] = True
        for b in reads:
            b.r.append(tok)
        for b in writes:
            b.w = tok
            b.r = []
        return tok

    def dma(self, e, out, in_, sembuf, reads=(), writes=(), group=False, **kw):
        if sembuf.dsem is None:
            sembuf.dsem = self.stack.enter_context(self.nc.semaphore("d%d_%s" % (self.nsem, sembuf.name)))
            self.nsem += 1
            self.dbufs.append(sembuf)
        self._hazards(e, reads, writes, group_sem=sembuf.dsem if group else None)
        if e == "pool":
            nd = 1
            for d_ in tuple(out.shape)[:-1]:
                nd *= int(d_)
            nd = max(4, (nd + 7) // 8)
            while self.swq and self.swq_sum + nd > 448:
                tok0, n0 = self.swq.pop(0)
                self.swq_sum -= n0
                self._wait(e, tok0)
        ins = self.eng[e].dma_start(out=out, in_=in_, **kw)
        sembuf.dcnt += 16
        ins.then_inc(sembuf.dsem, 16)
        tok = (sembuf.dsem, sembuf.dcnt, ("d", id(sembuf)))
        if e == "pool":
            self.swq.append((tok, nd))
            self.swq_sum += nd
        for b in reads:
            b.r.append(tok)
        for b in writes:
            b.w = tok
            b.r = []
        return tok

    def barrier(self, engines=None):
        toks = [(self.sem[k], self.cnt[k], k) for k in self.eng if self.cnt[k] > 0]
        toks += [(b.dsem, b.dcnt, ("d", id(b))) for b in self.dbufs]
        for e in (engines or self.eng):
            for t in toks:
                if t[2] == e:
                    continue
                self._wait(e, t)
        assert not self.pending["pe"]


def gate_pieces():
    pcs = []
    for n in range(16):
        lo, hi = 88 * n, 88 * n + 88
        for ci in range(NJ):
            r0, r1 = max(lo, 128 * ci), min(hi, 128 * ci + 128)
            if r0 >= r1:
                continue
            for co in range(NJ):
                c0, c1 = max(lo, 128 * co), min(hi, 128 * co + 128)
                if c0 >= c1:
                    continue
                d = ci - co + 1
                assert 0 <= d <= 2
                pcs.append((n, r0 - lo, r1 - lo, c0 - lo, c1 - lo, ci, co, d, r0 - 128 * ci, r1 - 128 * ci, c0 - 128 * co, c1 - 128 * co))
    return pcs


def gate_nbrs():
    nb = {co: set() for co in range(NJ)}
    for p in gate_pieces():
        nb[p[6]].add(p[5])
    return {co: sorted(v) for co, v in nb.items()}


VOFF = {}
_o = 0
for _name, _n in [("n1g0", 8), ("n2g0", 8), ("kvg", 8), ("n1g1", 8), ("n2g1", 8), ("convw", 44), ("convb", 11),
                  ("gab", 11), ("gxb", 11), ("lam", 11), ("gk", 1), ("gq", 1)]:
    VOFF[_name] = (_o, _n)
    _o += _n
NV = _o


def build(stop=None):
    nc = bass.Bass("TRN2", target_bir_lowering=False)
    dt_in = lambda name, shape: nc.dram_tensor(name, shape, F32, kind="ExternalInput").ap()
    xin = dt_in("xin", [NTILE * TT, D])
    ct_d = dt_in("ct", [128, 8])
    tflag_d = dt_in("tflag", [128, NTILE + 1])
    vecs_d = dt_in("vecs", [128, NV])
    ada_w = dt_in("ada_w", [2, D, 6 * D])
    ada_b = dt_in("ada_b", [2, 6 * D])
    kv_ada_w = dt_in("kv_ada_w", [D, 2 * D])
    kv_ada_b = dt_in("kv_ada_b", [1, 2 * D])
    w_in_d = dt_in("w_in", [D, 2 * W])
    ga_w_d = dt_in("ga_w", [16, 88, 88])
    gx_w_d = dt_in("gx_w", [16, 88, 88])
    w_out_d = dt_in("w_out", [W, D])
    w1_d = dt_in("mlp_w1", [2, D, DFF])
    w2_d = dt_in("mlp_w2", [2, DFF, D])
    kv_w_d = dt_in("kv_w", [D, 2 * D])
    wq_d = dt_in("wq", [D, D])
    wo_d = dt_in("wo", [D, D])
    biasT_d = dt_in("biasT", [128, 16 * 5 * 128])
    amask_d = dt_in("amask", [128, 5 * 128])
    ident_d = dt_in("ident", [128, 128])
    blk_d = dt_in("blk64", [128, 128])
    out_d = nc.dram_tensor("out", [4 * TT, D], F32, kind="ExternalOutput").ap()
    H = nc.dram_tensor("Hs", [NKV * TT, D], F32, kind="ExternalOutput" if stop else "Internal").ap()
    Gs = nc.dram_tensor("Gs", [4, 128, D], F32, kind="Internal").ap()

    with ExitStack() as top:
        P = Prog(nc, top)

        def SB(stack, name, shape, dt):
            return stack.enter_context(nc.sbuf_tensor("sb_" + name, shape, dt))

        def PS(stack, name, shape, dt):
            return stack.enter_context(nc.psum_tensor("ps_" + name, shape, dt))

        vecs = SB(top, "vecs", [128, NV], F32); b_vecs = Buf("vecs")
        tflag = SB(top, "tflag", [128, NTILE + 1], F32); b_tflag = Buf("tflag")
        identb = SB(top, "identb", [128, 128], BF16); b_identb = Buf("identb")
        Asc = SB(top, "Asc", [128, 5, 8], F32); Ash = SB(top, "Ash", [128, 5, 8], F32); b_mod = Buf("modvec")
        cch = SB(top, "cch", [128, 2, NJ], F32); b_cch = Buf("cch")
        nbias = SB(top, "nbias", [128, 2, NJ], F32); b_nbias = Buf("nbias")
        gqk = SB(top, "gqk", [128, 2], F32); b_gqk = Buf("gqk")

        P.dma("sp", vecs[:], vecs_d[:, :], b_vecs, writes=[b_vecs])
        P.dma("sp", tflag[:], tflag_d[:, :], b_tflag, writes=[b_tflag])
        P.dma("pool", identb[:], ident_d[:, :], b_identb, writes=[b_identb])

        def V(name):
            o, n = VOFF[name]
            return vecs[:, o:o + n]

        with ExitStack() as ph:
            ct = SB(ph, "ct", [128, 8], F32); b_ct = Buf("ct")
            t8a = SB(ph, "t8a", [128, 8], F32); t8b = SB(ph, "t8b", [128, 8], F32); b_t8 = Buf("t8")
            cact = SB(ph, "cact", [128, 8], F32); b_cact = Buf("cact")
            lc = SB(ph, "lc", [128, 8, 128], BF16); b_lc = Buf("lc")
            identf = SB(ph, "identf", [128, 128], F32); b_identf = Buf("identf")
            modrow = [SB(ph, "modrow%d" % i, [128, 6 * D], F32) for i in range(2)]
            kvrow = SB(ph, "kvrow", [128, 2 * D], F32)
            b_rows = [Buf("modrow0"), Buf("modrow1"), Buf("kvrow")]
            wch = [SB(ph, "wch%d" % i, [128, 8, 512], BF16) for i in range(3)]; b_wch = [Buf("wch%d" % i) for i in range(3)]
            dtmp = SB(ph, "dtmp", [128, 8, 128], F32); b_dtmp = Buf("dtmp")
            t11 = SB(ph, "t11", [128, NJ], F32); b_t11 = Buf("t11")
            pm = [PS(ph, "pm%d" % i, [128, 512], F32) for i in range(2)]; b_pm = [Buf("pm%d" % i) for i in range(2)]

            P.dma("sp", ct[:], ct_d[:, :], b_ct, writes=[b_ct])
            P.dma("sp", identf[:], ident_d[:, :], b_identf, writes=[b_identf])
            rows = [modrow[0], modrow[1], kvrow]
            P.dma("sp", modrow[0][:], ada_b[0:1, :].partition_broadcast(128), b_rows[0], writes=[b_rows[0]])
            P.dma("sp", modrow[1][:], ada_b[1:2, :].partition_broadcast(128), b_rows[1], writes=[b_rows[1]])
            P.dma("sp", kvrow[:], kv_ada_b[0:1, :].partition_broadcast(128), b_rows[2], writes=[b_rows[2]])
            P.op("act", lambda e: e.activation(out=t8a[:], in_=ct[:], func=AF.Exp, scale=-1.0), reads=[b_ct], writes=[b_t8])
            P.op("act", lambda e: e.activation(out=t8b[:], in_=t8a[:], func=AF.Ln, bias=1.0), reads=[b_t8], writes=[b_t8])
            P.op("act", lambda e: e.activation(out=t8a[:], in_=t8b[:], func=AF.Exp, scale=-1.0), reads=[b_t8], writes=[b_t8])
            P.op("dve", lambda e: e.tensor_tensor(out=cact[:], in0=t8a[:], in1=ct[:], op=ALU.mult), reads=[b_t8, b_ct], writes=[b_cact])
            P.op("dve", lambda e: e.tensor_copy(out=lc[:], in_=cact[:].unsqueeze(2).to_broadcast([128, 8, 128])), reads=[b_cact], writes=[b_lc])
            P.op("act", lambda e: e.activation(out=t11[:], in_=V("lam"), func=AF.Exp, scale=-1.0), reads=[b_vecs], writes=[b_t11])
            P.op("act", lambda e: e.activation(out=t11[:], in_=t11[:], func=AF.Ln, bias=1.0), reads=[b_t11], writes=[b_t11])
            P.op("dve", lambda e: e.tensor_scalar(out=cch[:, 0, :], in0=t11[:], scalar1=-8.0, scalar2=None, op0=ALU.mult), reads=[b_t11], writes=[b_cch])
            P.op("dve", lambda e: e.tensor_scalar(out=cch[:, 1, :], in0=t11[:], scalar1=-16.0, scalar2=None, op0=ALU.mult), reads=[b_t11], writes=[b_cch])
            P.op("dve", lambda e: e.tensor_scalar(out=nbias[:, 0, :], in0=V("gab"), scalar1=-1.0, scalar2=None, op0=ALU.mult), reads=[b_vecs], writes=[b_nbias])
            P.op("dve", lambda e: e.tensor_scalar(out=nbias[:, 1, :], in0=V("gxb"), scalar1=-1.0, scalar2=None, op0=ALU.mult), reads=[b_vecs], writes=[b_nbias])
            P.op("dve", lambda e: e.tensor_copy(out=gqk[:, 0:1], in_=V("gk")), reads=[b_vecs], writes=[b_gqk])
            P.op("dve", lambda e: e.tensor_scalar(out=gqk[:, 1:2], in0=V("gq"), scalar1=0.125, scalar2=None, op0=ALU.mult), reads=[b_vecs], writes=[b_gqk])

            srcs = [(ada_w[0], 6 * D, 0), (ada_w[1], 6 * D, 1), (kv_ada_w, 2 * D, 2)]
            it = 0
            for (wsrc, ncols, ri) in srcs:
                wv = wsrc.rearrange("(kc p) n -> p kc n", p=128)
                for j in range(ncols // 512):
                    wb_, bb_ = wch[it % 3], b_wch[it % 3]
                    pp, bp = pm[it % 2], b_pm[it % 2]
                    P.dma("pool", wb_[:], wv[:, :, j * 512:(j + 1) * 512], bb_, writes=[bb_])
                    for kc in range(8):
                        P.op("pe", lambda e, kc=kc, wb_=wb_, pp=pp: e.matmul(pp[:], lhsT=lc[:, kc, :], rhs=wb_[:, kc, :], start=(kc == 0), stop=(kc == 7)),
                             reads=[b_lc, bb_], writes=[bp], signal=(kc == 7))
                    rr = rows[ri]
                    P.op("dve", lambda e, rr=rr, pp=pp, j=j: e.tensor_tensor(out=rr[:, j * 512:(j + 1) * 512], in0=pp[:], in1=rr[:, j * 512:(j + 1) * 512], op=ALU.add),
                         reads=[bp, b_rows[ri]], writes=[b_rows[ri]])
                    it += 1

            specs = [(0, modrow[0], 0, 1, "n1g0"), (1, modrow[0], 3, 4, "n2g0"), (2, kvrow, 0, 1, "kvg"),
                     (3, modrow[1], 0, 1, "n1g1"), (4, modrow[1], 3, 4, "n2g1")]
            rbuf = {0: b_rows[0], 1: b_rows[0], 2: b_rows[2], 3: b_rows[1], 4: b_rows[1]}
            for (m, rr, sseg, cseg, gname) in specs:
                for (seg, dst) in [(sseg, Ash), (cseg, t8a)]:
                    P.op("dve", lambda e, rr=rr, seg=seg: e.tensor_tensor(out=dtmp[:], in0=rr[:, seg * D:(seg + 1) * D].rearrange("p (g k) -> p g k", k=128),
                                                                     in1=identf[:].unsqueeze(1).to_broadcast([128, 8, 128]), op=ALU.mult),
                         reads=[rbuf[m], b_identf], writes=[b_dtmp])
                    if dst is Ash:
                        P.op("dve", lambda e, m=m: e.tensor_reduce(out=Ash[:, m, :], in_=dtmp[:], axis=AX.X, op=ALU.add), reads=[b_dtmp], writes=[b_mod])
                    else:
                        P.op("dve", lambda e: e.tensor_reduce(out=t8a[:], in_=dtmp[:], axis=AX.X, op=ALU.add), reads=[b_dtmp], writes=[b_t8])
                P.op("dve", lambda e: e.tensor_scalar(out=t8b[:], in0=t8a[:], scalar1=1.0, scalar2=32.0, op0=ALU.add, op1=ALU.mult), reads=[b_t8], writes=[b_t8])
                P.op("dve", lambda e, m=m, gname=gname: e.tensor_tensor(out=Asc[:, m, :], in0=t8b[:], in1=V(gname), op=ALU.mult), reads=[b_t8, b_vecs], writes=[b_mod])
            for gi, (rr, seg, rb) in enumerate([(modrow[0], 2, b_rows[0]), (modrow[0], 5, b_rows[0]), (modrow[1], 2, b_rows[1]), (modrow[1], 5, b_rows[1])]):
                P.dma("sp", Gs[gi], rr[:, seg * D:(seg + 1) * D], rb, reads=[rb])
            P.barrier()

        def load_w(stack, name, src_view, shape, nsplit):
            t = SB(stack, name, shape, BF16)
            b = Buf(name)
            a = shape[1]
            step = (a + nsplit - 1) // nsplit
            for s0 in range(0, a, step):
                s1 = min(a, s0 + step)
                P.dma("pool", t[:, s0:s1, :], src_view[:, s0:s1, :], b, writes=[b], group=True)
            return t, b

        def load_w_cols(stack, name, src_view, shape, splits):
            t = SB(stack, name, shape, BF16)
            blocks = []
            for (c0, c1) in splits:
                b = Buf("%s_%d" % (name, c0))
                P.dma("pool", t[:, :, c0:c1], src_view[:, :, c0:c1], b, writes=[b])
                blocks.append((c0, c1, b))

            def bufs(c0, c1):
                return [b for (a0, a1, b) in blocks if a0 < c1 and c0 < a1]
            return t, bufs

        def norm_T(xt, b_xt, m, S):
            norm_a(xt, b_xt, S)
            norm_b(m, S)

        def norm_a(xt, b_xt, S):
            bx = b_xt if isinstance(b_xt, list) else [b_xt] * 4
            bxn = S["b_xn"] if isinstance(S["b_xn"], list) else [S["b_xn"]]
            bss = S["b_ss4"]
            for s in range(4):
                P.op("act", lambda e, s=s: e.activation(out=S["junk"][:], in_=xt[:, s, :], func=AF.Square, accum_out=S["ss"][:, s:s + 1]), reads=[bx[s]], writes=[S["b_junk"], bss[s]])
            for s in range(4):
                P.op("act", lambda e, s=s: e.activation(out=S["ss"][:, 4 + s:5 + s], in_=S["ss"][:, s:s + 1], func=AF.Ln, bias=S["epsb"][:, 0:1]), reads=[bss[s], S["b_epsb"]], writes=[bss[s]])
                P.op("act", lambda e, s=s: e.activation(out=S["ss"][:, 8 + s:9 + s], in_=S["ss"][:, 4 + s:5 + s], func=AF.Exp, scale=-0.5), reads=[bss[s]], writes=[bss[s]])
                P.op("dve", lambda e, s=s: e.tensor_scalar(out=S["xn"][:, s, :], in0=xt[:, s, :], scalar1=S["ss"][:, 8 + s:9 + s], scalar2=None, op0=ALU.mult),
                     reads=[bx[s], bss[s]], writes=bxn)

        def norm_b(m, S):
            for fc in range(8):
                pT, bT = S["pT"][fc % 2], S["b_pT"][fc % 2]
                for s in range(4):
                    P.op("pe", lambda e, s=s, fc=fc, pT=pT: e.transpose(out=pT[:, s * 128:(s + 1) * 128], in_=S["xn"][:, s, fc * 128:(fc + 1) * 128], identity=identb[:]),
                         reads=(S["b_xn"] if isinstance(S["b_xn"], list) else [S["b_xn"]]) + [b_identb], writes=[bT], signal=(s == 3))
                P.op("act", lambda e, fc=fc, pT=pT: e.activation(out=S["hnT"][:, fc, :], in_=pT[:, 0:TT], func=AF.Identity, scale=Asc[:, m, fc:fc + 1], bias=Ash[:, m, fc:fc + 1]),
                     reads=[bT, b_mod], writes=[S["b_hnT"]])

        def alloc_norm(stack, pfx):
            S = {}
            S["junk"] = SB(stack, pfx + "junk", [128, D], BF16); S["b_junk"] = Buf("junk")
            S["ss"] = SB(stack, pfx + "ss", [128, 12], F32); S["b_ss"] = Buf("ss"); S["b_ss4"] = [Buf("ss%d" % i) for i in range(4)]
            S["epsb"] = SB(stack, pfx + "epsb", [128, 2], F32); S["b_epsb"] = Buf("epsb")
            S["hnT"] = SB(stack, pfx + "hnT", [128, 8, TT], BF16); S["b_hnT"] = Buf("hnT")
            S["pT"] = [PS(stack, pfx + "pT%d" % i, [128, 2 * TT], BF16) for i in range(2)]; S["b_pT"] = [Buf("pT0"), Buf("pT1")]
            P.op("dve", lambda e: e.memset(S["epsb"][:, 0:1], 1024.0 * EPS), writes=[S["b_epsb"]])
            P.op("dve", lambda e: e.memset(S["epsb"][:, 1:2], EPS), writes=[S["b_epsb"]])
            return S

        def load_g(stack, name, gi):
            g = SB(stack, name, [128, D], F32); b = Buf(name)
            P.dma("sp", g[:], Gs[gi], b, writes=[b])
            return g, b

        def resid_out(xt, b_xt, s, nh, po, bpo, gi, S):
            P.op("dve", lambda e: e.tensor_tensor(out=S["rtmp"][:], in0=po[:], in1=S["g"][:, nh * 512:(nh + 1) * 512], op=ALU.mult),
                 reads=[bpo, S["b_g"]], writes=[S["b_rtmp"]])
            bx = b_xt[s] if isinstance(b_xt, list) else b_xt
            P.op("dve", lambda e: e.tensor_tensor(out=xt[:, s, nh * 512:(nh + 1) * 512], in0=S["rtmp"][:], in1=xt[:, s, nh * 512:(nh + 1) * 512], op=ALU.add),
                 reads=[S["b_rtmp"], bx], writes=[bx])

        def load_rows(xt, b_xt, src):
            for s in range(4):
                P.dma("sp", xt[:, s, :], src[s * 128:(s + 1) * 128, :], b_xt[s], writes=[b_xt[s]])

        def store_rows(xt, b_xt, dst, s):
            P.dma("sp", dst[s * 128:(s + 1) * 128, :], xt[:, s, :], b_xt[s], reads=[b_xt[s]])

        nbrs = gate_nbrs()
        with ExitStack() as ph:
            w_in_sb, w_in_bufs = load_w_cols(ph, "w_in_sb", w_in_d.rearrange("(kc p) n -> p kc n", p=128), [128, 8, 2 * W],
                                             [(W, W + 512), (W + 512, 2 * W), (0, 704), (704, W)])
            wg = SB(ph, "wg", [128, 2, NJ * 3, 128], BF16); b_wg = Buf("wg")
            P.op("dve", lambda e: e.memset(wg[:], 0.0), writes=[b_wg])
            for gidx, gsrc in enumerate([ga_w_d, gx_w_d]):
                for (n, r0, r1, c0, c1, ci, co, d, p0, p1, q0, q1) in gate_pieces():
                    P.dma("pool", wg[p0:p1, gidx, co * 3 + d, q0:q1], gsrc[n, r0:r1, c0:c1], b_wg, writes=[b_wg], group=True)
            w_out_sb, b_w_out = load_w(ph, "w_out_sb", w_out_d.rearrange("(jc p) n -> p jc n", p=128), [128, NJ, D], 2)

            S = alloc_norm(ph, "p1")
            S["g"], S["b_g"] = load_g(ph, "p1g", 0)
            xts = [SB(ph, "p1xt%d" % i, [128, 4, D], F32) for i in range(2)]; b_xts = [Buf("xt0"), Buf("xt1")]
            yb = SB(ph, "p1yb", [128, NJ, TT], BF16); b_yb = Buf("yb")
            NXR = 2
            xrb = [SB(ph, "p1xrb%d" % i, [128, TT + 3], F32) for i in range(NXR)]; b_xrb = [Buf("xrb%d" % i) for i in range(NXR)]
            halo = SB(ph, "p1halo", [128, NJ, 3], F32); b_halo = [Buf("halo%d" % j) for j in range(NJ)]
            xc = SB(ph, "p1xc", [128, NJ, TT], F32); b_xc = [Buf("xc%d" % j) for j in range(NJ)]
            xcb = SB(ph, "p1xcb", [128, NJ, TT], BF16); b_xcb = [Buf("xcb%d" % j) for j in range(NJ)]
            S["xn"] = xcb[:, 0:8, :].rearrange("p a c -> p (a c)").rearrange("p (s d) -> p s d", s=4)
            S["b_xn"] = b_xcb[0:8]
            state = SB(ph, "p1state", [128, NJ], F32); b_state = Buf("state")
            NTMP = 3
            tA = [SB(ph, "p1tA%d" % i, [128, TT], F32) for i in range(NTMP)]; b_tA = [Buf("tA%d" % i) for i in range(NTMP)]
            tQ = [SB(ph, "p1tQ%d" % i, [128, TT], F32) for i in range(NTMP)]; b_tQ = [Buf("tQ%d" % i) for i in range(NTMP)]
            tG, b_tG = tA, b_tA
            NGRP = 6
            Rst = SB(ph, "p1R", [128, NGRP, TT], F32); b_R = [Buf("R%d" % i) for i in range(NGRP)]
            S["rtmp"] = tA[0]; S["b_rtmp"] = b_tA[0]
            pz = [PS(ph, "p1pz%d" % i, [128, TT], F32) for i in range(6)]; b_pz = [Buf("pz%d" % i) for i in range(6)]
            pzi = [0]

            def nextpz():
                i = pzi[0] % 6
                pzi[0] += 1
                return pz[i], b_pz[i]

            P.op("dve", lambda e: e.memset(halo[:], 0.0), writes=b_halo)
            P.op("dve", lambda e: e.memset(state[:], 0.0), writes=[b_state])
            cw = V("convw")
            cb = V("convb")

            def load_x(t):
                P.dma("sp", xts[t % 2][:], xin[t * TT:(t + 1) * TT, :].rearrange("(s p) d -> p s d", p=128), b_xts[t % 2], writes=[b_xts[t % 2]])

            ntile_p1 = NTILE if stop != "p1a" else 5
            hnT, b_hnT = S["hnT"], S["b_hnT"]
            gab, gxb = V("gab"), V("gxb")

            pending_cast = []

            def cast_chunk(j):
                P.op("act", lambda e: e.activation(out=xcb[:, j, :], in_=xc[:, j, :], func=AF.Copy), reads=[b_xc[j]], writes=[b_xcb[j]])

            def conv_part(t, j0, j1):
                def halo_in(j):
                    xr, bxr = xrb[j % NXR], b_xrb[j % NXR]
                    P.op("pool", lambda e: e.tensor_copy(out=xr[:, 0:3], in_=halo[:, j, :]), reads=[b_halo[j]], writes=[bxr])

                if j0 == 0:
                    halo_in(0)
                for j in range(j0, j1):
                    oc = NJ + j
                    pp, bp = nextpz()
                    for kc in range(8):
                        P.op("pe", lambda e, kc=kc, oc=oc, pp=pp: e.matmul(pp[:], lhsT=w_in_sb[:, kc, oc * 128:(oc + 1) * 128], rhs=hnT[:, kc, :], start=(kc == 0), stop=(kc == 7)),
                             reads=w_in_bufs(oc * 128, oc * 128 + 128) + [b_hnT], writes=[bp], signal=(kc == 7))
                    xr, bxr = xrb[j % NXR], b_xrb[j % NXR]
                    P.op("act", lambda e, xr=xr, pp=pp: e.activation(out=xr[:, 3:TT + 3], in_=pp[:], func=AF.Copy), reads=[bp], writes=[bxr])
                    P.op("act", lambda e, pp=pp, j=j: e.activation(out=xc[:, j, :], in_=pp[:], func=AF.Identity, scale=cw[:, j * 4 + 3:j * 4 + 4], bias=cb[:, j:j + 1]),
                         reads=[bp, b_vecs], writes=[b_xc[j]])
                    if j + 1 < NJ:
                        halo_in(j + 1)
                    while pending_cast:
                        cast_chunk(pending_cast.pop(0))
                    for k in range(3):
                        P.op("dve", lambda e, xr=xr, j=j, k=k: e.scalar_tensor_tensor(out=xc[:, j, :], in0=xr[:, k:k + TT], scalar=cw[:, j * 4 + k:j * 4 + k + 1], in1=xc[:, j, :],
                                                                                  op0=ALU.mult, op1=ALU.add),
                             reads=[bxr, b_xc[j], b_vecs], writes=[b_xc[j]])
                    P.op("pool", lambda e, xr=xr, j=j: e.tensor_scalar(out=halo[:, j, :], in0=xr[:, TT:TT + 3], scalar1=tflag[:, t + 1:t + 2], scalar2=None, op0=ALU.mult),
                         reads=[bxr, b_tflag], writes=[b_halo[j]])
                    pending_cast.append(j)
                if j1 == NJ:
                    while pending_cast:
                        cast_chunk(pending_cast.pop(0))

            def gates_part(t):
                full = t >= 3
                if full:
                    for j in range(NJ):
                        pp, bp = nextpz()
                        for kc in range(8):
                            P.op("pe", lambda e, kc=kc, j=j, pp=pp: e.matmul(pp[:], lhsT=w_in_sb[:, kc, j * 128:(j + 1) * 128], rhs=hnT[:, kc, :], start=(kc == 0), stop=(kc == 7)),
                                 reads=w_in_bufs(j * 128, j * 128 + 128) + [b_hnT], writes=[bp], signal=(kc == 7))
                        P.op("act", lambda e, j=j, pp=pp: e.activation(out=yb[:, j, :], in_=pp[:], func=AF.Gelu_apprx_tanh), reads=[bp], writes=[b_yb])
                for gno, grp in enumerate([list(range(0, NGRP)), list(range(NGRP, NJ))]):
                    for gi_, co in enumerate(grp):
                        pa, bpa = nextpz()
                        px, bpx = nextpz()
                        cis = nbrs[co]
                        for gidx, (pg, bpg) in enumerate([(pa, bpa), (px, bpx)]):
                            for n_, ci in enumerate(cis):
                                d = ci - co + 1
                                P.op("pe", lambda e, pg=pg, gidx=gidx, d=d, ci=ci, n_=n_, co=co, cis=cis: e.matmul(pg[:], lhsT=wg[:, gidx, co * 3 + d, :], rhs=xcb[:, ci, :],
                                                                                                         start=(n_ == 0), stop=(n_ == len(cis) - 1)),
                                     reads=[b_wg, b_xcb[ci]], writes=[bpg], signal=(n_ == len(cis) - 1))
                        I_, bI = tQ[co % NTMP], b_tQ[co % NTMP]
                        P.op("act", lambda e, pa=pa, gi_=gi_, co=co: e.activation(out=Rst[:, gi_, :], in_=pa[:], func=AF.Sigmoid, bias=gab[:, co:co + 1]), reads=[bpa, b_vecs], writes=[b_R[gi_]])
                        P.op("act", lambda e, px=px, I_=I_, co=co: e.activation(out=I_[:], in_=px[:], func=AF.Sigmoid, bias=gxb[:, co:co + 1]), reads=[bpx, b_vecs], writes=[bI])
                        P.op("pool", lambda e, I_=I_, co=co: e.tensor_tensor(out=xc[:, co, :], in0=I_[:], in1=xc[:, co, :], op=ALU.mult), reads=[bI, b_xc[co]], writes=[b_xc[co]])

                    def stage_b(co, gi_, k_):
                        A_, bA = tA[k_], b_tA[k_]
                        Q_, bQ = tQ[k_], b_tQ[k_]
                        return [
                            lambda: P.op("act", lambda e: e.activation(out=A_[:], in_=Rst[:, gi_, :], func=AF.Exp, scale=cch[:, 0, co:co + 1]), reads=[b_R[gi_], b_cch], writes=[bA]),
                            lambda: P.op("pool", lambda e: e.tensor_tensor(out=Q_[:], in0=A_[:], in1=A_[:], op=ALU.mult), reads=[bA], writes=[bQ]),
                            lambda: P.op("act", lambda e: e.activation(out=Q_[:], in_=Q_[:], func=AF.Ln, scale=-1.0, bias=1.000001), reads=[bQ], writes=[bQ]),
                            lambda: P.op("act", lambda e: e.activation(out=Q_[:], in_=Q_[:], func=AF.Exp, scale=0.5), reads=[bQ], writes=[bQ]),
                            lambda: P.op("pool", lambda e: e.tensor_tensor(out=Q_[:], in0=Q_[:], in1=xc[:, co, :], op=ALU.mult), reads=[bQ, b_xc[co]], writes=[bQ]),
                            lambda: P.op("dve", lambda e: e.tensor_tensor_scan(out=xc[:, co, :], data0=A_[:], data1=Q_[:], initial=state[:, co:co + 1], op0=ALU.mult, op1=ALU.add),
                                         reads=[bA, bQ, b_state], writes=[b_xc[co]]),
                        ]

                    if gno == 1 and t + 1 < ntile_p1:
                        norm_a(xts[(t + 1) % 2], b_xts[(t + 1) % 2], S)
                    for c3 in range(0, len(grp), NTMP):
                        chains = [stage_b(co, c3 + k_, k_) for k_, co in enumerate(grp[c3:c3 + NTMP])]
                        for si in range(len(chains[0])):
                            for ch in chains:
                                ch[si]()
                P.op("dve", lambda e: e.tensor_scalar(out=state[:], in0=xc[:, :, TT - 1], scalar1=tflag[:, t + 1:t + 2], scalar2=None, op0=ALU.mult),
                     reads=b_xc + [b_tflag], writes=[b_state])

            def tail_a(t):
                for j in range(NJ):
                    P.op("dve", lambda e, j=j: e.tensor_tensor(out=yb[:, j, :], in0=yb[:, j, :], in1=xc[:, j, :], op=ALU.mult), reads=[b_yb, b_xc[j]], writes=[b_yb])

            def tail_b(t):
                xt, b_xt = xts[t % 2], b_xts[t % 2]
                for s_ in range(4):
                    for nh in range(2):
                        pp, bp = nextpz()
                        for jc in range(NJ):
                            P.op("pe", lambda e, jc=jc, s_=s_, nh=nh, pp=pp: e.matmul(pp[:], lhsT=yb[:, jc, s_ * 128:(s_ + 1) * 128], rhs=w_out_sb[:, jc, nh * 512:(nh + 1) * 512],
                                                                                  start=(jc == 0), stop=(jc == NJ - 1)),
                                 reads=[b_yb, b_w_out], writes=[bp], signal=(jc == NJ - 1))
                        resid_out(xt, b_xt, s_, nh, pp, bp, 0, S)
                P.dma("sp", H[(t - 3) * TT:(t - 2) * TT, :].rearrange("(s p) d -> p s d", p=128), xt[:], b_xt, reads=[b_xt])

            load_x(0)
            load_x(1)
            norm_a(xts[0], b_xts[0], S)
            norm_b(0, S)
            for t in range(ntile_p1):
                prev_full = (t - 1) >= 3
                if prev_full:
                    tail_a(t - 1)
                conv_part(t, 0, 6)
                if prev_full:
                    tail_b(t - 1)
                    if t + 1 < ntile_p1:
                        load_x(t + 1)
                elif 1 <= t and t + 1 < ntile_p1:
                    load_x(t + 1)
                conv_part(t, 6, NJ)
                gates_part(t)
                if t + 1 < ntile_p1:
                    norm_b(0, S)
            tail_a(ntile_p1 - 1)
            tail_b(ntile_p1 - 1)
            P.barrier()
        if stop in ("p1", "p1a"):
            return nc

        def mlp_phase(l, tiles, m, gi, final):
            with ExitStack() as ph:
                w1_sb, w1_bufs = load_w_cols(ph, "w1_sb%d" % l, w1_d[l].rearrange("(kc p) n -> p kc n", p=128), [128, 8, DFF],
                                             [(i * 512, (i + 1) * 512) for i in range(8)])
                w2_sb, b_w2 = load_w(ph, "w2_sb%d" % l, w2_d[l].rearrange("(fc p) n -> p fc n", p=128), [128, 32, D], 8)
                S = alloc_norm(ph, "m%d" % l)
                S["g"], S["b_g"] = load_g(ph, "m%dg" % l, gi)
                xt = SB(ph, "m%dxt" % l, [128, 4, D], F32); b_xt = [Buf("xt%d" % i) for i in range(4)]
                hid = SB(ph, "m%dhid" % l, [128, 32, TT], BF16); b_hid = Buf("hid")
                S["xn"] = hid[:, 0:8, :].rearrange("p a c -> p (a c)").rearrange("p (s d) -> p s d", s=4)
                S["b_xn"] = b_hid
                rl = [SB(ph, "m%drl%d" % (l, i), [128, TT], BF16) for i in range(3)]; b_rl = [Buf("rl%d" % i) for i in range(3)]
                S["rtmp"] = SB(ph, "m%drtmp" % l, [128, 512], F32); S["b_rtmp"] = Buf("rtmp")
                pz = [PS(ph, "m%dpz%d" % (l, i), [128, TT], F32) for i in range(6)]; b_pz = [Buf("pz%d" % i) for i in range(6)]
                pzi = [0]

                def nextpz():
                    i = pzi[0] % 6
                    pzi[0] += 1
                    return pz[i], b_pz[i]

                for t in tiles:
                    hrow = H[(t - 3) * TT:(t - 2) * TT, :]
                    load_rows(xt, b_xt, hrow)
                    norm_T(xt, b_xt, m, S)
                    hnT, b_hnT = S["hnT"], S["b_hnT"]
                    for fc in range(32):
                        pp, bp = nextpz()
                        for kc in range(8):
                            P.op("pe", lambda e, kc=kc, fc=fc, pp=pp: e.matmul(pp[:], lhsT=w1_sb[:, kc, fc * 128:(fc + 1) * 128], rhs=hnT[:, kc, :], start=(kc == 0), stop=(kc == 7)),
                                 reads=w1_bufs(fc * 128, fc * 128 + 128) + [b_hnT], writes=[bp], signal=(kc == 7))
                        r_, br = rl[fc % 3], b_rl[fc % 3]
                        P.op("act", lambda e, r_=r_, pp=pp: e.activation(out=r_[:], in_=pp[:], func=AF.Relu), reads=[bp], writes=[br])
                        P.op("dve", lambda e, r_=r_, fc=fc: e.tensor_tensor(out=hid[:, fc, :], in0=r_[:], in1=r_[:], op=ALU.mult), reads=[br], writes=[b_hid])
                    for s in range(4):
                        for nh in range(2):
                            pp, bp = nextpz()
                            for fc in range(32):
                                P.op("pe", lambda e, fc=fc, s=s, nh=nh, pp=pp: e.matmul(pp[:], lhsT=hid[:, fc, s * 128:(s + 1) * 128], rhs=w2_sb[:, fc, nh * 512:(nh + 1) * 512],
                                                                                    start=(fc == 0), stop=(fc == 31)),
                                     reads=[b_hid, b_w2], writes=[bp], signal=(fc == 31))
                            resid_out(xt, b_xt, s, nh, pp, bp, gi, S)
                        store_rows(xt, b_xt, out_d[(t - 4) * TT:(t - 3) * TT, :] if final else hrow, s)
                P.barrier()

        mlp_phase(0, range(3, 8), 1, 1, False)
        if stop == "p2":
            return nc

        with ExitStack() as ph:
            blkb = SB(ph, "blkb", [128, 128], BF16); b_blk = Buf("blkb")
            P.dma("pool", blkb[:], blk_d[:, :], b_blk, writes=[b_blk])
            expB = SB(ph, "expB", [128, 16, 640], BF16); b_expB = Buf("expB")
            KT = SB(ph, "KT", [128, 8, NKV * TT], BF16); b_KT = Buf("KT")
            Va = SB(ph, "Va", [128, NKV * 4, 16, 65], BF16); b_Va = Buf("Va")
            S = alloc_norm(ph, "p3")
            xt = SB(ph, "p3xt", [128, 4, D], F32); b_xt = [Buf("xt%d" % i) for i in range(4)]
            sq = [SB(ph, "p3sq%d" % i, [128, TT], BF16) for i in range(3)]; b_sq = [Buf("sq%d" % i) for i in range(3)]
            rs = [SB(ph, "p3rs%d" % i, [128, TT], F32) for i in range(3)]; b_rs = [Buf("rs%d" % i) for i in range(3)]
            pSt = [PS(ph, "p3pS%d" % i, [128, 1024], F32) for i in range(2)]
            pOt = [PS(ph, "p3pO%d" % i, [128, 4, 128], F32) for i in range(2)]
            b_bank = [Buf("bank%d" % i) for i in range(6)]
            pS = pSt; b_pS = [[b_bank[0], b_bank[1]], [b_bank[2], b_bank[3]]]
            pO = pOt; b_pO = [b_bank[4], b_bank[5]]
            pz = [pSt[0][:, 0:512], pSt[0][:, 512:1024], pSt[1][:, 0:512], pSt[1][:, 512:1024],
                  pOt[0][:].rearrange("p a b -> p (a b)"), pOt[1][:].rearrange("p a b -> p (a b)")]
            pzi = [0]

            def nextpz():
                i = pzi[0] % 6
                pzi[0] += 1
                return pz[i], b_bank[i]

            def proj_headnorm(w_sb, w_bufs, hnT, b_hnT, dstT, b_dst, col0, gcol):
                for g0 in range(0, 8, 3):
                    fcs = list(range(g0, min(8, g0 + 3)))
                    pps, pms = [], []
                    for fc in fcs:
                        pp, bp = nextpz()
                        for kc in range(8):
                            P.op("pe", lambda e, kc=kc, fc=fc, pp=pp: e.matmul(pp[:], lhsT=w_sb[:, kc, fc * 128:(fc + 1) * 128], rhs=hnT[:, kc, :], start=(kc == 0), stop=(kc == 7)),
                                 reads=w_bufs(fc * 128, fc * 128 + 128) + [b_hnT], writes=[bp], signal=(kc == 7))
                        pps.append((pp, bp))
                    for i, fc in enumerate(fcs):
                        pp, bp = pps[i]
                        P.op("act", lambda e, i=i, pp=pp: e.activation(out=sq[i][:], in_=pp[:], func=AF.Square), reads=[bp], writes=[b_sq[i]])
                    for i, fc in enumerate(fcs):
                        pm_, bpm = nextpz()
                        P.op("pe", lambda e, i=i, pm_=pm_: e.matmul(pm_[:], lhsT=blkb[:], rhs=sq[i][:], start=True, stop=True), reads=[b_blk, b_sq[i]], writes=[bpm])
                        pms.append((pm_, bpm))
                    for i, fc in enumerate(fcs):
                        pm_, bpm = pms[i]
                        P.op("act", lambda e, i=i, pm_=pm_: e.activation(out=rs[i][:], in_=pm_[:], func=AF.Ln, bias=S["epsb"][:, 1:2]), reads=[bpm, S["b_epsb"]], writes=[b_rs[i]])
                    for i, fc in enumerate(fcs):
                        P.op("act", lambda e, i=i: e.activation(out=rs[i][:], in_=rs[i][:], func=AF.Exp, scale=-0.5), reads=[b_rs[i]], writes=[b_rs[i]])
                    for i, fc in enumerate(fcs):
                        pp, bp = pps[i]
                        P.op("dve", lambda e, i=i, fc=fc, pp=pp: e.scalar_tensor_tensor(out=dstT[:, fc, col0:col0 + TT], in0=pp[:], scalar=gqk[:, gcol:gcol + 1], in1=rs[i][:],
                                                                                 op0=ALU.mult, op1=ALU.mult),
                             reads=[bp, b_gqk, b_rs[i]], writes=[b_dst])

            with ExitStack() as p1s:
                kvw_sb, kvw_bufs = load_w_cols(p1s, "kvw_sb", kv_w_d.rearrange("(kc p) n -> p kc n", p=128), [128, 8, 2 * D],
                                               [(i * 512, (i + 1) * 512) for i in range(4)])
                amask = SB(p1s, "amask", [128, 640], F32); b_amask = Buf("amask")
                P.dma("sp", amask[:], amask_d[:, :], b_amask, writes=[b_amask])
                btmp = [SB(p1s, "p3btmp%d" % i, [128, 640], F32) for i in range(2)]; b_btmp = [Buf("btmp0"), Buf("btmp1")]
                xn = SB(p1s, "p3xn", [128, 4, D], BF16); S["xn"] = xn; S["b_xn"] = Buf("xn")
                biasv = biasT_d.rearrange("p (h x) -> p h x", h=16)
                for h in range(16):
                    bt, bbt = btmp[h % 2], b_btmp[h % 2]
                    P.dma("sp", bt[:], biasv[:, h, :], bbt, writes=[bbt])
                    P.op("act", lambda e, bt=bt: e.activation(out=bt[:], in_=bt[:], func=AF.Exp), reads=[bbt], writes=[bbt])
                    P.op("dve", lambda e, h=h, bt=bt: e.tensor_tensor(out=expB[:, h, :], in0=bt[:], in1=amask[:], op=ALU.mult), reads=[bbt, b_amask], writes=[b_expB])
                P.op("dve", lambda e: e.memset(Va[:, :, :, 64:65], 1.0), writes=[b_Va])
                P.op("dve", lambda e: e.tensor_scalar(out=Va[:, 0:4, :, 64:65], in0=Va[:, 0:4, :, 64:65], scalar1=tflag[:, 4:5], scalar2=None, op0=ALU.mult),
                     reads=[b_tflag, b_Va], writes=[b_Va])
                for t in range(3, 8):
                    kt0 = t - 3
                    hrow = H[kt0 * TT:(kt0 + 1) * TT, :]
                    load_rows(xt, b_xt, hrow)
                    norm_T(xt, b_xt, 2, S)
                    hnT, b_hnT = S["hnT"], S["b_hnT"]
                    proj_headnorm(kvw_sb, kvw_bufs, hnT, b_hnT, KT, b_KT, kt0 * TT, 0)
                    for s in range(4):
                        for nh in range(2):
                            pp, bp = nextpz()
                            for kc in range(8):
                                P.op("pe", lambda e, kc=kc, s=s, nh=nh, pp=pp: e.matmul(pp[:], lhsT=hnT[:, kc, s * 128:(s + 1) * 128], rhs=kvw_sb[:, kc, D + nh * 512:D + (nh + 1) * 512],
                                                                                    start=(kc == 0), stop=(kc == 7)),
                                     reads=kvw_bufs(D + nh * 512, D + (nh + 1) * 512) + [b_hnT], writes=[bp], signal=(kc == 7))
                            if t == 3:
                                P.op("act", lambda e, s=s, nh=nh, pp=pp, kt0=kt0: e.activation(out=Va[:, kt0 * 4 + s, nh * 8:(nh + 1) * 8, 0:64], in_=pp[:].rearrange("p (h d) -> p h d", d=64),
                                                                                           func=AF.Identity, scale=tflag[:, 4:5]),
                                     reads=[bp, b_tflag], writes=[b_Va])
                            else:
                                P.op("act", lambda e, s=s, nh=nh, pp=pp, kt0=kt0: e.activation(out=Va[:, kt0 * 4 + s, nh * 8:(nh + 1) * 8, 0:64], in_=pp[:].rearrange("p (h d) -> p h d", d=64),
                                                                                           func=AF.Copy),
                                     reads=[bp], writes=[b_Va])
                P.barrier()
            if stop == "p3a":
                return nc
            with ExitStack() as p2s:
                wq_sb, wq_bufs = load_w_cols(p2s, "wq_sb", wq_d.rearrange("(kc p) n -> p kc n", p=128), [128, 8, D], [(0, 512), (512, 1024)])
                wo_sb, b_wo = load_w(p2s, "wo_sb", wo_d.rearrange("(kc p) n -> p kc n", p=128), [128, 8, D], 2)
                S["g"], S["b_g"] = load_g(p2s, "p3g", 2)
                QT = SB(p2s, "QT", [128, 8, TT], BF16); b_QT = Buf("QT")
                S["rtmp"] = SB(p2s, "p3rtmp", [128, 512], F32); S["b_rtmp"] = Buf("rtmp")
                Eb = [SB(p2s, "p3E%d" % i, [128, 640], BF16) for i in range(3)]; b_E = [Buf("E%d" % i) for i in range(3)]
                Pb = [SB(p2s, "p3P%d" % i, [128, 640], BF16) for i in range(3)]; b_P = [Buf("P%d" % i) for i in range(3)]
                rec = SB(p2s, "p3rec", [128, 2, 4], F32); b_rec = [Buf("rec0"), Buf("rec1")]
                attn = SB(p2s, "p3attn", [128, D], BF16); b_attn = Buf("attn")
                attnT = SB(p2s, "p3attnT", [128, 8, TT], BF16); b_attnT = Buf("attnT")
                S["xn"] = attnT[:].rearrange("p a c -> p (a c)").rearrange("p (s d) -> p s d", s=4); S["b_xn"] = b_attnT
                for t in range(4, 8):
                    kt0 = t - 3
                    hrow = H[kt0 * TT:(kt0 + 1) * TT, :]
                    load_rows(xt, b_xt, hrow)
                    norm_T(xt, b_xt, 3, S)
                    hnT, b_hnT = S["hnT"], S["b_hnT"]
                    proj_headnorm(wq_sb, wq_bufs, hnT, b_hnT, QT, b_QT, 0, 1)
                    items = [(qg, h) for qg in range(4) for h in range(16)]

                    def emit_S(idx):
                        qg, h = items[idx]
                        G = kt0 * 4 + qg
                        fc, hf = h // 2, h % 2
                        pS_, bpS = pS[idx % 2], b_pS[idx % 2]
                        for kt in range(5):
                            P.op("pe", lambda e, kt=kt: e.matmul(pS_[:, kt * 128:(kt + 1) * 128],
                                                                 lhsT=KT[hf * 64:(hf + 1) * 64, fc, (G - 4 + kt) * 128:(G - 3 + kt) * 128],
                                                                 rhs=QT[hf * 64:(hf + 1) * 64, fc, qg * 128:(qg + 1) * 128], start=True, stop=True),
                                 reads=[b_KT, b_QT], writes=bpS, signal=(kt == 4))

                    emit_S(0)
                    for idx, (qg, h) in enumerate(items):
                        G = kt0 * 4 + qg
                        h4, hh = h // 4, h % 4
                        if idx + 1 < len(items):
                            emit_S(idx + 1)
                        pS_, bpS = pS[idx % 2], b_pS[idx % 2]
                        pO_, bpO = pO[h4 % 2], b_pO[h4 % 2]
                        E_, bE = Eb[idx % 3], b_E[idx % 3]
                        P_, bP = Pb[idx % 3], b_P[idx % 3]
                        P.op("act", lambda e, E_=E_, pS_=pS_: e.activation(out=E_[:], in_=pS_[:, 0:640], func=AF.Exp), reads=bpS, writes=[bE])
                        P.op("dve", lambda e, E_=E_, P_=P_, h=h: e.tensor_tensor(out=P_[:], in0=E_[:], in1=expB[:, h, :], op=ALU.mult), reads=[bE, b_expB], writes=[bP])
                        for kt in range(5):
                            P.op("pe", lambda e, kt=kt, hh=hh, h=h, G=G, P_=P_, pO_=pO_: e.matmul(pO_[:, hh, 0:65], lhsT=P_[:, kt * 128:(kt + 1) * 128], rhs=Va[:, G - 4 + kt, h, :],
                                                                                           start=(kt == 0), stop=(kt == 4)),
                                 reads=[bP, b_Va], writes=[bpO], signal=(kt == 4))
                        if hh == 3:
                            r_ = h4 % 2
                            P.op("dve", lambda e, r_=r_, pO_=pO_: e.reciprocal(out=rec[:, r_, :], in_=pO_[:, :, 64]), reads=[bpO], writes=[b_rec[r_]])
                            P.op("dve", lambda e, r_=r_, pO_=pO_, h4=h4: e.tensor_tensor(out=attn[:, h4 * 256:(h4 + 1) * 256].rearrange("p (h d) -> p h d", d=64), in0=pO_[:, :, 0:64],
                                                                                    in1=rec[:, r_, :].unsqueeze(2).to_broadcast([128, 4, 64]), op=ALU.mult),
                                 reads=[bpO, b_rec[r_]], writes=[b_attn])
                        if h == 15:
                            for fc in range(8):
                                pT, bT = S["pT"][fc % 2], S["b_pT"][fc % 2]
                                P.op("pe", lambda e, fc=fc, pT=pT: e.transpose(out=pT[:, 0:128], in_=attn[:, fc * 128:(fc + 1) * 128], identity=identb[:]), reads=[b_attn, b_identb], writes=[bT])
                                P.op("act", lambda e, fc=fc, pT=pT, qg=qg: e.activation(out=attnT[:, fc, qg * 128:(qg + 1) * 128], in_=pT[:, 0:128], func=AF.Copy), reads=[bT], writes=[b_attnT])

                    for s in range(4):
                        for nh in range(2):
                            pp, bp = nextpz()
                            for kc in range(8):
                                P.op("pe", lambda e, kc=kc, s=s, nh=nh, pp=pp: e.matmul(pp[:], lhsT=attnT[:, kc, s * 128:(s + 1) * 128], rhs=wo_sb[:, kc, nh * 512:(nh + 1) * 512],
                                                                                    start=(kc == 0), stop=(kc == 7)),
                                     reads=[b_attnT, b_wo], writes=[bp], signal=(kc == 7))
                            resid_out(xt, b_xt, s, nh, pp, bp, 2, S)
                        store_rows(xt, b_xt, hrow, s)
                P.barrier()
        if stop == "p3":
            return nc

        mlp_phase(1, range(4, 8), 4, 3, True)
    return nc


def _host_consts():
    ident = np.eye(128, dtype=np.float32)
    blk = np.zeros((128, 128), np.float32)
    blk[:64, :64] = 1.0 / 64
    blk[64:, 64:] = 1.0 / 64
    kk = np.arange(128)[:, None, None]
    kt = np.arange(5)[None, :, None]
    q = np.arange(128)[None, None, :]
    kidx = kt * 128 + kk
    ck = kidx // 64
    cq = q // 64
    amask = ((ck >= cq) & (ck <= cq + 8)).astype(np.float32)
    bidx = np.minimum(640 + q - kidx, 256)
    bidx = np.maximum(bidx, 0)
    return ident, blk, amask.reshape(128, 640), bidx


def _pvec(v, n):
    return np.ascontiguousarray(np.asarray(v, np.float32).reshape(n, 128).T)


def make_in_maps(inputs):
    f = lambda k: np.ascontiguousarray(np.asarray(inputs[k], dtype=np.float32))
    x = f("x"); c = f("c")
    ident, blk, amask, bidx = _host_consts()
    rel_bias = f("rel_bias")[0]
    biasT = rel_bias[:, bidx]
    biasT = np.ascontiguousarray(biasT.transpose(1, 0, 2, 3).reshape(128, 16 * 640))
    conv_w = f("lru_conv_w")[0]
    convw = np.ascontiguousarray(conv_w.reshape(4, NJ, 128).transpose(2, 1, 0).reshape(128, NJ * 4))
    vec_parts = {
        "n1g0": _pvec(f("norm1_g")[0], 8), "n2g0": _pvec(f("norm2_g")[0], 8), "kvg": _pvec(f("kv_norm_g"), 8),
        "n1g1": _pvec(f("norm1_g")[1], 8), "n2g1": _pvec(f("norm2_g")[1], 8), "convw": convw,
        "convb": _pvec(f("lru_conv_b")[0], NJ), "gab": _pvec(f("lru_gate_a_b")[0], NJ), "gxb": _pvec(f("lru_gate_x_b")[0], NJ),
        "lam": _pvec(f("lru_lambda")[0], NJ),
        "gk": np.tile(f("k_norm_g"), 2).reshape(128, 1), "gq": np.tile(f("q_norm_g")[0], 2).reshape(128, 1),
    }
    vecs = np.zeros((128, NV), np.float32)
    for k, (o, n) in VOFF.items():
        vecs[:, o:o + n] = vec_parts[k]
    shared = {
        "vecs": vecs, "ada_w": f("ada_w"), "ada_b": f("ada_b"), "kv_ada_w": f("kv_ada_w"), "kv_ada_b": f("kv_ada_b").reshape(1, -1),
        "w_in": f("lru_w_in")[0], "ga_w": f("lru_gate_a_w")[0], "gx_w": f("lru_gate_x_w")[0], "w_out": f("lru_w_out")[0],
        "mlp_w1": f("mlp_w1"), "mlp_w2": f("mlp_w2"), "kv_w": f("kv_w"), "wq": f("attn_w_q")[0], "wo": f("attn_w_o")[0],
        "biasT": biasT, "amask": amask, "ident": ident, "blk64": blk,
    }
    maps = []
    for core in range(8):
        b, half = core // 2, core % 2
        if half == 1:
            xin = x[b]
        else:
            xin = np.concatenate([np.zeros((2048, D), np.float32), x[b, :2048]], axis=0)
        tflag = np.ones((128, NTILE + 1), np.float32)
        tflag[:, 4] = float(half)
        m = dict(shared)
        m["xin"] = np.ascontiguousarray(xin)
        m["ct"] = _pvec(c[b], 8)
        m["tflag"] = tflag
        maps.append(m)
    return maps


def kernel(**inputs):
    nc = build()
    maps = make_in_maps(inputs)
    res = run_bass_kernel_spmd(nc, maps, core_ids=list(range(8)))
    out = np.zeros((4, 4096, D), np.float32)
    for core in range(8):
        b, half = core // 2, core % 2
        out[b, half * 2048:(half + 1) * 2048] = res.results[core]["out"]
    return out
```

```python
import numpy as np
from contextlib import ExitStack
import concourse.bass as bass
import concourse.mybir as mybir
from concourse.bass_utils import run_bass_kernel_spmd

F32 = mybir.dt.float32
BF16 = mybir.dt.bfloat16
AF = mybir.ActivationFunctionType
ALU = mybir.AluOpType
AX = mybir.AxisListType

D = 1024
W = 1408
NJ = 11
DFF = 4096
TT = 512
NTILE = 8
NKV = 5
EPS = 1e-6


class Buf:
    __slots__ = ("name", "w", "r", "dsem", "dcnt")

    def __init__(self, name):
        self.name = name
        self.w = None
        self.r = []
        self.dsem = None
        self.dcnt = 0


class Prog:
    def __init__(self, nc, stack):
        self.nc = nc
        self.stack = stack
        self.eng = {"pe": nc.tensor, "act": nc.scalar, "dve": nc.vector, "pool": nc.gpsimd, "sp": nc.sync}
        self.sem = {k: stack.enter_context(nc.semaphore("s_" + k)) for k in self.eng}
        self.cnt = {k: 0 for k in self.eng}
        self.seen = {k: {} for k in self.eng}
        self.pending = {k: False for k in self.eng}
        self.dbufs = []
        self.nsem = 0
        self.swq = []
        self.swq_sum = 0

    def _wait(self, e, tok):
        if tok is None:
            return
        sem, val, key = tok
        if key == "pe" and e == "pe":
            return
        if self.seen[e].get(key, 0) >= val:
            return
        self.eng[e].wait_ge(sem, val)
        self.seen[e][key] = val

    def _hazards(self, e, reads, writes, group_sem=None):
        for b in reads:
            self._wait(e, b.w)
        for b in writes:
            if not (group_sem is not None and b.w is not None and b.w[0] is group_sem):
                self._wait(e, b.w)
            for t in b.r:
                self._wait(e, t)

    def op(self, e, ins_fn, reads=(), writes=(), signal=True):
        self._hazards(e, reads, writes)
        ins = ins_fn(self.eng[e])
        if signal:
            self.cnt[e] += 1
            ins.then_inc(self.sem[e], 1)
            tok = (self.sem[e], self.cnt[e], e)
            self.pending[e] = False
        else:
            assert e == "pe"
            tok = (self.sem[e], self.cnt[e] + 1, e)
            self.pending[e] = True
        for b in reads:
            b.r.append(tok)
        for b in writes:
            b.w = tok
            b.r = []
        return tok

    def dma(self, e, out, in_, sembuf, reads=(), writes=(), group=False, **kw):
        if sembuf.dsem is None:
            sembuf.dsem = self.stack.enter_context(self.nc.semaphore("d%d_%s" % (self.nsem, sembuf.name)))
            self.nsem += 1
            self.dbufs.append(sembuf)
        self._hazards(e, reads, writes, group_sem=sembuf.dsem if group else None)
        if e == "pool":
            nd = 1
            for d_ in tuple(out.shape)[:-1]:
                nd *= int(d_)
            nd = max(4, (nd + 7) // 8)
            while self.swq and self.swq_sum + nd > 448:
                tok0, n0 = self.swq.pop(0)
                self.swq_sum -= n0
                self._wait(e, tok0)
        ins = self.eng[e].dma_start(out=out, in_=in_, **kw)
        sembuf.dcnt += 16
        ins.then_inc(sembuf.dsem, 16)
        tok = (sembuf.dsem, sembuf.dcnt, ("d", id(sembuf)))
        if e == "pool":
            self.swq.append((tok, nd))
            self.swq_sum += nd
        for b in reads:
            b.r.append(tok)
        for b in writes:
            b.w = tok
            b.r = []
        return tok

    def barrier(self, engines=None):
        toks = [(self.sem[k], self.cnt[k], k) for k in self.eng if self.cnt[k] > 0]
        toks += [(b.dsem, b.dcnt, ("d", id(b))) for b in self.dbufs]
        for e in (engines or self.eng):
            for t in toks:
                if t[2] == e:
                    continue
                self._wait(e, t)
        assert not self.pending["pe"]


def gate_pieces():
    pcs = []
    for n in range(16):
        lo, hi = 88 * n, 88 * n + 88
        for ci in range(NJ):
            r0, r1 = max(lo, 128 * ci), min(hi, 128 * ci + 128)
            if r0 >= r1:
                continue
            for co in range(NJ):
                c0, c1 = max(lo, 128 * co), min(hi, 128 * co + 128)
                if c0 >= c1:
                    continue
                d = ci - co + 1
                assert 0 <= d <= 2
                pcs.append((n, r0 - lo, r1 - lo, c0 - lo, c1 - lo, ci, co, d, r0 - 128 * ci, r1 - 128 * ci, c0 - 128 * co, c1 - 128 * co))
    return pcs


def gate_nbrs():
    nb = {co: set() for co in range(NJ)}
    for p in gate_pieces():
        nb[p[6]].add(p[5])
    return {co: sorted(v) for co, v in nb.items()}


VOFF = {}
_o = 0
for _name, _n in [("n1g0", 8), ("n2g0", 8), ("kvg", 8), ("n1g1", 8), ("n2g1", 8), ("convw", 44), ("convb", 11),
                  ("gab", 11), ("gxb", 11), ("lam", 11), ("gk", 1), ("gq", 1)]:
    VOFF[_name] = (_o, _n)
    _o += _n
NV = _o


def build(stop=None):
    nc = bass.Bass("TRN2", target_bir_lowering=False)
    dt_in = lambda name, shape: nc.dram_tensor(name, shape, F32, kind="ExternalInput").ap()
    xin = dt_in("xin", [NTILE * TT, D])
    ct_d = dt_in("ct", [128, 8])
    tflag_d = dt_in("tflag", [128, NTILE + 1])
    vecs_d = dt_in("vecs", [128, NV])
    ada_w = dt_in("ada_w", [2, D, 6 * D])
    ada_b = dt_in("ada_b", [2, 6 * D])
    kv_ada_w = dt_in("kv_ada_w", [D, 2 * D])
    kv_ada_b = dt_in("kv_ada_b", [1, 2 * D])
    w_in_d = dt_in("w_in", [D, 2 * W])
    ga_w_d = dt_in("ga_w", [16, 88, 88])
    gx_w_d = dt_in("gx_w", [16, 88, 88])
    w_out_d = dt_in("w_out", [W, D])
    w1_d = dt_in("mlp_w1", [2, D, DFF])
    w2_d = dt_in("mlp_w2", [2, DFF, D])
    kv_w_d = dt_in("kv_w", [D, 2 * D])
    wq_d = dt_in("wq", [D, D])
    wo_d = dt_in("wo", [D, D])
    biasT_d = dt_in("biasT", [128, 16 * 5 * 128])
    amask_d = dt_in("amask", [128, 5 * 128])
    ident_d = dt_in("ident", [128, 128])
    blk_d = dt_in("blk64", [128, 128])
    out_d = nc.dram_tensor("out", [4 * TT, D], F32, kind="ExternalOutput").ap()
    H = nc.dram_tensor("Hs", [NKV * TT, D], F32, kind="ExternalOutput" if stop else "Internal").ap()
    Gs = nc.dram_tensor("Gs", [4, 128, D], F32, kind="Internal").ap()

    with ExitStack() as top:
        P = Prog(nc, top)

        def SB(stack, name, shape, dt):
            return stack.enter_context(nc.sbuf_tensor("sb_" + name, shape, dt))

        def PS(stack, name, shape, dt):
            return stack.enter_context(nc.psum_tensor("ps_" + name, shape, dt))

        vecs = SB(top, "vecs", [128, NV], F32); b_vecs = Buf("vecs")
        tflag = SB(top, "tflag", [128, NTILE + 1], F32); b_tflag = Buf("tflag")
        identb = SB(top, "identb", [128, 128], BF16); b_identb = Buf("identb")
        Asc = SB(top, "Asc", [128, 5, 8], F32); Ash = SB(top, "Ash", [128, 5, 8], F32); b_mod = Buf("modvec")
        cch = SB(top, "cch", [128, 2, NJ], F32); b_cch = Buf("cch")
        nbias = SB(top, "nbias", [128, 2, NJ], F32); b_nbias = Buf("nbias")
        gqk = SB(top, "gqk", [128, 2], F32); b_gqk = Buf("gqk")

        P.dma("sp", vecs[:], vecs_d[:, :], b_vecs, writes=[b_vecs])
        P.dma("sp", tflag[:], tflag_d[:, :], b_tflag, writes=[b_tflag])
        P.dma("pool", identb[:], ident_d[:, :], b_identb, writes=[b_identb])

        def V(name):
            o, n = VOFF[name]
            return vecs[:, o:o + n]

        with ExitStack() as ph:
            ct = SB(ph, "ct", [128, 8], F32); b_ct = Buf("ct")
            t8a = SB(ph, "t8a", [128, 8], F32); t8b = SB(ph, "t8b", [128, 8], F32); b_t8 = Buf("t8")
            cact = SB(ph, "cact", [128, 8], F32); b_cact = Buf("cact")
            lc = SB(ph, "lc", [128, 8, 128], BF16); b_lc = Buf("lc")
            identf = SB(ph, "identf", [128, 128], F32); b_identf = Buf("identf")
            modrow = [SB(ph, "modrow%d" % i, [128, 6 * D], F32) for i in range(2)]
            kvrow = SB(ph, "kvrow", [128, 2 * D], F32)
            b_rows = [Buf("modrow0"), Buf("modrow1"), Buf("kvrow")]
            wch = [SB(ph, "wch%d" % i, [128, 8, 512], BF16) for i in range(3)]; b_wch = [Buf("wch%d" % i) for i in range(3)]
            dtmp = SB(ph, "dtmp", [128, 8, 128], F32); b_dtmp = Buf("dtmp")
            t11 = SB(ph, "t11", [128, NJ], F32); b_t11 = Buf("t11")
            pm = [PS(ph, "pm%d" % i, [128, 512], F32) for i in range(2)]; b_pm = [Buf("pm%d" % i) for i in range(2)]

            P.dma("sp", ct[:], ct_d[:, :], b_ct, writes=[b_ct])
            P.dma("sp", identf[:], ident_d[:, :], b_identf, writes=[b_identf])
            rows = [modrow[0], modrow[1], kvrow]
            P.dma("sp", modrow[0][:], ada_b[0:1, :].partition_broadcast(128), b_rows[0], writes=[b_rows[0]])
            P.dma("sp", modrow[1][:], ada_b[1:2, :].partition_broadcast(128), b_rows[1], writes=[b_rows[1]])
            P.dma("sp", kvrow[:], kv_ada_b[0:1, :].partition_broadcast(128), b_rows[2], writes=[b_rows[2]])
            P.op("act", lambda e: e.activation(out=t8a[:], in_=ct[:], func=AF.Exp, scale=-1.0), reads=[b_ct], writes=[b_t8])
            P.op("act", lambda e: e.activation(out=t8b[:], in_=t8a[:], func=AF.Ln, bias=1.0), reads=[b_t8], writes=[b_t8])
            P.op("act", lambda e: e.activation(out=t8a[:], in_=t8b[:], func=AF.Exp, scale=-1.0), reads=[b_t8], writes=[b_t8])
            P.op("dve", lambda e: e.tensor_tensor(out=cact[:], in0=t8a[:], in1=ct[:], op=ALU.mult), reads=[b_t8, b_ct], writes=[b_cact])
            P.op("dve", lambda e: e.tensor_copy(out=lc[:], in_=cact[:].unsqueeze(2).to_broadcast([128, 8, 128])), reads=[b_cact], writes=[b_lc])
            P.op("act", lambda e: e.activation(out=t11[:], in_=V("lam"), func=AF.Exp, scale=-1.0), reads=[b_vecs], writes=[b_t11])
            P.op("act", lambda e: e.activation(out=t11[:], in_=t11[:], func=AF.Ln, bias=1.0), reads=[b_t11], writes=[b_t11])
            P.op("dve", lambda e: e.tensor_scalar(out=cch[:, 0, :], in0=t11[:], scalar1=-8.0, scalar2=None, op0=ALU.mult), reads=[b_t11], writes=[b_cch])
            P.op("dve", lambda e: e.tensor_scalar(out=cch[:, 1, :], in0=t11[:], scalar1=-16.0, scalar2=None, op0=ALU.mult), reads=[b_t11], writes=[b_cch])
            P.op("dve", lambda e: e.tensor_scalar(out=nbias[:, 0, :], in0=V("gab"), scalar1=-1.0, scalar2=None, op0=ALU.mult), reads=[b_vecs], writes=[b_nbias])
            P.op("dve", lambda e: e.tensor_scalar(out=nbias[:, 1, :], in0=V("gxb"), scalar1=-1.0, scalar2=None, op0=ALU.mult), reads=[b_vecs], writes=[b_nbias])
            P.op("dve", lambda e: e.tensor_copy(out=gqk[:, 0:1], in_=V("gk")), reads=[b_vecs], writes=[b_gqk])
            P.op("dve", lambda e: e.tensor_scalar(out=gqk[:, 1:2], in0=V("gq"), scalar1=0.125, scalar2=None, op0=ALU.mult), reads=[b_vecs], writes=[b_gqk])

            srcs = [(ada_w[0], 6 * D, 0), (ada_w[1], 6 * D, 1), (kv_ada_w, 2 * D, 2)]
            it = 0
            for (wsrc, ncols, ri) in srcs:
                wv = wsrc.rearrange("(kc p) n -> p kc n", p=128)
                for j in range(ncols // 512):
                    wb_, bb_ = wch[it % 3], b_wch[it % 3]
                    pp, bp = pm[it % 2], b_pm[it % 2]
                    P.dma("pool", wb_[:], wv[:, :, j * 512:(j + 1) * 512], bb_, writes=[bb_])
                    for kc in range(8):
                        P.op("pe", lambda e, kc=kc, wb_=wb_, pp=pp: e.matmul(pp[:], lhsT=lc[:, kc, :], rhs=wb_[:, kc, :], start=(kc == 0), stop=(kc == 7)),
                             reads=[b_lc, bb_], writes=[bp], signal=(kc == 7))
                    rr = rows[ri]
                    P.op("dve", lambda e, rr=rr, pp=pp, j=j: e.tensor_tensor(out=rr[:, j * 512:(j + 1) * 512], in0=pp[:], in1=rr[:, j * 512:(j + 1) * 512], op=ALU.add),
                         reads=[bp, b_rows[ri]], writes=[b_rows[ri]])
                    it += 1

            specs = [(0, modrow[0], 0, 1, "n1g0"), (1, modrow[0], 3, 4, "n2g0"), (2, kvrow, 0, 1, "kvg"),
                     (3, modrow[1], 0, 1, "n1g1"), (4, modrow[1], 3, 4, "n2g1")]
            rbuf = {0: b_rows[0], 1: b_rows[0], 2: b_rows[2], 3: b_rows[1], 4: b_rows[1]}
            for (m, rr, sseg, cseg, gname) in specs:
                for (seg, dst) in [(sseg, Ash), (cseg, t8a)]:
                    P.op("dve", lambda e, rr=rr, seg=seg: e.tensor_tensor(out=dtmp[:], in0=rr[:, seg * D:(seg + 1) * D].rearrange("p (g k) -> p g k", k=128),
                                                                     in1=identf[:].unsqueeze(1).to_broadcast([128, 8, 128]), op=ALU.mult),
                         reads=[rbuf[m], b_identf], writes=[b_dtmp])
                    if dst is Ash:
                        P.op("dve", lambda e, m=m: e.tensor_reduce(out=Ash[:, m, :], in_=dtmp[:], axis=AX.X, op=ALU.add), reads=[b_dtmp], writes=[b_mod])
                    else:
                        P.op("dve", lambda e: e.tensor_reduce(out=t8a[:], in_=dtmp[:], axis=AX.X, op=ALU.add), reads=[b_dtmp], writes=[b_t8])
                P.op("dve", lambda e: e.tensor_scalar(out=t8b[:], in0=t8a[:], scalar1=1.0, scalar2=32.0, op0=ALU.add, op1=ALU.mult), reads=[b_t8], writes=[b_t8])
                P.op("dve", lambda e, m=m, gname=gname: e.tensor_tensor(out=Asc[:, m, :], in0=t8b[:], in1=V(gname), op=ALU.mult), reads=[b_t8, b_vecs], writes=[b_mod])
            for gi, (rr, seg, rb) in enumerate([(modrow[0], 2, b_rows[0]), (modrow[0], 5, b_rows[0]), (modrow[1], 2, b_rows[1]), (modrow[1], 5, b_rows[1])]):
                P.dma("sp", Gs[gi], rr[:, seg * D:(seg + 1) * D], rb, reads=[rb])
            P.barrier()

        def load_w(stack, name, src_view, shape, nsplit):
            t = SB(stack, name, shape, BF16)
            b = Buf(name)
            a = shape[1]
            step = (a + nsplit - 1) // nsplit
            for s0 in range(0, a, step):
                s1 = min(a, s0 + step)
                P.dma("pool", t[:, s0:s1, :], src_view[:, s0:s1, :], b, writes=[b], group=True)
            return t, b

        def load_w_cols(stack, name, src_view, shape, splits):
            t = SB(stack, name, shape, BF16)
            blocks = []
            for (c0, c1) in splits:
                b = Buf("%s_%d" % (name, c0))
                P.dma("pool", t[:, :, c0:c1], src_view[:, :, c0:c1], b, writes=[b])
                blocks.append((c0, c1, b))

            def bufs(c0, c1):
                return [b for (a0, a1, b) in blocks if a0 < c1 and c0 < a1]
            return t, bufs

        def norm_T(xt, b_xt, m, S):
            norm_a(xt, b_xt, S)
            norm_b(m, S)

        def norm_a(xt, b_xt, S):
            bx = b_xt if isinstance(b_xt, list) else [b_xt] * 4
            bxn = S["b_xn"] if isinstance(S["b_xn"], list) else [S["b_xn"]]
            bss = S["b_ss4"]
            for s in range(4):
                P.op("act", lambda e, s=s: e.activation(out=S["junk"][:], in_=xt[:, s, :], func=AF.Square, accum_out=S["ss"][:, s:s + 1]), reads=[bx[s]], writes=[S["b_junk"], bss[s]])
            for s in range(4):
                P.op("act", lambda e, s=s: e.activation(out=S["ss"][:, 4 + s:5 + s], in_=S["ss"][:, s:s + 1], func=AF.Ln, bias=S["epsb"][:, 0:1]), reads=[bss[s], S["b_epsb"]], writes=[bss[s]])
                P.op("act", lambda e, s=s: e.activation(out=S["ss"][:, 8 + s:9 + s], in_=S["ss"][:, 4 + s:5 + s], func=AF.Exp, scale=-0.5), reads=[bss[s]], writes=[bss[s]])
                P.op("dve", lambda e, s=s: e.tensor_scalar(out=S["xn"][:, s, :], in0=xt[:, s, :], scalar1=S["ss"][:, 8 + s:9 + s], scalar2=None, op0=ALU.mult),
                     reads=[bx[s], bss[s]], writes=bxn)

        def norm_b(m, S):
            for fc in range(8):
                pT, bT = S["pT"][fc % 2], S["b_pT"][fc % 2]
                for s in range(4):
                    P.op("pe", lambda e, s=s, fc=fc, pT=pT: e.transpose(out=pT[:, s * 128:(s + 1) * 128], in_=S["xn"][:, s, fc * 128:(fc + 1) * 128], identity=identb[:]),
                         reads=(S["b_xn"] if isinstance(S["b_xn"], list) else [S["b_xn"]]) + [b_identb], writes=[bT], signal=(s == 3))
                P.op("act", lambda e, fc=fc, pT=pT: e.activation(out=S["hnT"][:, fc, :], in_=pT[:, 0:TT], func=AF.Identity, scale=Asc[:, m, fc:fc + 1], bias=Ash[:, m, fc:fc + 1]),
                     reads=[bT, b_mod], writes=[S["b_hnT"]])

        def alloc_norm(stack, pfx):
            S = {}
            S["junk"] = SB(stack, pfx + "junk", [128, D], BF16); S["b_junk"] = Buf("junk")
            S["ss"] = SB(stack, pfx + "ss", [128, 12], F32); S["b_ss"] = Buf("ss"); S["b_ss4"] = [Buf("ss%d" % i) for i in range(4)]
            S["epsb"] = SB(stack, pfx + "epsb", [128, 2], F32); S["b_epsb"] = Buf("epsb")
            S["hnT"] = SB(stack, pfx + "hnT", [128, 8, TT], BF16); S["b_hnT"] = Buf("hnT")
            S["pT"] = [PS(stack, pfx + "pT%d" % i, [128, 2 * TT], BF16) for i in range(2)]; S["b_pT"] = [Buf("pT0"), Buf("pT1")]
            P.op("dve", lambda e: e.memset(S["epsb"][:, 0:1], 1024.0 * EPS), writes=[S["b_epsb"]])
            P.op("dve", lambda e: e.memset(S["epsb"][:, 1:2], EPS), writes=[S["b_epsb"]])
            return S

        def load_g(stack, name, gi):
            g = SB(stack, name, [128, D], F32); b = Buf(name)
            P.dma("sp", g[:], Gs[gi], b, writes=[b])
            return g, b

        def resid_out(xt, b_xt, s, nh, po, bpo, gi, S):
            P.op("dve", lambda e: e.tensor_tensor(out=S["rtmp"][:], in0=po[:], in1=S["g"][:, nh * 512:(nh + 1) * 512], op=ALU.mult),
                 reads=[bpo, S["b_g"]], writes=[S["b_rtmp"]])
            bx = b_xt[s] if isinstance(b_xt, list) else b_xt
            P.op("dve", lambda e: e.tensor_tensor(out=xt[:, s, nh * 512:(nh + 1) * 512], in0=S["rtmp"][:], in1=xt[:, s, nh * 512:(nh + 1) * 512], op=ALU.add),
                 reads=[S["b_rtmp"], bx], writes=[bx])

        def load_rows(xt, b_xt, src):
            for s in range(4):
                P.dma("sp", xt[:, s, :], src[s * 128:(s + 1) * 128, :], b_xt[s], writes=[b_xt[s]])

        def store_rows(xt, b_xt, dst, s):
            P.dma("sp", dst[s * 128:(s + 1) * 128, :], xt[:, s, :], b_xt[s], reads=[b_xt[s]])

        nbrs = gate_nbrs()
        with ExitStack() as ph:
            w_in_sb, w_in_bufs = load_w_cols(ph, "w_in_sb", w_in_d.rearrange("(kc p) n -> p kc n", p=128), [128, 8, 2 * W],
                                             [(W, W + 512), (W + 512, 2 * W), (0, 704), (704, W)])
            wg = SB(ph, "wg", [128, 2, NJ * 3, 128], BF16); b_wg = Buf("wg")
            P.op("dve", lambda e: e.memset(wg[:], 0.0), writes=[b_wg])
            for gidx, gsrc in enumerate([ga_w_d, gx_w_d]):
                for (n, r0, r1, c0, c1, ci, co, d, p0, p1, q0, q1) in gate_pieces():
                    P.dma("pool", wg[p0:p1, gidx, co * 3 + d, q0:q1], gsrc[n, r0:r1, c0:c1], b_wg, writes=[b_wg], group=True)
            w_out_sb, b_w_out = load_w(ph, "w_out_sb", w_out_d.rearrange("(jc p) n -> p jc n", p=128), [128, NJ, D], 2)

            S = alloc_norm(ph, "p1")
            S["g"], S["b_g"] = load_g(ph, "p1g", 0)
            xts = [SB(ph, "p1xt%d" % i, [128, 4, D], F32) for i in range(2)]; b_xts = [Buf("xt0"), Buf("xt1")]
            yb = SB(ph, "p1yb", [128, NJ, TT], BF16); b_yb = Buf("yb")
            NXR = 2
            xrb = [SB(ph, "p1xrb%d" % i, [128, TT + 3], F32) for i in range(NXR)]; b_xrb = [Buf("xrb%d" % i) for i in range(NXR)]
            halo = SB(ph, "p1halo", [128, NJ, 3], F32); b_halo = [Buf("halo%d" % j) for j in range(NJ)]
            xc = SB(ph, "p1xc", [128, NJ, TT], F32); b_xc = [Buf("xc%d" % j) for j in range(NJ)]
            xcb = SB(ph, "p1xcb", [128, NJ, TT], BF16); b_xcb = [Buf("xcb%d" % j) for j in range(NJ)]
            S["xn"] = xcb[:, 0:8, :].rearrange("p a c -> p (a c)").rearrange("p (s d) -> p s d", s=4)
            S["b_xn"] = b_xcb[0:8]
            state = SB(ph, "p1state", [128, NJ], F32); b_state = Buf("state")
            NTMP = 3
            tA = [SB(ph, "p1tA%d" % i, [128, TT], F32) for i in range(NTMP)]; b_tA = [Buf("tA%d" % i) for i in range(NTMP)]
            tQ = [SB(ph, "p1tQ%d" % i, [128, TT], F32) for i in range(NTMP)]; b_tQ = [Buf("tQ%d" % i) for i in range(NTMP)]
            tG, b_tG = tA, b_tA
            NGRP = 6
            Rst = SB(ph, "p1R", [128, NGRP, TT], F32); b_R = [Buf("R%d" % i) for i in range(NGRP)]
            S["rtmp"] = tA[0]; S["b_rtmp"] = b_tA[0]
            pz = [PS(ph, "p1pz%d" % i, [128, TT], F32) for i in range(6)]; b_pz = [Buf("pz%d" % i) for i in range(6)]
            pzi = [0]

            def nextpz():
                i = pzi[0] % 6
                pzi[0] += 1
                return pz[i], b_pz[i]

            P.op("dve", lambda e: e.memset(halo[:], 0.0), writes=b_halo)
            P.op("dve", lambda e: e.memset(state[:], 0.0), writes=[b_state])
            cw = V("convw")
            cb = V("convb")

            def load_x(t):
                P.dma("sp", xts[t % 2][:], xin[t * TT:(t + 1) * TT, :].rearrange("(s p) d -> p s d", p=128), b_xts[t % 2], writes=[b_xts[t % 2]])

            ntile_p1 = NTILE if stop != "p1a" else 5
            hnT, b_hnT = S["hnT"], S["b_hnT"]
            gab, gxb = V("gab"), V("gxb")

            pending_cast = []

            def cast_chunk(j):
                P.op("act", lambda e: e.activation(out=xcb[:, j, :], in_=xc[:, j, :], func=AF.Copy), reads=[b_xc[j]], writes=[b_xcb[j]])

            def conv_part(t, j0, j1):
                def halo_in(j):
                    xr, bxr = xrb[j % NXR], b_xrb[j % NXR]
                    P.op("pool", lambda e: e.tensor_copy(out=xr[:, 0:3], in_=halo[:, j, :]), reads=[b_halo[j]], writes=[bxr])

                if j0 == 0:
                    halo_in(0)
                for j in range(j0, j1):
                    oc = NJ + j
                    pp, bp = nextpz()
                    for kc in range(8):
                        P.op("pe", lambda e, kc=kc, oc=oc, pp=pp: e.matmul(pp[:], lhsT=w_in_sb[:, kc, oc * 128:(oc + 1) * 128], rhs=hnT[:, kc, :], start=(kc == 0), stop=(kc == 7)),
                             reads=w_in_bufs(oc * 128, oc * 128 + 128) + [b_hnT], writes=[bp], signal=(kc == 7))
                    xr, bxr = xrb[j % NXR], b_xrb[j % NXR]
                    P.op("act", lambda e, xr=xr, pp=pp: e.activation(out=xr[:, 3:TT + 3], in_=pp[:], func=AF.Copy), reads=[bp], writes=[bxr])
                    P.op("act", lambda e, pp=pp, j=j: e.activation(out=xc[:, j, :], in_=pp[:], func=AF.Identity, scale=cw[:, j * 4 + 3:j * 4 + 4], bias=cb[:, j:j + 1]),
                         reads=[bp, b_vecs], writes=[b_xc[j]])
                    if j + 1 < NJ:
                        halo_in(j + 1)
                    while pending_cast:
                        cast_chunk(pending_cast.pop(0))
                    for k in range(3):
                        P.op("dve", lambda e, xr=xr, j=j, k=k: e.scalar_tensor_tensor(out=xc[:, j, :], in0=xr[:, k:k + TT], scalar=cw[:, j * 4 + k:j * 4 + k + 1], in1=xc[:, j, :],
                                                                                  op0=ALU.mult, op1=ALU.add),
                             reads=[bxr, b_xc[j], b_vecs], writes=[b_xc[j]])
                    P.op("pool", lambda e, xr=xr, j=j: e.tensor_scalar(out=halo[:, j, :], in0=xr[:, TT:TT + 3], scalar1=tflag[:, t + 1:t + 2], scalar2=None, op0=ALU.mult),
                         reads=[bxr, b_tflag], writes=[b_halo[j]])
                    pending_cast.append(j)
                if j1 == NJ:
                    while pending_cast:
                        cast_chunk(pending_cast.pop(0))

            def gates_part(t):
                full = t >= 3
                if full:
                    for j in range(NJ):
                        pp, bp = nextpz()
                        for kc in range(8):
                            P.op("pe", lambda e, kc=kc, j=j, pp=pp: e.matmul(pp[:], lhsT=w_in_sb[:, kc, j * 128:(j + 1) * 128], rhs=hnT[:, kc, :], start=(kc == 0), stop=(kc == 7)),
                                 reads=w_in_bufs(j * 128, j * 128 + 128) + [b_hnT], writes=[bp], signal=(kc == 7))
                        P.op("act", lambda e, j=j, pp=pp: e.activation(out=yb[:, j, :], in_=pp[:], func=AF.Gelu_apprx_tanh), reads=[bp], writes=[b_yb])
                for gno, grp in enumerate([list(range(0, NGRP)), list(range(NGRP, NJ))]):
                    for gi_, co in enumerate(grp):
                        pa, bpa = nextpz()
                        px, bpx = nextpz()
                        cis = nbrs[co]
                        for gidx, (pg, bpg) in enumerate([(pa, bpa), (px, bpx)]):
                            for n_, ci in enumerate(cis):
                                d = ci - co + 1
                                P.op("pe", lambda e, pg=pg, gidx=gidx, d=d, ci=ci, n_=n_, co=co, cis=cis: e.matmul(pg[:], lhsT=wg[:, gidx, co * 3 + d, :], rhs=xcb[:, ci, :],
                                                                                                         start=(n_ == 0), stop=(n_ == len(cis) - 1)),
                                     reads=[b_wg, b_xcb[ci]], writes=[bpg], signal=(n_ == len(cis) - 1))
                        I_, bI = tQ[co % NTMP], b_tQ[co % NTMP]
                        P.op("act", lambda e, pa=pa, gi_=gi_, co=co: e.activation(out=Rst[:, gi_, :], in_=pa[:], func=AF.Sigmoid, bias=gab[:, co:co + 1]), reads=[bpa, b_vecs], writes=[b_R[gi_]])
                        P.op("act", lambda e, px=px, I_=I_, co=co: e.activation(out=I_[:], in_=px[:], func=AF.Sigmoid, bias=gxb[:, co:co + 1]), reads=[bpx, b_vecs], writes=[bI])
                        P.op("pool", lambda e, I_=I_, co=co: e.tensor_tensor(out=xc[:, co, :], in0=I_[:], in1=xc[:, co, :], op=ALU.mult), reads=[bI, b_xc[co]], writes=[b_xc[co]])

                    def stage_b(co, gi_, k_):
                        A_, bA = tA[k_], b_tA[k_]
                        Q_, bQ = tQ[k_], b_tQ[k_]
                        return [
                            lambda: P.op("act", lambda e: e.activation(out=A_[:], in_=Rst[:, gi_, :], func=AF.Exp, scale=cch[:, 0, co:co + 1]), reads=[b_R[gi_], b_cch], writes=[bA]),
                            lambda: P.op("pool", lambda e: e.tensor_tensor(out=Q_[:], in0=A_[:], in1=A_[:], op=ALU.mult), reads=[bA], writes=[bQ]),
                            lambda: P.op("act", lambda e: e.activation(out=Q_[:], in_=Q_[:], func=AF.Ln, scale=-1.0, bias=1.000001), reads=[bQ], writes=[bQ]),
                            lambda: P.op("act", lambda e: e.activation(out=Q_[:], in_=Q_[:], func=AF.Exp, scale=0.5), reads=[bQ], writes=[bQ]),
                            lambda: P.op("pool", lambda e: e.tensor_tensor(out=Q_[:], in0=Q_[:], in1=xc[:, co, :], op=ALU.mult), reads=[bQ, b_xc[co]], writes=[bQ]),
                            lambda: P.op("dve", lambda e: e.tensor_tensor_scan(out=xc[:, co, :], data0=A_[:], data1=Q_[:], initial=state[:, co:co + 1], op0=ALU.mult, op1=ALU.add),
                                         reads=[bA, bQ, b_state], writes=[b_xc[co]]),
                        ]

                    if gno == 1 and t + 1 < ntile_p1:
                        norm_a(xts[(t + 1) % 2], b_xts[(t + 1) % 2], S)
                    for c3 in range(0, len(grp), NTMP):
                        chains = [stage_b(co, c3 + k_, k_) for k_, co in enumerate(grp[c3:c3 + NTMP])]
                        for si in range(len(chains[0])):
                            for ch in chains:
                                ch[si]()
                P.op("dve", lambda e: e.tensor_scalar(out=state[:], in0=xc[:, :, TT - 1], scalar1=tflag[:, t + 1:t + 2], scalar2=None, op0=ALU.mult),
                     reads=b_xc + [b_tflag], writes=[b_state])

            def tail_a(t):
                for j in range(NJ):
                    P.op("dve", lambda e, j=j: e.tensor_tensor(out=yb[:, j, :], in0=yb[:, j, :], in1=xc[:, j, :], op=ALU.mult), reads=[b_yb, b_xc[j]], writes=[b_yb])

            def tail_b(t):
                xt, b_xt = xts[t % 2], b_xts[t % 2]
                for s_ in range(4):
                    for nh in range(2):
                        pp, bp = nextpz()
                        for jc in range(NJ):
                            P.op("pe", lambda e, jc=jc, s_=s_, nh=nh, pp=pp: e.matmul(pp[:], lhsT=yb[:, jc, s_ * 128:(s_ + 1) * 128], rhs=w_out_sb[:, jc, nh * 512:(nh + 1) * 512],
                                                                                  start=(jc == 0), stop=(jc == NJ - 1)),
                                 reads=[b_yb, b_w_out], writes=[bp], signal=(jc == NJ - 1))
                        resid_out(xt, b_xt, s_, nh, pp, bp, 0, S)
                P.dma("sp", H[(t - 3) * TT:(t - 2) * TT, :].rearrange("(s p) d -> p s d", p=128), xt[:], b_xt, reads=[b_xt])

            load_x(0)
            load_x(1)
            norm_a(xts[0], b_xts[0], S)
            norm_b(0, S)
            for t in range(ntile_p1):
                prev_full = (t - 1) >= 3
                if prev_full:
                    tail_a(t - 1)
                conv_part(t, 0, 6)
                if prev_full:
                    tail_b(t - 1)
                    if t + 1 < ntile_p1:
                        load_x(t + 1)
                elif 1 <= t and t + 1 < ntile_p1:
                    load_x(t + 1)
                conv_part(t, 6, NJ)
                gates_part(t)
                if t + 1 < ntile_p1:
                    norm_b(0, S)
            tail_a(ntile_p1 - 1)
            tail_b(ntile_p1 - 1)
            P.barrier()
        if stop in ("p1", "p1a"):
            return nc

        def mlp_phase(l, tiles, m, gi, final):
            with ExitStack() as ph:
                w1_sb, w1_bufs = load_w_cols(ph, "w1_sb%d" % l, w1_d[l].rearrange("(kc p) n -> p kc n", p=128), [128, 8, DFF],
                                             [(i * 512, (i + 1) * 512) for i in range(8)])
                w2_sb, b_w2 = load_w(ph, "w2_sb%d" % l, w2_d[l].rearrange("(fc p) n -> p fc n", p=128), [128, 32, D], 8)
                S = alloc_norm(ph, "m%d" % l)
                S["g"], S["b_g"] = load_g(ph, "m%dg" % l, gi)
                xt = SB(ph, "m%dxt" % l, [128, 4, D], F32); b_xt = [Buf("xt%d" % i) for i in range(4)]
                hid = SB(ph, "m%dhid" % l, [128, 32, TT], BF16); b_hid = Buf("hid")
                S["xn"] = hid[:, 0:8, :].rearrange("p a c -> p (a c)").rearrange("p (s d) -> p s d", s=4)
                S["b_xn"] = b_hid
                rl = [SB(ph, "m%drl%d" % (l, i), [128, TT], BF16) for i in range(3)]; b_rl = [Buf("rl%d" % i) for i in range(3)]
                S["rtmp"] = SB(ph, "m%drtmp" % l, [128, 512], F32); S["b_rtmp"] = Buf("rtmp")
                pz = [PS(ph, "m%dpz%d" % (l, i), [128, TT], F32) for i in range(6)]; b_pz = [Buf("pz%d" % i) for i in range(6)]
                pzi = [0]

                def nextpz():
                    i = pzi[0] % 6
                    pzi[0] += 1
                    return pz[i], b_pz[i]

                for t in tiles:
                    hrow = H[(t - 3) * TT:(t - 2) * TT, :]
                    load_rows(xt, b_xt, hrow)
                    norm_T(xt, b_xt, m, S)
                    hnT, b_hnT = S["hnT"], S["b_hnT"]
                    for fc in range(32):
                        pp, bp = nextpz()
                        for kc in range(8):
                            P.op("pe", lambda e, kc=kc, fc=fc, pp=pp: e.matmul(pp[:], lhsT=w1_sb[:, kc, fc * 128:(fc + 1) * 128], rhs=hnT[:, kc, :], start=(kc == 0), stop=(kc == 7)),
                                 reads=w1_bufs(fc * 128, fc * 128 + 128) + [b_hnT], writes=[bp], signal=(kc == 7))
                        r_, br = rl[fc % 3], b_rl[fc % 3]
                        P.op("act", lambda e, r_=r_, pp=pp: e.activation(out=r_[:], in_=pp[:], func=AF.Relu), reads=[bp], writes=[br])
                        P.op("dve", lambda e, r_=r_, fc=fc: e.tensor_tensor(out=hid[:, fc, :], in0=r_[:], in1=r_[:], op=ALU.mult), reads=[br], writes=[b_hid])
                    for s in range(4):
                        for nh in range(2):
                            pp, bp = nextpz()
                            for fc in range(32):
                                P.op("pe", lambda e, fc=fc, s=s, nh=nh, pp=pp: e.matmul(pp[:], lhsT=hid[:, fc, s * 128:(s + 1) * 128], rhs=w2_sb[:, fc, nh * 512:(nh + 1) * 512],
                                                                                    start=(fc == 0), stop=(fc == 31)),
                                     reads=[b_hid, b_w2], writes=[bp], signal=(fc == 31))
                            resid_out(xt, b_xt, s, nh, pp, bp, gi, S)
                        store_rows(xt, b_xt, out_d[(t - 4) * TT:(t - 3) * TT, :] if final else hrow, s)
                P.barrier()

        mlp_phase(0, range(3, 8), 1, 1, False)
        if stop == "p2":
            return nc

        with ExitStack() as ph:
            blkb = SB(ph, "blkb", [128, 128], BF16); b_blk = Buf("blkb")
            P.dma("pool", blkb[:], blk_d[:, :], b_blk, writes=[b_blk])
            expB = SB(ph, "expB", [128, 16, 640], BF16); b_expB = Buf("expB")
            KT = SB(ph, "KT", [128, 8, NKV * TT], BF16); b_KT = Buf("KT")
            Va = SB(ph, "Va", [128, NKV * 4, 16, 65], BF16); b_Va = Buf("Va")
            S = alloc_norm(ph, "p3")
            xt = SB(ph, "p3xt", [128, 4, D], F32); b_xt = [Buf("xt%d" % i) for i in range(4)]
            sq = [SB(ph, "p3sq%d" % i, [128, TT], BF16) for i in range(3)]; b_sq = [Buf("sq%d" % i) for i in range(3)]
            rs = [SB(ph, "p3rs%d" % i, [128, TT], F32) for i in range(3)]; b_rs = [Buf("rs%d" % i) for i in range(3)]
            pSt = [PS(ph, "p3pS%d" % i, [128, 1024], F32) for i in range(2)]
            pOt = [PS(ph, "p3pO%d" % i, [128, 4, 128], F32) for i in range(2)]
            b_bank = [Buf("bank%d" % i) for i in range(6)]
            pS = pSt; b_pS = [[b_bank[0], b_bank[1]], [b_bank[2], b_bank[3]]]
            pO = pOt; b_pO = [b_bank[4], b_bank[5]]
            pz = [pSt[0][:, 0:512], pSt[0][:, 512:1024], pSt[1][:, 0:512], pSt[1][:, 512:1024],
                  pOt[0][:].rearrange("p a b -> p (a b)"), pOt[1][:].rearrange("p a b -> p (a b)")]
            pzi = [0]

            def nextpz():
                i = pzi[0] % 6
                pzi[0] += 1
                return pz[i], b_bank[i]

            def proj_headnorm(w_sb, w_bufs, hnT, b_hnT, dstT, b_dst, col0, gcol):
                for g0 in range(0, 8, 3):
                    fcs = list(range(g0, min(8, g0 + 3)))
                    pps, pms = [], []
                    for fc in fcs:
                        pp, bp = nextpz()
                        for kc in range(8):
                            P.op("pe", lambda e, kc=kc, fc=fc, pp=pp: e.matmul(pp[:], lhsT=w_sb[:, kc, fc * 128:(fc + 1) * 128], rhs=hnT[:, kc, :], start=(kc == 0), stop=(kc == 7)),
                                 reads=w_bufs(fc * 128, fc * 128 + 128) + [b_hnT], writes=[bp], signal=(kc == 7))
                        pps.append((pp, bp))
                    for i, fc in enumerate(fcs):
                        pp, bp = pps[i]
                        P.op("act", lambda e, i=i, pp=pp: e.activation(out=sq[i][:], in_=pp[:], func=AF.Square), reads=[bp], writes=[b_sq[i]])
                    for i, fc in enumerate(fcs):
                        pm_, bpm = nextpz()
                        P.op("pe", lambda e, i=i, pm_=pm_: e.matmul(pm_[:], lhsT=blkb[:], rhs=sq[i][:], start=True, stop=True), reads=[b_blk, b_sq[i]], writes=[bpm])
                        pms.append((pm_, bpm))
                    for i, fc in enumerate(fcs):
                        pm_, bpm = pms[i]
                        P.op("act", lambda e, i=i, pm_=pm_: e.activation(out=rs[i][:], in_=pm_[:], func=AF.Ln, bias=S["epsb"][:, 1:2]), reads=[bpm, S["b_epsb"]], writes=[b_rs[i]])
                    for i, fc in enumerate(fcs):
                        P.op("act", lambda e, i=i: e.activation(out=rs[i][:], in_=rs[i][:], func=AF.Exp, scale=-0.5), reads=[b_rs[i]], writes=[b_rs[i]])
                    for i, fc in enumerate(fcs):
                        pp, bp = pps[i]
                        P.op("dve", lambda e, i=i, fc=fc, pp=pp: e.scalar_tensor_tensor(out=dstT[:, fc, col0:col0 + TT], in0=pp[:], scalar=gqk[:, gcol:gcol + 1], in1=rs[i][:],
                                                                                 op0=ALU.mult, op1=ALU.mult),
                             reads=[bp, b_gqk, b_rs[i]], writes=[b_dst])

            with ExitStack() as p1s:
                kvw_sb, kvw_bufs = load_w_cols(p1s, "kvw_sb", kv_w_d.rearrange("(kc p) n -> p kc n", p=128), [128, 8, 2 * D],
                                               [(i * 512, (i + 1) * 512) for i in range(4)])
                amask = SB(p1s, "amask", [128, 640], F32); b_amask = Buf("amask")
                P.dma("sp", amask[:], amask_d[:, :], b_amask, writes=[b_amask])
                btmp = [SB(p1s, "p3btmp%d" % i, [128, 640], F32) for i in range(2)]; b_btmp = [Buf("btmp0"), Buf("btmp1")]
                xn = SB(p1s, "p3xn", [128, 4, D], BF16); S["xn"] = xn; S["b_xn"] = Buf("xn")
                biasv = biasT_d.rearrange("p (h x) -> p h x", h=16)

                def build_expB():
                    for h in range(16):
                        bt, bbt = btmp[h % 2], b_btmp[h % 2]
                        P.dma("pool", bt[:], biasv[:, h, :], bbt, writes=[bbt])
                        P.op("act", lambda e, bt=bt: e.activation(out=bt[:], in_=bt[:], func=AF.Exp), reads=[bbt], writes=[bbt])
                        P.op("dve", lambda e, h=h, bt=bt: e.tensor_tensor(out=expB[:, h, :], in0=bt[:], in1=amask[:], op=ALU.mult), reads=[bbt, b_amask], writes=[b_expB])
                P.op("dve", lambda e: e.memset(Va[:, :, :, 64:65], 1.0), writes=[b_Va])
                P.op("dve", lambda e: e.tensor_scalar(out=Va[:, 0:4, :, 64:65], in0=Va[:, 0:4, :, 64:65], scalar1=tflag[:, 4:5], scalar2=None, op0=ALU.mult),
                     reads=[b_tflag, b_Va], writes=[b_Va])
                for t in range(3, 8):
                    kt0 = t - 3
                    hrow = H[kt0 * TT:(kt0 + 1) * TT, :]
                    load_rows(xt, b_xt, hrow)
                    norm_T(xt, b_xt, 2, S)
                    hnT, b_hnT = S["hnT"], S["b_hnT"]
                    proj_headnorm(kvw_sb, kvw_bufs, hnT, b_hnT, KT, b_KT, kt0 * TT, 0)
                    for s in range(4):
                        for nh in range(2):
                            pp, bp = nextpz()
                            for kc in range(8):
                                P.op("pe", lambda e, kc=kc, s=s, nh=nh, pp=pp: e.matmul(pp[:], lhsT=hnT[:, kc, s * 128:(s + 1) * 128], rhs=kvw_sb[:, kc, D + nh * 512:D + (nh + 1) * 512],
                                                                                    start=(kc == 0), stop=(kc == 7)),
                                     reads=kvw_bufs(D + nh * 512, D + (nh + 1) * 512) + [b_hnT], writes=[bp], signal=(kc == 7))
                            if t == 3:
                                P.op("act", lambda e, s=s, nh=nh, pp=pp, kt0=kt0: e.activation(out=Va[:, kt0 * 4 + s, nh * 8:(nh + 1) * 8, 0:64], in_=pp[:].rearrange("p (h d) -> p h d", d=64),
                                                                                           func=AF.Identity, scale=tflag[:, 4:5]),
                                     reads=[bp, b_tflag], writes=[b_Va])
                            else:
                                P.op("act", lambda e, s=s, nh=nh, pp=pp, kt0=kt0: e.activation(out=Va[:, kt0 * 4 + s, nh * 8:(nh + 1) * 8, 0:64], in_=pp[:].rearrange("p (h d) -> p h d", d=64),
                                                                                           func=AF.Copy),
                                     reads=[bp], writes=[b_Va])
                    if t == 3:
                        build_expB()
                P.barrier()
            if stop == "p3a":
                return nc
            with ExitStack() as p2s:
                wq_sb, wq_bufs = load_w_cols(p2s, "wq_sb", wq_d.rearrange("(kc p) n -> p kc n", p=128), [128, 8, D], [(0, 512), (512, 1024)])
                wo_sb, b_wo = load_w(p2s, "wo_sb", wo_d.rearrange("(kc p) n -> p kc n", p=128), [128, 8, D], 2)
                S["g"], S["b_g"] = load_g(p2s, "p3g", 2)
                QT = SB(p2s, "QT", [128, 8, TT], BF16); b_QT = Buf("QT")
                S["rtmp"] = SB(p2s, "p3rtmp", [128, 512], F32); S["b_rtmp"] = Buf("rtmp")
                Eb = [SB(p2s, "p3E%d" % i, [128, 640], BF16) for i in range(3)]; b_E = [Buf("E%d" % i) for i in range(3)]
                Pb = [SB(p2s, "p3P%d" % i, [128, 640], BF16) for i in range(3)]; b_P = [Buf("P%d" % i) for i in range(3)]
                rec = SB(p2s, "p3rec", [128, 2, 4], F32); b_rec = [Buf("rec0"), Buf("rec1")]
                attn = SB(p2s, "p3attn", [128, D], BF16); b_attn = Buf("attn")
                attnT = SB(p2s, "p3attnT", [128, 8, TT], BF16); b_attnT = Buf("attnT")
                S["xn"] = attnT[:].rearrange("p a c -> p (a c)").rearrange("p (s d) -> p s d", s=4); S["b_xn"] = b_attnT
                for t in range(4, 8):
                    kt0 = t - 3
                    hrow = H[kt0 * TT:(kt0 + 1) * TT, :]
                    load_rows(xt, b_xt, hrow)
                    norm_T(xt, b_xt, 3, S)
                    hnT, b_hnT = S["hnT"], S["b_hnT"]
                    proj_headnorm(wq_sb, wq_bufs, hnT, b_hnT, QT, b_QT, 0, 1)
                    items = [(qg, h) for qg in range(4) for h in range(16)]

                    def emit_S(idx):
                        qg, h = items[idx]
                        G = kt0 * 4 + qg
                        fc, hf = h // 2, h % 2
                        pS_, bpS = pS[idx % 2], b_pS[idx % 2]
                        for kt in range(5):
                            P.op("pe", lambda e, kt=kt: e.matmul(pS_[:, kt * 128:(kt + 1) * 128],
                                                                 lhsT=KT[hf * 64:(hf + 1) * 64, fc, (G - 4 + kt) * 128:(G - 3 + kt) * 128],
                                                                 rhs=QT[hf * 64:(hf + 1) * 64, fc, qg * 128:(qg + 1) * 128], start=True, stop=True),
                                 reads=[b_KT, b_QT], writes=bpS, signal=(kt == 4))

                    emit_S(0)
                    for idx, (qg, h) in enumerate(items):
                        G = kt0 * 4 + qg
                        h4, hh = h // 4, h % 4
                        if idx + 1 < len(items):
                            emit_S(idx + 1)
                        pS_, bpS = pS[idx % 2], b_pS[idx % 2]
                        pO_, bpO = pO[h4 % 2], b_pO[h4 % 2]
                        E_, bE = Eb[idx % 3], b_E[idx % 3]
                        P_, bP = Pb[idx % 3], b_P[idx % 3]
                        P.op("act", lambda e, E_=E_, pS_=pS_: e.activation(out=E_[:], in_=pS_[:, 0:640], func=AF.Exp), reads=bpS, writes=[bE])
                        P.op("dve", lambda e, E_=E_, P_=P_, h=h: e.tensor_tensor(out=P_[:], in0=E_[:], in1=expB[:, h, :], op=ALU.mult), reads=[bE, b_expB], writes=[bP])
                        for kt in range(5):
                            P.op("pe", lambda e, kt=kt, hh=hh, h=h, G=G, P_=P_, pO_=pO_: e.matmul(pO_[:, hh, 0:65], lhsT=P_[:, kt * 128:(kt + 1) * 128], rhs=Va[:, G - 4 + kt, h, :],
                                                                                           start=(kt == 0), stop=(kt == 4)),
                                 reads=[bP, b_Va], writes=[bpO], signal=(kt == 4))
                        if hh == 3:
                            r_ = h4 % 2
                            P.op("dve", lambda e, r_=r_, pO_=pO_: e.reciprocal(out=rec[:, r_, :], in_=pO_[:, :, 64]), reads=[bpO], writes=[b_rec[r_]])
                            P.op("dve", lambda e, r_=r_, pO_=pO_, h4=h4: e.tensor_tensor(out=attn[:, h4 * 256:(h4 + 1) * 256].rearrange("p (h d) -> p h d", d=64), in0=pO_[:, :, 0:64],
                                                                                    in1=rec[:, r_, :].unsqueeze(2).to_broadcast([128, 4, 64]), op=ALU.mult),
                                 reads=[bpO, b_rec[r_]], writes=[b_attn])
                        if h == 15:
                            for fc in range(8):
                                pT, bT = S["pT"][fc % 2], S["b_pT"][fc % 2]
                                P.op("pe", lambda e, fc=fc, pT=pT: e.transpose(out=pT[:, 0:128], in_=attn[:, fc * 128:(fc + 1) * 128], identity=identb[:]), reads=[b_attn, b_identb], writes=[bT])
                                P.op("act", lambda e, fc=fc, pT=pT, qg=qg: e.activation(out=attnT[:, fc, qg * 128:(qg + 1) * 128], in_=pT[:, 0:128], func=AF.Copy), reads=[bT], writes=[b_attnT])

                    for s in range(4):
                        for nh in range(2):
                            pp, bp = nextpz()
                            for kc in range(8):
                                P.op("pe", lambda e, kc=kc, s=s, nh=nh, pp=pp: e.matmul(pp[:], lhsT=attnT[:, kc, s * 128:(s + 1) * 128], rhs=wo_sb[:, kc, nh * 512:(nh + 1) * 512],
                                                                                    start=(kc == 0), stop=(kc == 7)),
                                     reads=[b_attnT, b_wo], writes=[bp], signal=(kc == 7))
                            resid_out(xt, b_xt, s, nh, pp, bp, 2, S)
                        store_rows(xt, b_xt, hrow, s)
                P.barrier()
        if stop == "p3":
            return nc

        mlp_phase(1, range(4, 8), 4, 3, True)
    return nc


def _host_consts():
    ident = np.eye(128, dtype=np.float32)
    blk = np.zeros((128, 128), np.float32)
    blk[:64, :64] = 1.0 / 64
    blk[64:, 64:] = 1.0 / 64
    kk = np.arange(128)[:, None, None]
    kt = np.arange(5)[None, :, None]
    q = np.arange(128)[None, None, :]
    kidx = kt * 128 + kk
    ck = kidx // 64
    cq = q // 64
    amask = ((ck >= cq) & (ck <= cq + 8)).astype(np.float32)
    bidx = np.minimum(640 + q - kidx, 256)
    bidx = np.maximum(bidx, 0)
    return ident, blk, amask.reshape(128, 640), bidx


def _pvec(v, n):
    return np.ascontiguousarray(np.asarray(v, np.float32).reshape(n, 128).T)


def make_in_maps(inputs):
    f = lambda k: np.ascontiguousarray(np.asarray(inputs[k], dtype=np.float32))
    x = f("x"); c = f("c")
    ident, blk, amask, bidx = _host_consts()
    rel_bias = f("rel_bias")[0]
    biasT = rel_bias[:, bidx]
    biasT = np.ascontiguousarray(biasT.transpose(1, 0, 2, 3).reshape(128, 16 * 640))
    conv_w = f("lru_conv_w")[0]
    convw = np.ascontiguousarray(conv_w.reshape(4, NJ, 128).transpose(2, 1, 0).reshape(128, NJ * 4))
    vec_parts = {
        "n1g0": _pvec(f("norm1_g")[0], 8), "n2g0": _pvec(f("norm2_g")[0], 8), "kvg": _pvec(f("kv_norm_g"), 8),
        "n1g1": _pvec(f("norm1_g")[1], 8), "n2g1": _pvec(f("norm2_g")[1], 8), "convw": convw,
        "convb": _pvec(f("lru_conv_b")[0], NJ), "gab": _pvec(f("lru_gate_a_b")[0], NJ), "gxb": _pvec(f("lru_gate_x_b")[0], NJ),
        "lam": _pvec(f("lru_lambda")[0], NJ),
        "gk": np.tile(f("k_norm_g"), 2).reshape(128, 1), "gq": np.tile(f("q_norm_g")[0], 2).reshape(128, 1),
    }
    vecs = np.zeros((128, NV), np.float32)
    for k, (o, n) in VOFF.items():
        vecs[:, o:o + n] = vec_parts[k]
    shared = {
        "vecs": vecs, "ada_w": f("ada_w"), "ada_b": f("ada_b"), "kv_ada_w": f("kv_ada_w"), "kv_ada_b": f("kv_ada_b").reshape(1, -1),
        "w_in": f("lru_w_in")[0], "ga_w": f("lru_gate_a_w")[0], "gx_w": f("lru_gate_x_w")[0], "w_out": f("lru_w_out")[0],
        "mlp_w1": f("mlp_w1"), "mlp_w2": f("mlp_w2"), "kv_w": f("kv_w"), "wq": f("attn_w_q")[0], "wo": f("attn_w_o")[0],
        "biasT": biasT, "amask": amask, "ident": ident, "blk64": blk,
    }
    maps = []
    for core in range(8):
        b, half = core // 2, core % 2
        if half == 1:
            xin = x[b]
        else:
            xin = np.concatenate([np.zeros((2048, D), np.float32), x[b, :2048]], axis=0)
        tflag = np.ones((128, NTILE + 1), np.float32)
        tflag[:, 4] = float(half)
        m = dict(shared)
        m["xin"] = np.ascontiguousarray(xin)
        m["ct"] = _pvec(c[b], 8)
        m["tflag"] = tflag
        maps.append(m)
    return maps


def kernel(**inputs):
    nc = build()
    maps = make_in_maps(inputs)
    res = run_bass_kernel_spmd(nc, maps, core_ids=list(range(8)))
    out = np.zeros((4, 4096, D), np.float32)
    for core in range(8):
        b, half = core // 2, core % 2
        out[b, half * 2048:(half + 1) * 2048] = res.results[core]["out"]
    return out
```

```python
import numpy as np
from contextlib import ExitStack
import concourse.bass as bass
import concourse.mybir as mybir
from concourse.bass_utils import run_bass_kernel_spmd

F32 = mybir.dt.float32
BF16 = mybir.dt.bfloat16
AF = mybir.ActivationFunctionType
ALU = mybir.AluOpType
AX = mybir.AxisListType

D = 1024
W = 1408
NJ = 11
DFF = 4096
TT = 512
NTILE = 8
NKV = 5
EPS = 1e-6


class Buf:
    __slots__ = ("name", "w", "r", "dsem", "dcnt")

    def __init__(self, name):
        self.name = name
        self.w = None
        self.r = []
        self.dsem = None
        self.dcnt = 0


class Prog:
    def __init__(self, nc, stack):
        self.nc = nc
        self.stack = stack
        self.eng = {"pe": nc.tensor, "act": nc.scalar, "dve": nc.vector, "pool": nc.gpsimd, "sp": nc.sync}
        self.sem = {k: stack.enter_context(nc.semaphore("s_" + k)) for k in self.eng}
        self.cnt = {k: 0 for k in self.eng}
        self.seen = {k: {} for k in self.eng}
        self.pending = {k: False for k in self.eng}
        self.dbufs = []
        self.nsem = 0
        self.swq = []
        self.swq_sum = 0

    def _wait(self, e, tok):
        if tok is None:
            return
        sem, val, key = tok
        if key == "pe" and e == "pe":
            return
        if self.seen[e].get(key, 0) >= val:
            return
        self.eng[e].wait_ge(sem, val)
        self.seen[e][key] = val

    def _hazards(self, e, reads, writes, group_sem=None):
        for b in reads:
            self._wait(e, b.w)
        for b in writes:
            if not (group_sem is not None and b.w is not None and b.w[0] is group_sem):
                self._wait(e, b.w)
            for t in b.r:
                self._wait(e, t)

    def op(self, e, ins_fn, reads=(), writes=(), signal=True):
        self._hazards(e, reads, writes)
        ins = ins_fn(self.eng[e])
        if signal:
            self.cnt[e] += 1
            ins.then_inc(self.sem[e], 1)
            tok = (self.sem[e], self.cnt[e], e)
            self.pending[e] = False
        else:
            assert e == "pe"
            tok = (self.sem[e], self.cnt[e] + 1, e)
            self.pending[e] = True
        for b in reads:
            b.r.append(tok)
        for b in writes:
            b.w = tok
            b.r = []
        return tok

    def dma(self, e, out, in_, sembuf, reads=(), writes=(), group=False, **kw):
        if sembuf.dsem is None:
            sembuf.dsem = self.stack.enter_context(self.nc.semaphore("d%d_%s" % (self.nsem, sembuf.name)))
            self.nsem += 1
            self.dbufs.append(sembuf)
        self._hazards(e, reads, writes, group_sem=sembuf.dsem if group else None)
        if e == "pool":
            nd = 1
            for d_ in tuple(out.shape)[:-1]:
                nd *= int(d_)
            nd = max(4, (nd + 7) // 8)
            while self.swq and self.swq_sum + nd > 448:
                sem0 = self.swq[0][0][0]
                last = max(i for i, (tk, _) in enumerate(self.swq) if tk[0] is sem0)
                self._wait(e, self.swq[last][0])
                keep = []
                for i, (tk, n0) in enumerate(self.swq):
                    if tk[0] is sem0 and i <= last:
                        self.swq_sum -= n0
                    else:
                        keep.append((tk, n0))
                self.swq = keep
        ins = self.eng[e].dma_start(out=out, in_=in_, **kw)
        sembuf.dcnt += 16
        ins.then_inc(sembuf.dsem, 16)
        tok = (sembuf.dsem, sembuf.dcnt, ("d", id(sembuf)))
        if e == "pool":
            self.swq.append((tok, nd))
            self.swq_sum += nd
        for b in reads:
            b.r.append(tok)
        for b in writes:
            b.w = tok
            b.r = []
        return tok

    def barrier(self, engines=None):
        toks = [(self.sem[k], self.cnt[k], k) for k in self.eng if self.cnt[k] > 0]
        toks += [(b.dsem, b.dcnt, ("d", id(b))) for b in self.dbufs]
        for e in (engines or self.eng):
            for t in toks:
                if t[2] == e:
                    continue
                self._wait(e, t)
        assert not self.pending["pe"]


def gate_pieces():
    pcs = []
    for n in range(16):
        lo, hi = 88 * n, 88 * n + 88
        for ci in range(NJ):
            r0, r1 = max(lo, 128 * ci), min(hi, 128 * ci + 128)
            if r0 >= r1:
                continue
            for co in range(NJ):
                c0, c1 = max(lo, 128 * co), min(hi, 128 * co + 128)
                if c0 >= c1:
                    continue
                d = ci - co + 1
                assert 0 <= d <= 2
                pcs.append((n, r0 - lo, r1 - lo, c0 - lo, c1 - lo, ci, co, d, r0 - 128 * ci, r1 - 128 * ci, c0 - 128 * co, c1 - 128 * co))
    return pcs


def gate_nbrs():
    nb = {co: set() for co in range(NJ)}
    for p in gate_pieces():
        nb[p[6]].add(p[5])
    return {co: sorted(v) for co, v in nb.items()}


VOFF = {}
_o = 0
for _name, _n in [("n1g0", 8), ("n2g0", 8), ("kvg", 8), ("n1g1", 8), ("n2g1", 8), ("convw", 44), ("convb", 11),
                  ("gab", 11), ("gxb", 11), ("lam", 11), ("gk", 1), ("gq", 1)]:
    VOFF[_name] = (_o, _n)
    _o += _n
NV = _o


def build(stop=None):
    nc = bass.Bass("TRN2", target_bir_lowering=False)
    dt_in = lambda name, shape: nc.dram_tensor(name, shape, F32, kind="ExternalInput").ap()
    xin = dt_in("xin", [NTILE * TT, D])
    ct_d = dt_in("ct", [128, 8])
    tflag_d = dt_in("tflag", [128, NTILE + 1])
    vecs_d = dt_in("vecs", [128, NV])
    ada_w = dt_in("ada_w", [2, D, 6 * D])
    ada_b = dt_in("ada_b", [2, 6 * D])
    kv_ada_w = dt_in("kv_ada_w", [D, 2 * D])
    kv_ada_b = dt_in("kv_ada_b", [1, 2 * D])
    w_in_d = dt_in("w_in", [D, 2 * W])
    ga_w_d = dt_in("ga_w", [16, 88, 88])
    gx_w_d = dt_in("gx_w", [16, 88, 88])
    w_out_d = dt_in("w_out", [W, D])
    w1_d = dt_in("mlp_w1", [2, D, DFF])
    w2_d = dt_in("mlp_w2", [2, DFF, D])
    kv_w_d = dt_in("kv_w", [D, 2 * D])
    wq_d = dt_in("wq", [D, D])
    wo_d = dt_in("wo", [D, D])
    biasT_d = dt_in("biasT", [128, 16 * 5 * 128])
    amask_d = dt_in("amask", [128, 5 * 128])
    ident_d = dt_in("ident", [128, 128])
    blk_d = dt_in("blk64", [128, 128])
    out_d = nc.dram_tensor("out", [4 * TT, D], F32, kind="ExternalOutput").ap()
    H = nc.dram_tensor("Hs", [NKV * TT, D], F32, kind="ExternalOutput" if stop else "Internal").ap()
    Gs = nc.dram_tensor("Gs", [4, 128, D], F32, kind="Internal").ap()

    with ExitStack() as top:
        P = Prog(nc, top)

        def SB(stack, name, shape, dt):
            return stack.enter_context(nc.sbuf_tensor("sb_" + name, shape, dt))

        def PS(stack, name, shape, dt):
            return stack.enter_context(nc.psum_tensor("ps_" + name, shape, dt))

        vecs = SB(top, "vecs", [128, NV], F32); b_vecs = Buf("vecs")
        tflag = SB(top, "tflag", [128, NTILE + 1], F32); b_tflag = Buf("tflag")
        identb = SB(top, "identb", [128, 128], BF16); b_identb = Buf("identb")
        Asc = SB(top, "Asc", [128, 5, 8], F32); Ash = SB(top, "Ash", [128, 5, 8], F32); b_mod = Buf("modvec")
        cch = SB(top, "cch", [128, 2, NJ], F32); b_cch = Buf("cch")
        nbias = SB(top, "nbias", [128, 2, NJ], F32); b_nbias = Buf("nbias")
        gqk = SB(top, "gqk", [128, 2], F32); b_gqk = Buf("gqk")

        P.dma("sp", vecs[:], vecs_d[:, :], b_vecs, writes=[b_vecs])
        P.dma("sp", tflag[:], tflag_d[:, :], b_tflag, writes=[b_tflag])
        P.dma("pool", identb[:], ident_d[:, :], b_identb, writes=[b_identb])

        def V(name):
            o, n = VOFF[name]
            return vecs[:, o:o + n]

        with ExitStack() as ph:
            ct = SB(ph, "ct", [128, 8], F32); b_ct = Buf("ct")
            t8a = SB(ph, "t8a", [128, 8], F32); t8b = SB(ph, "t8b", [128, 8], F32); b_t8 = Buf("t8")
            cact = SB(ph, "cact", [128, 8], F32); b_cact = Buf("cact")
            lc = SB(ph, "lc", [128, 8, 128], BF16); b_lc = Buf("lc")
            identf = SB(ph, "identf", [128, 128], F32); b_identf = Buf("identf")
            modrow = [SB(ph, "modrow%d" % i, [128, 6 * D], F32) for i in range(2)]
            kvrow = SB(ph, "kvrow", [128, 2 * D], F32)
            b_rows = [Buf("modrow0"), Buf("modrow1"), Buf("kvrow")]
            wch = [SB(ph, "wch%d" % i, [128, 8, 512], BF16) for i in range(3)]; b_wch = [Buf("wch%d" % i) for i in range(3)]
            dtmp = SB(ph, "dtmp", [128, 8, 128], F32); b_dtmp = Buf("dtmp")
            t11 = SB(ph, "t11", [128, NJ], F32); b_t11 = Buf("t11")
            pm = [PS(ph, "pm%d" % i, [128, 512], F32) for i in range(2)]; b_pm = [Buf("pm%d" % i) for i in range(2)]

            P.dma("sp", ct[:], ct_d[:, :], b_ct, writes=[b_ct])
            P.dma("sp", identf[:], ident_d[:, :], b_identf, writes=[b_identf])
            rows = [modrow[0], modrow[1], kvrow]
            P.dma("sp", modrow[0][:], ada_b[0:1, :].partition_broadcast(128), b_rows[0], writes=[b_rows[0]])
            P.dma("sp", modrow[1][:], ada_b[1:2, :].partition_broadcast(128), b_rows[1], writes=[b_rows[1]])
            P.dma("sp", kvrow[:], kv_ada_b[0:1, :].partition_broadcast(128), b_rows[2], writes=[b_rows[2]])
            P.op("act", lambda e: e.activation(out=t8a[:], in_=ct[:], func=AF.Exp, scale=-1.0), reads=[b_ct], writes=[b_t8])
            P.op("act", lambda e: e.activation(out=t8b[:], in_=t8a[:], func=AF.Ln, bias=1.0), reads=[b_t8], writes=[b_t8])
            P.op("act", lambda e: e.activation(out=t8a[:], in_=t8b[:], func=AF.Exp, scale=-1.0), reads=[b_t8], writes=[b_t8])
            P.op("dve", lambda e: e.tensor_tensor(out=cact[:], in0=t8a[:], in1=ct[:], op=ALU.mult), reads=[b_t8, b_ct], writes=[b_cact])
            P.op("dve", lambda e: e.tensor_copy(out=lc[:], in_=cact[:].unsqueeze(2).to_broadcast([128, 8, 128])), reads=[b_cact], writes=[b_lc])
            P.op("act", lambda e: e.activation(out=t11[:], in_=V("lam"), func=AF.Exp, scale=-1.0), reads=[b_vecs], writes=[b_t11])
            P.op("act", lambda e: e.activation(out=t11[:], in_=t11[:], func=AF.Ln, bias=1.0), reads=[b_t11], writes=[b_t11])
            P.op("dve", lambda e: e.tensor_scalar(out=cch[:, 0, :], in0=t11[:], scalar1=-8.0, scalar2=None, op0=ALU.mult), reads=[b_t11], writes=[b_cch])
            P.op("dve", lambda e: e.tensor_scalar(out=cch[:, 1, :], in0=t11[:], scalar1=-16.0, scalar2=None, op0=ALU.mult), reads=[b_t11], writes=[b_cch])
            P.op("dve", lambda e: e.tensor_scalar(out=nbias[:, 0, :], in0=V("gab"), scalar1=-1.0, scalar2=None, op0=ALU.mult), reads=[b_vecs], writes=[b_nbias])
            P.op("dve", lambda e: e.tensor_scalar(out=nbias[:, 1, :], in0=V("gxb"), scalar1=-1.0, scalar2=None, op0=ALU.mult), reads=[b_vecs], writes=[b_nbias])
            P.op("dve", lambda e: e.tensor_copy(out=gqk[:, 0:1], in_=V("gk")), reads=[b_vecs], writes=[b_gqk])
            P.op("dve", lambda e: e.tensor_scalar(out=gqk[:, 1:2], in0=V("gq"), scalar1=0.125, scalar2=None, op0=ALU.mult), reads=[b_vecs], writes=[b_gqk])

            srcs = [(ada_w[0], 6 * D, 0), (ada_w[1], 6 * D, 1), (kv_ada_w, 2 * D, 2)]
            it = 0
            for (wsrc, ncols, ri) in srcs:
                wv = wsrc.rearrange("(kc p) n -> p kc n", p=128)
                for j in range(ncols // 512):
                    wb_, bb_ = wch[it % 3], b_wch[it % 3]
                    pp, bp = pm[it % 2], b_pm[it % 2]
                    P.dma("pool", wb_[:], wv[:, :, j * 512:(j + 1) * 512], bb_, writes=[bb_])
                    for kc in range(8):
                        P.op("pe", lambda e, kc=kc, wb_=wb_, pp=pp: e.matmul(pp[:], lhsT=lc[:, kc, :], rhs=wb_[:, kc, :], start=(kc == 0), stop=(kc == 7)),
                             reads=[b_lc, bb_], writes=[bp], signal=(kc == 7))
                    rr = rows[ri]
                    P.op("dve", lambda e, rr=rr, pp=pp, j=j: e.tensor_tensor(out=rr[:, j * 512:(j + 1) * 512], in0=pp[:], in1=rr[:, j * 512:(j + 1) * 512], op=ALU.add),
                         reads=[bp, b_rows[ri]], writes=[b_rows[ri]])
                    it += 1

            specs = [(0, modrow[0], 0, 1, "n1g0"), (1, modrow[0], 3, 4, "n2g0"), (2, kvrow, 0, 1, "kvg"),
                     (3, modrow[1], 0, 1, "n1g1"), (4, modrow[1], 3, 4, "n2g1")]
            rbuf = {0: b_rows[0], 1: b_rows[0], 2: b_rows[2], 3: b_rows[1], 4: b_rows[1]}
            for (m, rr, sseg, cseg, gname) in specs:
                for (seg, dst) in [(sseg, Ash), (cseg, t8a)]:
                    P.op("dve", lambda e, rr=rr, seg=seg: e.tensor_tensor(out=dtmp[:], in0=rr[:, seg * D:(seg + 1) * D].rearrange("p (g k) -> p g k", k=128),
                                                                     in1=identf[:].unsqueeze(1).to_broadcast([128, 8, 128]), op=ALU.mult),
                         reads=[rbuf[m], b_identf], writes=[b_dtmp])
                    if dst is Ash:
                        P.op("dve", lambda e, m=m: e.tensor_reduce(out=Ash[:, m, :], in_=dtmp[:], axis=AX.X, op=ALU.add), reads=[b_dtmp], writes=[b_mod])
                    else:
                        P.op("dve", lambda e: e.tensor_reduce(out=t8a[:], in_=dtmp[:], axis=AX.X, op=ALU.add), reads=[b_dtmp], writes=[b_t8])
                P.op("dve", lambda e: e.tensor_scalar(out=t8b[:], in0=t8a[:], scalar1=1.0, scalar2=32.0, op0=ALU.add, op1=ALU.mult), reads=[b_t8], writes=[b_t8])
                P.op("dve", lambda e, m=m, gname=gname: e.tensor_tensor(out=Asc[:, m, :], in0=t8b[:], in1=V(gname), op=ALU.mult), reads=[b_t8, b_vecs], writes=[b_mod])
            for gi, (rr, seg, rb) in enumerate([(modrow[0], 2, b_rows[0]), (modrow[0], 5, b_rows[0]), (modrow[1], 2, b_rows[1]), (modrow[1], 5, b_rows[1])]):
                P.dma("sp", Gs[gi], rr[:, seg * D:(seg + 1) * D], rb, reads=[rb])
            P.barrier()

        def load_w(stack, name, src_view, shape, nsplit):
            t = SB(stack, name, shape, BF16)
            b = Buf(name)
            a = shape[1]
            step = (a + nsplit - 1) // nsplit
            for s0 in range(0, a, step):
                s1 = min(a, s0 + step)
                P.dma("pool", t[:, s0:s1, :], src_view[:, s0:s1, :], b, writes=[b], group=True)
            return t, b

        def load_w_cols(stack, name, src_view, shape, splits):
            t = SB(stack, name, shape, BF16)
            blocks = []
            for (c0, c1) in splits:
                b = Buf("%s_%d" % (name, c0))
                P.dma("pool", t[:, :, c0:c1], src_view[:, :, c0:c1], b, writes=[b])
                blocks.append((c0, c1, b))

            def bufs(c0, c1):
                return [b for (a0, a1, b) in blocks if a0 < c1 and c0 < a1]
            return t, bufs

        def norm_T(xt, b_xt, m, S):
            norm_a(xt, b_xt, S)
            norm_b(m, S)

        def norm_a(xt, b_xt, S):
            bx = b_xt if isinstance(b_xt, list) else [b_xt] * 4
            bxn = S["b_xn"] if isinstance(S["b_xn"], list) else [S["b_xn"]]
            bss = S["b_ss4"]
            for s in range(4):
                P.op("act", lambda e, s=s: e.activation(out=S["junk"][:], in_=xt[:, s, :], func=AF.Square, accum_out=S["ss"][:, s:s + 1]), reads=[bx[s]], writes=[S["b_junk"], bss[s]])
            for s in range(4):
                P.op("act", lambda e, s=s: e.activation(out=S["ss"][:, 4 + s:5 + s], in_=S["ss"][:, s:s + 1], func=AF.Ln, bias=S["epsb"][:, 0:1]), reads=[bss[s], S["b_epsb"]], writes=[bss[s]])
                P.op("act", lambda e, s=s: e.activation(out=S["ss"][:, 8 + s:9 + s], in_=S["ss"][:, 4 + s:5 + s], func=AF.Exp, scale=-0.5), reads=[bss[s]], writes=[bss[s]])
                P.op("dve", lambda e, s=s: e.tensor_scalar(out=S["xn"][:, s, :], in0=xt[:, s, :], scalar1=S["ss"][:, 8 + s:9 + s], scalar2=None, op0=ALU.mult),
                     reads=[bx[s], bss[s]], writes=bxn)

        def norm_b(m, S):
            for fc in range(8):
                pT, bT = S["pT"][fc % 2], S["b_pT"][fc % 2]
                for s in range(4):
                    P.op("pe", lambda e, s=s, fc=fc, pT=pT: e.transpose(out=pT[:, s * 128:(s + 1) * 128], in_=S["xn"][:, s, fc * 128:(fc + 1) * 128], identity=identb[:]),
                         reads=(S["b_xn"] if isinstance(S["b_xn"], list) else [S["b_xn"]]) + [b_identb], writes=[bT], signal=(s == 3))
                P.op("act", lambda e, fc=fc, pT=pT: e.activation(out=S["hnT"][:, fc, :], in_=pT[:, 0:TT], func=AF.Identity, scale=Asc[:, m, fc:fc + 1], bias=Ash[:, m, fc:fc + 1]),
                     reads=[bT, b_mod], writes=[S["b_hnT"]])

        def alloc_norm(stack, pfx):
            S = {}
            S["junk"] = SB(stack, pfx + "junk", [128, D], BF16); S["b_junk"] = Buf("junk")
            S["ss"] = SB(stack, pfx + "ss", [128, 12], F32); S["b_ss"] = Buf("ss"); S["b_ss4"] = [Buf("ss%d" % i) for i in range(4)]
            S["epsb"] = SB(stack, pfx + "epsb", [128, 2], F32); S["b_epsb"] = Buf("epsb")
            S["hnT"] = SB(stack, pfx + "hnT", [128, 8, TT], BF16); S["b_hnT"] = Buf("hnT")
            S["pT"] = [PS(stack, pfx + "pT%d" % i, [128, 2 * TT], BF16) for i in range(2)]; S["b_pT"] = [Buf("pT0"), Buf("pT1")]
            P.op("dve", lambda e: e.memset(S["epsb"][:, 0:1], 1024.0 * EPS), writes=[S["b_epsb"]])
            P.op("dve", lambda e: e.memset(S["epsb"][:, 1:2], EPS), writes=[S["b_epsb"]])
            return S

        def load_g(stack, name, gi):
            g = SB(stack, name, [128, D], F32); b = Buf(name)
            P.dma("sp", g[:], Gs[gi], b, writes=[b])
            return g, b

        def resid_out(xt, b_xt, s, nh, po, bpo, gi, S):
            P.op("dve", lambda e: e.tensor_tensor(out=S["rtmp"][:], in0=po[:], in1=S["g"][:, nh * 512:(nh + 1) * 512], op=ALU.mult),
                 reads=[bpo, S["b_g"]], writes=[S["b_rtmp"]])
            bx = b_xt[s] if isinstance(b_xt, list) else b_xt
            P.op("dve", lambda e: e.tensor_tensor(out=xt[:, s, nh * 512:(nh + 1) * 512], in0=S["rtmp"][:], in1=xt[:, s, nh * 512:(nh + 1) * 512], op=ALU.add),
                 reads=[S["b_rtmp"], bx], writes=[bx])

        def load_rows(xt, b_xt, src):
            for s in range(4):
                P.dma("sp", xt[:, s, :], src[s * 128:(s + 1) * 128, :], b_xt[s], writes=[b_xt[s]])

        def store_rows(xt, b_xt, dst, s):
            P.dma("sp", dst[s * 128:(s + 1) * 128, :], xt[:, s, :], b_xt[s], reads=[b_xt[s]])

        nbrs = gate_nbrs()
        with ExitStack() as ph:
            w_in_sb, w_in_bufs = load_w_cols(ph, "w_in_sb", w_in_d.rearrange("(kc p) n -> p kc n", p=128), [128, 8, 2 * W],
                                             [(W, W + 512), (W + 512, 2 * W), (0, 704), (704, W)])
            wg = SB(ph, "wg", [128, 2, NJ * 3, 128], BF16); b_wg = Buf("wg")
            P.op("dve", lambda e: e.memset(wg[:], 0.0), writes=[b_wg])
            for gidx, gsrc in enumerate([ga_w_d, gx_w_d]):
                for (n, r0, r1, c0, c1, ci, co, d, p0, p1, q0, q1) in gate_pieces():
                    P.dma("pool", wg[p0:p1, gidx, co * 3 + d, q0:q1], gsrc[n, r0:r1, c0:c1], b_wg, writes=[b_wg], group=True)
            w_out_sb, b_w_out = load_w(ph, "w_out_sb", w_out_d.rearrange("(jc p) n -> p jc n", p=128), [128, NJ, D], 2)

            S = alloc_norm(ph, "p1")
            S["g"], S["b_g"] = load_g(ph, "p1g", 0)
            xts = [SB(ph, "p1xt%d" % i, [128, 4, D], F32) for i in range(2)]; b_xts = [Buf("xt0"), Buf("xt1")]
            yb = SB(ph, "p1yb", [128, NJ, TT], BF16); b_yb = Buf("yb")
            NXR = 2
            xrb = [SB(ph, "p1xrb%d" % i, [128, TT + 3], F32) for i in range(NXR)]; b_xrb = [Buf("xrb%d" % i) for i in range(NXR)]
            halo = SB(ph, "p1halo", [128, NJ, 3], F32); b_halo = [Buf("halo%d" % j) for j in range(NJ)]
            xc = SB(ph, "p1xc", [128, NJ, TT], F32); b_xc = [Buf("xc%d" % j) for j in range(NJ)]
            xcb = SB(ph, "p1xcb", [128, NJ, TT], BF16); b_xcb = [Buf("xcb%d" % j) for j in range(NJ)]
            S["xn"] = xcb[:, 0:8, :].rearrange("p a c -> p (a c)").rearrange("p (s d) -> p s d", s=4)
            S["b_xn"] = b_xcb[0:8]
            state = SB(ph, "p1state", [128, NJ], F32); b_state = Buf("state")
            NTMP = 3
            tA = [SB(ph, "p1tA%d" % i, [128, TT], F32) for i in range(NTMP)]; b_tA = [Buf("tA%d" % i) for i in range(NTMP)]
            tQ = [SB(ph, "p1tQ%d" % i, [128, TT], F32) for i in range(NTMP)]; b_tQ = [Buf("tQ%d" % i) for i in range(NTMP)]
            tG, b_tG = tA, b_tA
            NGRP = 6
            Rst = SB(ph, "p1R", [128, NGRP, TT], F32); b_R = [Buf("R%d" % i) for i in range(NGRP)]
            S["rtmp"] = tA[0]; S["b_rtmp"] = b_tA[0]
            pz = [PS(ph, "p1pz%d" % i, [128, TT], F32) for i in range(6)]; b_pz = [Buf("pz%d" % i) for i in range(6)]
            pzi = [0]

            def nextpz():
                i = pzi[0] % 6
                pzi[0] += 1
                return pz[i], b_pz[i]

            P.op("dve", lambda e: e.memset(halo[:], 0.0), writes=b_halo)
            P.op("dve", lambda e: e.memset(state[:], 0.0), writes=[b_state])
            cw = V("convw")
            cb = V("convb")

            def load_x(t):
                P.dma("sp", xts[t % 2][:], xin[t * TT:(t + 1) * TT, :].rearrange("(s p) d -> p s d", p=128), b_xts[t % 2], writes=[b_xts[t % 2]])

            ntile_p1 = NTILE if stop != "p1a" else 5
            hnT, b_hnT = S["hnT"], S["b_hnT"]
            gab, gxb = V("gab"), V("gxb")

            pending_cast = []

            def cast_chunk(j):
                P.op("act", lambda e: e.activation(out=xcb[:, j, :], in_=xc[:, j, :], func=AF.Copy), reads=[b_xc[j]], writes=[b_xcb[j]])

            def conv_part(t, j0, j1):
                def halo_in(j):
                    xr, bxr = xrb[j % NXR], b_xrb[j % NXR]
                    P.op("pool", lambda e: e.tensor_copy(out=xr[:, 0:3], in_=halo[:, j, :]), reads=[b_halo[j]], writes=[bxr])

                if j0 == 0:
                    halo_in(0)
                for j in range(j0, j1):
                    oc = NJ + j
                    pp, bp = nextpz()
                    for kc in range(8):
                        P.op("pe", lambda e, kc=kc, oc=oc, pp=pp: e.matmul(pp[:], lhsT=w_in_sb[:, kc, oc * 128:(oc + 1) * 128], rhs=hnT[:, kc, :], start=(kc == 0), stop=(kc == 7)),
                             reads=w_in_bufs(oc * 128, oc * 128 + 128) + [b_hnT], writes=[bp], signal=(kc == 7))
                    xr, bxr = xrb[j % NXR], b_xrb[j % NXR]
                    P.op("act", lambda e, xr=xr, pp=pp: e.activation(out=xr[:, 3:TT + 3], in_=pp[:], func=AF.Copy), reads=[bp], writes=[bxr])
                    P.op("act", lambda e, pp=pp, j=j: e.activation(out=xc[:, j, :], in_=pp[:], func=AF.Identity, scale=cw[:, j * 4 + 3:j * 4 + 4], bias=cb[:, j:j + 1]),
                         reads=[bp, b_vecs], writes=[b_xc[j]])
                    if j + 1 < NJ:
                        halo_in(j + 1)
                    while pending_cast:
                        cast_chunk(pending_cast.pop(0))
                    for k in range(3):
                        P.op("dve", lambda e, xr=xr, j=j, k=k: e.scalar_tensor_tensor(out=xc[:, j, :], in0=xr[:, k:k + TT], scalar=cw[:, j * 4 + k:j * 4 + k + 1], in1=xc[:, j, :],
                                                                                  op0=ALU.mult, op1=ALU.add),
                             reads=[bxr, b_xc[j], b_vecs], writes=[b_xc[j]])
                    P.op("pool", lambda e, xr=xr, j=j: e.tensor_scalar(out=halo[:, j, :], in0=xr[:, TT:TT + 3], scalar1=tflag[:, t + 1:t + 2], scalar2=None, op0=ALU.mult),
                         reads=[bxr, b_tflag], writes=[b_halo[j]])
                    pending_cast.append(j)
                if j1 == NJ:
                    while pending_cast:
                        cast_chunk(pending_cast.pop(0))

            def gates_part(t):
                full = t >= 3
                if full:
                    for j in range(NJ):
                        pp, bp = nextpz()
                        for kc in range(8):
                            P.op("pe", lambda e, kc=kc, j=j, pp=pp: e.matmul(pp[:], lhsT=w_in_sb[:, kc, j * 128:(j + 1) * 128], rhs=hnT[:, kc, :], start=(kc == 0), stop=(kc == 7)),
                                 reads=w_in_bufs(j * 128, j * 128 + 128) + [b_hnT], writes=[bp], signal=(kc == 7))
                        P.op("act", lambda e, j=j, pp=pp: e.activation(out=yb[:, j, :], in_=pp[:], func=AF.Gelu_apprx_tanh), reads=[bp], writes=[b_yb])
                for gno, grp in enumerate([list(range(0, NGRP)), list(range(NGRP, NJ))]):
                    for gi_, co in enumerate(grp):
                        pa, bpa = nextpz()
                        px, bpx = nextpz()
                        cis = nbrs[co]
                        for gidx, (pg, bpg) in enumerate([(pa, bpa), (px, bpx)]):
                            for n_, ci in enumerate(cis):
                                d = ci - co + 1
                                P.op("pe", lambda e, pg=pg, gidx=gidx, d=d, ci=ci, n_=n_, co=co, cis=cis: e.matmul(pg[:], lhsT=wg[:, gidx, co * 3 + d, :], rhs=xcb[:, ci, :],
                                                                                                         start=(n_ == 0), stop=(n_ == len(cis) - 1)),
                                     reads=[b_wg, b_xcb[ci]], writes=[bpg], signal=(n_ == len(cis) - 1))
                        I_, bI = tQ[co % NTMP], b_tQ[co % NTMP]
                        P.op("act", lambda e, pa=pa, gi_=gi_, co=co: e.activation(out=Rst[:, gi_, :], in_=pa[:], func=AF.Sigmoid, bias=gab[:, co:co + 1]), reads=[bpa, b_vecs], writes=[b_R[gi_]])
                        P.op("act", lambda e, px=px, I_=I_, co=co: e.activation(out=I_[:], in_=px[:], func=AF.Sigmoid, bias=gxb[:, co:co + 1]), reads=[bpx, b_vecs], writes=[bI])
                        P.op("pool", lambda e, I_=I_, co=co: e.tensor_tensor(out=xc[:, co, :], in0=I_[:], in1=xc[:, co, :], op=ALU.mult), reads=[bI, b_xc[co]], writes=[b_xc[co]])

                    def stage_b(co, gi_, k_):
                        A_, bA = tA[k_], b_tA[k_]
                        Q_, bQ = tQ[k_], b_tQ[k_]
                        return [
                            lambda: P.op("act", lambda e: e.activation(out=A_[:], in_=Rst[:, gi_, :], func=AF.Exp, scale=cch[:, 0, co:co + 1]), reads=[b_R[gi_], b_cch], writes=[bA]),
                            lambda: P.op("pool", lambda e: e.tensor_tensor(out=Q_[:], in0=A_[:], in1=A_[:], op=ALU.mult), reads=[bA], writes=[bQ]),
                            lambda: P.op("act", lambda e: e.activation(out=Q_[:], in_=Q_[:], func=AF.Ln, scale=-1.0, bias=1.000001), reads=[bQ], writes=[bQ]),
                            lambda: P.op("act", lambda e: e.activation(out=Q_[:], in_=Q_[:], func=AF.Exp, scale=0.5), reads=[bQ], writes=[bQ]),
                            lambda: P.op("pool", lambda e: e.tensor_tensor(out=Q_[:], in0=Q_[:], in1=xc[:, co, :], op=ALU.mult), reads=[bQ, b_xc[co]], writes=[bQ]),
                            lambda: P.op("dve", lambda e: e.tensor_tensor_scan(out=xc[:, co, :], data0=A_[:], data1=Q_[:], initial=state[:, co:co + 1], op0=ALU.mult, op1=ALU.add),
                                         reads=[bA, bQ, b_state], writes=[b_xc[co]]),
                        ]

                    if gno == 1 and t + 1 < ntile_p1:
                        norm_a(xts[(t + 1) % 2], b_xts[(t + 1) % 2], S)
                    for c3 in range(0, len(grp), NTMP):
                        chains = [stage_b(co, c3 + k_, k_) for k_, co in enumerate(grp[c3:c3 + NTMP])]
                        for si in range(len(chains[0])):
                            for ch in chains:
                                ch[si]()
                P.op("dve", lambda e: e.tensor_scalar(out=state[:], in0=xc[:, :, TT - 1], scalar1=tflag[:, t + 1:t + 2], scalar2=None, op0=ALU.mult),
                     reads=b_xc + [b_tflag], writes=[b_state])

            def tail_a(t):
                for j in range(NJ):
                    P.op("dve", lambda e, j=j: e.tensor_tensor(out=yb[:, j, :], in0=yb[:, j, :], in1=xc[:, j, :], op=ALU.mult), reads=[b_yb, b_xc[j]], writes=[b_yb])

            def tail_b(t):
                xt, b_xt = xts[t % 2], b_xts[t % 2]
                for s_ in range(4):
                    for nh in range(2):
                        pp, bp = nextpz()
                        for jc in range(NJ):
                            P.op("pe", lambda e, jc=jc, s_=s_, nh=nh, pp=pp: e.matmul(pp[:], lhsT=yb[:, jc, s_ * 128:(s_ + 1) * 128], rhs=w_out_sb[:, jc, nh * 512:(nh + 1) * 512],
                                                                                  start=(jc == 0), stop=(jc == NJ - 1)),
                                 reads=[b_yb, b_w_out], writes=[bp], signal=(jc == NJ - 1))
                        resid_out(xt, b_xt, s_, nh, pp, bp, 0, S)
                P.dma("sp", H[(t - 3) * TT:(t - 2) * TT, :].rearrange("(s p) d -> p s d", p=128), xt[:], b_xt, reads=[b_xt])

            load_x(0)
            load_x(1)
            norm_a(xts[0], b_xts[0], S)
            norm_b(0, S)
            for t in range(ntile_p1):
                prev_full = (t - 1) >= 3
                if prev_full:
                    tail_a(t - 1)
                conv_part(t, 0, 6)
                if prev_full:
                    tail_b(t - 1)
                    if t + 1 < ntile_p1:
                        load_x(t + 1)
                elif 1 <= t and t + 1 < ntile_p1:
                    load_x(t + 1)
                conv_part(t, 6, NJ)
                gates_part(t)
                if t + 1 < ntile_p1:
                    norm_b(0, S)
            tail_a(ntile_p1 - 1)
            tail_b(ntile_p1 - 1)
            P.barrier()
        if stop in ("p1", "p1a"):
            return nc

        def mlp_phase(l, tiles, m, gi, final):
            with ExitStack() as ph:
                w1_sb, w1_bufs = load_w_cols(ph, "w1_sb%d" % l, w1_d[l].rearrange("(kc p) n -> p kc n", p=128), [128, 8, DFF],
                                             [(i * 512, (i + 1) * 512) for i in range(8)])
                w2_sb, b_w2 = load_w(ph, "w2_sb%d" % l, w2_d[l].rearrange("(fc p) n -> p fc n", p=128), [128, 32, D], 8)
                S = alloc_norm(ph, "m%d" % l)
                S["g"], S["b_g"] = load_g(ph, "m%dg" % l, gi)
                xt = SB(ph, "m%dxt" % l, [128, 4, D], F32); b_xt = [Buf("xt%d" % i) for i in range(4)]
                hid = SB(ph, "m%dhid" % l, [128, 32, TT], BF16); b_hid = Buf("hid")
                S["xn"] = hid[:, 0:8, :].rearrange("p a c -> p (a c)").rearrange("p (s d) -> p s d", s=4)
                S["b_xn"] = b_hid
                rl = [SB(ph, "m%drl%d" % (l, i), [128, TT], BF16) for i in range(3)]; b_rl = [Buf("rl%d" % i) for i in range(3)]
                S["rtmp"] = SB(ph, "m%drtmp" % l, [128, 512], F32); S["b_rtmp"] = Buf("rtmp")
                pz = [PS(ph, "m%dpz%d" % (l, i), [128, TT], F32) for i in range(6)]; b_pz = [Buf("pz%d" % i) for i in range(6)]
                pzi = [0]

                def nextpz():
                    i = pzi[0] % 6
                    pzi[0] += 1
                    return pz[i], b_pz[i]

                for t in tiles:
                    hrow = H[(t - 3) * TT:(t - 2) * TT, :]
                    load_rows(xt, b_xt, hrow)
                    norm_T(xt, b_xt, m, S)
                    hnT, b_hnT = S["hnT"], S["b_hnT"]
                    for fc in range(32):
                        pp, bp = nextpz()
                        for kc in range(8):
                            P.op("pe", lambda e, kc=kc, fc=fc, pp=pp: e.matmul(pp[:], lhsT=w1_sb[:, kc, fc * 128:(fc + 1) * 128], rhs=hnT[:, kc, :], start=(kc == 0), stop=(kc == 7)),
                                 reads=w1_bufs(fc * 128, fc * 128 + 128) + [b_hnT], writes=[bp], signal=(kc == 7))
                        r_, br = rl[fc % 3], b_rl[fc % 3]
                        P.op("act", lambda e, r_=r_, pp=pp: e.activation(out=r_[:], in_=pp[:], func=AF.Relu), reads=[bp], writes=[br])
                        P.op("dve", lambda e, r_=r_, fc=fc: e.tensor_tensor(out=hid[:, fc, :], in0=r_[:], in1=r_[:], op=ALU.mult), reads=[br], writes=[b_hid])
                    for s in range(4):
                        for nh in range(2):
                            pp, bp = nextpz()
                            for fc in range(32):
                                P.op("pe", lambda e, fc=fc, s=s, nh=nh, pp=pp: e.matmul(pp[:], lhsT=hid[:, fc, s * 128:(s + 1) * 128], rhs=w2_sb[:, fc, nh * 512:(nh + 1) * 512],
                                                                                    start=(fc == 0), stop=(fc == 31)),
                                     reads=[b_hid, b_w2], writes=[bp], signal=(fc == 31))
                            resid_out(xt, b_xt, s, nh, pp, bp, gi, S)
                        store_rows(xt, b_xt, out_d[(t - 4) * TT:(t - 3) * TT, :] if final else hrow, s)
                P.barrier()

        mlp_phase(0, range(3, 8), 1, 1, False)
        if stop == "p2":
            return nc

        with ExitStack() as ph:
            blkb = SB(ph, "blkb", [128, 128], BF16); b_blk = Buf("blkb")
            P.dma("pool", blkb[:], blk_d[:, :], b_blk, writes=[b_blk])
            expB = SB(ph, "expB", [128, 16, 640], BF16); b_expB = Buf("expB")
            KT = SB(ph, "KT", [128, 8, NKV * TT], BF16); b_KT = Buf("KT")
            Va = SB(ph, "Va", [128, NKV * 4, 16, 65], BF16); b_Va = Buf("Va")
            S = alloc_norm(ph, "p3")
            xt = SB(ph, "p3xt", [128, 4, D], F32); b_xt = [Buf("xt%d" % i) for i in range(4)]
            sq = [SB(ph, "p3sq%d" % i, [128, TT], BF16) for i in range(3)]; b_sq = [Buf("sq%d" % i) for i in range(3)]
            rs = [SB(ph, "p3rs%d" % i, [128, TT], F32) for i in range(3)]; b_rs = [Buf("rs%d" % i) for i in range(3)]
            pSt = [PS(ph, "p3pS%d" % i, [128, 1024], F32) for i in range(2)]
            pOt = [PS(ph, "p3pO%d" % i, [128, 4, 128], F32) for i in range(2)]
            b_bank = [Buf("bank%d" % i) for i in range(6)]
            pS = pSt; b_pS = [[b_bank[0], b_bank[1]], [b_bank[2], b_bank[3]]]
            pO = pOt; b_pO = [b_bank[4], b_bank[5]]
            pz = [pSt[0][:, 0:512], pSt[0][:, 512:1024], pSt[1][:, 0:512], pSt[1][:, 512:1024],
                  pOt[0][:].rearrange("p a b -> p (a b)"), pOt[1][:].rearrange("p a b -> p (a b)")]
            pzi = [0]

            def nextpz():
                i = pzi[0] % 6
                pzi[0] += 1
                return pz[i], b_bank[i]

            def proj_headnorm(w_sb, w_bufs, hnT, b_hnT, dstT, b_dst, col0, gcol):
                for g0 in range(0, 8, 3):
                    fcs = list(range(g0, min(8, g0 + 3)))
                    pps, pms = [], []
                    for fc in fcs:
                        pp, bp = nextpz()
                        for kc in range(8):
                            P.op("pe", lambda e, kc=kc, fc=fc, pp=pp: e.matmul(pp[:], lhsT=w_sb[:, kc, fc * 128:(fc + 1) * 128], rhs=hnT[:, kc, :], start=(kc == 0), stop=(kc == 7)),
                                 reads=w_bufs(fc * 128, fc * 128 + 128) + [b_hnT], writes=[bp], signal=(kc == 7))
                        pps.append((pp, bp))
                    for i, fc in enumerate(fcs):
                        pp, bp = pps[i]
                        P.op("act", lambda e, i=i, pp=pp: e.activation(out=sq[i][:], in_=pp[:], func=AF.Square), reads=[bp], writes=[b_sq[i]])
                    for i, fc in enumerate(fcs):
                        pm_, bpm = nextpz()
                        P.op("pe", lambda e, i=i, pm_=pm_: e.matmul(pm_[:], lhsT=blkb[:], rhs=sq[i][:], start=True, stop=True), reads=[b_blk, b_sq[i]], writes=[bpm])
                        pms.append((pm_, bpm))
                    for i, fc in enumerate(fcs):
                        pm_, bpm = pms[i]
                        P.op("act", lambda e, i=i, pm_=pm_: e.activation(out=rs[i][:], in_=pm_[:], func=AF.Ln, bias=S["epsb"][:, 1:2]), reads=[bpm, S["b_epsb"]], writes=[b_rs[i]])
                    for i, fc in enumerate(fcs):
                        P.op("act", lambda e, i=i: e.activation(out=rs[i][:], in_=rs[i][:], func=AF.Exp, scale=-0.5), reads=[b_rs[i]], writes=[b_rs[i]])
                    for i, fc in enumerate(fcs):
                        pp, bp = pps[i]
                        P.op("dve", lambda e, i=i, fc=fc, pp=pp: e.scalar_tensor_tensor(out=dstT[:, fc, col0:col0 + TT], in0=pp[:], scalar=gqk[:, gcol:gcol + 1], in1=rs[i][:],
                                                                                 op0=ALU.mult, op1=ALU.mult),
                             reads=[bp, b_gqk, b_rs[i]], writes=[b_dst])

            with ExitStack() as p1s:
                kvw_sb, kvw_bufs = load_w_cols(p1s, "kvw_sb", kv_w_d.rearrange("(kc p) n -> p kc n", p=128), [128, 8, 2 * D],
                                               [(i * 512, (i + 1) * 512) for i in range(4)])
                amask = SB(p1s, "amask", [128, 640], F32); b_amask = Buf("amask")
                P.dma("sp", amask[:], amask_d[:, :], b_amask, writes=[b_amask])
                btmp = [SB(p1s, "p3btmp%d" % i, [128, 640], F32) for i in range(2)]; b_btmp = [Buf("btmp0"), Buf("btmp1")]
                xn = SB(p1s, "p3xn", [128, 4, D], BF16); S["xn"] = xn; S["b_xn"] = Buf("xn")
                biasv = biasT_d.rearrange("p (h x) -> p h x", h=16)

                def build_expB():
                    for h in range(16):
                        bt, bbt = btmp[h % 2], b_btmp[h % 2]
                        P.dma("pool", bt[:], biasv[:, h, :], bbt, writes=[bbt])
                        P.op("act", lambda e, bt=bt: e.activation(out=bt[:], in_=bt[:], func=AF.Exp), reads=[bbt], writes=[bbt])
                        P.op("dve", lambda e, h=h, bt=bt: e.tensor_tensor(out=expB[:, h, :], in0=bt[:], in1=amask[:], op=ALU.mult), reads=[bbt, b_amask], writes=[b_expB])
                P.op("dve", lambda e: e.memset(Va[:, :, :, 64:65], 1.0), writes=[b_Va])
                P.op("dve", lambda e: e.tensor_scalar(out=Va[:, 0:4, :, 64:65], in0=Va[:, 0:4, :, 64:65], scalar1=tflag[:, 4:5], scalar2=None, op0=ALU.mult),
                     reads=[b_tflag, b_Va], writes=[b_Va])
                for t in range(3, 8):
                    kt0 = t - 3
                    hrow = H[kt0 * TT:(kt0 + 1) * TT, :]
                    load_rows(xt, b_xt, hrow)
                    norm_T(xt, b_xt, 2, S)
                    hnT, b_hnT = S["hnT"], S["b_hnT"]
                    proj_headnorm(kvw_sb, kvw_bufs, hnT, b_hnT, KT, b_KT, kt0 * TT, 0)
                    for s in range(4):
                        for nh in range(2):
                            pp, bp = nextpz()
                            for kc in range(8):
                                P.op("pe", lambda e, kc=kc, s=s, nh=nh, pp=pp: e.matmul(pp[:], lhsT=hnT[:, kc, s * 128:(s + 1) * 128], rhs=kvw_sb[:, kc, D + nh * 512:D + (nh + 1) * 512],
                                                                                    start=(kc == 0), stop=(kc == 7)),
                                     reads=kvw_bufs(D + nh * 512, D + (nh + 1) * 512) + [b_hnT], writes=[bp], signal=(kc == 7))
                            if t == 3:
                                P.op("act", lambda e, s=s, nh=nh, pp=pp, kt0=kt0: e.activation(out=Va[:, kt0 * 4 + s, nh * 8:(nh + 1) * 8, 0:64], in_=pp[:].rearrange("p (h d) -> p h d", d=64),
                                                                                           func=AF.Identity, scale=tflag[:, 4:5]),
                                     reads=[bp, b_tflag], writes=[b_Va])
                            else:
                                P.op("act", lambda e, s=s, nh=nh, pp=pp, kt0=kt0: e.activation(out=Va[:, kt0 * 4 + s, nh * 8:(nh + 1) * 8, 0:64], in_=pp[:].rearrange("p (h d) -> p h d", d=64),
                                                                                           func=AF.Copy),
                                     reads=[bp], writes=[b_Va])
                    if t == 3:
                        build_expB()
                P.barrier()
            if stop == "p3a":
                return nc
            with ExitStack() as p2s:
                wq_sb, wq_bufs = load_w_cols(p2s, "wq_sb", wq_d.rearrange("(kc p) n -> p kc n", p=128), [128, 8, D], [(0, 512), (512, 1024)])
                wo_sb, b_wo = load_w(p2s, "wo_sb", wo_d.rearrange("(kc p) n -> p kc n", p=128), [128, 8, D], 2)
                S["g"], S["b_g"] = load_g(p2s, "p3g", 2)
                QT = SB(p2s, "QT", [128, 8, TT], BF16); b_QT = Buf("QT")
                S["rtmp"] = SB(p2s, "p3rtmp", [128, 512], F32); S["b_rtmp"] = Buf("rtmp")
                Eb = [SB(p2s, "p3E%d" % i, [128, 640], BF16) for i in range(3)]; b_E = [Buf("E%d" % i) for i in range(3)]
                Pb = [SB(p2s, "p3P%d" % i, [128, 640], BF16) for i in range(3)]; b_P = [Buf("P%d" % i) for i in range(3)]
                rec = SB(p2s, "p3rec", [128, 2, 4], F32); b_rec = [Buf("rec0"), Buf("rec1")]
                attn = SB(p2s, "p3attn", [128, D], BF16); b_attn = Buf("attn")
                attnT = SB(p2s, "p3attnT", [128, 8, TT], BF16); b_attnT = Buf("attnT")
                S["xn"] = attnT[:].rearrange("p a c -> p (a c)").rearrange("p (s d) -> p s d", s=4); S["b_xn"] = b_attnT
                for t in range(4, 8):
                    kt0 = t - 3
                    hrow = H[kt0 * TT:(kt0 + 1) * TT, :]
                    load_rows(xt, b_xt, hrow)
                    norm_T(xt, b_xt, 3, S)
                    hnT, b_hnT = S["hnT"], S["b_hnT"]
                    proj_headnorm(wq_sb, wq_bufs, hnT, b_hnT, QT, b_QT, 0, 1)
                    items = [(qg, h) for qg in range(4) for h in range(16)]

                    def emit_S(idx):
                        qg, h = items[idx]
                        G = kt0 * 4 + qg
                        fc, hf = h // 2, h % 2
                        pS_, bpS = pS[idx % 2], b_pS[idx % 2]
                        for kt in range(5):
                            P.op("pe", lambda e, kt=kt: e.matmul(pS_[:, kt * 128:(kt + 1) * 128],
                                                                 lhsT=KT[hf * 64:(hf + 1) * 64, fc, (G - 4 + kt) * 128:(G - 3 + kt) * 128],
                                                                 rhs=QT[hf * 64:(hf + 1) * 64, fc, qg * 128:(qg + 1) * 128], start=True, stop=True),
                                 reads=[b_KT, b_QT], writes=bpS, signal=(kt == 4))

                    emit_S(0)
                    for idx, (qg, h) in enumerate(items):
                        G = kt0 * 4 + qg
                        h4, hh = h // 4, h % 4
                        if idx + 1 < len(items):
                            emit_S(idx + 1)
                        pS_, bpS = pS[idx % 2], b_pS[idx % 2]
                        pO_, bpO = pO[h4 % 2], b_pO[h4 % 2]
                        E_, bE = Eb[idx % 3], b_E[idx % 3]
                        P_, bP = Pb[idx % 3], b_P[idx % 3]
                        P.op("act", lambda e, E_=E_, pS_=pS_: e.activation(out=E_[:], in_=pS_[:, 0:640], func=AF.Exp), reads=bpS, writes=[bE])
                        P.op("dve", lambda e, E_=E_, P_=P_, h=h: e.tensor_tensor(out=P_[:], in0=E_[:], in1=expB[:, h, :], op=ALU.mult), reads=[bE, b_expB], writes=[bP])
                        for kt in range(5):
                            P.op("pe", lambda e, kt=kt, hh=hh, h=h, G=G, P_=P_, pO_=pO_: e.matmul(pO_[:, hh, 0:65], lhsT=P_[:, kt * 128:(kt + 1) * 128], rhs=Va[:, G - 4 + kt, h, :],
                                                                                           start=(kt == 0), stop=(kt == 4)),
                                 reads=[bP, b_Va], writes=[bpO], signal=(kt == 4))
                        if hh == 3:
                            r_ = h4 % 2
                            P.op("dve", lambda e, r_=r_, pO_=pO_: e.reciprocal(out=rec[:, r_, :], in_=pO_[:, :, 64]), reads=[bpO], writes=[b_rec[r_]])
                            P.op("dve", lambda e, r_=r_, pO_=pO_, h4=h4: e.tensor_tensor(out=attn[:, h4 * 256:(h4 + 1) * 256].rearrange("p (h d) -> p h d", d=64), in0=pO_[:, :, 0:64],
                                                                                    in1=rec[:, r_, :].unsqueeze(2).to_broadcast([128, 4, 64]), op=ALU.mult),
                                 reads=[bpO, b_rec[r_]], writes=[b_attn])
                        if h == 15:
                            for fc in range(8):
                                pT, bT = S["pT"][fc % 2], S["b_pT"][fc % 2]
                                P.op("pe", lambda e, fc=fc, pT=pT: e.transpose(out=pT[:, 0:128], in_=attn[:, fc * 128:(fc + 1) * 128], identity=identb[:]), reads=[b_attn, b_identb], writes=[bT])
                                P.op("act", lambda e, fc=fc, pT=pT, qg=qg: e.activation(out=attnT[:, fc, qg * 128:(qg + 1) * 128], in_=pT[:, 0:128], func=AF.Copy), reads=[bT], writes=[b_attnT])

                    for s in range(4):
                        for nh in range(2):
                            pp, bp = nextpz()
                            for kc in range(8):
                                P.op("pe", lambda e, kc=kc, s=s, nh=nh, pp=pp: e.matmul(pp[:], lhsT=attnT[:, kc, s * 128:(s + 1) * 128], rhs=wo_sb[:, kc, nh * 512:(nh + 1) * 512],
                                                                                    start=(kc == 0), stop=(kc == 7)),
                                     reads=[b_attnT, b_wo], writes=[bp], signal=(kc == 7))
                            resid_out(xt, b_xt, s, nh, pp, bp, 2, S)
                        store_rows(xt, b_xt, hrow, s)
                P.barrier()
        if stop == "p3":
            return nc

        mlp_phase(1, range(4, 8), 4, 3, True)
    return nc


def _host_consts():
    ident = np.eye(128, dtype=np.float32)
    blk = np.zeros((128, 128), np.float32)
    blk[:64, :64] = 1.0 / 64
    blk[64:, 64:] = 1.0 / 64
    kk = np.arange(128)[:, None, None]
    kt = np.arange(5)[None, :, None]
    q = np.arange(128)[None, None, :]
    kidx = kt * 128 + kk
    ck = kidx // 64
    cq = q // 64
    amask = ((ck >= cq) & (ck <= cq + 8)).astype(np.float32)
    bidx = np.minimum(640 + q - kidx, 256)
    bidx = np.maximum(bidx, 0)
    return ident, blk, amask.reshape(128, 640), bidx


def _pvec(v, n):
    return np.ascontiguousarray(np.asarray(v, np.float32).reshape(n, 128).T)


def make_in_maps(inputs):
    f = lambda k: np.ascontiguousarray(np.asarray(inputs[k], dtype=np.float32))
    x = f("x"); c = f("c")
    ident, blk, amask, bidx = _host_consts()
    rel_bias = f("rel_bias")[0]
    biasT = rel_bias[:, bidx]
    biasT = np.ascontiguousarray(biasT.transpose(1, 0, 2, 3).reshape(128, 16 * 640))
    conv_w = f("lru_conv_w")[0]
    convw = np.ascontiguousarray(conv_w.reshape(4, NJ, 128).transpose(2, 1, 0).reshape(128, NJ * 4))
    vec_parts = {
        "n1g0": _pvec(f("norm1_g")[0], 8), "n2g0": _pvec(f("norm2_g")[0], 8), "kvg": _pvec(f("kv_norm_g"), 8),
        "n1g1": _pvec(f("norm1_g")[1], 8), "n2g1": _pvec(f("norm2_g")[1], 8), "convw": convw,
        "convb": _pvec(f("lru_conv_b")[0], NJ), "gab": _pvec(f("lru_gate_a_b")[0], NJ), "gxb": _pvec(f("lru_gate_x_b")[0], NJ),
        "lam": _pvec(f("lru_lambda")[0], NJ),
        "gk": np.tile(f("k_norm_g"), 2).reshape(128, 1), "gq": np.tile(f("q_norm_g")[0], 2).reshape(128, 1),
    }
    vecs = np.zeros((128, NV), np.float32)
    for k, (o, n) in VOFF.items():
        vecs[:, o:o + n] = vec_parts[k]
    shared = {
        "vecs": vecs, "ada_w": f("ada_w"), "ada_b": f("ada_b"), "kv_ada_w": f("kv_ada_w"), "kv_ada_b": f("kv_ada_b").reshape(1, -1),
        "w_in": f("lru_w_in")[0], "ga_w": f("lru_gate_a_w")[0], "gx_w": f("lru_gate_x_w")[0], "w_out": f("lru_w_out")[0],
        "mlp_w1": f("mlp_w1"), "mlp_w2": f("mlp_w2"), "kv_w": f("kv_w"), "wq": f("attn_w_q")[0], "wo": f("attn_w_o")[0],
        "biasT": biasT, "amask": amask, "ident": ident, "blk64": blk,
    }
    maps = []
    for core in range(8):
        b, half = core // 2, core % 2
        if half == 1:
            xin = x[b]
        else:
            xin = np.concatenate([np.zeros((2048, D), np.float32), x[b, :2048]], axis=0)
        tflag = np.ones((128, NTILE + 1), np.float32)
        tflag[:, 4] = float(half)
        m = dict(shared)
        m["xin"] = np.ascontiguousarray(xin)
        m["ct"] = _pvec(c[b], 8)
        m["tflag"] = tflag
        maps.append(m)
    return maps


def kernel(**inputs):
    nc = build()
    maps = make_in_maps(inputs)
    res = run_bass_kernel_spmd(nc, maps, core_ids=list(range(8)))
    out = np.zeros((4, 4096, D), np.float32)
    for core in range(8):
        b, half = core // 2, core % 2
        out[b, half * 2048:(half + 1) * 2048] = res.results[core]["out"]
    return out
```

```python
import numpy as np
from contextlib import ExitStack
import concourse.bass as bass
import concourse.mybir as mybir
from concourse.bass_utils import run_bass_kernel_spmd

F32 = mybir.dt.float32
BF16 = mybir.dt.bfloat16
AF = mybir.ActivationFunctionType
ALU = mybir.AluOpType
AX = mybir.AxisListType

D = 1024
W = 1408
NJ = 11
DFF = 4096
TT = 512
NTILE = 8
NKV = 5
EPS = 1e-6


class Buf:
    __slots__ = ("name", "w", "r", "dsem", "dcnt")

    def __init__(self, name):
        self.name = name
        self.w = None
        self.r = []
        self.dsem = None
        self.dcnt = 0


class Prog:
    def __init__(self, nc, stack):
        self.nc = nc
        self.stack = stack
        self.eng = {"pe": nc.tensor, "act": nc.scalar, "dve": nc.vector, "pool": nc.gpsimd, "sp": nc.sync}
        self.sem = {k: stack.enter_context(nc.semaphore("s_" + k)) for k in self.eng}
        self.cnt = {k: 0 for k in self.eng}
        self.seen = {k: {} for k in self.eng}
        self.pending = {k: False for k in self.eng}
        self.dbufs = []
        self.nsem = 0
        self.swq = []
        self.swq_sum = 0

    def _wait(self, e, tok):
        if tok is None:
            return
        sem, val, key = tok
        if key == "pe" and e == "pe":
            return
        if self.seen[e].get(key, 0) >= val:
            return
        self.eng[e].wait_ge(sem, val)
        self.seen[e][key] = val

    def _hazards(self, e, reads, writes, group_sem=None):
        for b in reads:
            self._wait(e, b.w)
        for b in writes:
            if not (group_sem is not None and b.w is not None and b.w[0] is group_sem):
                self._wait(e, b.w)
            for t in b.r:
                self._wait(e, t)

    def op(self, e, ins_fn, reads=(), writes=(), signal=True):
        self._hazards(e, reads, writes)
        ins = ins_fn(self.eng[e])
        if signal:
            self.cnt[e] += 1
            ins.then_inc(self.sem[e], 1)
            tok = (self.sem[e], self.cnt[e], e)
            self.pending[e] = False
        else:
            assert e == "pe"
            tok = (self.sem[e], self.cnt[e] + 1, e)
            self.pending[e] = True
        for b in reads:
            b.r.append(tok)
        for b in writes:
            b.w = tok
            b.r = []
        return tok

    def dma(self, e, out, in_, sembuf, reads=(), writes=(), group=False, **kw):
        if sembuf.dsem is None:
            sembuf.dsem = self.stack.enter_context(self.nc.semaphore("d%d_%s" % (self.nsem, sembuf.name)))
            self.nsem += 1
            self.dbufs.append(sembuf)
        self._hazards(e, reads, writes, group_sem=sembuf.dsem if group else None)
        if e == "pool":
            nd = 1
            for d_ in tuple(out.shape)[:-1]:
                nd *= int(d_)
            nd = max(4, (nd + 7) // 8)
            while self.swq and self.swq_sum + nd > 448:
                sem0 = self.swq[0][0][0]
                last = max(i for i, (tk, _) in enumerate(self.swq) if tk[0] is sem0)
                self._wait(e, self.swq[last][0])
                keep = []
                for i, (tk, n0) in enumerate(self.swq):
                    if tk[0] is sem0 and i <= last:
                        self.swq_sum -= n0
                    else:
                        keep.append((tk, n0))
                self.swq = keep
        ins = self.eng[e].dma_start(out=out, in_=in_, **kw)
        sembuf.dcnt += 16
        ins.then_inc(sembuf.dsem, 16)
        tok = (sembuf.dsem, sembuf.dcnt, ("d", id(sembuf)))
        if e == "pool":
            self.swq.append((tok, nd))
            self.swq_sum += nd
        for b in reads:
            b.r.append(tok)
        for b in writes:
            b.w = tok
            b.r = []
        return tok

    def barrier(self, engines=None):
        toks = [(self.sem[k], self.cnt[k], k) for k in self.eng if self.cnt[k] > 0]
        toks += [(b.dsem, b.dcnt, ("d", id(b))) for b in self.dbufs]
        for e in (engines or self.eng):
            for t in toks:
                if t[2] == e:
                    continue
                self._wait(e, t)
        assert not self.pending["pe"]


def gate_pieces():
    pcs = []
    for n in range(16):
        lo, hi = 88 * n, 88 * n + 88
        for ci in range(NJ):
            r0, r1 = max(lo, 128 * ci), min(hi, 128 * ci + 128)
            if r0 >= r1:
                continue
            for co in range(NJ):
                c0, c1 = max(lo, 128 * co), min(hi, 128 * co + 128)
                if c0 >= c1:
                    continue
                d = ci - co + 1
                assert 0 <= d <= 2
                pcs.append((n, r0 - lo, r1 - lo, c0 - lo, c1 - lo, ci, co, d, r0 - 128 * ci, r1 - 128 * ci, c0 - 128 * co, c1 - 128 * co))
    return pcs


def gate_nbrs():
    nb = {co: set() for co in range(NJ)}
    for p in gate_pieces():
        nb[p[6]].add(p[5])
    return {co: sorted(v) for co, v in nb.items()}


VOFF = {}
_o = 0
for _name, _n in [("n1g0", 8), ("n2g0", 8), ("kvg", 8), ("n1g1", 8), ("n2g1", 8), ("convw", 44), ("convb", 11),
                  ("gab", 11), ("gxb", 11), ("lam", 11), ("gk", 1), ("gq", 1)]:
    VOFF[_name] = (_o, _n)
    _o += _n
NV = _o


def build(stop=None):
    nc = bass.Bass("TRN2", target_bir_lowering=False)
    dt_in = lambda name, shape: nc.dram_tensor(name, shape, F32, kind="ExternalInput").ap()
    xin = dt_in("xin", [NTILE * TT, D])
    ct_d = dt_in("ct", [128, 8])
    tflag_d = dt_in("tflag", [128, NTILE + 1])
    vecs_d = dt_in("vecs", [128, NV])
    ada_w = dt_in("ada_w", [2, D, 6 * D])
    ada_b = dt_in("ada_b", [2, 6 * D])
    kv_ada_w = dt_in("kv_ada_w", [D, 2 * D])
    kv_ada_b = dt_in("kv_ada_b", [1, 2 * D])
    w_in_d = dt_in("w_in", [D, 2 * W])
    ga_w_d = dt_in("ga_w", [16, 88, 88])
    gx_w_d = dt_in("gx_w", [16, 88, 88])
    w_out_d = dt_in("w_out", [W, D])
    w1_d = dt_in("mlp_w1", [2, D, DFF])
    w2_d = dt_in("mlp_w2", [2, DFF, D])
    kv_w_d = dt_in("kv_w", [D, 2 * D])
    wq_d = dt_in("wq", [D, D])
    wo_d = dt_in("wo", [D, D])
    biasT_d = dt_in("biasT", [128, 16 * 5 * 128])
    amask_d = dt_in("amask", [128, 5 * 128])
    ident_d = dt_in("ident", [128, 128])
    blk_d = dt_in("blk64", [128, 128])
    out_d = nc.dram_tensor("out", [4 * TT, D], F32, kind="ExternalOutput").ap()
    H = nc.dram_tensor("Hs", [NKV * TT, D], F32, kind="ExternalOutput" if stop else "Internal").ap()
    Gs = nc.dram_tensor("Gs", [4, 128, D], F32, kind="Internal").ap()

    with ExitStack() as top:
        P = Prog(nc, top)

        def SB(stack, name, shape, dt):
            return stack.enter_context(nc.sbuf_tensor("sb_" + name, shape, dt))

        def PS(stack, name, shape, dt):
            return stack.enter_context(nc.psum_tensor("ps_" + name, shape, dt))

        vecs = SB(top, "vecs", [128, NV], F32); b_vecs = Buf("vecs")
        tflag = SB(top, "tflag", [128, NTILE + 1], F32); b_tflag = Buf("tflag")
        identb = SB(top, "identb", [128, 128], BF16); b_identb = Buf("identb")
        Asc = SB(top, "Asc", [128, 5, 8], F32); Ash = SB(top, "Ash", [128, 5, 8], F32); b_mod = Buf("modvec")
        cch = SB(top, "cch", [128, 2, NJ], F32); b_cch = Buf("cch")
        nbias = SB(top, "nbias", [128, 2, NJ], F32); b_nbias = Buf("nbias")
        gqk = SB(top, "gqk", [128, 2], F32); b_gqk = Buf("gqk")

        P.dma("sp", vecs[:], vecs_d[:, :], b_vecs, writes=[b_vecs])
        P.dma("sp", tflag[:], tflag_d[:, :], b_tflag, writes=[b_tflag])
        P.dma("pool", identb[:], ident_d[:, :], b_identb, writes=[b_identb])

        def V(name):
            o, n = VOFF[name]
            return vecs[:, o:o + n]

        with ExitStack() as ph:
            ct = SB(ph, "ct", [128, 8], F32); b_ct = Buf("ct")
            t8a = SB(ph, "t8a", [128, 8], F32); t8b = SB(ph, "t8b", [128, 8], F32); b_t8 = Buf("t8")
            cact = SB(ph, "cact", [128, 8], F32); b_cact = Buf("cact")
            lc = SB(ph, "lc", [128, 8, 128], BF16); b_lc = Buf("lc")
            identf = SB(ph, "identf", [128, 128], F32); b_identf = Buf("identf")
            modrow = [SB(ph, "modrow%d" % i, [128, 6 * D], F32) for i in range(2)]
            kvrow = SB(ph, "kvrow", [128, 2 * D], F32)
            b_rows = [Buf("modrow0"), Buf("modrow1"), Buf("kvrow")]
            wch = [SB(ph, "wch%d" % i, [128, 8, 512], BF16) for i in range(3)]; b_wch = [Buf("wch%d" % i) for i in range(3)]
            dtmp = SB(ph, "dtmp", [128, 8, 128], F32); b_dtmp = Buf("dtmp")
            t11 = SB(ph, "t11", [128, NJ], F32); b_t11 = Buf("t11")
            pm = [PS(ph, "pm%d" % i, [128, 512], F32) for i in range(2)]; b_pm = [Buf("pm%d" % i) for i in range(2)]

            P.dma("sp", ct[:], ct_d[:, :], b_ct, writes=[b_ct])
            P.dma("sp", identf[:], ident_d[:, :], b_identf, writes=[b_identf])
            rows = [modrow[0], modrow[1], kvrow]
            P.dma("sp", modrow[0][:], ada_b[0:1, :].partition_broadcast(128), b_rows[0], writes=[b_rows[0]])
            P.dma("sp", modrow[1][:], ada_b[1:2, :].partition_broadcast(128), b_rows[1], writes=[b_rows[1]])
            P.dma("sp", kvrow[:], kv_ada_b[0:1, :].partition_broadcast(128), b_rows[2], writes=[b_rows[2]])
            P.op("act", lambda e: e.activation(out=t8a[:], in_=ct[:], func=AF.Exp, scale=-1.0), reads=[b_ct], writes=[b_t8])
            P.op("act", lambda e: e.activation(out=t8b[:], in_=t8a[:], func=AF.Ln, bias=1.0), reads=[b_t8], writes=[b_t8])
            P.op("act", lambda e: e.activation(out=t8a[:], in_=t8b[:], func=AF.Exp, scale=-1.0), reads=[b_t8], writes=[b_t8])
            P.op("dve", lambda e: e.tensor_tensor(out=cact[:], in0=t8a[:], in1=ct[:], op=ALU.mult), reads=[b_t8, b_ct], writes=[b_cact])
            P.op("dve", lambda e: e.tensor_copy(out=lc[:], in_=cact[:].unsqueeze(2).to_broadcast([128, 8, 128])), reads=[b_cact], writes=[b_lc])
            P.op("act", lambda e: e.activation(out=t11[:], in_=V("lam"), func=AF.Exp, scale=-1.0), reads=[b_vecs], writes=[b_t11])
            P.op("act", lambda e: e.activation(out=t11[:], in_=t11[:], func=AF.Ln, bias=1.0), reads=[b_t11], writes=[b_t11])
            P.op("dve", lambda e: e.tensor_scalar(out=cch[:, 0, :], in0=t11[:], scalar1=-8.0, scalar2=None, op0=ALU.mult), reads=[b_t11], writes=[b_cch])
            P.op("dve", lambda e: e.tensor_scalar(out=cch[:, 1, :], in0=t11[:], scalar1=-16.0, scalar2=None, op0=ALU.mult), reads=[b_t11], writes=[b_cch])
            P.op("dve", lambda e: e.tensor_scalar(out=nbias[:, 0, :], in0=V("gab"), scalar1=-1.0, scalar2=None, op0=ALU.mult), reads=[b_vecs], writes=[b_nbias])
            P.op("dve", lambda e: e.tensor_scalar(out=nbias[:, 1, :], in0=V("gxb"), scalar1=-1.0, scalar2=None, op0=ALU.mult), reads=[b_vecs], writes=[b_nbias])
            P.op("dve", lambda e: e.tensor_copy(out=gqk[:, 0:1], in_=V("gk")), reads=[b_vecs], writes=[b_gqk])
            P.op("dve", lambda e: e.tensor_scalar(out=gqk[:, 1:2], in0=V("gq"), scalar1=0.125, scalar2=None, op0=ALU.mult), reads=[b_vecs], writes=[b_gqk])

            srcs = [(ada_w[0], 6 * D, 0), (ada_w[1], 6 * D, 1), (kv_ada_w, 2 * D, 2)]
            it = 0
            for (wsrc, ncols, ri) in srcs:
                wv = wsrc.rearrange("(kc p) n -> p kc n", p=128)
                for j in range(ncols // 512):
                    wb_, bb_ = wch[it % 3], b_wch[it % 3]
                    pp, bp = pm[it % 2], b_pm[it % 2]
                    P.dma("pool", wb_[:], wv[:, :, j * 512:(j + 1) * 512], bb_, writes=[bb_])
                    for kc in range(8):
                        P.op("pe", lambda e, kc=kc, wb_=wb_, pp=pp: e.matmul(pp[:], lhsT=lc[:, kc, :], rhs=wb_[:, kc, :], start=(kc == 0), stop=(kc == 7)),
                             reads=[b_lc, bb_], writes=[bp], signal=(kc == 7))
                    rr = rows[ri]
                    P.op("dve", lambda e, rr=rr, pp=pp, j=j: e.tensor_tensor(out=rr[:, j * 512:(j + 1) * 512], in0=pp[:], in1=rr[:, j * 512:(j + 1) * 512], op=ALU.add),
                         reads=[bp, b_rows[ri]], writes=[b_rows[ri]])
                    it += 1

            specs = [(0, modrow[0], 0, 1, "n1g0"), (1, modrow[0], 3, 4, "n2g0"), (2, kvrow, 0, 1, "kvg"),
                     (3, modrow[1], 0, 1, "n1g1"), (4, modrow[1], 3, 4, "n2g1")]
            rbuf = {0: b_rows[0], 1: b_rows[0], 2: b_rows[2], 3: b_rows[1], 4: b_rows[1]}
            for (m, rr, sseg, cseg, gname) in specs:
                for (seg, dst) in [(sseg, Ash), (cseg, t8a)]:
                    P.op("dve", lambda e, rr=rr, seg=seg: e.tensor_tensor(out=dtmp[:], in0=rr[:, seg * D:(seg + 1) * D].rearrange("p (g k) -> p g k", k=128),
                                                                     in1=identf[:].unsqueeze(1).to_broadcast([128, 8, 128]), op=ALU.mult),
                         reads=[rbuf[m], b_identf], writes=[b_dtmp])
                    if dst is Ash:
                        P.op("dve", lambda e, m=m: e.tensor_reduce(out=Ash[:, m, :], in_=dtmp[:], axis=AX.X, op=ALU.add), reads=[b_dtmp], writes=[b_mod])
                    else:
                        P.op("dve", lambda e: e.tensor_reduce(out=t8a[:], in_=dtmp[:], axis=AX.X, op=ALU.add), reads=[b_dtmp], writes=[b_t8])
                P.op("dve", lambda e: e.tensor_scalar(out=t8b[:], in0=t8a[:], scalar1=1.0, scalar2=32.0, op0=ALU.add, op1=ALU.mult), reads=[b_t8], writes=[b_t8])
                P.op("dve", lambda e, m=m, gname=gname: e.tensor_tensor(out=Asc[:, m, :], in0=t8b[:], in1=V(gname), op=ALU.mult), reads=[b_t8, b_vecs], writes=[b_mod])
            for gi, (rr, seg, rb) in enumerate([(modrow[0], 2, b_rows[0]), (modrow[0], 5, b_rows[0]), (modrow[1], 2, b_rows[1]), (modrow[1], 5, b_rows[1])]):
                P.dma("sp", Gs[gi], rr[:, seg * D:(seg + 1) * D], rb, reads=[rb])
            P.barrier()

        def load_w(stack, name, src_view, shape, nsplit):
            t = SB(stack, name, shape, BF16)
            b = Buf(name)
            a = shape[1]
            step = (a + nsplit - 1) // nsplit
            for s0 in range(0, a, step):
                s1 = min(a, s0 + step)
                P.dma("pool", t[:, s0:s1, :], src_view[:, s0:s1, :], b, writes=[b], group=True)
            return t, b

        def load_w_cols(stack, name, src_view, shape, splits):
            t = SB(stack, name, shape, BF16)
            blocks = []
            for (c0, c1) in splits:
                b = Buf("%s_%d" % (name, c0))
                P.dma("pool", t[:, :, c0:c1], src_view[:, :, c0:c1], b, writes=[b])
                blocks.append((c0, c1, b))

            def bufs(c0, c1):
                return [b for (a0, a1, b) in blocks if a0 < c1 and c0 < a1]
            return t, bufs

        def norm_T(xt, b_xt, m, S):
            norm_a(xt, b_xt, S)
            norm_b(m, S)

        def norm_a(xt, b_xt, S):
            bx = b_xt if isinstance(b_xt, list) else [b_xt] * 4
            bxn = S["b_xn"] if isinstance(S["b_xn"], list) else [S["b_xn"]]
            bss = S["b_ss4"]
            for s in range(4):
                P.op("act", lambda e, s=s: e.activation(out=S["junk"][:], in_=xt[:, s, :], func=AF.Square, accum_out=S["ss"][:, s:s + 1]), reads=[bx[s]], writes=[S["b_junk"], bss[s]])
            for s in range(4):
                P.op("act", lambda e, s=s: e.activation(out=S["ss"][:, 4 + s:5 + s], in_=S["ss"][:, s:s + 1], func=AF.Ln, bias=S["epsb"][:, 0:1]), reads=[bss[s], S["b_epsb"]], writes=[bss[s]])
                P.op("act", lambda e, s=s: e.activation(out=S["ss"][:, 8 + s:9 + s], in_=S["ss"][:, 4 + s:5 + s], func=AF.Exp, scale=-0.5), reads=[bss[s]], writes=[bss[s]])
                P.op("dve", lambda e, s=s: e.tensor_scalar(out=S["xn"][:, s, :], in0=xt[:, s, :], scalar1=S["ss"][:, 8 + s:9 + s], scalar2=None, op0=ALU.mult),
                     reads=[bx[s], bss[s]], writes=bxn)

        def norm_b(m, S):
            for fc in range(8):
                pT, bT = S["pT"][fc % 2], S["b_pT"][fc % 2]
                for s in range(4):
                    P.op("pe", lambda e, s=s, fc=fc, pT=pT: e.transpose(out=pT[:, s * 128:(s + 1) * 128], in_=S["xn"][:, s, fc * 128:(fc + 1) * 128], identity=identb[:]),
                         reads=(S["b_xn"] if isinstance(S["b_xn"], list) else [S["b_xn"]]) + [b_identb], writes=[bT], signal=(s == 3))
                P.op("act", lambda e, fc=fc, pT=pT: e.activation(out=S["hnT"][:, fc, :], in_=pT[:, 0:TT], func=AF.Identity, scale=Asc[:, m, fc:fc + 1], bias=Ash[:, m, fc:fc + 1]),
                     reads=[bT, b_mod], writes=[S["b_hnT"]])

        def alloc_norm(stack, pfx):
            S = {}
            S["junk"] = SB(stack, pfx + "junk", [128, D], BF16); S["b_junk"] = Buf("junk")
            S["ss"] = SB(stack, pfx + "ss", [128, 12], F32); S["b_ss"] = Buf("ss"); S["b_ss4"] = [Buf("ss%d" % i) for i in range(4)]
            S["epsb"] = SB(stack, pfx + "epsb", [128, 2], F32); S["b_epsb"] = Buf("epsb")
            S["hnT"] = SB(stack, pfx + "hnT", [128, 8, TT], BF16); S["b_hnT"] = Buf("hnT")
            S["pT"] = [PS(stack, pfx + "pT%d" % i, [128, 2 * TT], BF16) for i in range(2)]; S["b_pT"] = [Buf("pT0"), Buf("pT1")]
            P.op("dve", lambda e: e.memset(S["epsb"][:, 0:1], 1024.0 * EPS), writes=[S["b_epsb"]])
            P.op("dve", lambda e: e.memset(S["epsb"][:, 1:2], EPS), writes=[S["b_epsb"]])
            return S

        def load_g(stack, name, gi):
            g = SB(stack, name, [128, D], F32); b = Buf(name)
            P.dma("sp", g[:], Gs[gi], b, writes=[b])
            return g, b

        def resid_out(xt, b_xt, s, nh, po, bpo, gi, S):
            P.op("dve", lambda e: e.tensor_tensor(out=S["rtmp"][:], in0=po[:], in1=S["g"][:, nh * 512:(nh + 1) * 512], op=ALU.mult),
                 reads=[bpo, S["b_g"]], writes=[S["b_rtmp"]])
            bx = b_xt[s] if isinstance(b_xt, list) else b_xt
            P.op("dve", lambda e: e.tensor_tensor(out=xt[:, s, nh * 512:(nh + 1) * 512], in0=S["rtmp"][:], in1=xt[:, s, nh * 512:(nh + 1) * 512], op=ALU.add),
                 reads=[S["b_rtmp"], bx], writes=[bx])

        def load_rows(xt, b_xt, src):
            for s in range(4):
                P.dma("sp", xt[:, s, :], src[s * 128:(s + 1) * 128, :], b_xt[s], writes=[b_xt[s]])

        def store_rows(xt, b_xt, dst, s):
            P.dma("sp", dst[s * 128:(s + 1) * 128, :], xt[:, s, :], b_xt[s], reads=[b_xt[s]])

        nbrs = gate_nbrs()
        with ExitStack() as ph:
            w_in_sb, w_in_bufs = load_w_cols(ph, "w_in_sb", w_in_d.rearrange("(kc p) n -> p kc n", p=128), [128, 8, 2 * W],
                                             [(W, W + 512), (W + 512, 2 * W), (0, 704), (704, W)])
            wg = SB(ph, "wg", [128, 2, NJ * 3, 128], BF16); b_wg = Buf("wg")
            P.op("dve", lambda e: e.memset(wg[:], 0.0), writes=[b_wg])
            for gidx, gsrc in enumerate([ga_w_d, gx_w_d]):
                for (n, r0, r1, c0, c1, ci, co, d, p0, p1, q0, q1) in gate_pieces():
                    P.dma("pool", wg[p0:p1, gidx, co * 3 + d, q0:q1], gsrc[n, r0:r1, c0:c1], b_wg, writes=[b_wg], group=True)
            w_out_sb, b_w_out = load_w(ph, "w_out_sb", w_out_d.rearrange("(jc p) n -> p jc n", p=128), [128, NJ, D], 2)

            S = alloc_norm(ph, "p1")
            S["g"], S["b_g"] = load_g(ph, "p1g", 0)
            xts = [SB(ph, "p1xt%d" % i, [128, 4, D], F32) for i in range(2)]; b_xts = [Buf("xt0"), Buf("xt1")]
            yb = SB(ph, "p1yb", [128, NJ, TT], BF16); b_yb = Buf("yb")
            NXR = 2
            xrb = [SB(ph, "p1xrb%d" % i, [128, TT + 3], F32) for i in range(NXR)]; b_xrb = [Buf("xrb%d" % i) for i in range(NXR)]
            halo = SB(ph, "p1halo", [128, NJ, 3], F32); b_halo = [Buf("halo%d" % j) for j in range(NJ)]
            xc = SB(ph, "p1xc", [128, NJ, TT], F32); b_xc = [Buf("xc%d" % j) for j in range(NJ)]
            xcb = SB(ph, "p1xcb", [128, NJ, TT], BF16); b_xcb = [Buf("xcb%d" % j) for j in range(NJ)]
            S["xn"] = xcb[:, 0:8, :].rearrange("p a c -> p (a c)").rearrange("p (s d) -> p s d", s=4)
            S["b_xn"] = b_xcb[0:8]
            state = SB(ph, "p1state", [128, NJ], F32); b_state = Buf("state")
            NTMP = 3
            tA = [SB(ph, "p1tA%d" % i, [128, TT], F32) for i in range(NTMP)]; b_tA = [Buf("tA%d" % i) for i in range(NTMP)]
            tQ = [SB(ph, "p1tQ%d" % i, [128, TT], F32) for i in range(NTMP)]; b_tQ = [Buf("tQ%d" % i) for i in range(NTMP)]
            tG, b_tG = tA, b_tA
            NGRP = 6
            Rst = SB(ph, "p1R", [128, NGRP, TT], F32); b_R = [Buf("R%d" % i) for i in range(NGRP)]
            S["rtmp"] = tA[0]; S["b_rtmp"] = b_tA[0]
            pz = [PS(ph, "p1pz%d" % i, [128, TT], F32) for i in range(6)]; b_pz = [Buf("pz%d" % i) for i in range(6)]
            pzi = [0]

            def nextpz():
                i = pzi[0] % 6
                pzi[0] += 1
                return pz[i], b_pz[i]

            P.op("dve", lambda e: e.memset(halo[:], 0.0), writes=b_halo)
            P.op("dve", lambda e: e.memset(state[:], 0.0), writes=[b_state])
            cw = V("convw")
            cb = V("convb")

            def load_x(t):
                P.dma("sp", xts[t % 2][:], xin[t * TT:(t + 1) * TT, :].rearrange("(s p) d -> p s d", p=128), b_xts[t % 2], writes=[b_xts[t % 2]])

            ntile_p1 = NTILE if stop != "p1a" else 5
            hnT, b_hnT = S["hnT"], S["b_hnT"]
            gab, gxb = V("gab"), V("gxb")

            pending_cast = []

            def cast_chunk(j):
                P.op("act", lambda e: e.activation(out=xcb[:, j, :], in_=xc[:, j, :], func=AF.Copy), reads=[b_xc[j]], writes=[b_xcb[j]])

            def conv_part(t, j0, j1):
                def halo_in(j):
                    xr, bxr = xrb[j % NXR], b_xrb[j % NXR]
                    P.op("pool", lambda e: e.tensor_copy(out=xr[:, 0:3], in_=halo[:, j, :]), reads=[b_halo[j]], writes=[bxr])

                if j0 == 0:
                    halo_in(0)
                for j in range(j0, j1):
                    oc = NJ + j
                    pp, bp = nextpz()
                    for kc in range(8):
                        P.op("pe", lambda e, kc=kc, oc=oc, pp=pp: e.matmul(pp[:], lhsT=w_in_sb[:, kc, oc * 128:(oc + 1) * 128], rhs=hnT[:, kc, :], start=(kc == 0), stop=(kc == 7)),
                             reads=w_in_bufs(oc * 128, oc * 128 + 128) + [b_hnT], writes=[bp], signal=(kc == 7))
                    xr, bxr = xrb[j % NXR], b_xrb[j % NXR]
                    P.op("act", lambda e, xr=xr, pp=pp: e.activation(out=xr[:, 3:TT + 3], in_=pp[:], func=AF.Copy), reads=[bp], writes=[bxr])
                    P.op("act", lambda e, pp=pp, j=j: e.activation(out=xc[:, j, :], in_=pp[:], func=AF.Identity, scale=cw[:, j * 4 + 3:j * 4 + 4], bias=cb[:, j:j + 1]),
                         reads=[bp, b_vecs], writes=[b_xc[j]])
                    if j + 1 < NJ:
                        halo_in(j + 1)
                    while pending_cast:
                        cast_chunk(pending_cast.pop(0))
                    for k in range(3):
                        P.op("dve", lambda e, xr=xr, j=j, k=k: e.scalar_tensor_tensor(out=xc[:, j, :], in0=xr[:, k:k + TT], scalar=cw[:, j * 4 + k:j * 4 + k + 1], in1=xc[:, j, :],
                                                                                  op0=ALU.mult, op1=ALU.add),
                             reads=[bxr, b_xc[j], b_vecs], writes=[b_xc[j]])
                    P.op("pool", lambda e, xr=xr, j=j: e.tensor_scalar(out=halo[:, j, :], in0=xr[:, TT:TT + 3], scalar1=tflag[:, t + 1:t + 2], scalar2=None, op0=ALU.mult),
                         reads=[bxr, b_tflag], writes=[b_halo[j]])
                    pending_cast.append(j)
                if j1 == NJ:
                    while pending_cast:
                        cast_chunk(pending_cast.pop(0))

            def gates_part(t):
                full = t >= 3
                if full:
                    for j in range(NJ):
                        pp, bp = nextpz()
                        for kc in range(8):
                            P.op("pe", lambda e, kc=kc, j=j, pp=pp: e.matmul(pp[:], lhsT=w_in_sb[:, kc, j * 128:(j + 1) * 128], rhs=hnT[:, kc, :], start=(kc == 0), stop=(kc == 7)),
                                 reads=w_in_bufs(j * 128, j * 128 + 128) + [b_hnT], writes=[bp], signal=(kc == 7))
                        P.op("act", lambda e, j=j, pp=pp: e.activation(out=yb[:, j, :], in_=pp[:], func=AF.Gelu_apprx_tanh), reads=[bp], writes=[b_yb])
                for gno, grp in enumerate([list(range(0, NGRP)), list(range(NGRP, NJ))]):
                    for gi_, co in enumerate(grp):
                        pa, bpa = nextpz()
                        px, bpx = nextpz()
                        cis = nbrs[co]
                        for gidx, (pg, bpg) in enumerate([(pa, bpa), (px, bpx)]):
                            for n_, ci in enumerate(cis):
                                d = ci - co + 1
                                P.op("pe", lambda e, pg=pg, gidx=gidx, d=d, ci=ci, n_=n_, co=co, cis=cis: e.matmul(pg[:], lhsT=wg[:, gidx, co * 3 + d, :], rhs=xcb[:, ci, :],
                                                                                                         start=(n_ == 0), stop=(n_ == len(cis) - 1)),
                                     reads=[b_wg, b_xcb[ci]], writes=[bpg], signal=(n_ == len(cis) - 1))
                        I_, bI = tQ[co % NTMP], b_tQ[co % NTMP]
                        P.op("act", lambda e, pa=pa, gi_=gi_, co=co: e.activation(out=Rst[:, gi_, :], in_=pa[:], func=AF.Sigmoid, bias=gab[:, co:co + 1]), reads=[bpa, b_vecs], writes=[b_R[gi_]])
                        P.op("act", lambda e, px=px, I_=I_, co=co: e.activation(out=I_[:], in_=px[:], func=AF.Sigmoid, bias=gxb[:, co:co + 1]), reads=[bpx, b_vecs], writes=[bI])
                        P.op("pool", lambda e, I_=I_, co=co: e.tensor_tensor(out=xc[:, co, :], in0=I_[:], in1=xc[:, co, :], op=ALU.mult), reads=[bI, b_xc[co]], writes=[b_xc[co]])

                    def stage_b(co, gi_, k_):
                        A_, bA = tA[k_], b_tA[k_]
                        Q_, bQ = tQ[k_], b_tQ[k_]
                        return [
                            lambda: P.op("act", lambda e: e.activation(out=A_[:], in_=Rst[:, gi_, :], func=AF.Exp, scale=cch[:, 0, co:co + 1]), reads=[b_R[gi_], b_cch], writes=[bA]),
                            lambda: P.op("pool", lambda e: e.tensor_tensor(out=Q_[:], in0=A_[:], in1=A_[:], op=ALU.mult), reads=[bA], writes=[bQ]),
                            lambda: P.op("act", lambda e: e.activation(out=Q_[:], in_=Q_[:], func=AF.Ln, scale=-1.0, bias=1.000001), reads=[bQ], writes=[bQ]),
                            lambda: P.op("act", lambda e: e.activation(out=Q_[:], in_=Q_[:], func=AF.Exp, scale=0.5), reads=[bQ], writes=[bQ]),
                            lambda: P.op("pool", lambda e: e.tensor_tensor(out=Q_[:], in0=Q_[:], in1=xc[:, co, :], op=ALU.mult), reads=[bQ, b_xc[co]], writes=[bQ]),
                            lambda: P.op("dve", lambda e: e.tensor_tensor_scan(out=xc[:, co, :], data0=A_[:], data1=Q_[:], initial=state[:, co:co + 1], op0=ALU.mult, op1=ALU.add),
                                         reads=[bA, bQ, b_state], writes=[b_xc[co]]),
                        ]

                    if gno == 1 and t + 1 < ntile_p1:
                        norm_a(xts[(t + 1) % 2], b_xts[(t + 1) % 2], S)
                    for c3 in range(0, len(grp), NTMP):
                        chains = [stage_b(co, c3 + k_, k_) for k_, co in enumerate(grp[c3:c3 + NTMP])]
                        for si in range(len(chains[0])):
                            for ch in chains:
                                ch[si]()
                P.op("dve", lambda e: e.tensor_scalar(out=state[:], in0=xc[:, :, TT - 1], scalar1=tflag[:, t + 1:t + 2], scalar2=None, op0=ALU.mult),
                     reads=b_xc + [b_tflag], writes=[b_state])

            def tail_a(t):
                for j in range(NJ):
                    P.op("dve", lambda e, j=j: e.tensor_tensor(out=yb[:, j, :], in0=yb[:, j, :], in1=xc[:, j, :], op=ALU.mult), reads=[b_yb, b_xc[j]], writes=[b_yb])

            def tail_b(t):
                xt, b_xt = xts[t % 2], b_xts[t % 2]
                for s_ in range(4):
                    for nh in range(2):
                        pp, bp = nextpz()
                        for jc in range(NJ):
                            P.op("pe", lambda e, jc=jc, s_=s_, nh=nh, pp=pp: e.matmul(pp[:], lhsT=yb[:, jc, s_ * 128:(s_ + 1) * 128], rhs=w_out_sb[:, jc, nh * 512:(nh + 1) * 512],
                                                                                  start=(jc == 0), stop=(jc == NJ - 1)),
                                 reads=[b_yb, b_w_out], writes=[bp], signal=(jc == NJ - 1))
                        resid_out(xt, b_xt, s_, nh, pp, bp, 0, S)
                P.dma("sp", H[(t - 3) * TT:(t - 2) * TT, :].rearrange("(s p) d -> p s d", p=128), xt[:], b_xt, reads=[b_xt])

            load_x(0)
            load_x(1)
            norm_a(xts[0], b_xts[0], S)
            norm_b(0, S)
            for t in range(ntile_p1):
                prev_full = (t - 1) >= 3
                if prev_full:
                    tail_a(t - 1)
                conv_part(t, 0, 6)
                if prev_full:
                    tail_b(t - 1)
                    if t + 1 < ntile_p1:
                        load_x(t + 1)
                elif 1 <= t and t + 1 < ntile_p1:
                    load_x(t + 1)
                conv_part(t, 6, NJ)
                gates_part(t)
                if t + 1 < ntile_p1:
                    norm_b(0, S)
            tail_a(ntile_p1 - 1)
            tail_b(ntile_p1 - 1)
            P.barrier()
        if stop in ("p1", "p1a"):
            return nc

        def mlp_phase(l, tiles, m, gi, final):
            with ExitStack() as ph:
                w1_sb, w1_bufs = load_w_cols(ph, "w1_sb%d" % l, w1_d[l].rearrange("(kc p) n -> p kc n", p=128), [128, 8, DFF],
                                             [(i * 512, (i + 1) * 512) for i in range(8)])
                w2_sb, w2_bufs = load_w_cols(ph, "w2_sb%d" % l, w2_d[l].rearrange("(fc p) n -> p fc n", p=128), [128, 32, D], [(0, 512), (512, 1024)])
                S = alloc_norm(ph, "m%d" % l)
                S["g"], S["b_g"] = load_g(ph, "m%dg" % l, gi)
                xt = SB(ph, "m%dxt" % l, [128, 4, D], F32); b_xt = [Buf("xt%d" % i) for i in range(4)]
                hid = SB(ph, "m%dhid" % l, [128, 32, TT], BF16); b_hid = Buf("hid")
                S["xn"] = hid[:, 0:8, :].rearrange("p a c -> p (a c)").rearrange("p (s d) -> p s d", s=4)
                S["b_xn"] = b_hid
                rl = [SB(ph, "m%drl%d" % (l, i), [128, TT], BF16) for i in range(3)]; b_rl = [Buf("rl%d" % i) for i in range(3)]
                S["rtmp"] = SB(ph, "m%drtmp" % l, [128, 512], F32); S["b_rtmp"] = Buf("rtmp")
                pz = [PS(ph, "m%dpz%d" % (l, i), [128, TT], F32) for i in range(6)]; b_pz = [Buf("pz%d" % i) for i in range(6)]
                pzi = [0]

                def nextpz():
                    i = pzi[0] % 6
                    pzi[0] += 1
                    return pz[i], b_pz[i]

                for t in tiles:
                    hrow = H[(t - 3) * TT:(t - 2) * TT, :]
                    load_rows(xt, b_xt, hrow)
                    norm_T(xt, b_xt, m, S)
                    hnT, b_hnT = S["hnT"], S["b_hnT"]
                    for fc in range(32):
                        pp, bp = nextpz()
                        for kc in range(8):
                            P.op("pe", lambda e, kc=kc, fc=fc, pp=pp: e.matmul(pp[:], lhsT=w1_sb[:, kc, fc * 128:(fc + 1) * 128], rhs=hnT[:, kc, :], start=(kc == 0), stop=(kc == 7)),
                                 reads=w1_bufs(fc * 128, fc * 128 + 128) + [b_hnT], writes=[bp], signal=(kc == 7))
                        r_, br = rl[fc % 3], b_rl[fc % 3]
                        P.op("act", lambda e, r_=r_, pp=pp: e.activation(out=r_[:], in_=pp[:], func=AF.Relu), reads=[bp], writes=[br])
                        P.op("dve", lambda e, r_=r_, fc=fc: e.tensor_tensor(out=hid[:, fc, :], in0=r_[:], in1=r_[:], op=ALU.mult), reads=[br], writes=[b_hid])
                    for nh in range(2):
                        for s in range(4):
                            pp, bp = nextpz()
                            for fc in range(32):
                                P.op("pe", lambda e, fc=fc, s=s, nh=nh, pp=pp: e.matmul(pp[:], lhsT=hid[:, fc, s * 128:(s + 1) * 128], rhs=w2_sb[:, fc, nh * 512:(nh + 1) * 512],
                                                                                    start=(fc == 0), stop=(fc == 31)),
                                     reads=[b_hid] + w2_bufs(nh * 512, (nh + 1) * 512), writes=[bp], signal=(fc == 31))
                            resid_out(xt, b_xt, s, nh, pp, bp, gi, S)
                            if nh == 1:
                                store_rows(xt, b_xt, out_d[(t - 4) * TT:(t - 3) * TT, :] if final else hrow, s)
                P.barrier()

        mlp_phase(0, range(3, 8), 1, 1, False)
        if stop == "p2":
            return nc

        with ExitStack() as ph:
            blkb = SB(ph, "blkb", [128, 128], BF16); b_blk = Buf("blkb")
            P.dma("pool", blkb[:], blk_d[:, :], b_blk, writes=[b_blk])
            expB = SB(ph, "expB", [128, 16, 640], BF16); b_expB = Buf("expB")
            KT = SB(ph, "KT", [128, 8, NKV * TT], BF16); b_KT = Buf("KT")
            Va = SB(ph, "Va", [128, NKV * 4, 16, 65], BF16); b_Va = Buf("Va")
            S = alloc_norm(ph, "p3")
            xt = SB(ph, "p3xt", [128, 4, D], F32); b_xt = [Buf("xt%d" % i) for i in range(4)]
            sq = [SB(ph, "p3sq%d" % i, [128, TT], BF16) for i in range(3)]; b_sq = [Buf("sq%d" % i) for i in range(3)]
            rs = [SB(ph, "p3rs%d" % i, [128, TT], F32) for i in range(3)]; b_rs = [Buf("rs%d" % i) for i in range(3)]
            pSt = [PS(ph, "p3pS%d" % i, [128, 1024], F32) for i in range(2)]
            pOt = [PS(ph, "p3pO%d" % i, [128, 4, 128], F32) for i in range(2)]
            b_bank = [Buf("bank%d" % i) for i in range(6)]
            pS = pSt; b_pS = [[b_bank[0], b_bank[1]], [b_bank[2], b_bank[3]]]
            pO = pOt; b_pO = [b_bank[4], b_bank[5]]
            pz = [pSt[0][:, 0:512], pSt[0][:, 512:1024], pSt[1][:, 0:512], pSt[1][:, 512:1024],
                  pOt[0][:].rearrange("p a b -> p (a b)"), pOt[1][:].rearrange("p a b -> p (a b)")]
            pzi = [0]

            def nextpz():
                i = pzi[0] % 6
                pzi[0] += 1
                return pz[i], b_bank[i]

            def proj_headnorm(w_sb, w_bufs, hnT, b_hnT, dstT, b_dst, col0, gcol):
                for g0 in range(0, 8, 3):
                    fcs = list(range(g0, min(8, g0 + 3)))
                    pps, pms = [], []
                    for fc in fcs:
                        pp, bp = nextpz()
                        for kc in range(8):
                            P.op("pe", lambda e, kc=kc, fc=fc, pp=pp: e.matmul(pp[:], lhsT=w_sb[:, kc, fc * 128:(fc + 1) * 128], rhs=hnT[:, kc, :], start=(kc == 0), stop=(kc == 7)),
                                 reads=w_bufs(fc * 128, fc * 128 + 128) + [b_hnT], writes=[bp], signal=(kc == 7))
                        pps.append((pp, bp))
                    for i, fc in enumerate(fcs):
                        pp, bp = pps[i]
                        P.op("act", lambda e, i=i, pp=pp: e.activation(out=sq[i][:], in_=pp[:], func=AF.Square), reads=[bp], writes=[b_sq[i]])
                    for i, fc in enumerate(fcs):
                        pm_, bpm = nextpz()
                        P.op("pe", lambda e, i=i, pm_=pm_: e.matmul(pm_[:], lhsT=blkb[:], rhs=sq[i][:], start=True, stop=True), reads=[b_blk, b_sq[i]], writes=[bpm])
                        pms.append((pm_, bpm))
                    for i, fc in enumerate(fcs):
                        pm_, bpm = pms[i]
                        P.op("act", lambda e, i=i, pm_=pm_: e.activation(out=rs[i][:], in_=pm_[:], func=AF.Ln, bias=S["epsb"][:, 1:2]), reads=[bpm, S["b_epsb"]], writes=[b_rs[i]])
                    for i, fc in enumerate(fcs):
                        P.op("act", lambda e, i=i: e.activation(out=rs[i][:], in_=rs[i][:], func=AF.Exp, scale=-0.5), reads=[b_rs[i]], writes=[b_rs[i]])
                    for i, fc in enumerate(fcs):
                        pp, bp = pps[i]
                        P.op("dve", lambda e, i=i, fc=fc, pp=pp: e.scalar_tensor_tensor(out=dstT[:, fc, col0:col0 + TT], in0=pp[:], scalar=gqk[:, gcol:gcol + 1], in1=rs[i][:],
                                                                                 op0=ALU.mult, op1=ALU.mult),
                             reads=[bp, b_gqk, b_rs[i]], writes=[b_dst])

            with ExitStack() as p1s:
                kvw_sb, kvw_bufs = load_w_cols(p1s, "kvw_sb", kv_w_d.rearrange("(kc p) n -> p kc n", p=128), [128, 8, 2 * D],
                                               [(i * 512, (i + 1) * 512) for i in range(4)])
                amask = SB(p1s, "amask", [128, 640], F32); b_amask = Buf("amask")
                P.dma("sp", amask[:], amask_d[:, :], b_amask, writes=[b_amask])
                btmp = [SB(p1s, "p3btmp%d" % i, [128, 640], F32) for i in range(2)]; b_btmp = [Buf("btmp0"), Buf("btmp1")]
                xn = SB(p1s, "p3xn", [128, 4, D], BF16); S["xn"] = xn; S["b_xn"] = Buf("xn")
                biasv = biasT_d.rearrange("p (h x) -> p h x", h=16)

                def build_expB():
                    for h in range(16):
                        bt, bbt = btmp[h % 2], b_btmp[h % 2]
                        P.dma("pool", bt[:], biasv[:, h, :], bbt, writes=[bbt])
                        P.op("act", lambda e, bt=bt: e.activation(out=bt[:], in_=bt[:], func=AF.Exp), reads=[bbt], writes=[bbt])
                        P.op("dve", lambda e, h=h, bt=bt: e.tensor_tensor(out=expB[:, h, :], in0=bt[:], in1=amask[:], op=ALU.mult), reads=[bbt, b_amask], writes=[b_expB])
                P.op("dve", lambda e: e.memset(Va[:, :, :, 64:65], 1.0), writes=[b_Va])
                P.op("dve", lambda e: e.tensor_scalar(out=Va[:, 0:4, :, 64:65], in0=Va[:, 0:4, :, 64:65], scalar1=tflag[:, 4:5], scalar2=None, op0=ALU.mult),
                     reads=[b_tflag, b_Va], writes=[b_Va])
                for t in range(3, 8):
                    kt0 = t - 3
                    hrow = H[kt0 * TT:(kt0 + 1) * TT, :]
                    load_rows(xt, b_xt, hrow)
                    norm_T(xt, b_xt, 2, S)
                    hnT, b_hnT = S["hnT"], S["b_hnT"]
                    proj_headnorm(kvw_sb, kvw_bufs, hnT, b_hnT, KT, b_KT, kt0 * TT, 0)
                    for s in range(4):
                        for nh in range(2):
                            pp, bp = nextpz()
                            for kc in range(8):
                                P.op("pe", lambda e, kc=kc, s=s, nh=nh, pp=pp: e.matmul(pp[:], lhsT=hnT[:, kc, s * 128:(s + 1) * 128], rhs=kvw_sb[:, kc, D + nh * 512:D + (nh + 1) * 512],
                                                                                    start=(kc == 0), stop=(kc == 7)),
                                     reads=kvw_bufs(D + nh * 512, D + (nh + 1) * 512) + [b_hnT], writes=[bp], signal=(kc == 7))
                            if t == 3:
                                P.op("act", lambda e, s=s, nh=nh, pp=pp, kt0=kt0: e.activation(out=Va[:, kt0 * 4 + s, nh * 8:(nh + 1) * 8, 0:64], in_=pp[:].rearrange("p (h d) -> p h d", d=64),
                                                                                           func=AF.Identity, scale=tflag[:, 4:5]),
                                     reads=[bp, b_tflag], writes=[b_Va])
                            else:
                                P.op("act", lambda e, s=s, nh=nh, pp=pp, kt0=kt0: e.activation(out=Va[:, kt0 * 4 + s, nh * 8:(nh + 1) * 8, 0:64], in_=pp[:].rearrange("p (h d) -> p h d", d=64),
                                                                                           func=AF.Copy),
                                     reads=[bp], writes=[b_Va])
                    if t == 3:
                        build_expB()
                P.barrier()
            if stop == "p3a":
                return nc
            with ExitStack() as p2s:
                wq_sb, wq_bufs = load_w_cols(p2s, "wq_sb", wq_d.rearrange("(kc p) n -> p kc n", p=128), [128, 8, D], [(0, 512), (512, 1024)])
                wo_sb, b_wo = load_w(p2s, "wo_sb", wo_d.rearrange("(kc p) n -> p kc n", p=128), [128, 8, D], 2)
                S["g"], S["b_g"] = load_g(p2s, "p3g", 2)
                QT = SB(p2s, "QT", [128, 8, TT], BF16); b_QT = Buf("QT")
                S["rtmp"] = SB(p2s, "p3rtmp", [128, 512], F32); S["b_rtmp"] = Buf("rtmp")
                Eb = [SB(p2s, "p3E%d" % i, [128, 640], BF16) for i in range(3)]; b_E = [Buf("E%d" % i) for i in range(3)]
                Pb = [SB(p2s, "p3P%d" % i, [128, 640], BF16) for i in range(3)]; b_P = [Buf("P%d" % i) for i in range(3)]
                rec = SB(p2s, "p3rec", [128, 2, 4], F32); b_rec = [Buf("rec0"), Buf("rec1")]
                attn = SB(p2s, "p3attn", [128, D], BF16); b_attn = Buf("attn")
                attnT = SB(p2s, "p3attnT", [128, 8, TT], BF16); b_attnT = Buf("attnT")
                S["xn"] = attnT[:].rearrange("p a c -> p (a c)").rearrange("p (s d) -> p s d", s=4); S["b_xn"] = b_attnT
                for t in range(4, 8):
                    kt0 = t - 3
                    hrow = H[kt0 * TT:(kt0 + 1) * TT, :]
                    load_rows(xt, b_xt, hrow)
                    norm_T(xt, b_xt, 3, S)
                    hnT, b_hnT = S["hnT"], S["b_hnT"]
                    proj_headnorm(wq_sb, wq_bufs, hnT, b_hnT, QT, b_QT, 0, 1)
                    items = [(qg, h) for qg in range(4) for h in range(16)]

                    def emit_S(idx):
                        qg, h = items[idx]
                        G = kt0 * 4 + qg
                        fc, hf = h // 2, h % 2
                        pS_, bpS = pS[idx % 2], b_pS[idx % 2]
                        for kt in range(5):
                            P.op("pe", lambda e, kt=kt: e.matmul(pS_[:, kt * 128:(kt + 1) * 128],
                                                                 lhsT=KT[hf * 64:(hf + 1) * 64, fc, (G - 4 + kt) * 128:(G - 3 + kt) * 128],
                                                                 rhs=QT[hf * 64:(hf + 1) * 64, fc, qg * 128:(qg + 1) * 128], start=True, stop=True),
                                 reads=[b_KT, b_QT], writes=bpS, signal=(kt == 4))

                    emit_S(0)
                    for idx, (qg, h) in enumerate(items):
                        G = kt0 * 4 + qg
                        h4, hh = h // 4, h % 4
                        if idx + 1 < len(items):
                            emit_S(idx + 1)
                        pS_, bpS = pS[idx % 2], b_pS[idx % 2]
                        pO_, bpO = pO[h4 % 2], b_pO[h4 % 2]
                        E_, bE = Eb[idx % 3], b_E[idx % 3]
                        P_, bP = Pb[idx % 3], b_P[idx % 3]
                        P.op("act", lambda e, E_=E_, pS_=pS_: e.activation(out=E_[:], in_=pS_[:, 0:640], func=AF.Exp), reads=bpS, writes=[bE])
                        P.op("dve", lambda e, E_=E_, P_=P_, h=h: e.tensor_tensor(out=P_[:], in0=E_[:], in1=expB[:, h, :], op=ALU.mult), reads=[bE, b_expB], writes=[bP])
                        for kt in range(5):
                            P.op("pe", lambda e, kt=kt, hh=hh, h=h, G=G, P_=P_, pO_=pO_: e.matmul(pO_[:, hh, 0:65], lhsT=P_[:, kt * 128:(kt + 1) * 128], rhs=Va[:, G - 4 + kt, h, :],
                                                                                           start=(kt == 0), stop=(kt == 4)),
                                 reads=[bP, b_Va], writes=[bpO], signal=(kt == 4))
                        if hh == 3:
                            r_ = h4 % 2
                            P.op("dve", lambda e, r_=r_, pO_=pO_: e.reciprocal(out=rec[:, r_, :], in_=pO_[:, :, 64]), reads=[bpO], writes=[b_rec[r_]])
                            P.op("dve", lambda e, r_=r_, pO_=pO_, h4=h4: e.tensor_tensor(out=attn[:, h4 * 256:(h4 + 1) * 256].rearrange("p (h d) -> p h d", d=64), in0=pO_[:, :, 0:64],
                                                                                    in1=rec[:, r_, :].unsqueeze(2).to_broadcast([128, 4, 64]), op=ALU.mult),
                                 reads=[bpO, b_rec[r_]], writes=[b_attn])
                        if h == 15:
                            for fc in range(8):
                                pT, bT = S["pT"][fc % 2], S["b_pT"][fc % 2]
                                P.op("pe", lambda e, fc=fc, pT=pT: e.transpose(out=pT[:, 0:128], in_=attn[:, fc * 128:(fc + 1) * 128], identity=identb[:]), reads=[b_attn, b_identb], writes=[bT])
                                P.op("act", lambda e, fc=fc, pT=pT, qg=qg: e.activation(out=attnT[:, fc, qg * 128:(qg + 1) * 128], in_=pT[:, 0:128], func=AF.Copy), reads=[bT], writes=[b_attnT])

                    for s in range(4):
                        for nh in range(2):
                            pp, bp = nextpz()
                            for kc in range(8):
                                P.op("pe", lambda e, kc=kc, s=s, nh=nh, pp=pp: e.matmul(pp[:], lhsT=attnT[:, kc, s * 128:(s + 1) * 128], rhs=wo_sb[:, kc, nh * 512:(nh + 1) * 512],
                                                                                    start=(kc == 0), stop=(kc == 7)),
                                     reads=[b_attnT, b_wo], writes=[bp], signal=(kc == 7))
                            resid_out(xt, b_xt, s, nh, pp, bp, 2, S)
                        store_rows(xt, b_xt, hrow, s)
                P.barrier()
        if stop == "p3":
            return nc

        mlp_phase(1, range(4, 8), 4, 3, True)
    return nc


def _host_consts():
    ident = np.eye(128, dtype=np.float32)
    blk = np.zeros((128, 128), np.float32)
    blk[:64, :64] = 1.0 / 64
    blk[64:, 64:] = 1.0 / 64
    kk = np.arange(128)[:, None, None]
    kt = np.arange(5)[None, :, None]
    q = np.arange(128)[None, None, :]
    kidx = kt * 128 + kk
    ck = kidx // 64
    cq = q // 64
    amask = ((ck >= cq) & (ck <= cq + 8)).astype(np.float32)
    bidx = np.minimum(640 + q - kidx, 256)
    bidx = np.maximum(bidx, 0)
    return ident, blk, amask.reshape(128, 640), bidx


def _pvec(v, n):
    return np.ascontiguousarray(np.asarray(v, np.float32).reshape(n, 128).T)


def make_in_maps(inputs):
    f = lambda k: np.ascontiguousarray(np.asarray(inputs[k], dtype=np.float32))
    x = f("x"); c = f("c")
    ident, blk, amask, bidx = _host_consts()
    rel_bias = f("rel_bias")[0]
    biasT = rel_bias[:, bidx]
    biasT = np.ascontiguousarray(biasT.transpose(1, 0, 2, 3).reshape(128, 16 * 640))
    conv_w = f("lru_conv_w")[0]
    convw = np.ascontiguousarray(conv_w.reshape(4, NJ, 128).transpose(2, 1, 0).reshape(128, NJ * 4))
    vec_parts = {
        "n1g0": _pvec(f("norm1_g")[0], 8), "n2g0": _pvec(f("norm2_g")[0], 8), "kvg": _pvec(f("kv_norm_g"), 8),
        "n1g1": _pvec(f("norm1_g")[1], 8), "n2g1": _pvec(f("norm2_g")[1], 8), "convw": convw,
        "convb": _pvec(f("lru_conv_b")[0], NJ), "gab": _pvec(f("lru_gate_a_b")[0], NJ), "gxb": _pvec(f("lru_gate_x_b")[0], NJ),
        "lam": _pvec(f("lru_lambda")[0], NJ),
        "gk": np.tile(f("k_norm_g"), 2).reshape(128, 1), "gq": np.tile(f("q_norm_g")[0], 2).reshape(128, 1),
    }
    vecs = np.zeros((128, NV), np.float32)
    for k, (o, n) in VOFF.items():
        vecs[:, o:o + n] = vec_parts[k]
    shared = {
        "vecs": vecs, "ada_w": f("ada_w"), "ada_b": f("ada_b"), "kv_ada_w": f("kv_ada_w"), "kv_ada_b": f("kv_ada_b").reshape(1, -1),
        "w_in": f("lru_w_in")[0], "ga_w": f("lru_gate_a_w")[0], "gx_w": f("lru_gate_x_w")[0], "w_out": f("lru_w_out")[0],
        "mlp_w1": f("mlp_w1"), "mlp_w2": f("mlp_w2"), "kv_w": f("kv_w"), "wq": f("attn_w_q")[0], "wo": f("attn_w_o")[0],
        "biasT": biasT, "amask": amask, "ident": ident, "blk64": blk,
    }
    maps = []
    for core in range(8):
        b, half = core // 2, core % 2
        if half == 1:
            xin = x[b]
        else:
            xin = np.concatenate([np.zeros((2048, D), np.float32), x[b, :2048]], axis=0)
        tflag = np.ones((128, NTILE + 1), np.float32)
        tflag[:, 4] = float(half)
        m = dict(shared)
        m["xin"] = np.ascontiguousarray(xin)
        m["ct"] = _pvec(c[b], 8)
        m["tflag"] = tflag
        maps.append(m)
    return maps


def kernel(**inputs):
    nc = build()
    maps = make_in_maps(inputs)
    res = run_bass_kernel_spmd(nc, maps, core_ids=list(range(8)))
    out = np.zeros((4, 4096, D), np.float32)
    for core in range(8):
        b, half = core // 2, core % 2
        out[b, half * 2048:(half + 1) * 2048] = res.results[core]["out"]
    return out
```
